# Optimizing a Trainium2 kernel written in Bass

```python
import jax, jax.numpy as jnp
from jax import lax
import numpy as np

D_MODEL = 1024
BATCH = 8
SEQ = 2048
DEPTH = 2

HEAD_SIZE = 64
D_RWKV = D_MODEL
N_HEADS_RWKV = D_RWKV // HEAD_SIZE
DECAY_RANK = 64
AAA_RANK = 64
GATE_RANK = 128
VRES_RANK = 32
D_CONV = D_MODEL
CONV_WIDTH = 31
D_FF = 4 * D_MODEL

RMS_EPS = 1e-6
LN_EPS = 1e-5
GN_EPS = 64e-5

N_SHIFT = 3 * D_RWKV + DECAY_RANK + AAA_RANK + GATE_RANK
N_COLS = N_SHIFT + 2 * D_CONV + D_RWKV + D_CONV
RWKV_SPLITS = [D_RWKV, 2 * D_RWKV, 3 * D_RWKV, 3 * D_RWKV + DECAY_RANK, 3 * D_RWKV + DECAY_RANK + AAA_RANK]

kernel_name = "rwkv7_conformer_gated_hybrid"


def _rms_norm(x, g):
    xf = x.astype(jnp.float32)
    y = xf * lax.rsqrt(jnp.mean(xf * xf, axis=-1, keepdims=True) + RMS_EPS)
    return y.astype(x.dtype) * g


def _layer_norm(z, g, b):
    zf = z.astype(jnp.float32)
    mu = jnp.mean(zf, axis=-1, keepdims=True)
    var = jnp.mean(jnp.square(zf - mu), axis=-1, keepdims=True)
    return ((zf - mu) * lax.rsqrt(var + LN_EPS)).astype(z.dtype) * g + b


def _token_shift(p, mu):
    p_prev = jnp.pad(p, ((0, 0), (1, 0), (0, 0)))[:, :-1]
    return p + (p_prev - p) * mu


def _rwkv7_scan(r, decay, k, v, a, b):
    bsz, _, h, n = r.shape

    def step(S, inp):
        r_t, w_t, k_t, v_t, a_t, b_t = inp
        sa = jnp.einsum('bhij,bhj->bhi', S, a_t)
        S = S * w_t[:, :, None, :] + sa[..., None] * b_t[:, :, None, :] + v_t[..., None] * k_t[:, :, None, :]
        return S, jnp.einsum('bhij,bhj->bhi', S, r_t)

    seq_major = tuple(jnp.moveaxis(t, 1, 0) for t in (r, decay, k, v, a, b))
    s0 = jnp.zeros((bsz, h, n, n), jnp.float32)
    _, y = lax.scan(step, s0, seq_major)
    return jnp.moveaxis(y, 0, 1)


def _rwkv7_time_mix(xs, v_first, vres, w0, w_decay_up, a0, w_aaa_up, w_gate_up,
                    k_k, k_a, r_k, gn_gain, gn_bias):
    bsz, t, _ = xs.shape
    r, k, v, w_lo, a_lo, g_lo = jnp.split(xs, RWKV_SPLITS, axis=-1)
    w = -jax.nn.softplus(-(w0 + jnp.tanh(w_lo) @ w_decay_up)) - 0.5
    a = jax.nn.sigmoid(a0 + a_lo @ w_aaa_up)
    g = jax.nn.sigmoid(g_lo) @ w_gate_up
    if v_first is None:
        v_first = v
    else:
        v_lo, v0, w_vres_up = vres
        v = v + (v_first - v) * jax.nn.sigmoid(v0 + v_lo @ w_vres_up)
    heads = lambda z: z.reshape(bsz, t, N_HEADS_RWKV, HEAD_SIZE).astype(jnp.float32)
    kk = heads(k * k_k)
    kk = kk / jnp.maximum(jnp.sqrt(jnp.sum(kk * kk, axis=-1, keepdims=True)), 1e-12)
    k = k * (1.0 + (a - 1.0) * k_a)
    rh, kh, vh, ah = heads(r), heads(k), heads(v), heads(a)
    decay = jnp.exp(-jnp.exp(heads(w)))
    y = _rwkv7_scan(rh, decay, kh, vh, -kk, kk * ah)
    mu = jnp.mean(y, axis=-1, keepdims=True)
    var = jnp.mean(jnp.square(y - mu), axis=-1, keepdims=True)
    y = (y - mu) * lax.rsqrt(var + GN_EPS)
    y = y.reshape(bsz, t, D_RWKV).astype(xs.dtype) * gn_gain + gn_bias
    bonus = (jnp.sum(rh * kh * r_k, axis=-1, keepdims=True) * vh).reshape(bsz, t, D_RWKV).astype(xs.dtype)
    return (y + bonus) * g, v_first


def _conformer_conv(u, conv_w, conv_b, ln_gain, ln_bias):
    glu = u[..., :D_CONV] * jax.nn.sigmoid(u[..., D_CONV:])
    z = lax.conv_general_dilated(
        glu, conv_w[:, None, :].astype(glu.dtype), window_strides=(1,),
        padding=((CONV_WIDTH - 1, 0),), dimension_numbers=('NWC', 'WIO', 'NWC'),
        feature_group_count=D_CONV) + conv_b
    return jax.nn.silu(_layer_norm(z, ln_gain, ln_bias))


def setup_inputs(seed: int = 0) -> dict:
    key = jax.random.key(seed)
    ks = iter(jax.random.split(key, 40))
    L, LV = DEPTH, DEPTH - 1

    def nrm(shape, s):
        return jax.random.normal(next(ks), shape, jnp.float32) * s

    def unif(shape, lo, hi):
        return jax.random.uniform(next(ks), shape, jnp.float32, lo, hi)

    return {
        "x": nrm((BATCH, SEQ, D_MODEL), 1.0),
        "c": nrm((BATCH, D_MODEL), 1.0),
        "norm_mix_gain": 1.0 + nrm((L, D_MODEL), 0.02),
        "norm_ffn_gain": 1.0 + nrm((L, D_MODEL), 0.02),
        "ada_w": nrm((L, D_MODEL, 6 * D_MODEL), 0.3 * D_MODEL ** -0.5),
        "ada_b": nrm((L, 6 * D_MODEL), 0.01),
        "w_in": nrm((L, D_MODEL, N_COLS), D_MODEL ** -0.5),
        "w_in_vres": nrm((LV, D_MODEL, VRES_RANK), D_MODEL ** -0.5),
        "mu_shift": unif((L, N_SHIFT), 0.0, 1.0),
        "mu_vres": unif((LV, VRES_RANK), 0.0, 1.0),
        "w0": unif((L, D_RWKV), -6.0, -1.0),
        "w_decay_up": nrm((L, DECAY_RANK, D_RWKV), 0.5 * DECAY_RANK ** -0.5),
        "a0": nrm((L, D_RWKV), 0.1),
        "w_aaa_up": nrm((L, AAA_RANK, D_RWKV), AAA_RANK ** -0.5),
        "w_gate_up": nrm((L, GATE_RANK, D_RWKV), GATE_RANK ** -0.5),
        "k_k": 0.85 + nrm((L, D_RWKV), 0.05),
        "k_a": 1.0 + nrm((L, D_RWKV), 0.05),
        "r_k": nrm((L, N_HEADS_RWKV, HEAD_SIZE), 0.1),
        "gn_gain": 1.0 + nrm((L, D_RWKV), 0.02),
        "gn_bias": nrm((L, D_RWKV), 0.01),
        "v0": nrm((LV, D_RWKV), 0.1),
        "w_vres_up": nrm((LV, VRES_RANK, D_RWKV), VRES_RANK ** -0.5),
        "conv_w": nrm((L, CONV_WIDTH, D_CONV), CONV_WIDTH ** -0.5),
        "conv_b": nrm((L, D_CONV), 0.01),
        "conv_ln_gain": 1.0 + nrm((L, D_CONV), 0.02),
        "conv_ln_bias": nrm((L, D_CONV), 0.01),
        "w_out": nrm((L, D_RWKV + D_CONV, D_MODEL), (D_RWKV + D_CONV) ** -0.5),
        "w_ff_in": nrm((L, D_MODEL, D_FF), D_MODEL ** -0.5),
        "w_ff_out": nrm((L, D_FF, D_MODEL), D_FF ** -0.5),
        "final_gain": 1.0 + nrm((D_MODEL,), 0.02),
    }


def reference(x, c, norm_mix_gain, norm_ffn_gain, ada_w, ada_b, w_in, w_in_vres, mu_shift, mu_vres,
              w0, w_decay_up, a0, w_aaa_up, w_gate_up, k_k, k_a, r_k, gn_gain, gn_bias, v0, w_vres_up,
              conv_w, conv_b, conv_ln_gain, conv_ln_bias, w_out, w_ff_in, w_ff_out, final_gain):
    c_act = jax.nn.silu(c)
    v_first = None
    for l in range(DEPTH):
        mod = c_act @ ada_w[l] + ada_b[l]
        sh_m, sc_m, gt_m, sh_f, sc_f, gt_f = [m[:, None, :] for m in jnp.split(mod, 6, axis=-1)]

        h = _rms_norm(x, norm_mix_gain[l]) * (1.0 + sc_m) + sh_m
        w_comb = w_in[l] if l == 0 else jnp.concatenate([w_in[l], w_in_vres[l - 1]], axis=1)
        proj = h @ w_comb
        rwkv_in = _token_shift(proj[..., :N_SHIFT], mu_shift[l])
        conv_in = proj[..., N_SHIFT:N_SHIFT + 2 * D_CONV]
        gates = jax.nn.sigmoid(proj[..., N_SHIFT + 2 * D_CONV:N_COLS])
        vres = None if l == 0 else (_token_shift(proj[..., N_COLS:], mu_vres[l - 1]), v0[l - 1], w_vres_up[l - 1])
        y_a, v_first = _rwkv7_time_mix(rwkv_in, v_first, vres, w0[l], w_decay_up[l], a0[l], w_aaa_up[l],
                                       w_gate_up[l], k_k[l], k_a[l], r_k[l], gn_gain[l], gn_bias[l])
        y_b = _conformer_conv(conv_in, conv_w[l], conv_b[l], conv_ln_gain[l], conv_ln_bias[l])
        merged = jnp.concatenate([y_a, y_b], axis=-1) * gates
        x = x + gt_m * (merged @ w_out[l])

        h = _rms_norm(x, norm_ffn_gain[l]) * (1.0 + sc_f) + sh_f
        x = x + gt_f * (jnp.square(jax.nn.relu(h @ w_ff_in[l])) @ w_ff_out[l])
    return _rms_norm(x, final_gain)
```

```python
import contextlib
import math
import numpy as np
import concourse.bass as bass
import concourse.mybir as mybir
from concourse.bass_utils import run_bass_kernel_spmd

F32 = mybir.dt.float32
BF16 = mybir.dt.bfloat16
AF = mybir.ActivationFunctionType
ALU = mybir.AluOpType

D = 1024
T = 2048
NB = 8
TT = 512
C = 128
NCH = TT // C
L = 2
NCORES = 8
N_SHIFT = 3328
N_COLS = 7424
RMS_EPS = 1e-6
LN_EPS = 1e-5
GN_EPS = 64e-5
CEXP = -math.exp(-0.5)

V_GMIX, V_GFFN, V_MU, V_MUV, V_W0, V_A0, V_KK, V_KA, V_RK, V_GNG, V_GNB, V_V0, V_CB, V_LNG, V_LNB, V_FG, V_ADAB, V_CW = (
    0, 8, 16, 42, 43, 51, 59, 67, 75, 83, 91, 99, 107, 115, 123, 131, 139, 187)
NV = 187 + 8 * 31
DV_MOD, DV_GSCM, DV_GSCF, DV_OMU = 0, 48, 56, 64
ND = 64 + 27
M_SHM, M_SCM, M_GTM, M_SHF, M_SCF, M_GTF = 0, 8, 16, 24, 32, 40

def _wbig_layout(l):
    off = 0
    lay = {}
    nlo = 256 + (32 if l >= 1 else 0)
    lay["lo"] = (off, 8, nlo); off += 8 * nlo
    for hp in range(8):
        lay["hp%d" % hp] = (off, 8, 512); off += 8 * 512
    for cb in range(8):
        lay["cb%d" % cb] = (off, 8, 384); off += 8 * 384
    for g in range(4):
        lay["wo%d" % g] = (off, 32, 128); off += 32 * 128
    for pc in range(8):
        lay["w1%d" % pc] = (off, 8, 512); off += 8 * 512
    for ob in range(8):
        lay["w2%d" % ob] = (off, 32, 128); off += 32 * 128
    return lay, off


N_DMA_SEMS = 24
ENGS = ("pe", "act", "dve", "pool", "sp")


class Sched:
    def __init__(self, nc):
        self.nc = nc
        self.ops = {e: [] for e in ENGS}
        self.cnt = {e: 0 for e in ENGS}
        self.known = {e: {} for e in ENGS}
        self.last_w = {}
        self.readers = {}
        self.dma_cnt = [0] * N_DMA_SEMS
        self.dma_rr = 0

    def _deps(self, reads, writes):
        deps = []
        for t in reads:
            w = self.last_w.get(t)
            if w is not None:
                deps.append(w)
        for t in writes:
            w = self.last_w.get(t)
            if w is not None:
                deps.append(w)
            deps.extend(self.readers.get(t, ()))
        return deps

    def _filter(self, eng, deps):
        need = {}
        for k, v in deps:
            if eng == "pe" and k == "pe":
                continue
            if v > need.get(k, 0):
                need[k] = v
        out = []
        kn = self.known[eng]
        for k, v in need.items():
            if kn.get(k, 0) >= v:
                continue
            kn[k] = v
            out.append((k, v))
        return out

    def _commit(self, reads, writes, me):
        for t in reads:
            self.readers.setdefault(t, []).append(me)
        for t in writes:
            self.last_w[t] = me
            self.readers[t] = []

    def op(self, eng, fn, reads=(), writes=()):
        waits = self._filter(eng, self._deps(reads, writes))
        self.cnt[eng] += 1
        me = (eng, self.cnt[eng])
        self.ops[eng].append((waits, fn, (eng, 1)))
        self._commit(reads, writes, me)

    def dma(self, queue, fn, reads=(), writes=()):
        k = self.dma_rr
        self.dma_rr = (self.dma_rr + 1) % N_DMA_SEMS
        semkey = ("dma", k)
        deps = self._deps(reads, writes)
        if self.dma_cnt[k] > 0:
            deps.append((semkey, self.dma_cnt[k]))
        waits = self._filter(queue, deps)
        self.dma_cnt[k] += 16
        me = (semkey, self.dma_cnt[k])
        self.ops[queue].append((waits, fn, (semkey, 16)))
        self._commit(reads, writes, me)

    def wait_all(self, eng, tokens):
        deps = [self.last_w[t] for t in tokens if t in self.last_w]
        waits = self._filter(eng, deps)
        self.ops[eng].append((waits, None, None))

    def emit(self, st):
        nc = self.nc
        sems = {}
        for e in ENGS:
            sems[e] = st.enter_context(nc.semaphore("s_" + e))
        for k in range(N_DMA_SEMS):
            sems[("dma", k)] = st.enter_context(nc.semaphore("s_dma%d" % k))
        block = st.enter_context(nc.Block())

        def run(eng_name):
            def body(engine):
                for waits, fn, inc in self.ops[eng_name]:
                    for k, v in waits:
                        engine.wait_ge(sems[k], v)
                    if fn is None:
                        continue
                    ins = fn(engine)
                    ins.then_inc(sems[inc[0]], inc[1])
            return body

        block.tensor(run("pe"))
        block.scalar(run("act"))
        block.vector(run("dve"))
        block.gpsimd(run("pool"))
        block.sync(run("sp"))


def _tok(ap):
    return ap.name


class KB:
    def __init__(self, ntiles=4, nlayers=2, dbg=None, stage=99):
        self.ntiles = ntiles
        self.nlayers = nlayers
        self.dbg_names = dbg or []
        self.stage = stage
        self.nc = bass.Bass("TRN2", target_bir_lowering=False)
        self.st = contextlib.ExitStack()
        self.S = Sched(self.nc)
        self.dbg_out = {}
        self.out_tokens = []
        self.psum_names = set()

    def sb(self, name, shape, dt):
        return self.st.enter_context(self.nc.sbuf_tensor(name, shape, dt))

    def ps(self, name, shape, dt):
        self.psum_names.add(name)
        return self.st.enter_context(self.nc.psum_tensor(name, shape, dt))

    def dram_in(self, name, shape, dt=F32):
        return self.nc.dram_tensor(name, shape, dt, kind="ExternalInput").ap()

    def dram_out(self, name, shape, dt=F32):
        return self.nc.dram_tensor(name, shape, dt, kind="ExternalOutput").ap()

    def _rw(self, outs, ins, r, w):
        reads = list(r) if r is not None else [_tok(a) for a in ins if hasattr(a, "name")]
        writes = list(w) if w is not None else [_tok(a) for a in outs]
        ex = [t for t in reads if t in self.psum_names]
        if ex:
            reads = [t for t in reads if t not in self.psum_names]
            writes = writes + [t for t in ex if t not in writes]
        return reads, writes

    def act(self, out, in_, func, scale=1.0, bias=0.0, r=None, w=None):
        extra = [a for a in (scale, bias) if hasattr(a, "name")]
        reads, writes = self._rw([out], [in_] + extra, r, w)
        self.S.op("act", lambda e: e.activation(out=out, in_=in_, func=func, scale=scale, bias=bias), reads, writes)

    def tt(self, eng, out, in0, in1, op, r=None, w=None):
        reads, writes = self._rw([out], [in0, in1], r, w)
        self.S.op(eng, lambda e: e.tensor_tensor(out=out, in0=in0, in1=in1, op=op), reads, writes)

    def ts(self, eng, out, in0, s1, s2, op0, op1=None, r=None, w=None):
        extra = [a for a in (s1, s2) if hasattr(a, "name")]
        reads, writes = self._rw([out], [in0] + extra, r, w)
        if op1 is None:
            self.S.op(eng, lambda e: e.tensor_scalar(out=out, in0=in0, scalar1=s1, scalar2=None, op0=op0), reads, writes)
        else:
            self.S.op(eng, lambda e: e.tensor_scalar(out=out, in0=in0, scalar1=s1, scalar2=s2, op0=op0, op1=op1), reads, writes)

    def stt(self, out, in0, scalar, in1, op0, op1, r=None, w=None):
        extra = [scalar] if hasattr(scalar, "name") else []
        reads, writes = self._rw([out], [in0, in1] + extra, r, w)
        self.S.op("dve", lambda e: e.scalar_tensor_tensor(out=out, in0=in0, scalar=scalar, in1=in1, op0=op0, op1=op1), reads, writes)

    def copy(self, eng, out, in_, r=None, w=None):
        reads, writes = self._rw([out], [in_], r, w)
        if eng == "act":
            self.S.op("act", lambda e: e.copy(out=out, in_=in_), reads, writes)
        else:
            self.S.op(eng, lambda e: e.tensor_copy(out=out, in_=in_), reads, writes)

    def recip(self, out, in_, r=None, w=None):
        reads, writes = self._rw([out], [in_], r, w)
        self.S.op("dve", lambda e: e.reciprocal(out=out, in_=in_), reads, writes)

    def scan(self, out, d0, d1, r=None, w=None):
        reads, writes = self._rw([out], [d0, d1], r, w)
        self.S.op("dve", lambda e: e.tensor_tensor_scan(out=out, data0=d0, data1=d1, initial=0.0, op0=ALU.mult, op1=ALU.add), reads, writes)

    def memset(self, eng, out, val, r=None, w=None):
        reads, writes = self._rw([out], [], r, w)
        self.S.op(eng, lambda e: e.memset(out, val), reads, writes)

    def mm(self, out, pairs, r=None, w=None):
        ins = []
        for a, b in pairs:
            ins += [a, b]
        reads, writes = self._rw([out], ins, r, w)
        n = len(pairs)

        def fn(e):
            last = None
            for i, (a, b) in enumerate(pairs):
                last = e.matmul(out, lhsT=a, rhs=b, start=(i == 0), stop=(i == n - 1))
            return last
        self.S.op("pe", fn, reads, writes)

    def tr(self, out, in_, ident, r=None, w=None):
        reads, writes = self._rw([out], [in_, ident], r, w)
        self.S.op("pe", lambda e: e.transpose(out=out, in_=in_, identity=ident), reads, writes)

    def dma(self, q, out, in_, r=None, w=None):
        reads, writes = self._rw([out], [in_], r, w)
        self.S.dma(q, lambda e: e.dma_start(out=out, in_=in_), reads, writes)

    def dump(self, name, ap, shape, dt=F32):
        if name not in self.dbg_names:
            return
        if dt != F32 or ap.dtype != F32:
            if not hasattr(self, "_dbgtmp"):
                self._dbgtmp = self.sb("dbgtmp", [128, 1024], F32)
            tmp = self._dbgtmp[0:shape[0], 0:shape[1]]
            self.copy("dve", tmp, ap)
            ap = tmp
        o = self.dram_out("dbg_" + name, list(shape))
        self.dma("sp", o, ap)
        self.out_tokens.append(_tok(o))
        self.dbg_out[name] = "dbg_" + name


def build(ntiles=4, nlayers=2, dbg=None, stage=99):
    K = KB(ntiles, nlayers, dbg, stage)
    nc = K.nc
    xT = K.dram_in("xT", [D, T])
    cT = K.dram_in("cT", [128, 8])
    ada = [K.dram_in("ada%d" % l, [D, 6 * D]) for l in range(L)]
    vecs_d = [K.dram_in("vecs%d" % l, [128, NV]) for l in range(L)]
    wsm_d = [K.dram_in("wsm%d" % l, [128, 3, D]) for l in range(L)]
    lay = [_wbig_layout(l) for l in range(L)]
    wbig = [K.dram_in("wbig%d" % l, [128, lay[l][1]]) for l in range(L)]
    outT = K.dram_out("outT", [D, T])

    sb, ps = K.sb, K.ps
    ident16 = sb("ident16", [128, 128], BF16)
    ones16 = sb("ones16", [128, 128], BF16)
    bd16 = sb("bd16", [128, 128], BF16)
    onesf = sb("onesf", [128, 128], F32)
    mask512 = sb("mask512", [128, 512], F32)
    masksl = sb("masksl", [128, 128], F32)
    rst = sb("rst", [128, 512], F32)
    vecs = [sb("vecs_s%d" % l, [128, NV], F32) for l in range(L)]
    dv = [sb("dv%d" % l, [128, ND], F32) for l in range(L)]
    wsm = [sb("wsm_s%d" % l, [128, 3, D], BF16) for l in range(L)]
    c32 = sb("c32", [128, 8], F32)
    c16 = sb("c16", [128, 8], BF16)
    S32 = [sb("S32_%d" % l, [128, 8, 64], F32) for l in range(L)]
    S16 = [sb("S16_%d" % l, [128, 8, 64], BF16) for l in range(L)]
    ctail = [sb("ctail%d" % l, [128, 8, 30], BF16) for l in range(L)]
    carry = [sb("carry%d" % l, [128, 27], F32) for l in range(L)]
    xt = [sb("xt%d" % k, [128, TT], F32) for k in range(NB)]
    ht = [sb("ht%d" % k, [128, TT], BF16) for k in range(NB)]
    merged = [sb("mg%d" % k, [128, TT], BF16) for k in range(16)]
    vf = [sb("vf%d" % k, [128, TT], BF16) for k in range(NB)]
    wpool = [sb("wp%d" % i, [128, 4096], BF16) for i in range(4)]
    lo16 = [sb("lo16_%d" % j, [128, TT], BF16) for j in range(2)]
    vlo16 = sb("vlo16", [32, TT], BF16)
    names32 = ["r32", "k32", "v32", "sg32", "a32", "g32", "sv32", "dd32", "cs32", "E1", "E2", "E3", "E4",
               "kk32", "nrm", "kkn", "km", "b32", "bon32", "y32", "yc", "sd"]
    W = {n: sb(n, [128, TT], F32) for n in names32}
    W["d2"] = W["dd32"]; W["d4"] = W["sv32"]; W["rn"] = W["nrm"]; W["t1"] = W["km"]; W["rsd"] = W["sd"]
    W["yn"] = W["yc"]; W["yg"] = W["yc"]; W["o1"] = W["yc"]; W["o2"] = W["yc"]
    names16 = ["gA16", "kk2", "kt16", "bt16", "Kh16", "Bh16", "v16", "rk16", "y16", "yc2"]
    H = {n: sb(n, [128, TT], BF16) for n in names16}
    sq16 = [H[n] for n in names16[0:8]]
    lo32 = [W["yc"], W["y32"]]
    vlo32 = W["bon32"]
    rs32 = W["nrm"]; rstd = W["sd"]; t32 = [W["yc"], W["y32"]]
    AR = sb("AR", [128, NCH, 2, C], BF16)
    KBtok = sb("KBtok", [128, 8, 128], BF16)
    Vtok = sb("Vtok", [128, 4, 128], BF16)
    Amat = [[sb("Amat%d_%d" % (h, c), [128, 512], BF16) for c in range(NCH)] for h in range(2)]
    Tfin = [[sb("Tfin%d_%d" % (h, c), [128, 128], BF16) for c in range(NCH)] for h in range(2)]
    Q0 = [sb("Q0_%d" % i, [128, 128], BF16) for i in range(4)]
    PQT = [[sb("PQT%d_%d" % (i, j), [128, 3, 128], BF16) for j in range(2)] for i in range(4)]
    XT16 = sb("XT16", [128, 128], BF16)
    UT16 = sb("UT16", [128, 128], BF16)
    sgb = W["E4"]
    gbuf = sb("gbuf", [128, 30 + TT], BF16)
    dg = sb("dg", [128, 31, 128], BF16)
    z32 = [W[n] for n in ("r32", "k32", "v32", "sg32", "a32", "g32", "sv32", "dd32")]

    def halves(t):
        v = t[:].bitcast(BF16)
        return [v[:, 0:TT], v[:, TT:2 * TT]]
    gB16v = []
    for n in ("cs32", "E1", "E2", "E3"):
        gB16v += halves(W[n])
    hidv = []
    for n in ("r32", "k32", "v32", "sg32", "a32", "g32", "sv32", "dd32", "cs32", "E1", "E2", "E3", "E4", "kk32", "kkn", "km"):
        hidv += halves(W[n])
    mmb = [ps("mmb%d" % i, [128, 512], F32) for i in range(2)]
    dblb = [ps("dblb%d" % i, [128, 512], F32) for i in range(4)]
    seqb = ps("seqb", [128, 512], F32)
    yb = ps("yb", [128, 512], F32)
    mm_rr = [0]

    def bank():
        b = mmb[mm_rr[0] % 2]
        mm_rr[0] += 1
        return b

    def dbl_region(i, j):
        if j < 2:
            col = (i % 2) * 256 + j * 128
            return dblb[i // 2][:, col:col + 128]
        return dblb[2][:, i * 128:(i + 1) * 128]

    wp_rr = [0]

    def load_piece(l, key):
        off, k, c = lay[l][0][key]
        buf = wpool[wp_rr[0] % 4]
        wp_rr[0] += 1
        src = wbig[l][:, off:off + k * c].rearrange("p (k c) -> p k c", c=c)
        dst = buf[:, 0:k * c].rearrange("p (k c) -> p k c", c=c)
        K.dma("pool", dst, src)
        return dst

    K.memset("pool", onesf[:], 1.0)
    K.memset("pool", ones16[:], 1.0)
    K.memset("pool", bd16[:], 0.0)
    K.memset("pool", bd16[0:64, 0:64], 1.0)
    K.memset("pool", bd16[64:128, 64:128], 1.0)

    def asel(out, in_, pattern, op, base, cm):
        K.S.op("pool", lambda e: e.affine_select(out=out, in_=in_, pattern=pattern, compare_op=op, fill=0.0, base=base, channel_multiplier=cm),
               [_tok(in_)], [_tok(out)])
    asel(ident16[:], ones16[:], [[1, 128]], ALU.is_equal, 0, -1)
    for j in range(4):
        asel(mask512[:, j * 128:(j + 1) * 128], onesf[:, 0:128], [[1, 128]], ALU.is_gt if j % 2 == 0 else ALU.is_ge, 0, -1)
    asel(masksl[:], onesf[:, 0:128], [[-1, 128]], ALU.is_gt, 0, 1)
    K.memset("pool", rst[:], 1.0)
    for j in range(NCH):
        K.memset("pool", rst[:, j * C:j * C + 1], 0.0)
    for l in range(L):
        K.dma("sp", vecs[l][:], vecs_d[l])
        K.dma("pool", wsm[l][:], wsm_d[l])
        K.memset("pool", S32[l][:], 0.0)
        K.memset("pool", S16[l][:], 0.0)
        K.memset("pool", ctail[l][:], 0.0)
        K.memset("pool", carry[l][:], 0.0)
    K.dma("sp", c32[:], cT)
    K.act(c16[:], c32[:], AF.Silu)

    for l in range(K.nlayers):
        pmod = bank()
        adav = ada[l].rearrange("(k p) c -> p k c", p=128)
        for pc in range(12):
            buf = wpool[wp_rr[0] % 4]
            wp_rr[0] += 1
            dst = buf[:].rearrange("p (k c) -> p k c", c=512)
            K.dma("pool", dst, adav[:, :, pc * 512:(pc + 1) * 512])
            for m in range(4):
                j = pc * 4 + m
                K.mm(pmod[:, j:j + 1], [(dst[:, k, m * 128:(m + 1) * 128], c16[:, k:k + 1]) for k in range(8)])
        K.tt("dve", dv[l][:, DV_MOD:DV_MOD + 48], pmod[:, 0:48], vecs[l][:, V_ADAB:V_ADAB + 48], ALU.add)
        K.stt(dv[l][:, DV_GSCM:DV_GSCM + 8], dv[l][:, DV_MOD + M_SCM:DV_MOD + M_SCM + 8], 1.0, vecs[l][:, V_GMIX:V_GMIX + 8], ALU.add, ALU.mult)
        K.stt(dv[l][:, DV_GSCF:DV_GSCF + 8], dv[l][:, DV_MOD + M_SCF:DV_MOD + M_SCF + 8], 1.0, vecs[l][:, V_GFFN:V_GFFN + 8], ALU.add, ALU.mult)
        K.ts("dve", dv[l][:, DV_OMU:DV_OMU + 27], vecs[l][:, V_MU:V_MU + 27], -1.0, 1.0, ALU.mult, ALU.add)
        K.dump("dv%d" % l, dv[l][:], [128, ND])

    def rmsnorm_to_ht(l, gsc_col, sh_col):
        for k in range(NB):
            K.act(sq16[k][:], xt[k][:], AF.Square)
        p = bank()
        K.mm(p[:], [(ones16[:], sq16[k][:]) for k in range(NB)])
        K.act(rs32[:], p[:], AF.Sqrt, scale=1.0 / D, bias=RMS_EPS)
        K.recip(rstd[:], rs32[:])
        for k in range(NB):
            t = t32[k % 2]
            K.stt(t[:], xt[k][:], dv[l][:, gsc_col + k:gsc_col + k + 1], rstd[:], ALU.mult, ALU.mult)
            K.act(ht[k][:], t[:], AF.Identity, bias=dv[l][:, sh_col + k:sh_col + k + 1])

    def proj(wp, j0, M=128):
        p = bank()
        K.mm(p[0:M, :], [(wp[:, k, j0:j0 + M], ht[k][:]) for k in range(NB)])
        return p

    def shift(l, p, mucol, dst, M=128):
        mu = vecs[l][0:M, V_MU + mucol:V_MU + mucol + 1]
        omu = dv[l][0:M, DV_OMU + mucol:DV_OMU + mucol + 1]
        cy = carry[l][0:M, mucol:mucol + 1]
        K.act(dst[0:M, :], p[0:M, :], AF.Identity, scale=omu)
        K.stt(dst[0:M, 1:TT], p[0:M, 0:TT - 1], mu, dst[0:M, 1:TT], ALU.mult, ALU.add)
        K.stt(dst[0:M, 0:1], cy, mu, dst[0:M, 0:1], ALU.mult, ALU.add)
        K.copy("dve", cy, p[0:M, TT - 1:TT])

    xTv = xT.rearrange("(k p) t -> k p t", p=128)
    oTv = outT.rearrange("(k p) t -> k p t", p=128)
    for it in range(K.ntiles):
        t0 = it * TT
        for k in range(NB):
            K.dma("sp", xt[k][:], xTv[k, :, t0:t0 + TT])
        for l in range(K.nlayers):
            V = vecs[l]
            DVl = dv[l]
            rmsnorm_to_ht(l, DV_GSCM, DV_MOD + M_SHM)
            if it == 0 and l == 0:
                for k in (0, 7):
                    K.dump("ht%d" % k, ht[k][:], [128, TT])
            wlo = load_piece(l, "lo")
            for j in range(2):
                p = proj(wlo, j * 128)
                shift(l, p, 24 + j, lo32[j])
            K.act(lo16[0][0:64, :], lo32[0][0:64, :], AF.Tanh)
            K.copy("act", lo16[0][64:128, :], lo32[0][64:128, :])
            K.act(lo16[1][:], lo32[1][:], AF.Sigmoid)
            if l >= 1:
                p = proj(wlo, 256, M=32)
                shift(l, p, 26, vlo32, M=32)
                K.copy("act", vlo16[:], vlo32[0:32, :])
            if K.stage < 1:
                continue
            for hp in range(8):
                whp = load_piece(l, "hp%d" % hp)
                cs = slice(hp * 128, (hp + 1) * 128)
                p = proj(whp, 0);   shift(l, p, hp, W["r32"])
                p = proj(whp, 128); shift(l, p, 8 + hp, W["k32"])
                p = proj(whp, 256); shift(l, p, 16 + hp, W["v32"])
                p = proj(whp, 384); K.act(H["gA16"][:], p[:], AF.Sigmoid)
                p = bank(); K.mm(p[:], [(wsm[l][0:64, 0, cs], lo16[0][0:64, :])])
                K.act(W["sg32"][:], p[:], AF.Sigmoid, bias=V[:, V_W0 + hp:V_W0 + hp + 1])
                p = bank(); K.mm(p[:], [(wsm[l][64:128, 0, cs], lo16[0][64:128, :])])
                K.act(W["a32"][:], p[:], AF.Sigmoid, bias=V[:, V_A0 + hp:V_A0 + hp + 1])
                p = bank(); K.mm(p[:], [(wsm[l][:, 1, cs], lo16[1][:])])
                K.copy("act", W["g32"][:], p[:])
                if l >= 1:
                    p = bank(); K.mm(p[:], [(wsm[l][0:32, 2, cs], vlo16[:])])
                    K.act(W["sv32"][:], p[:], AF.Sigmoid, bias=V[:, V_V0 + hp:V_V0 + hp + 1])
                    K.tt("pool", W["dd32"][:], vf[hp][:], W["v32"][:], ALU.subtract)
                    K.tt("pool", W["dd32"][:], W["dd32"][:], W["sv32"][:], ALU.mult)
                    K.tt("pool", W["v32"][:], W["v32"][:], W["dd32"][:], ALU.add)
                else:
                    K.copy("pool", vf[hp][:], W["v32"][:])
                K.scan(W["cs32"][:], rst[:], W["sg32"][:])
                K.act(W["E1"][:], W["cs32"][:], AF.Exp, scale=CEXP)
                K.act(W["E3"][:], W["cs32"][:], AF.Exp, scale=-CEXP)
                K.tt("pool", W["d2"][:], W["cs32"][:], W["sg32"][:], ALU.subtract)
                K.act(W["E2"][:], W["d2"][:], AF.Exp, scale=CEXP)
                cs3 = W["cs32"][:].rearrange("p (c t) -> p c t", t=C)
                K.tt("pool", W["d4"][:].rearrange("p (c t) -> p c t", t=C), cs3, cs3[:, :, C - 1:C].to_broadcast([128, NCH, C]), ALU.subtract)
                K.act(W["E4"][:], W["d4"][:], AF.Exp, scale=-CEXP)
                K.act(W["kk32"][:], W["k32"][:], AF.Identity, scale=V[:, V_KK + hp:V_KK + hp + 1])
                K.act(H["kk2"][:], W["kk32"][:], AF.Square)
                p = bank(); K.mm(p[:], [(bd16[:], H["kk2"][:])])
                K.act(W["nrm"][:], p[:], AF.Sqrt)
                K.ts("dve", W["nrm"][:], W["nrm"][:], 1e-12, None, ALU.max)
                K.recip(W["rn"][:], W["nrm"][:])
                K.tt("pool", W["kkn"][:], W["kk32"][:], W["rn"][:], ALU.mult)
                K.ts("dve", W["t1"][:], W["a32"][:], -1.0, V[:, V_KA + hp:V_KA + hp + 1], ALU.add, ALU.mult)
                K.stt(W["km"][:], W["t1"][:], 1.0, W["k32"][:], ALU.add, ALU.mult)
                K.tt("pool", W["b32"][:], W["kkn"][:], W["a32"][:], ALU.mult)
                AR4 = AR[:]
                K.tt("dve", AR4[:, :, 1, :], W["r32"][:].rearrange("p (c t) -> p c t", t=C), W["E1"][:].rearrange("p (c t) -> p c t", t=C), ALU.mult)
                K.stt(AR4[:, :, 0, :], W["kkn"][:].rearrange("p (c t) -> p c t", t=C), -1.0, W["E2"][:].rearrange("p (c t) -> p c t", t=C), ALU.mult, ALU.mult)
                K.tt("pool", H["kt16"][:], W["km"][:], W["E3"][:], ALU.mult)
                K.tt("pool", H["bt16"][:], W["b32"][:], W["E3"][:], ALU.mult)
                K.tt("pool", H["Kh16"][:], W["km"][:], W["E4"][:], ALU.mult)
                K.tt("pool", H["Bh16"][:], W["b32"][:], W["E4"][:], ALU.mult)
                K.copy("pool", H["v16"][:], W["v32"][:])
                K.stt(H["rk16"][:], W["r32"][:], V[:, V_RK + hp:V_RK + hp + 1], W["km"][:], ALU.mult, ALU.mult)
                p = bank(); K.mm(p[:], [(bd16[:], H["rk16"][:])])
                K.tt("dve", W["bon32"][:], p[:], W["v32"][:], ALU.mult)
                trv = seqb[:].bitcast(BF16).rearrange("p (a t) -> p a t", t=128)
                for c in range(NCH):
                    K.tr(trv[:, c, :], H["Kh16"][:, c * C:(c + 1) * C], ident16[:])
                    K.tr(trv[:, 4 + c, :], H["Bh16"][:, c * C:(c + 1) * C], ident16[:])
                K.copy("act", KBtok[:], trv)
                for c in range(NCH):
                    K.tr(trv[:, c, :], H["v16"][:, c * C:(c + 1) * C], ident16[:])
                K.copy("dve", Vtok[:], trv[:, 0:4, :])
                if it == 0 and l == 0 and hp == 0:
                    for nm in ("r32", "k32", "v32", "sg32", "a32", "g32", "cs32", "E1", "E2", "E3", "E4", "kkn", "km", "b32", "bon32"):
                        K.dump(nm, W[nm][:], [128, TT])
                    K.dump("KBtok", KBtok[:].rearrange("p a t -> p (a t)"), [128, 1024], BF16)
                    K.dump("Vtok", Vtok[:].rearrange("p a t -> p (a t)"), [128, 512], BF16)
                if K.stage < 2:
                    continue
                for h in range(2):
                    hs = slice(h * 64, h * 64 + 64)
                    for c in range(NCH):
                        ck = slice(c * C, (c + 1) * C)
                        bk = dblb[c]
                        ar = AR[hs, c, :, :].rearrange("p a t -> p (a t)")
                        K.mm(bk[:, 0:256], [(H["bt16"][hs, ck], ar)])
                        K.mm(bk[:, 256:512], [(H["kt16"][hs, ck], ar)])
                        K.tt("dve", Amat[h][c][:], bk[:], mask512[:], ALU.mult)
                        K.mm(bk[:, 0:128], [(AR[hs, c, 0, :], H["bt16"][hs, ck])])
                        K.tt("dve", Q0[c][:], bk[:, 0:128], masksl[:], ALU.mult)
                        K.tt("pool", PQT[c][1][:, 2, :], Amat[h][c][:, 0:128], ident16[:], ALU.add)
                    for c in range(NCH):
                        bk = dblb[c]
                        P0 = Amat[h][c][:, 0:128]
                        K.mm(bk[:, 0:128], [(Q0[c][:], P0)])
                        K.mm(bk[:, 128:256], [(P0, Q0[c][:])])
                        K.copy("act" if c % 2 == 0 else "dve", PQT[c][1][:, 0:2, :], bk[:, 0:256].rearrange("p (a t) -> p a t", t=128))
                    for lv in range(1, 7):
                        cur, nxt = lv % 2, (lv + 1) % 2
                        for c in range(NCH):
                            bk = dblb[c]
                            Pk = PQT[c][cur][:, 0, :]
                            Qk = PQT[c][cur][:, 1, :]
                            Tk = PQT[c][cur][:, 2, :]
                            eng = "act" if (c + lv) % 2 == 0 else "dve"
                            if lv < 6:
                                K.mm(bk[:, 0:128], [(Qk, Pk)])
                                K.mm(bk[:, 128:256], [(Pk, Qk)])
                                K.mm(bk[:, 256:384], [(Qk, Tk), (ident16[:], Tk)])
                                K.copy(eng, PQT[c][nxt][:], bk[:, 0:384].rearrange("p (a t) -> p a t", t=128))
                            else:
                                K.mm(bk[:, 256:384], [(Qk, Tk), (ident16[:], Tk)])
                                K.copy(eng, Tfin[h][c][:], bk[:, 256:384])
                if it == 0 and l == 0 and hp == 0:
                    K.dump("Amat00", Amat[0][0][:], [128, 512], BF16)
                    K.dump("Tfin00", Tfin[0][0][:], [128, 128], BF16)
                    K.dump("Tfin13", Tfin[1][3][:], [128, 128], BF16)
                if K.stage < 3:
                    continue
                for c in range(NCH):
                    for h in range(2):
                        hs = slice(h * 64, h * 64 + 64)
                        K.mm(seqb[:, h * 64:(h + 1) * 64], [(AR[hs, c, 0, :], S16[l][hs, hp, :]),
                                                             (Amat[h][c][:, 256:384], Vtok[:, c, hs])])
                    K.copy("act", XT16[:], seqb[:, 0:128])
                    for h in range(2):
                        hs = slice(h * 64, h * 64 + 64)
                        K.mm(seqb[:, 128 + h * 64:128 + (h + 1) * 64], [(Tfin[h][c][:], XT16[:, hs])])
                    K.copy("dve", UT16[:], seqb[:, 128:256])
                    for h in range(2):
                        hs = slice(h * 64, h * 64 + 64)
                        K.mm(yb[hs, c * C:(c + 1) * C], [(S16[l][hs, hp, :], AR[hs, c, 1, :]),
                                                          (UT16[:, hs], Amat[h][c][:, 128:256]),
                                                          (Vtok[:, c, hs], Amat[h][c][:, 384:512])])
                        K.mm(seqb[hs, 256:320], [(KBtok[:, 4 + c, hs], UT16[:, hs]),
                                                  (KBtok[:, c, hs], Vtok[:, c, hs])])
                    wc = W["E1"][:, c * C + C - 1:c * C + C]
                    K.stt(S32[l][:, hp, :], S32[l][:, hp, :], wc, seqb[:, 256:320], ALU.mult, ALU.add)
                    K.copy("act", S16[l][:, hp, :], S32[l][:, hp, :])
                K.copy("act", W["y32"][:], yb[:])
                K.copy("dve", H["y16"][:], yb[:])
                if it == 0 and l == 0 and hp == 0:
                    K.dump("y32", W["y32"][:], [128, TT])
                p = bank(); K.mm(p[:], [(bd16[:], H["y16"][:])])
                K.stt(W["yc"][:], p[:], -1.0 / 64, W["y32"][:], ALU.mult, ALU.add)
                K.act(H["yc2"][:], W["yc"][:], AF.Square)
                p = bank(); K.mm(p[:], [(bd16[:], H["yc2"][:])])
                K.act(W["sd"][:], p[:], AF.Sqrt, scale=1.0 / 64, bias=GN_EPS)
                K.recip(W["rsd"][:], W["sd"][:])
                K.tt("pool", W["yn"][:], W["yc"][:], W["rsd"][:], ALU.mult)
                K.act(W["yg"][:], W["yn"][:], AF.Identity, scale=V[:, V_GNG + hp:V_GNG + hp + 1], bias=V[:, V_GNB + hp:V_GNB + hp + 1])
                K.tt("pool", W["o1"][:], W["yg"][:], W["bon32"][:], ALU.add)
                K.tt("pool", W["o2"][:], W["o1"][:], W["g32"][:], ALU.mult)
                K.tt("pool", merged[hp][:], W["o2"][:], H["gA16"][:], ALU.mult)
                if it == 0 and l == 0 and hp in (0, 7):
                    K.dump("mg%d" % hp, merged[hp][:], [128, TT], BF16)
            if K.stage < 4:
                continue
            for cb in range(8):
                wcb = load_piece(l, "cb%d" % cb)
                pa = proj(wcb, 0)
                pb = proj(wcb, 128)
                K.act(sgb[:], pb[:], AF.Sigmoid)
                K.copy("pool", gbuf[:, 0:30], ctail[l][:, cb, :])
                K.tt("dve", gbuf[:, 30:30 + TT], pa[:], sgb[:], ALU.mult)
                K.copy("pool", ctail[l][:, cb, :], gbuf[:, TT:TT + 30])
                pg = proj(wcb, 256)
                K.act(gB16v[cb], pg[:], AF.Sigmoid)
                K.tt("pool", dg[:], ident16[:].unsqueeze(1).to_broadcast([128, 31, 128]),
                     V[:, V_CW + cb * 31:V_CW + (cb + 1) * 31].unsqueeze(2).to_broadcast([128, 31, 128]), ALU.mult)
                pz = bank()
                K.mm(pz[:], [(dg[:, k, :], gbuf[:, k:k + TT]) for k in range(31)])
                K.act(z32[cb][:], pz[:], AF.Identity, bias=V[:, V_CB + cb:V_CB + cb + 1])
            if it == 0 and l == 0:
                K.dump("z0", z32[0][:], [128, TT])
            for cb in range(8):
                K.copy("act", sq16[cb][:], z32[cb][:])
            p = bank(); K.mm(p[:], [(ones16[:], sq16[cb][:]) for cb in range(8)])
            for cb in range(8):
                K.stt(z32[cb][:], p[:], -1.0 / D, z32[cb][:], ALU.mult, ALU.add)
            for cb in range(8):
                K.act(sq16[cb][:], z32[cb][:], AF.Square)
            p = bank(); K.mm(p[:], [(ones16[:], sq16[cb][:]) for cb in range(8)])
            K.act(rs32[:], p[:], AF.Sqrt, scale=1.0 / D, bias=LN_EPS)
            K.recip(rstd[:], rs32[:])
            for cb in range(8):
                t = t32[cb % 2]
                K.tt("pool", t[:], z32[cb][:], rstd[:], ALU.mult)
                K.act(t[:], t[:], AF.Silu, scale=V[:, V_LNG + cb:V_LNG + cb + 1], bias=V[:, V_LNB + cb:V_LNB + cb + 1])
                K.tt("pool", merged[8 + cb][:], t[:], gB16v[cb], ALU.mult)
            if it == 0 and l == 0:
                K.dump("mg8", merged[8][:], [128, TT], BF16)
            for g in range(4):
                wo = load_piece(l, "wo%d" % g)
                for o2 in range(2):
                    ob = g * 2 + o2
                    p = bank()
                    K.mm(p[:], [(wo[:, o2 * 16 + kc, :], merged[kc][:]) for kc in range(16)])
                    K.stt(xt[ob][:], p[:], DVl[:, DV_MOD + M_GTM + ob:DV_MOD + M_GTM + ob + 1], xt[ob][:], ALU.mult, ALU.add)
            if it == 0 and l == 0:
                K.dump("xmix0", xt[0][:], [128, TT])
            if K.stage < 5:
                continue
            rmsnorm_to_ht(l, DV_GSCF, DV_MOD + M_SHF)
            for pc in range(8):
                w1 = load_piece(l, "w1%d" % pc)
                for j in range(4):
                    hb = pc * 4 + j
                    p = proj(w1, j * 128)
                    t = t32[hb % 2]
                    K.act(t[:], p[:], AF.Relu)
                    K.tt("pool", hidv[hb], t[:], t[:], ALU.mult)
            for ob in range(8):
                w2 = load_piece(l, "w2%d" % ob)
                p = bank()
                K.mm(p[:], [(w2[:, hb, :], hidv[hb]) for hb in range(32)])
                K.stt(xt[ob][:], p[:], DVl[:, DV_MOD + M_GTF + ob:DV_MOD + M_GTF + ob + 1], xt[ob][:], ALU.mult, ALU.add)
            if it == 0 and l == 0:
                K.dump("xffn0", xt[0][:], [128, TT])
        for k in range(NB):
            K.act(sq16[k][:], xt[k][:], AF.Square)
        p = bank()
        K.mm(p[:], [(ones16[:], sq16[k][:]) for k in range(NB)])
        K.act(rs32[:], p[:], AF.Sqrt, scale=1.0 / D, bias=RMS_EPS)
        K.recip(rstd[:], rs32[:])
        for k in range(NB):
            t = t32[k % 2]
            K.stt(t[:], xt[k][:], vecs[0][:, V_FG + k:V_FG + k + 1], rstd[:], ALU.mult, ALU.mult)
            K.dma("sp", oTv[k, :, t0:t0 + TT], t[:], w=["outT%d_%d" % (it, k)])
            K.out_tokens.append("outT%d_%d" % (it, k))
    K.S.wait_all("sp", K.out_tokens)
    K.S.emit(K.st)
    K.st.close()
    return K


def dblb_view(dblb, c):
    col = (c % 2) * 256
    return dblb[c // 2][:, col:col + 256].rearrange("p (a t) -> p a t", t=128)


def _fm(v):
    v = np.asarray(v, np.float32)
    return np.ascontiguousarray(v.reshape(-1, 128).T)


def prep_shared(inp):
    shared = {}
    for l in range(L):
        vec = np.zeros((128, NV), np.float32)
        vec[:, V_GMIX:V_GMIX + 8] = _fm(inp["norm_mix_gain"][l])
        vec[:, V_GFFN:V_GFFN + 8] = _fm(inp["norm_ffn_gain"][l])
        vec[:, V_MU:V_MU + 26] = _fm(inp["mu_shift"][l])
        if l >= 1:
            vec[0:32, V_MUV] = inp["mu_vres"][l - 1]
            vec[:, V_V0:V_V0 + 8] = _fm(inp["v0"][l - 1])
        vec[:, V_W0:V_W0 + 8] = _fm(inp["w0"][l])
        vec[:, V_A0:V_A0 + 8] = _fm(inp["a0"][l])
        vec[:, V_KK:V_KK + 8] = _fm(inp["k_k"][l])
        vec[:, V_KA:V_KA + 8] = _fm(inp["k_a"][l])
        vec[:, V_RK:V_RK + 8] = _fm(inp["r_k"][l].reshape(-1))
        vec[:, V_GNG:V_GNG + 8] = _fm(inp["gn_gain"][l])
        vec[:, V_GNB:V_GNB + 8] = _fm(inp["gn_bias"][l])
        vec[:, V_CB:V_CB + 8] = _fm(inp["conv_b"][l])
        vec[:, V_LNG:V_LNG + 8] = _fm(inp["conv_ln_gain"][l])
        vec[:, V_LNB:V_LNB + 8] = _fm(inp["conv_ln_bias"][l])
        vec[:, V_FG:V_FG + 8] = _fm(inp["final_gain"])
        vec[:, V_ADAB:V_ADAB + 48] = _fm(inp["ada_b"][l])
        cw = np.asarray(inp["conv_w"][l], np.float32)
        vec[:, V_CW:V_CW + 248] = cw.reshape(31, 8, 128).transpose(2, 1, 0).reshape(128, 248)
        shared["vecs%d" % l] = vec
        wsm = np.zeros((128, 3, D), np.float32)
        wsm[0:64, 0] = inp["w_decay_up"][l]
        wsm[64:128, 0] = inp["w_aaa_up"][l]
        wsm[:, 1] = inp["w_gate_up"][l]
        if l >= 1:
            wsm[0:32, 2] = inp["w_vres_up"][l - 1]
        shared["wsm%d" % l] = wsm
        shared["ada%d" % l] = np.ascontiguousarray(inp["ada_w"][l], dtype=np.float32)
        lay, tot = _wbig_layout(l)
        wb = np.empty((128, tot), np.float32)
        win = np.asarray(inp["w_in"][l], np.float32)
        if l >= 1:
            win = np.concatenate([win, np.asarray(inp["w_in_vres"][l - 1], np.float32)], axis=1)

        def put(key, cols_matrix):
            off, k, c = lay[key]
            wb[:, off:off + k * c] = cols_matrix.reshape(k, 128, c).transpose(1, 0, 2).reshape(128, k * c)
        lo_idx = list(range(3072, 3328)) + (list(range(N_COLS, N_COLS + 32)) if l >= 1 else [])
        put("lo", win[:, lo_idx])
        for hp in range(8):
            idx = np.concatenate([np.arange(hp * 128, hp * 128 + 128), 1024 + np.arange(hp * 128, hp * 128 + 128),
                                  2048 + np.arange(hp * 128, hp * 128 + 128), 5376 + np.arange(hp * 128, hp * 128 + 128)])
            put("hp%d" % hp, win[:, idx])
        for cb in range(8):
            idx = np.concatenate([3328 + np.arange(cb * 128, cb * 128 + 128), 4352 + np.arange(cb * 128, cb * 128 + 128),
                                  6400 + np.arange(cb * 128, cb * 128 + 128)])
            put("cb%d" % cb, win[:, idx])
        wo = np.asarray(inp["w_out"][l], np.float32)
        for g in range(4):
            off, k, c = lay["wo%d" % g]
            blk = wo[:, g * 256:(g + 1) * 256].reshape(16, 128, 2, 128)
            wb[:, off:off + k * c] = blk.transpose(1, 2, 0, 3).reshape(128, 32 * 128)
        w1 = np.asarray(inp["w_ff_in"][l], np.float32)
        for pc in range(8):
            put("w1%d" % pc, w1[:, pc * 512:(pc + 1) * 512])
        w2 = np.asarray(inp["w_ff_out"][l], np.float32)
        for ob in range(8):
            put("w2%d" % ob, w2[:, ob * 128:(ob + 1) * 128])
        shared["wbig%d" % l] = wb
    return shared


def prep_core(inp, b):
    return {"xT": np.ascontiguousarray(np.asarray(inp["x"][b], np.float32).T),
            "cT": _fm(inp["c"][b])}


_CACHE = {}


def kernel(**inputs):
    inp = {k: np.asarray(v) for k, v in inputs.items()}
    if "K" not in _CACHE:
        _CACHE["K"] = build()
    K = _CACHE["K"]
    shared = prep_shared(inp)
    in_maps = []
    for b in range(NCORES):
        m = dict(shared)
        m.update(prep_core(inp, b))
        in_maps.append(m)
    res = run_bass_kernel_spmd(K.nc, in_maps, core_ids=list(range(NCORES)))
    out = np.stack([np.ascontiguousarray(res.results[b]["outT"].T) for b in range(NCORES)], axis=0)
    return out.astype(np.float32)
```

```python
import contextlib
import math
import numpy as np
import concourse.bass as bass
import concourse.mybir as mybir
from concourse.bass_utils import run_bass_kernel_spmd

F32 = mybir.dt.float32
BF16 = mybir.dt.bfloat16
AF = mybir.ActivationFunctionType
ALU = mybir.AluOpType

D = 1024
T = 2048
NB = 8
TT = 512
C = 128
NCH = TT // C
L = 2
NCORES = 8
N_SHIFT = 3328
N_COLS = 7424
RMS_EPS = 1e-6
LN_EPS = 1e-5
GN_EPS = 64e-5
CEXP = -math.exp(-0.5)

V_GMIX, V_GFFN, V_MU, V_MUV, V_W0, V_A0, V_KK, V_KA, V_RK, V_GNG, V_GNB, V_V0, V_CB, V_LNG, V_LNB, V_FG, V_ADAB, V_CW = (
    0, 8, 16, 42, 43, 51, 59, 67, 75, 83, 91, 99, 107, 115, 123, 131, 139, 187)
NV = 187 + 8 * 31
DV_MOD, DV_GSCM, DV_GSCF, DV_OMU = 0, 48, 56, 64
ND = 64 + 27
M_SHM, M_SCM, M_GTM, M_SHF, M_SCF, M_GTF = 0, 8, 16, 24, 32, 40

def _wbig_layout(l):
    off = 0
    lay = {}
    nlo = 256 + (32 if l >= 1 else 0)
    lay["lo"] = (off, 8, nlo); off += 8 * nlo
    for hp in range(8):
        lay["hp%d" % hp] = (off, 8, 512); off += 8 * 512
    for cb in range(8):
        lay["cb%d" % cb] = (off, 8, 384); off += 8 * 384
    for g in range(4):
        lay["wo%d" % g] = (off, 32, 128); off += 32 * 128
    for pc in range(8):
        lay["w1%d" % pc] = (off, 8, 512); off += 8 * 512
    for ob in range(8):
        lay["w2%d" % ob] = (off, 32, 128); off += 32 * 128
    return lay, off


ENGS = ("pe", "act", "dve", "pool", "sp")
DMA_SEMS = {"sp": 8, "pool": 16, "act": 4}
SEM_LAT = 0.25
LIST_SCHED = True


class Sched:
    def __init__(self, nc):
        self.nc = nc
        self.recs = []
        self.last_w = {}
        self.readers = {}

    def _deps(self, reads, writes):
        deps = set()
        for t in reads:
            w = self.last_w.get(t)
            if w is not None:
                deps.add(w)
        for t in writes:
            w = self.last_w.get(t)
            if w is not None:
                deps.add(w)
            deps.update(self.readers.get(t, ()))
        return deps

    def _commit(self, reads, writes, me):
        for t in reads:
            self.readers.setdefault(t, []).append(me)
        for t in writes:
            self.last_w[t] = me
            self.readers[t] = []

    def op(self, eng, fn, reads=(), writes=(), dur=0.5):
        deps = self._deps(reads, writes)
        me = len(self.recs)
        self.recs.append([eng, fn, deps, dur, False, dur])
        self._commit(reads, writes, me)

    def dma(self, queue, fn, reads=(), writes=(), nbytes=0):
        deps = self._deps(reads, writes)
        me = len(self.recs)
        self.recs.append([queue, fn, deps, 0.15, True, 2.0 + nbytes / 340e3])
        self._commit(reads, writes, me)

    def wait_all(self, eng, tokens):
        deps = set(self.last_w[t] for t in tokens if t in self.last_w)
        me = len(self.recs)
        self.recs.append([eng, None, deps, 0.01, False, 0.01])
        self.final = me

    def finalize(self):
        recs = self.recs
        n = len(recs)
        order = {e: [] for e in ENGS}
        if not LIST_SCHED:
            for i, r in enumerate(recs):
                order[r[0]].append(i)
            self.order = order
            return
        succs = [[] for _ in range(n)]
        indeg = [0] * n
        for i, r in enumerate(recs):
            for d in r[2]:
                succs[d].append(i)
            indeg[i] = len(r[2])
        prio = [0.0] * n
        for i in range(n - 1, -1, -1):
            m = 0.0
            for sx in succs[i]:
                if prio[sx] > m:
                    m = prio[sx]
            prio[i] = m + recs[i][5] + SEM_LAT
        import heapq
        ready = {e: [] for e in ENGS}
        finish = [0.0] * n
        rdy_t = [0.0] * n
        for i in range(n):
            if indeg[i] == 0:
                heapq.heappush(ready[recs[i][0]], (0.0, -prio[i], i))
        efree = {e: 0.0 for e in ENGS}
        done = 0
        WINDOW = 0.3
        while done < n:
            best = None
            for e in ENGS:
                h = ready[e]
                if not h:
                    continue
                st = max(efree[e], h[0][0])
                if best is None or st < best[0]:
                    best = (st, e)
            st, e = best
            h = ready[e]
            cands = []
            while h and h[0][0] <= st + WINDOW and len(cands) < 24:
                cands.append(heapq.heappop(h))
            cands.sort(key=lambda c: c[1])
            pick = cands[0]
            for c in cands[1:]:
                heapq.heappush(h, c)
            i = pick[2]
            start = max(efree[e], pick[0])
            efree[e] = start + recs[i][3]
            finish[i] = start + recs[i][5]
            order[e].append(i)
            done += 1
            for sx in succs[i]:
                t = finish[i] + SEM_LAT
                if t > rdy_t[sx]:
                    rdy_t[sx] = t
                indeg[sx] -= 1
                if indeg[sx] == 0:
                    heapq.heappush(ready[recs[sx][0]], (rdy_t[sx], -prio[sx], sx))
        self.order = order
        self.sim_time = max(finish)

    def emit(self, st):
        nc = self.nc
        recs = self.recs
        self.finalize()
        sems = {}
        for e in ENGS:
            sems[e] = st.enter_context(nc.semaphore("s_" + e))
        for q, nq in DMA_SEMS.items():
            for k in range(nq):
                sems[("dma", q, k)] = st.enter_context(nc.semaphore("s_dma_%s%d" % (q, k)))
        comp = [None] * len(recs)
        prev_same_sem = {}
        for e in ENGS:
            cc = 0
            dk = 0
            for i in self.order[e]:
                r = recs[i]
                if r[1] is None:
                    continue
                if r[4]:
                    nq = DMA_SEMS[e]
                    key = ("dma", e, dk % nq)
                    val = 16 * (dk // nq + 1)
                    comp[i] = (key, val)
                    if dk >= nq:
                        prev_same_sem[i] = (key, val - 16)
                    dk += 1
                else:
                    cc += 1
                    comp[i] = (e, cc)
        block = st.enter_context(nc.Block())

        def run(eng_name):
            def body(engine):
                known = {}
                for i in self.order[eng_name]:
                    r = recs[i]
                    need = {}
                    for d in r[2]:
                        k, v = comp[d]
                        if eng_name == "pe" and k == "pe":
                            continue
                        if v > need.get(k, 0):
                            need[k] = v
                    if i in prev_same_sem:
                        k, v = prev_same_sem[i]
                        if v > need.get(k, 0):
                            need[k] = v
                    for k, v in need.items():
                        if known.get(k, 0) >= v:
                            continue
                        known[k] = v
                        engine.wait_ge(sems[k], v)
                    if r[1] is None:
                        continue
                    ins = r[1](engine)
                    k, v = comp[i]
                    ins.then_inc(sems[k], 16 if r[4] else 1)
            return body

        block.tensor(run("pe"))
        block.scalar(run("act"))
        block.vector(run("dve"))
        block.gpsimd(run("pool"))
        block.sync(run("sp"))


def _tok(ap):
    return ap.name


def _n(ap):
    n = 1
    for d in ap.shape[1:]:
        n *= int(d)
    return n


def _bytes(ap):
    n = int(ap.shape[0]) * _n(ap)
    return n * (2 if ap.dtype == BF16 else 4)


class KB:
    def __init__(self, ntiles=4, nlayers=2, dbg=None, stage=99):
        self.ntiles = ntiles
        self.nlayers = nlayers
        self.dbg_names = dbg or []
        self.stage = stage
        self.nc = bass.Bass("TRN2", target_bir_lowering=False)
        self.st = contextlib.ExitStack()
        self.S = Sched(self.nc)
        self.dbg_out = {}
        self.out_tokens = []
        self.psum_names = set()

    def sb(self, name, shape, dt):
        return self.st.enter_context(self.nc.sbuf_tensor(name, shape, dt))

    def ps(self, name, shape, dt):
        self.psum_names.add(name)
        return self.st.enter_context(self.nc.psum_tensor(name, shape, dt))

    def dram_in(self, name, shape, dt=F32):
        return self.nc.dram_tensor(name, shape, dt, kind="ExternalInput").ap()

    def dram_out(self, name, shape, dt=F32):
        return self.nc.dram_tensor(name, shape, dt, kind="ExternalOutput").ap()

    def _rw(self, outs, ins, r, w):
        reads = list(r) if r is not None else [_tok(a) for a in ins if hasattr(a, "name")]
        writes = list(w) if w is not None else [_tok(a) for a in outs]
        ex = [t for t in reads if t in self.psum_names]
        if ex:
            reads = [t for t in reads if t not in self.psum_names]
            writes = writes + [t for t in ex if t not in writes]
        return reads, writes

    def act(self, out, in_, func, scale=1.0, bias=0.0, r=None, w=None):
        extra = [a for a in (scale, bias) if hasattr(a, "name")]
        reads, writes = self._rw([out], [in_] + extra, r, w)
        self.S.op("act", lambda e: e.activation(out=out, in_=in_, func=func, scale=scale, bias=bias), reads, writes, dur=0.22 + _n(out) / 1400.0)

    def tt(self, eng, out, in0, in1, op, r=None, w=None):
        reads, writes = self._rw([out], [in0, in1], r, w)
        self.S.op(eng, lambda e: e.tensor_tensor(out=out, in0=in0, in1=in1, op=op), reads, writes, dur=self._vdur(eng, out))

    def ts(self, eng, out, in0, s1, s2, op0, op1=None, r=None, w=None):
        extra = [a for a in (s1, s2) if hasattr(a, "name")]
        reads, writes = self._rw([out], [in0] + extra, r, w)
        if op1 is None:
            self.S.op(eng, lambda e: e.tensor_scalar(out=out, in0=in0, scalar1=s1, scalar2=None, op0=op0), reads, writes, dur=self._vdur(eng, out))
        else:
            self.S.op(eng, lambda e: e.tensor_scalar(out=out, in0=in0, scalar1=s1, scalar2=s2, op0=op0, op1=op1), reads, writes, dur=self._vdur(eng, out))

    def stt(self, out, in0, scalar, in1, op0, op1, r=None, w=None):
        extra = [scalar] if hasattr(scalar, "name") else []
        reads, writes = self._rw([out], [in0, in1] + extra, r, w)
        self.S.op("dve", lambda e: e.scalar_tensor_tensor(out=out, in0=in0, scalar=scalar, in1=in1, op0=op0, op1=op1), reads, writes, dur=self._vdur("dve", out))

    def copy(self, eng, out, in_, r=None, w=None):
        reads, writes = self._rw([out], [in_], r, w)
        if eng == "act":
            self.S.op("act", lambda e: e.copy(out=out, in_=in_), reads, writes, dur=0.22 + _n(out) / 1400.0)
        else:
            self.S.op(eng, lambda e: e.tensor_copy(out=out, in_=in_), reads, writes, dur=self._vdur(eng, out))

    def recip(self, out, in_, r=None, w=None):
        reads, writes = self._rw([out], [in_], r, w)
        self.S.op("dve", lambda e: e.reciprocal(out=out, in_=in_), reads, writes, dur=0.1 + _n(out) / 155.0)

    def scan(self, out, d0, d1, r=None, w=None):
        reads, writes = self._rw([out], [d0, d1], r, w)
        self.S.op("dve", lambda e: e.tensor_tensor_scan(out=out, data0=d0, data1=d1, initial=0.0, op0=ALU.mult, op1=ALU.add), reads, writes, dur=0.12 + _n(out) / 480.0)

    def memset(self, eng, out, val, r=None, w=None):
        reads, writes = self._rw([out], [], r, w)
        self.S.op(eng, lambda e: e.memset(out, val), reads, writes, dur=self._vdur(eng, out))

    def mm(self, out, pairs, r=None, w=None):
        ins = []
        for a, b in pairs:
            ins += [a, b]
        reads, writes = self._rw([out], ins, r, w)
        n = len(pairs)

        def fn(e):
            last = None
            for i, (a, b) in enumerate(pairs):
                last = e.matmul(out, lhsT=a, rhs=b, start=(i == 0), stop=(i == n - 1))
            return last
        self.S.op("pe", fn, reads, writes, dur=sum(0.035 + max(_n(b_), 64) / 2000.0 for a_, b_ in pairs))

    def _vdur(self, eng, out):
        if eng == "pool":
            return 0.2 + _n(out) / 570.0
        return 0.12 + _n(out) / 960.0

    def tr(self, out, in_, ident, r=None, w=None):
        reads, writes = self._rw([out], [in_, ident], r, w)
        self.S.op("pe", lambda e: e.transpose(out=out, in_=in_, identity=ident), reads, writes, dur=0.1)

    def dma(self, q, out, in_, r=None, w=None):
        reads, writes = self._rw([out], [in_], r, w)
        self.S.dma(q, lambda e: e.dma_start(out=out, in_=in_), reads, writes, nbytes=max(_bytes(out), _bytes(in_)))

    def dump(self, name, ap, shape, dt=F32):
        if name not in self.dbg_names:
            return
        if dt != F32 or ap.dtype != F32:
            if not hasattr(self, "_dbgtmp"):
                self._dbgtmp = self.sb("dbgtmp", [128, 1024], F32)
            tmp = self._dbgtmp[0:shape[0], 0:shape[1]]
            self.copy("dve", tmp, ap)
            ap = tmp
        o = self.dram_out("dbg_" + name, list(shape))
        self.dma("sp", o, ap)
        self.out_tokens.append(_tok(o))
        self.dbg_out[name] = "dbg_" + name


def build(ntiles=4, nlayers=2, dbg=None, stage=99):
    K = KB(ntiles, nlayers, dbg, stage)
    nc = K.nc
    xT = K.dram_in("xT", [D, T])
    cT = K.dram_in("cT", [128, 8])
    ada = [K.dram_in("ada%d" % l, [D, 6 * D]) for l in range(L)]
    vecs_d = [K.dram_in("vecs%d" % l, [128, NV]) for l in range(L)]
    wsm_d = [K.dram_in("wsm%d" % l, [128, 3, D]) for l in range(L)]
    lay = [_wbig_layout(l) for l in range(L)]
    wbig = [K.dram_in("wbig%d" % l, [128, lay[l][1]]) for l in range(L)]
    outT = K.dram_out("outT", [D, T])

    sb, ps = K.sb, K.ps
    ident16 = sb("ident16", [128, 128], BF16)
    ones16 = sb("ones16", [128, 128], BF16)
    bd16 = sb("bd16", [128, 128], BF16)
    onesf = sb("onesf", [128, 128], F32)
    mask512 = sb("mask512", [128, 512], BF16)
    masksl = sb("masksl", [128, 128], BF16)
    rst = sb("rst", [128, 512], BF16)
    vecs = [sb("vecs_s%d" % l, [128, NV], F32) for l in range(L)]
    dv = [sb("dv%d" % l, [128, ND], F32) for l in range(L)]
    wsm = [sb("wsm_s%d" % l, [128, 3, D], BF16) for l in range(L)]
    c32 = sb("c32", [128, 8], F32)
    epsc = sb("epsc", [128, 4], F32)
    c16 = sb("c16", [128, 8], BF16)
    S32 = [sb("S32_%d" % l, [128, 8, 64], F32) for l in range(L)]
    S16 = [sb("S16_%d" % l, [128, 8, 64], BF16) for l in range(L)]
    ctail = [sb("ctail%d" % l, [128, 8, 30], BF16) for l in range(L)]
    carry = [sb("carry%d" % l, [128, 27], F32) for l in range(L)]
    xt = [sb("xt%d" % k, [128, TT], F32) for k in range(NB)]
    ht = [sb("ht%d" % k, [128, TT], BF16) for k in range(NB)]
    merged = [sb("mg%d" % k, [128, TT], BF16) for k in range(16)]
    vf = [sb("vf%d" % k, [128, TT], BF16) for k in range(NB)]
    wpool = [sb("wp%d" % i, [128, 4096], BF16) for i in range(4)]
    lo16 = [sb("lo16_%d" % j, [128, TT], BF16) for j in range(2)]
    vlo16 = sb("vlo16", [32, TT], BF16)
    names32 = ["r32", "k32", "v32", "sg32", "a32", "sv32", "dd32", "cs32", "E1", "E3",
               "kk32", "nrm", "km", "b32", "y32", "yc", "sd"]
    W = {n: sb(n, [128, TT], F32) for n in names32}
    W["d2"] = W["dd32"]; W["E2"] = W["dd32"]; W["d4"] = W["sv32"]; W["E4"] = W["sv32"]; W["rn"] = W["nrm"]
    W["kkn"] = W["kk32"]; W["t1"] = W["km"]; W["rsd"] = W["sd"]
    W["yn"] = W["yc"]; W["yg"] = W["yc"]; W["o1"] = W["yc"]; W["o2"] = W["yc"]
    names16 = ["kk2", "Kh16", "Bh16", "v16", "rk16", "y16", "yc2", "sqa", "sqb", "sqc"]
    H = {n: sb(n, [128, TT], BF16) for n in names16}
    sq16 = [H[n] for n in ("kk2", "Kh16", "Bh16", "v16", "rk16", "sqa", "sqb", "sqc")]
    g32p = [sb("g32_%d" % i, [128, TT], F32) for i in range(2)]
    bon32p = [sb("bon32_%d" % i, [128, TT], F32) for i in range(2)]
    gA16p = [sb("gA16_%d" % i, [128, TT], BF16) for i in range(2)]
    kt16p = [sb("kt16_%d" % i, [128, TT], BF16) for i in range(2)]
    bt16p = [sb("bt16_%d" % i, [128, TT], BF16) for i in range(2)]
    WCp = [sb("WC_%d" % i, [128, NCH], F32) for i in range(2)]
    lo32 = [W["yc"], W["y32"]]
    vlo32 = bon32p[0]
    rs32 = W["nrm"]; rstd = W["sd"]; t32 = [W["yc"], W["y32"]]
    ARp = [sb("AR_%d" % i, [128, NCH, 2, C], BF16) for i in range(2)]
    KBtokp = [sb("KBtok_%d" % i, [128, 8, 128], BF16) for i in range(2)]
    Vtokp = [sb("Vtok_%d" % i, [128, 4, 128], BF16) for i in range(2)]
    Amat = [[sb("Amat%d_%d" % (h, c), [128, 512], BF16) for c in range(NCH)] for h in range(2)]
    Tfin = [[sb("Tfin%d_%d" % (h, c), [128, 128], BF16) for c in range(NCH)] for h in range(2)]
    Q0 = [sb("Q0_%d" % i, [128, 128], BF16) for i in range(4)]
    PQT = [[sb("PQT%d_%d" % (i, j), [128, 3, 128], BF16) for j in range(2)] for i in range(4)]
    XT16 = sb("XT16", [128, 128], BF16)
    UT16 = sb("UT16", [128, 128], BF16)
    sgb = W["b32"]
    gbuf = sb("gbuf", [128, 30 + TT], BF16)
    dg = sb("dg", [128, 31, 128], BF16)
    z32 = [W[n] for n in ("r32", "k32", "v32", "sg32", "a32", "sv32", "dd32", "kk32")]

    def halves(t):
        v = t[:].bitcast(BF16)
        return [v[:, 0:TT], v[:, TT:2 * TT]]
    gB16v = []
    for n in ("cs32", "E1", "E3", "km"):
        gB16v += halves(W[n])
    hidv = []
    for t_ in [W[n] for n in ("r32", "k32", "v32", "sg32", "a32", "sv32", "dd32", "kk32", "cs32", "E1", "E3", "km", "b32")] + [g32p[0], g32p[1], bon32p[0]]:
        hidv += halves(t_)
    mmb = [ps("mmb%d" % i, [128, 512], F32) for i in range(2)]
    dblb = [ps("dblb%d" % i, [128, 512], F32) for i in range(4)]
    seqb = ps("seqb", [128, 512], F32)
    yb = ps("yb", [128, 512], F32)
    mm_rr = [0]

    def bank():
        b = mmb[mm_rr[0] % 2]
        mm_rr[0] += 1
        return b

    def dbl_region(i, j):
        if j < 2:
            col = (i % 2) * 256 + j * 128
            return dblb[i // 2][:, col:col + 128]
        return dblb[2][:, i * 128:(i + 1) * 128]

    wp_rr = [0]
    PF = 3
    piece_list = []
    for l_ in range(K.nlayers):
        for pc in range(12):
            piece_list.append(("ada", l_, pc))
    for it_ in range(K.ntiles):
        for l_ in range(K.nlayers):
            keys = ["lo"] + ["hp%d" % i for i in range(8)]
            if K.stage >= 4:
                keys += ["cb%d" % i for i in range(8)] + ["wo%d" % i for i in range(4)]
            if K.stage >= 5:
                keys += ["w1%d" % i for i in range(8)] + ["w2%d" % i for i in range(8)]
            for key in keys:
                piece_list.append(("w", l_, key))
    issued = [0]
    piece_dst = {}

    def _issue(idx):
        kind, l_, key = piece_list[idx]
        buf = wpool[idx % 4]
        if kind == "ada":
            src = ada[l_].rearrange("(k p) c -> p k c", p=128)[:, :, key * 512:(key + 1) * 512]
            dst = buf[:].rearrange("p (k c) -> p k c", c=512)
        else:
            off, k, c = lay[l_][0][key]
            src = wbig[l_][:, off:off + k * c].rearrange("p (k c) -> p k c", c=c)
            dst = buf[:, 0:k * c].rearrange("p (k c) -> p k c", c=c)
        K.dma("pool", dst, src)
        piece_dst[idx] = dst

    def next_piece(expect):
        idx = wp_rr[0]
        wp_rr[0] += 1
        assert piece_list[idx] == expect, (piece_list[idx], expect)
        while issued[0] < min(len(piece_list), idx + PF + 1):
            _issue(issued[0])
            issued[0] += 1
        return piece_dst.pop(idx)

    def load_piece(l, key):
        return next_piece(("w", l, key))

    K.memset("pool", onesf[:], 1.0)
    K.memset("pool", ones16[:], 1.0)
    K.memset("pool", bd16[:], 0.0)
    K.memset("pool", bd16[0:64, 0:64], 1.0)
    K.memset("pool", bd16[64:128, 64:128], 1.0)

    def asel(out, in_, pattern, op, base, cm):
        K.S.op("pool", lambda e: e.affine_select(out=out, in_=in_, pattern=pattern, compare_op=op, fill=0.0, base=base, channel_multiplier=cm),
               [_tok(in_)], [_tok(out)], dur=0.3)
    asel(ident16[:], ones16[:], [[1, 128]], ALU.is_equal, 0, -1)
    for j in range(4):
        asel(mask512[:, j * 128:(j + 1) * 128], onesf[:, 0:128], [[1, 128]], ALU.is_gt if j % 2 == 0 else ALU.is_ge, 0, -1)
    asel(masksl[:], onesf[:, 0:128], [[-1, 128]], ALU.is_gt, 0, 1)
    K.memset("pool", rst[:], 1.0)
    for j in range(NCH):
        K.memset("pool", rst[:, j * C:j * C + 1], 0.0)
    for l in range(L):
        K.dma("sp", vecs[l][:], vecs_d[l])
        K.dma("pool", wsm[l][:], wsm_d[l])
        K.memset("pool", S32[l][:], 0.0)
        K.memset("pool", S16[l][:], 0.0)
        K.memset("pool", ctail[l][:], 0.0)
        K.memset("pool", carry[l][:], 0.0)
    K.dma("sp", c32[:], cT)
    K.memset("pool", epsc[:, 0:1], RMS_EPS)
    K.memset("pool", epsc[:, 1:2], LN_EPS)
    K.memset("pool", epsc[:, 2:3], GN_EPS)
    K.memset("pool", epsc[:, 3:4], 1e-24)
    K.act(c16[:], c32[:], AF.Silu)

    for l in range(K.nlayers):
        pmod = bank()
        adav = ada[l].rearrange("(k p) c -> p k c", p=128)
        for pc in range(12):
            dst = next_piece(("ada", l, pc))
            for m in range(4):
                j = pc * 4 + m
                K.mm(pmod[:, j:j + 1], [(dst[:, k, m * 128:(m + 1) * 128], c16[:, k:k + 1]) for k in range(8)])
        K.tt("dve", dv[l][:, DV_MOD:DV_MOD + 48], pmod[:, 0:48], vecs[l][:, V_ADAB:V_ADAB + 48], ALU.add)
        K.stt(dv[l][:, DV_GSCM:DV_GSCM + 8], dv[l][:, DV_MOD + M_SCM:DV_MOD + M_SCM + 8], 1.0, vecs[l][:, V_GMIX:V_GMIX + 8], ALU.add, ALU.mult)
        K.stt(dv[l][:, DV_GSCF:DV_GSCF + 8], dv[l][:, DV_MOD + M_SCF:DV_MOD + M_SCF + 8], 1.0, vecs[l][:, V_GFFN:V_GFFN + 8], ALU.add, ALU.mult)
        K.ts("dve", dv[l][:, DV_OMU:DV_OMU + 27], vecs[l][:, V_MU:V_MU + 27], -1.0, 1.0, ALU.mult, ALU.add)
        K.dump("dv%d" % l, dv[l][:], [128, ND])

    def rmsnorm_to_ht(l, gsc_col, sh_col):
        for k in range(NB):
            K.act(sq16[k][:], xt[k][:], AF.Square)
        p = bank()
        K.mm(p[:], [(ones16[:], sq16[k][:]) for k in range(NB)])
        K.act(rs32[:], p[:], AF.Ln, scale=1.0 / D, bias=epsc[:, 0:1])
        K.act(rstd[:], rs32[:], AF.Exp, scale=-0.5)
        for k in range(NB):
            t = t32[k % 2]
            K.stt(t[:], xt[k][:], dv[l][:, gsc_col + k:gsc_col + k + 1], rstd[:], ALU.mult, ALU.mult)
            K.act(ht[k][:], t[:], AF.Identity, bias=dv[l][:, sh_col + k:sh_col + k + 1])

    def proj(wp, j0, M=128):
        p = bank()
        K.mm(p[0:M, :], [(wp[:, k, j0:j0 + M], ht[k][:]) for k in range(NB)])
        return p

    def shift(l, p, mucol, dst, M=128):
        mu = vecs[l][0:M, V_MU + mucol:V_MU + mucol + 1]
        omu = dv[l][0:M, DV_OMU + mucol:DV_OMU + mucol + 1]
        cy = carry[l][0:M, mucol:mucol + 1]
        K.act(dst[0:M, :], p[0:M, :], AF.Identity, scale=omu)
        K.stt(dst[0:M, 1:TT], p[0:M, 0:TT - 1], mu, dst[0:M, 1:TT], ALU.mult, ALU.add)
        K.stt(dst[0:M, 0:1], cy, mu, dst[0:M, 0:1], ALU.mult, ALU.add)
        K.copy("dve", cy, p[0:M, TT - 1:TT])

    xTv = xT.rearrange("(k p) t -> k p t", p=128)
    oTv = outT.rearrange("(k p) t -> k p t", p=128)
    for it in range(K.ntiles):
        t0 = it * TT
        for k in range(NB):
            K.dma("sp", xt[k][:], xTv[k, :, t0:t0 + TT])
        for l in range(K.nlayers):
            V = vecs[l]
            DVl = dv[l]
            rmsnorm_to_ht(l, DV_GSCM, DV_MOD + M_SHM)
            if it == 0 and l == 0:
                for k in (0, 7):
                    K.dump("ht%d" % k, ht[k][:], [128, TT])
            wlo = load_piece(l, "lo")
            for j in range(2):
                p = proj(wlo, j * 128)
                shift(l, p, 24 + j, lo32[j])
            K.act(lo16[0][0:64, :], lo32[0][0:64, :], AF.Tanh)
            K.copy("act", lo16[0][64:128, :], lo32[0][64:128, :])
            K.act(lo16[1][:], lo32[1][:], AF.Sigmoid)
            if l >= 1:
                p = proj(wlo, 256, M=32)
                shift(l, p, 26, vlo32, M=32)
                K.copy("act", vlo16[:], vlo32[0:32, :])
            if K.stage < 1:
                continue
            def front(hp, par):
                AR = ARp[par]; KBtok = KBtokp[par]; Vtok = Vtokp[par]
                g32 = g32p[par]; bon32 = bon32p[par]; gA16 = gA16p[par]; kt16 = kt16p[par]; bt16 = bt16p[par]; WC = WCp[par]
                whp = load_piece(l, "hp%d" % hp)
                cs = slice(hp * 128, (hp + 1) * 128)
                p = proj(whp, 0);   shift(l, p, hp, W["r32"])
                yield
                p = proj(whp, 128); shift(l, p, 8 + hp, W["k32"])
                yield
                p = proj(whp, 256); shift(l, p, 16 + hp, W["v32"])
                yield
                p = proj(whp, 384); K.act(gA16[:], p[:], AF.Sigmoid)
                p = bank(); K.mm(p[:], [(wsm[l][0:64, 0, cs], lo16[0][0:64, :])])
                K.act(W["sg32"][:], p[:], AF.Sigmoid, bias=V[:, V_W0 + hp:V_W0 + hp + 1])
                yield
                p = bank(); K.mm(p[:], [(wsm[l][64:128, 0, cs], lo16[0][64:128, :])])
                K.act(W["a32"][:], p[:], AF.Sigmoid, bias=V[:, V_A0 + hp:V_A0 + hp + 1])
                p = bank(); K.mm(p[:], [(wsm[l][:, 1, cs], lo16[1][:])])
                K.copy("act", g32[:], p[:])
                yield
                if l >= 1:
                    p = bank(); K.mm(p[:], [(wsm[l][0:32, 2, cs], vlo16[:])])
                    K.act(W["sv32"][:], p[:], AF.Sigmoid, bias=V[:, V_V0 + hp:V_V0 + hp + 1])
                    K.tt("pool", W["dd32"][:], vf[hp][:], W["v32"][:], ALU.subtract)
                    K.tt("pool", W["dd32"][:], W["dd32"][:], W["sv32"][:], ALU.mult)
                    K.tt("pool", W["v32"][:], W["v32"][:], W["dd32"][:], ALU.add)
                else:
                    K.copy("pool", vf[hp][:], W["v32"][:])
                yield
                K.scan(W["cs32"][:], rst[:], W["sg32"][:])
                K.act(W["E1"][:], W["cs32"][:], AF.Exp, scale=CEXP)
                K.act(W["E3"][:], W["cs32"][:], AF.Exp, scale=-CEXP)
                K.tt("pool", W["d2"][:], W["cs32"][:], W["sg32"][:], ALU.subtract)
                K.act(W["E2"][:], W["d2"][:], AF.Exp, scale=CEXP)
                yield
                cs3 = W["cs32"][:].rearrange("p (c t) -> p c t", t=C)
                K.tt("pool", W["d4"][:].rearrange("p (c t) -> p c t", t=C), cs3, cs3[:, :, C - 1:C].to_broadcast([128, NCH, C]), ALU.subtract)
                K.act(W["E4"][:], W["d4"][:], AF.Exp, scale=-CEXP)
                K.copy("dve", WC[:], W["E1"][:].rearrange("p (c t) -> p c t", t=C)[:, :, C - 1])
                yield
                K.act(W["kk32"][:], W["k32"][:], AF.Identity, scale=V[:, V_KK + hp:V_KK + hp + 1])
                K.act(H["kk2"][:], W["kk32"][:], AF.Square)
                p = bank(); K.mm(p[:], [(bd16[:], H["kk2"][:])])
                K.act(W["nrm"][:], p[:], AF.Ln, bias=epsc[:, 3:4])
                K.act(W["rn"][:], W["nrm"][:], AF.Exp, scale=-0.5)
                yield
                K.tt("pool", W["kkn"][:], W["kk32"][:], W["rn"][:], ALU.mult)
                K.ts("dve", W["t1"][:], W["a32"][:], -1.0, V[:, V_KA + hp:V_KA + hp + 1], ALU.add, ALU.mult)
                K.stt(W["km"][:], W["t1"][:], 1.0, W["k32"][:], ALU.add, ALU.mult)
                K.tt("pool", W["b32"][:], W["kkn"][:], W["a32"][:], ALU.mult)
                yield
                K.tt("dve", AR[:, :, 1, :], W["r32"][:].rearrange("p (c t) -> p c t", t=C), W["E1"][:].rearrange("p (c t) -> p c t", t=C), ALU.mult)
                K.stt(AR[:, :, 0, :], W["kkn"][:].rearrange("p (c t) -> p c t", t=C), -1.0, W["E2"][:].rearrange("p (c t) -> p c t", t=C), ALU.mult, ALU.mult)
                K.tt("pool", kt16[:], W["km"][:], W["E3"][:], ALU.mult)
                K.tt("pool", bt16[:], W["b32"][:], W["E3"][:], ALU.mult)
                yield
                K.tt("pool", H["Kh16"][:], W["km"][:], W["E4"][:], ALU.mult)
                K.tt("dve", H["Bh16"][:], W["b32"][:], W["E4"][:], ALU.mult)
                K.copy("pool", H["v16"][:], W["v32"][:])
                yield
                K.stt(H["rk16"][:], W["r32"][:], V[:, V_RK + hp:V_RK + hp + 1], W["km"][:], ALU.mult, ALU.mult)
                p = bank(); K.mm(p[:], [(bd16[:], H["rk16"][:])])
                K.tt("dve", bon32[:], p[:], W["v32"][:], ALU.mult)
                yield
                trb = bank()
                trv = trb[:].bitcast(BF16).rearrange("p (a t) -> p a t", t=128)
                for c in range(NCH):
                    K.tr(trv[:, c, :], H["Kh16"][:, c * C:(c + 1) * C], ident16[:])
                    K.tr(trv[:, 4 + c, :], H["Bh16"][:, c * C:(c + 1) * C], ident16[:])
                K.copy("act", KBtok[:], trv)
                yield
                trb = bank()
                trv = trb[:].bitcast(BF16).rearrange("p (a t) -> p a t", t=128)
                for c in range(NCH):
                    K.tr(trv[:, c, :], H["v16"][:, c * C:(c + 1) * C], ident16[:])
                K.copy("dve", Vtok[:], trv[:, 0:4, :])
                yield

            def back(hp, par):
                AR = ARp[par]; KBtok = KBtokp[par]; Vtok = Vtokp[par]
                g32 = g32p[par]; bon32 = bon32p[par]; gA16 = gA16p[par]; kt16 = kt16p[par]; bt16 = bt16p[par]; WC = WCp[par]
                for h in range(2):
                    hs = slice(h * 64, h * 64 + 64)
                    for c in range(NCH):
                        ck = slice(c * C, (c + 1) * C)
                        bk = dblb[c]
                        ar = AR[hs, c, :, :].rearrange("p a t -> p (a t)")
                        K.mm(bk[:, 0:256], [(bt16[hs, ck], ar)])
                        K.mm(bk[:, 256:512], [(kt16[hs, ck], ar)])
                        K.tt("dve", Amat[h][c][:], bk[:], mask512[:], ALU.mult)
                        K.mm(bk[:, 0:128], [(AR[hs, c, 0, :], bt16[hs, ck])])
                        K.tt("dve", Q0[c][:], bk[:, 0:128], masksl[:], ALU.mult)
                        K.tt("pool", PQT[c][1][:, 2, :], Amat[h][c][:, 0:128], ident16[:], ALU.add)
                        yield
                    for c in range(NCH):
                        bk = dblb[c]
                        P0 = Amat[h][c][:, 0:128]
                        K.mm(bk[:, 0:128], [(Q0[c][:], P0)])
                        K.mm(bk[:, 128:256], [(P0, Q0[c][:])])
                        K.copy("act" if c % 2 == 0 else "dve", PQT[c][1][:, 0:2, :], bk[:, 0:256].rearrange("p (a t) -> p a t", t=128))
                    yield
                    for lv in range(1, 7):
                        cur, nxt = lv % 2, (lv + 1) % 2
                        for c in range(NCH):
                            bk = dblb[c]
                            Pk = PQT[c][cur][:, 0, :]
                            Qk = PQT[c][cur][:, 1, :]
                            Tk = PQT[c][cur][:, 2, :]
                            eng = "act" if (c + lv) % 2 == 0 else "dve"
                            if lv < 6:
                                K.mm(bk[:, 0:128], [(Qk, Pk)])
                                K.mm(bk[:, 128:256], [(Pk, Qk)])
                                K.mm(bk[:, 256:384], [(Qk, Tk), (ident16[:], Tk)])
                                K.copy(eng, PQT[c][nxt][:], bk[:, 0:384].rearrange("p (a t) -> p a t", t=128))
                            else:
                                K.mm(bk[:, 256:384], [(Qk, Tk), (ident16[:], Tk)])
                                K.copy(eng, Tfin[h][c][:], bk[:, 256:384])
                        yield
                for c in range(NCH):
                    for h in range(2):
                        hs = slice(h * 64, h * 64 + 64)
                        K.mm(seqb[:, h * 64:(h + 1) * 64], [(AR[hs, c, 0, :], S16[l][hs, hp, :]),
                                                             (Amat[h][c][:, 256:384], Vtok[:, c, hs])])
                    K.copy("act", XT16[:], seqb[:, 0:128])
                    yield
                    for h in range(2):
                        hs = slice(h * 64, h * 64 + 64)
                        K.mm(seqb[:, 128 + h * 64:128 + (h + 1) * 64], [(Tfin[h][c][:], XT16[:, hs])])
                    K.copy("dve", UT16[:], seqb[:, 128:256])
                    yield
                    for h in range(2):
                        hs = slice(h * 64, h * 64 + 64)
                        K.mm(yb[hs, c * C:(c + 1) * C], [(S16[l][hs, hp, :], AR[hs, c, 1, :]),
                                                          (UT16[:, hs], Amat[h][c][:, 128:256]),
                                                          (Vtok[:, c, hs], Amat[h][c][:, 384:512])])
                        K.mm(seqb[hs, 256:320], [(KBtok[:, 4 + c, hs], UT16[:, hs]),
                                                  (KBtok[:, c, hs], Vtok[:, c, hs])])
                    K.stt(S32[l][:, hp, :], S32[l][:, hp, :], WC[:, c:c + 1], seqb[:, 256:320], ALU.mult, ALU.add)
                    K.copy("act", S16[l][:, hp, :], S32[l][:, hp, :])
                    yield
                K.copy("act", W["y32"][:], yb[:])
                K.copy("dve", H["y16"][:], yb[:])
                if it == 0 and l == 0 and hp == 0:
                    K.dump("y32", W["y32"][:], [128, TT])
                yield
                p = bank(); K.mm(p[:], [(bd16[:], H["y16"][:])])
                K.stt(W["yc"][:], p[:], -1.0 / 64, W["y32"][:], ALU.mult, ALU.add)
                yield
                K.act(H["yc2"][:], W["yc"][:], AF.Square)
                p = bank(); K.mm(p[:], [(bd16[:], H["yc2"][:])])
                K.act(W["sd"][:], p[:], AF.Ln, scale=1.0 / 64, bias=epsc[:, 2:3])
                yield
                K.act(W["rsd"][:], W["sd"][:], AF.Exp, scale=-0.5)
                K.tt("pool", W["yn"][:], W["yc"][:], W["rsd"][:], ALU.mult)
                K.act(W["yg"][:], W["yn"][:], AF.Identity, scale=V[:, V_GNG + hp:V_GNG + hp + 1], bias=V[:, V_GNB + hp:V_GNB + hp + 1])
                yield
                K.tt("pool", W["o1"][:], W["yg"][:], bon32[:], ALU.add)
                K.tt("pool", W["o2"][:], W["o1"][:], g32[:], ALU.mult)
                K.tt("pool", merged[hp][:], W["o2"][:], gA16[:], ALU.mult)
                if it == 0 and l == 0 and hp in (0, 7):
                    K.dump("mg%d" % hp, merged[hp][:], [128, TT], BF16)
                yield

            def drain(g):
                for _ in g:
                    pass

            def interleave(g1, g2):
                a1 = a2 = True
                while a1 or a2:
                    if a1:
                        try:
                            next(g1)
                        except StopIteration:
                            a1 = False
                    if a2:
                        try:
                            next(g2)
                        except StopIteration:
                            a2 = False

            if K.stage < 3:
                for hp in range(8):
                    drain(front(hp, hp % 2))
            else:
                drain(front(0, 0))
                for hp in range(8):
                    if hp + 1 < 8:
                        interleave(back(hp, hp % 2), front(hp + 1, (hp + 1) % 2))
                    else:
                        drain(back(hp, hp % 2))
            if K.stage < 4:
                continue
            for cb in range(8):
                wcb = load_piece(l, "cb%d" % cb)
                pa = proj(wcb, 0)
                pb = proj(wcb, 128)
                K.act(sgb[:], pb[:], AF.Sigmoid)
                K.copy("pool", gbuf[:, 0:30], ctail[l][:, cb, :])
                K.tt("dve", gbuf[:, 30:30 + TT], pa[:], sgb[:], ALU.mult)
                K.copy("pool", ctail[l][:, cb, :], gbuf[:, TT:TT + 30])
                pg = proj(wcb, 256)
                K.act(gB16v[cb], pg[:], AF.Sigmoid)
                K.tt("pool", dg[:], ident16[:].unsqueeze(1).to_broadcast([128, 31, 128]),
                     V[:, V_CW + cb * 31:V_CW + (cb + 1) * 31].unsqueeze(2).to_broadcast([128, 31, 128]), ALU.mult)
                pz = bank()
                K.mm(pz[:], [(dg[:, k, :], gbuf[:, k:k + TT]) for k in range(31)])
                K.act(z32[cb][:], pz[:], AF.Identity, bias=V[:, V_CB + cb:V_CB + cb + 1])
            if it == 0 and l == 0:
                K.dump("z0", z32[0][:], [128, TT])
            for cb in range(8):
                K.copy("act", sq16[cb][:], z32[cb][:])
            p = bank(); K.mm(p[:], [(ones16[:], sq16[cb][:]) for cb in range(8)])
            for cb in range(8):
                K.stt(z32[cb][:], p[:], -1.0 / D, z32[cb][:], ALU.mult, ALU.add)
            for cb in range(8):
                K.act(sq16[cb][:], z32[cb][:], AF.Square)
            p = bank(); K.mm(p[:], [(ones16[:], sq16[cb][:]) for cb in range(8)])
            K.act(rs32[:], p[:], AF.Ln, scale=1.0 / D, bias=epsc[:, 1:2])
            K.act(rstd[:], rs32[:], AF.Exp, scale=-0.5)
            for cb in range(8):
                t = t32[cb % 2]
                K.tt("pool", t[:], z32[cb][:], rstd[:], ALU.mult)
                K.act(t[:], t[:], AF.Silu, scale=V[:, V_LNG + cb:V_LNG + cb + 1], bias=V[:, V_LNB + cb:V_LNB + cb + 1])
                K.tt("pool", merged[8 + cb][:], t[:], gB16v[cb], ALU.mult)
            if it == 0 and l == 0:
                K.dump("mg8", merged[8][:], [128, TT], BF16)
            for g in range(4):
                wo = load_piece(l, "wo%d" % g)
                for o2 in range(2):
                    ob = g * 2 + o2
                    p = bank()
                    K.mm(p[:], [(wo[:, o2 * 16 + kc, :], merged[kc][:]) for kc in range(16)])
                    K.stt(xt[ob][:], p[:], DVl[:, DV_MOD + M_GTM + ob:DV_MOD + M_GTM + ob + 1], xt[ob][:], ALU.mult, ALU.add)
            if it == 0 and l == 0:
                K.dump("xmix0", xt[0][:], [128, TT])
            if K.stage < 5:
                continue
            rmsnorm_to_ht(l, DV_GSCF, DV_MOD + M_SHF)
            for pc in range(8):
                w1 = load_piece(l, "w1%d" % pc)
                for j in range(4):
                    hb = pc * 4 + j
                    p = proj(w1, j * 128)
                    t = t32[hb % 2]
                    K.act(t[:], p[:], AF.Relu)
                    K.tt("pool", hidv[hb], t[:], t[:], ALU.mult)
            for ob in range(8):
                w2 = load_piece(l, "w2%d" % ob)
                p = bank()
                K.mm(p[:], [(w2[:, hb, :], hidv[hb]) for hb in range(32)])
                K.stt(xt[ob][:], p[:], DVl[:, DV_MOD + M_GTF + ob:DV_MOD + M_GTF + ob + 1], xt[ob][:], ALU.mult, ALU.add)
            if it == 0 and l == 0:
                K.dump("xffn0", xt[0][:], [128, TT])
        for k in range(NB):
            K.act(sq16[k][:], xt[k][:], AF.Square)
        p = bank()
        K.mm(p[:], [(ones16[:], sq16[k][:]) for k in range(NB)])
        K.act(rs32[:], p[:], AF.Ln, scale=1.0 / D, bias=epsc[:, 0:1])
        K.act(rstd[:], rs32[:], AF.Exp, scale=-0.5)
        for k in range(NB):
            t = t32[k % 2]
            K.stt(t[:], xt[k][:], vecs[0][:, V_FG + k:V_FG + k + 1], rstd[:], ALU.mult, ALU.mult)
            K.dma("sp", oTv[k, :, t0:t0 + TT], t[:], w=["outT%d_%d" % (it, k)])
            K.out_tokens.append("outT%d_%d" % (it, k))
    K.S.wait_all("sp", K.out_tokens)
    K.S.emit(K.st)
    K.st.close()
    return K


def dblb_view(dblb, c):
    col = (c % 2) * 256
    return dblb[c // 2][:, col:col + 256].rearrange("p (a t) -> p a t", t=128)


def _fm(v):
    v = np.asarray(v, np.float32)
    return np.ascontiguousarray(v.reshape(-1, 128).T)


def prep_shared(inp):
    shared = {}
    for l in range(L):
        vec = np.zeros((128, NV), np.float32)
        vec[:, V_GMIX:V_GMIX + 8] = _fm(inp["norm_mix_gain"][l])
        vec[:, V_GFFN:V_GFFN + 8] = _fm(inp["norm_ffn_gain"][l])
        vec[:, V_MU:V_MU + 26] = _fm(inp["mu_shift"][l])
        if l >= 1:
            vec[0:32, V_MUV] = inp["mu_vres"][l - 1]
            vec[:, V_V0:V_V0 + 8] = _fm(inp["v0"][l - 1])
        vec[:, V_W0:V_W0 + 8] = _fm(inp["w0"][l])
        vec[:, V_A0:V_A0 + 8] = _fm(inp["a0"][l])
        vec[:, V_KK:V_KK + 8] = _fm(inp["k_k"][l])
        vec[:, V_KA:V_KA + 8] = _fm(inp["k_a"][l])
        vec[:, V_RK:V_RK + 8] = _fm(inp["r_k"][l].reshape(-1))
        vec[:, V_GNG:V_GNG + 8] = _fm(inp["gn_gain"][l])
        vec[:, V_GNB:V_GNB + 8] = _fm(inp["gn_bias"][l])
        vec[:, V_CB:V_CB + 8] = _fm(inp["conv_b"][l])
        vec[:, V_LNG:V_LNG + 8] = _fm(inp["conv_ln_gain"][l])
        vec[:, V_LNB:V_LNB + 8] = _fm(inp["conv_ln_bias"][l])
        vec[:, V_FG:V_FG + 8] = _fm(inp["final_gain"])
        vec[:, V_ADAB:V_ADAB + 48] = _fm(inp["ada_b"][l])
        cw = np.asarray(inp["conv_w"][l], np.float32)
        vec[:, V_CW:V_CW + 248] = cw.reshape(31, 8, 128).transpose(2, 1, 0).reshape(128, 248)
        shared["vecs%d" % l] = vec
        wsm = np.zeros((128, 3, D), np.float32)
        wsm[0:64, 0] = inp["w_decay_up"][l]
        wsm[64:128, 0] = inp["w_aaa_up"][l]
        wsm[:, 1] = inp["w_gate_up"][l]
        if l >= 1:
            wsm[0:32, 2] = inp["w_vres_up"][l - 1]
        shared["wsm%d" % l] = wsm
        shared["ada%d" % l] = np.ascontiguousarray(inp["ada_w"][l], dtype=np.float32)
        lay, tot = _wbig_layout(l)
        wb = np.empty((128, tot), np.float32)
        win = np.asarray(inp["w_in"][l], np.float32)
        if l >= 1:
            win = np.concatenate([win, np.asarray(inp["w_in_vres"][l - 1], np.float32)], axis=1)

        def put(key, cols_matrix):
            off, k, c = lay[key]
            wb[:, off:off + k * c] = cols_matrix.reshape(k, 128, c).transpose(1, 0, 2).reshape(128, k * c)
        lo_idx = list(range(3072, 3328)) + (list(range(N_COLS, N_COLS + 32)) if l >= 1 else [])
        put("lo", win[:, lo_idx])
        for hp in range(8):
            idx = np.concatenate([np.arange(hp * 128, hp * 128 + 128), 1024 + np.arange(hp * 128, hp * 128 + 128),
                                  2048 + np.arange(hp * 128, hp * 128 + 128), 5376 + np.arange(hp * 128, hp * 128 + 128)])
            put("hp%d" % hp, win[:, idx])
        for cb in range(8):
            idx = np.concatenate([3328 + np.arange(cb * 128, cb * 128 + 128), 4352 + np.arange(cb * 128, cb * 128 + 128),
                                  6400 + np.arange(cb * 128, cb * 128 + 128)])
            put("cb%d" % cb, win[:, idx])
        wo = np.asarray(inp["w_out"][l], np.float32)
        for g in range(4):
            off, k, c = lay["wo%d" % g]
            blk = wo[:, g * 256:(g + 1) * 256].reshape(16, 128, 2, 128)
            wb[:, off:off + k * c] = blk.transpose(1, 2, 0, 3).reshape(128, 32 * 128)
        w1 = np.asarray(inp["w_ff_in"][l], np.float32)
        for pc in range(8):
            put("w1%d" % pc, w1[:, pc * 512:(pc + 1) * 512])
        w2 = np.asarray(inp["w_ff_out"][l], np.float32)
        for ob in range(8):
            put("w2%d" % ob, w2[:, ob * 128:(ob + 1) * 128])
        shared["wbig%d" % l] = wb
    return shared


def prep_core(inp, b):
    return {"xT": np.ascontiguousarray(np.asarray(inp["x"][b], np.float32).T),
            "cT": _fm(inp["c"][b])}


_CACHE = {}


def kernel(**inputs):
    inp = {k: np.asarray(v) for k, v in inputs.items()}
    if "K" not in _CACHE:
        _CACHE["K"] = build()
    K = _CACHE["K"]
    shared = prep_shared(inp)
    in_maps = []
    for b in range(NCORES):
        m = dict(shared)
        m.update(prep_core(inp, b))
        in_maps.append(m)
    res = run_bass_kernel_spmd(K.nc, in_maps, core_ids=list(range(NCORES)))
    out = np.stack([np.ascontiguousarray(res.results[b]["outT"].T) for b in range(NCORES)], axis=0)
    return out.astype(np.float32)
```

```python
import contextlib
import math
import numpy as np
import concourse.bass as bass
import concourse.mybir as mybir
from concourse.bass_utils import run_bass_kernel_spmd

F32 = mybir.dt.float32
BF16 = mybir.dt.bfloat16
AF = mybir.ActivationFunctionType
ALU = mybir.AluOpType

D = 1024
T = 2048
NB = 8
TT = 512
C = 128
NCH = TT // C
L = 2
NCORES = 8
N_SHIFT = 3328
N_COLS = 7424
RMS_EPS = 1e-6
LN_EPS = 1e-5
GN_EPS = 64e-5
CEXP = -math.exp(-0.5)

V_GMIX, V_GFFN, V_MU, V_MUV, V_W0, V_A0, V_KK, V_KA, V_RK, V_GNG, V_GNB, V_V0, V_CB, V_LNG, V_LNB, V_FG, V_ADAB, V_CW = (
    0, 8, 16, 42, 43, 51, 59, 67, 75, 83, 91, 99, 107, 115, 123, 131, 139, 187)
NV = 187 + 8 * 31
DV_MOD, DV_GSCM, DV_GSCF, DV_OMU = 0, 48, 56, 64
ND = 64 + 27
M_SHM, M_SCM, M_GTM, M_SHF, M_SCF, M_GTF = 0, 8, 16, 24, 32, 40

def _wbig_layout(l):
    off = 0
    lay = {}
    nlo = 256 + (32 if l >= 1 else 0)
    lay["lo"] = (off, 8, nlo); off += 8 * nlo
    for hp in range(8):
        lay["hp%d" % hp] = (off, 8, 512); off += 8 * 512 + 3 * 128
    for cb in range(8):
        lay["cb%d" % cb] = (off, 8, 256); off += 8 * 256
    for cb in range(8):
        lay["gb%d" % cb] = (off, 8, 128); off += 8 * 128
    for g in range(4):
        lay["wo%d" % g] = (off, 32, 128); off += 32 * 128
    for pc in range(8):
        lay["w1%d" % pc] = (off, 8, 512); off += 8 * 512
    for ob in range(8):
        lay["w2%d" % ob] = (off, 32, 128); off += 32 * 128
    return lay, off


ENGS = ("pe", "act", "dve", "pool", "sp")
DMA_SEMS = {"sp": 8, "pool": 16, "act": 4}
SEM_LAT = 0.25
ACT_WINDOW = 0.3
LIST_SCHED = True


class Sched:
    def __init__(self, nc):
        self.nc = nc
        self.recs = []
        self.last_w = {}
        self.readers = {}

    def _deps(self, reads, writes):
        deps = set()
        for t in reads:
            w = self.last_w.get(t)
            if w is not None:
                deps.add(w)
        for t in writes:
            w = self.last_w.get(t)
            if w is not None:
                deps.add(w)
            deps.update(self.readers.get(t, ()))
        return deps

    def _commit(self, reads, writes, me):
        for t in reads:
            self.readers.setdefault(t, []).append(me)
        for t in writes:
            self.last_w[t] = me
            self.readers[t] = []

    def op(self, eng, fn, reads=(), writes=(), dur=0.5, tbl=0):
        deps = self._deps(reads, writes)
        me = len(self.recs)
        self.recs.append([eng, fn, deps, dur, False, dur, (list(writes) or ["?"])[0], tbl])
        self._commit(reads, writes, me)

    def dma(self, queue, fn, reads=(), writes=(), nbytes=0):
        deps = self._deps(reads, writes)
        me = len(self.recs)
        self.recs.append([queue, fn, deps, 0.15, True, 2.0 + nbytes / 340e3, (list(writes) or ["?"])[0]])
        self._commit(reads, writes, me)

    def wait_all(self, eng, tokens):
        deps = set(self.last_w[t] for t in tokens if t in self.last_w)
        me = len(self.recs)
        self.recs.append([eng, None, deps, 0.01, False, 0.01, "final"])
        self.final = me

    def finalize(self):
        recs = self.recs
        n = len(recs)
        order = {e: [] for e in ENGS}
        if not LIST_SCHED:
            for i, r in enumerate(recs):
                order[r[0]].append(i)
            self.order = order
            return
        succs = [[] for _ in range(n)]
        indeg = [0] * n
        for i, r in enumerate(recs):
            for d in r[2]:
                succs[d].append(i)
            indeg[i] = len(r[2])
        prio = [0.0] * n
        for i in range(n - 1, -1, -1):
            m = 0.0
            for sx in succs[i]:
                if prio[sx] > m:
                    m = prio[sx]
            prio[i] = m + recs[i][5] + SEM_LAT
        import heapq
        ready = {e: [] for e in ENGS}
        finish = [0.0] * n
        rdy_t = [0.0] * n
        for i in range(n):
            if indeg[i] == 0:
                heapq.heappush(ready[recs[i][0]], (0.0, -prio[i], i))
        efree = {e: 0.0 for e in ENGS}
        done = 0
        WINDOW = 0.3
        cur_tbl = 0
        self.n_tbl_switch = 0
        while done < n:
            best = None
            for e in ENGS:
                h = ready[e]
                if not h:
                    continue
                st = max(efree[e], h[0][0])
                if best is None or st < best[0]:
                    best = (st, e)
            st, e = best
            h = ready[e]
            cands = []
            win = ACT_WINDOW if e == "act" else WINDOW
            while h and h[0][0] <= st + win and len(cands) < 32:
                cands.append(heapq.heappop(h))
            cands.sort(key=lambda c: c[1])
            pick = cands[0]
            pen = 0.0
            if e == "act":
                ok = [c for c in cands if len(recs[c[2]]) < 8 or recs[c[2]][7] in (0, cur_tbl)]
                if ok:
                    pick = ok[0]
                else:
                    pen = 1.3
                    self.n_tbl_switch += 1
                t_ = recs[pick[2]][7] if len(recs[pick[2]]) >= 8 else 0
                if t_:
                    cur_tbl = t_
            for c in cands:
                if c is not pick:
                    heapq.heappush(h, c)
            i = pick[2]
            start = max(efree[e], pick[0]) + pen
            efree[e] = start + recs[i][3]
            finish[i] = start + recs[i][5]
            order[e].append(i)
            done += 1
            for sx in succs[i]:
                t = finish[i] + SEM_LAT
                if t > rdy_t[sx]:
                    rdy_t[sx] = t
                indeg[sx] -= 1
                if indeg[sx] == 0:
                    heapq.heappush(ready[recs[sx][0]], (rdy_t[sx], -prio[sx], sx))
        self.order = order
        self.sim_time = max(finish)

    def emit(self, st):
        nc = self.nc
        recs = self.recs
        self.finalize()
        sems = {}
        for e in ENGS:
            sems[e] = st.enter_context(nc.semaphore("s_" + e))
        for q, nq in DMA_SEMS.items():
            for k in range(nq):
                sems[("dma", q, k)] = st.enter_context(nc.semaphore("s_dma_%s%d" % (q, k)))
        comp = [None] * len(recs)
        prev_same_sem = {}
        for e in ENGS:
            cc = 0
            dk = 0
            for i in self.order[e]:
                r = recs[i]
                if r[1] is None:
                    continue
                if r[4]:
                    nq = DMA_SEMS[e]
                    key = ("dma", e, dk % nq)
                    val = 16 * (dk // nq + 1)
                    comp[i] = (key, val)
                    if dk >= nq:
                        prev_same_sem[i] = (key, val - 16)
                    dk += 1
                else:
                    cc += 1
                    comp[i] = (e, cc)
        block = st.enter_context(nc.Block())

        def run(eng_name):
            def body(engine):
                known = {}
                for i in self.order[eng_name]:
                    r = recs[i]
                    need = {}
                    for d in r[2]:
                        k, v = comp[d]
                        if eng_name == "pe" and k == "pe":
                            continue
                        if v > need.get(k, 0):
                            need[k] = v
                    if i in prev_same_sem:
                        k, v = prev_same_sem[i]
                        if v > need.get(k, 0):
                            need[k] = v
                    for k, v in need.items():
                        if known.get(k, 0) >= v:
                            continue
                        known[k] = v
                        engine.wait_ge(sems[k], v)
                    if r[1] is None:
                        continue
                    ins = r[1](engine)
                    k, v = comp[i]
                    ins.then_inc(sems[k], 16 if r[4] else 1)
            return body

        block.tensor(run("pe"))
        block.scalar(run("act"))
        block.vector(run("dve"))
        block.gpsimd(run("pool"))
        block.sync(run("sp"))


def _tok(ap):
    return ap.name


def _n(ap):
    n = 1
    for d in ap.shape[1:]:
        n *= int(d)
    return n


def _bytes(ap):
    n = int(ap.shape[0]) * _n(ap)
    return n * (2 if ap.dtype == BF16 else 4)


class KB:
    def __init__(self, ntiles=4, nlayers=2, dbg=None, stage=99):
        self.ntiles = ntiles
        self.nlayers = nlayers
        self.dbg_names = dbg or []
        self.stage = stage
        self.nc = bass.Bass("TRN2", target_bir_lowering=False)
        self.st = contextlib.ExitStack()
        self.S = Sched(self.nc)
        self.dbg_out = {}
        self.out_tokens = []
        self.psum_names = set()

    def sb(self, name, shape, dt):
        return self.st.enter_context(self.nc.sbuf_tensor(name, shape, dt))

    def ps(self, name, shape, dt):
        self.psum_names.add(name)
        return self.st.enter_context(self.nc.psum_tensor(name, shape, dt))

    def dram_in(self, name, shape, dt=F32):
        return self.nc.dram_tensor(name, shape, dt, kind="ExternalInput").ap()

    def dram_out(self, name, shape, dt=F32):
        return self.nc.dram_tensor(name, shape, dt, kind="ExternalOutput").ap()

    def _rw(self, outs, ins, r, w):
        reads = list(r) if r is not None else [_tok(a) for a in ins if hasattr(a, "name")]
        writes = list(w) if w is not None else [_tok(a) for a in outs]
        ex = [t for t in reads if t in self.psum_names]
        if ex:
            reads = [t for t in reads if t not in self.psum_names]
            writes = writes + [t for t in ex if t not in writes]
        return reads, writes

    def act(self, out, in_, func, scale=1.0, bias=0.0, r=None, w=None):
        extra = [a for a in (scale, bias) if hasattr(a, "name")]
        reads, writes = self._rw([out], [in_] + extra, r, w)
        tbl = {AF.Exp: 1, AF.Ln: 1, AF.Sigmoid: 2, AF.Tanh: 2, AF.Silu: 3}.get(func, 0)
        self.S.op("act", lambda e: e.activation(out=out, in_=in_, func=func, scale=scale, bias=bias), reads, writes, dur=0.22 + _n(out) / 1400.0, tbl=tbl)

    def tt(self, eng, out, in0, in1, op, r=None, w=None):
        reads, writes = self._rw([out], [in0, in1], r, w)
        self.S.op(eng, lambda e: e.tensor_tensor(out=out, in0=in0, in1=in1, op=op), reads, writes, dur=self._vdur(eng, out))

    def ts(self, eng, out, in0, s1, s2, op0, op1=None, r=None, w=None):
        extra = [a for a in (s1, s2) if hasattr(a, "name")]
        reads, writes = self._rw([out], [in0] + extra, r, w)
        if op1 is None:
            self.S.op(eng, lambda e: e.tensor_scalar(out=out, in0=in0, scalar1=s1, scalar2=None, op0=op0), reads, writes, dur=self._vdur(eng, out))
        else:
            self.S.op(eng, lambda e: e.tensor_scalar(out=out, in0=in0, scalar1=s1, scalar2=s2, op0=op0, op1=op1), reads, writes, dur=self._vdur(eng, out))

    def stt(self, out, in0, scalar, in1, op0, op1, r=None, w=None):
        extra = [scalar] if hasattr(scalar, "name") else []
        reads, writes = self._rw([out], [in0, in1] + extra, r, w)
        self.S.op("dve", lambda e: e.scalar_tensor_tensor(out=out, in0=in0, scalar=scalar, in1=in1, op0=op0, op1=op1), reads, writes, dur=self._vdur("dve", out))

    def copy(self, eng, out, in_, r=None, w=None):
        reads, writes = self._rw([out], [in_], r, w)
        if eng == "act":
            self.S.op("act", lambda e: e.copy(out=out, in_=in_), reads, writes, dur=0.22 + _n(out) / 1400.0)
        else:
            self.S.op(eng, lambda e: e.tensor_copy(out=out, in_=in_), reads, writes, dur=self._vdur(eng, out))

    def recip(self, out, in_, r=None, w=None):
        reads, writes = self._rw([out], [in_], r, w)
        self.S.op("dve", lambda e: e.reciprocal(out=out, in_=in_), reads, writes, dur=0.1 + _n(out) / 155.0)

    def scan(self, out, d0, d1, r=None, w=None):
        reads, writes = self._rw([out], [d0, d1], r, w)
        self.S.op("dve", lambda e: e.tensor_tensor_scan(out=out, data0=d0, data1=d1, initial=0.0, op0=ALU.mult, op1=ALU.add), reads, writes, dur=0.12 + _n(out) / 480.0)

    def memset(self, eng, out, val, r=None, w=None):
        reads, writes = self._rw([out], [], r, w)
        self.S.op(eng, lambda e: e.memset(out, val), reads, writes, dur=self._vdur(eng, out))

    def mm(self, out, pairs, r=None, w=None):
        ins = []
        for a, b in pairs:
            ins += [a, b]
        reads, writes = self._rw([out], ins, r, w)
        n = len(pairs)

        def fn(e):
            last = None
            for i, (a, b) in enumerate(pairs):
                last = e.matmul(out, lhsT=a, rhs=b, start=(i == 0), stop=(i == n - 1))
            return last
        self.S.op("pe", fn, reads, writes, dur=sum(0.035 + max(_n(b_), 64) / 2000.0 for a_, b_ in pairs))

    def mmx(self, items):
        ins = []
        outs = []
        for o, a, b, st_, sp_ in items:
            ins += [a, b]
            outs.append(o)
        reads, writes = self._rw(outs[:1], ins, None, None)

        def fn(e):
            last = None
            for o, a, b, st_, sp_ in items:
                last = e.matmul(o, lhsT=a, rhs=b, start=st_, stop=sp_)
            return last
        self.S.op("pe", fn, reads, writes, dur=sum(0.035 + max(_n(b_), 64) / 2000.0 for o_, a_, b_, s1, s2 in items))

    def _vdur(self, eng, out):
        if eng == "pool":
            return 0.2 + _n(out) / 570.0
        return 0.12 + _n(out) / 960.0

    def tr(self, out, in_, ident, r=None, w=None):
        reads, writes = self._rw([out], [in_, ident], r, w)
        self.S.op("pe", lambda e: e.transpose(out=out, in_=in_, identity=ident), reads, writes, dur=0.1)

    def dma(self, q, out, in_, r=None, w=None):
        reads, writes = self._rw([out], [in_], r, w)
        self.S.dma(q, lambda e: e.dma_start(out=out, in_=in_), reads, writes, nbytes=max(_bytes(out), _bytes(in_)))

    def dump(self, name, ap, shape, dt=F32):
        if name not in self.dbg_names:
            return
        if dt != F32 or ap.dtype != F32:
            if not hasattr(self, "_dbgtmp"):
                self._dbgtmp = self.sb("dbgtmp", [128, 1024], F32)
            tmp = self._dbgtmp[0:shape[0], 0:shape[1]]
            self.copy("dve", tmp, ap)
            ap = tmp
        o = self.dram_out("dbg_" + name, list(shape))
        self.dma("sp", o, ap)
        self.out_tokens.append(_tok(o))
        self.dbg_out[name] = "dbg_" + name


def build(ntiles=4, nlayers=2, dbg=None, stage=99):
    K = KB(ntiles, nlayers, dbg, stage)
    nc = K.nc
    xT = K.dram_in("xT", [D, T])
    cT = K.dram_in("cT", [128, 8])
    ada = [K.dram_in("ada%d" % l, [D, 6 * D]) for l in range(L)]
    vecs_d = [K.dram_in("vecs%d" % l, [128, NV]) for l in range(L)]
    lay = [_wbig_layout(l) for l in range(L)]
    wbig = [K.dram_in("wbig%d" % l, [128, lay[l][1]]) for l in range(L)]
    outT = K.dram_out("outT", [D, T])

    sb, ps = K.sb, K.ps
    ident16 = sb("ident16", [128, 128], BF16)
    ones16 = sb("ones16", [128, 128], BF16)
    bd16 = sb("bd16", [128, 128], BF16)
    onesf = sb("onesf", [128, 128], F32)
    mask512 = sb("mask512", [128, 512], BF16)
    masksl = sb("masksl", [128, 128], BF16)
    rst = sb("rst", [128, 512], BF16)
    vecs = [sb("vecs_s%d" % l, [128, NV], F32) for l in range(L)]
    dv = [sb("dv%d" % l, [128, ND], F32) for l in range(L)]
    c32 = sb("c32", [128, 8], F32)
    epsc = sb("epsc", [128, 4], F32)
    c16 = sb("c16", [128, 8], BF16)
    S32 = [sb("S32_%d" % l, [128, 8, 64], F32) for l in range(L)]
    S16 = [sb("S16_%d" % l, [128, 8, 64], BF16) for l in range(L)]
    ctail = [sb("ctail%d" % l, [128, 8, 30], F32) for l in range(L)]
    carry = [sb("carry%d" % l, [128, 27], F32) for l in range(L)]
    xt = [sb("xt%d" % k, [128, TT], F32) for k in range(NB)]
    ht = [sb("ht%d" % k, [128, TT], BF16) for k in range(NB)]
    merged = [sb("mg%d" % k, [128, TT], BF16) for k in range(16)]
    vf = [sb("vf%d" % k, [128, TT], BF16) for k in range(NB)]
    NWP = 3
    WPN = 4480
    wpool = [sb("wp%d" % i, [128, WPN], BF16) for i in range(NWP)]
    lo16 = [sb("lo16_%d" % j, [128, TT], BF16) for j in range(2)]
    vlo16 = sb("vlo16", [32, TT], BF16)
    names32 = ["r32", "k32", "v32", "sg32", "a32", "sv32", "dd32", "cs32", "E1", "E3",
               "kk32", "nrm", "km", "b32", "y32", "yc", "sd"]
    W = {n: sb(n, [128, TT], F32) for n in names32}
    W["d2"] = W["dd32"]; W["E2"] = W["dd32"]; W["d4"] = W["sv32"]; W["E4"] = W["sv32"]; W["rn"] = W["nrm"]
    W["kkn"] = W["kk32"]; W["t1"] = W["km"]; W["rsd"] = W["sd"]
    W["yn"] = W["yc"]; W["yg"] = W["yc"]; W["o1"] = W["yc"]; W["o2"] = W["yc"]
    names16 = ["kk2", "Kh16", "Bh16", "v16", "rk16", "y16", "yc2", "sqa", "sqb", "sqc"]
    H = {n: sb(n, [128, TT], BF16) for n in names16}
    sq16 = [H[n] for n in ("kk2", "Kh16", "Bh16", "v16", "rk16", "sqa", "sqb", "sqc")]
    g32p = [sb("g32_%d" % i, [128, TT], F32) for i in range(2)]
    bon32p = [sb("bon32_%d" % i, [128, TT], F32) for i in range(2)]
    gA16p = [sb("gA16_%d" % i, [128, TT], BF16) for i in range(2)]
    kt16p = [sb("kt16_%d" % i, [128, TT], BF16) for i in range(2)]
    bt16p = [sb("bt16_%d" % i, [128, TT], BF16) for i in range(2)]
    WCp = [sb("WC_%d" % i, [128, NCH], F32) for i in range(2)]
    lo32 = [W["yc"], W["y32"]]
    vlo32 = bon32p[0]
    rs32 = W["nrm"]; rstd = W["sd"]; t32 = [W["yc"], W["y32"]]
    ARp = [sb("AR_%d" % i, [128, NCH, 2, C], BF16) for i in range(2)]
    KBtokp = [sb("KBtok_%d" % i, [128, 8, 128], BF16) for i in range(2)]
    Vtokp = [sb("Vtok_%d" % i, [128, 4, 128], BF16) for i in range(2)]
    Amatp = [[[sb("Amat%d_%d_%d" % (q, h, c), [128, 512], BF16) for c in range(NCH)] for h in range(2)] for q in range(2)]
    Tfinp = [[[sb("Tfin%d_%d_%d" % (q, h, c), [128, 128], BF16) for c in range(NCH)] for h in range(2)] for q in range(2)]
    Q0 = [sb("Q0_%d" % i, [128, 128], BF16) for i in range(4)]
    PQT = [[sb("PQT%d_%d" % (i, j), [128, 3, 128], BF16) for j in range(2)] for i in range(4)]
    XT16 = sb("XT16", [128, 128], BF16)
    UT16 = sb("UT16", [128, 128], BF16)
    sgb = sb("sgb16", [128, TT], BF16)
    gbuf = sb("gbuf", [128, 30 + TT], F32)
    cacc = [sb("cacc%d" % i, [128, TT], F32) for i in range(2)]
    ctmp = [sb("ctmp%d" % i, [128, TT], F32) for i in range(2)]
    z16 = [sb("z16_%d" % k, [128, TT], BF16) for k in range(NB)]

    def halves(t):
        v = t[:].bitcast(BF16)
        return [v[:, 0:TT], v[:, TT:2 * TT]]
    hidv = []
    for t_ in [W[n] for n in ("r32", "k32", "v32", "sg32", "a32", "sv32", "dd32", "kk32", "cs32", "E1", "E3", "km", "b32")] + [g32p[0], g32p[1], bon32p[0]]:
        hidv += halves(t_)
    mmb = [ps("mmb%d" % i, [128, 512], F32) for i in range(2)]
    dblb = [ps("dblb%d" % i, [128, 512], F32) for i in range(4)]
    seqb = ps("seqb", [128, 512], F32)
    yb = ps("yb", [128, 512], F32)
    mm_rr = [0]

    def bank():
        b = mmb[mm_rr[0] % 2]
        mm_rr[0] += 1
        return b

    def dbl_region(i, j):
        if j < 2:
            col = (i % 2) * 256 + j * 128
            return dblb[i // 2][:, col:col + 128]
        return dblb[2][:, i * 128:(i + 1) * 128]

    wp_rr = [0]
    PF = 2
    piece_list = []
    for l_ in range(K.nlayers):
        for pc in range(12):
            piece_list.append(("ada", l_, pc))
    for it_ in range(K.ntiles):
        for l_ in range(K.nlayers):
            if K.stage >= 4:
                keys = ["lo", "hp0"]
                for i in range(8):
                    if i + 1 < 8:
                        keys.append("hp%d" % (i + 1))
                    keys.append("cb%d" % i)
                keys += ["gb%d" % i for i in range(8)] + ["wo%d" % i for i in range(4)]
            else:
                keys = ["lo"] + ["hp%d" % i for i in range(8)]
            if K.stage >= 5:
                keys += ["w1%d" % i for i in range(8)] + ["w2%d" % i for i in range(8)]
            for key in keys:
                piece_list.append(("w", l_, key))
    issued = [0]
    piece_dst = {}

    def _issue(idx):
        kind, l_, key = piece_list[idx]
        buf = wpool[idx % NWP]
        if kind == "ada":
            src = ada[l_].rearrange("(k p) c -> p k c", p=128)[:, :, key * 512:(key + 1) * 512]
            dst = buf[:, 0:4096].rearrange("p (k c) -> p k c", c=512)
        elif key.startswith("hp"):
            off, k, c = lay[l_][0][key]
            src = wbig[l_][:, off:off + 4480].rearrange("p (a b) -> p a b", b=640)
            K.dma("pool", buf[:, 0:4480].rearrange("p (a b) -> p a b", b=640), src)
            piece_dst[idx] = (buf[:, 0:4096].rearrange("p (k c) -> p k c", c=512), buf[:, 4096:4480].rearrange("p (a c) -> p a c", c=128))
            return
        else:
            off, k, c = lay[l_][0][key]
            src = wbig[l_][:, off:off + k * c].rearrange("p (k c) -> p k c", c=c)
            dst = buf[:, 0:k * c].rearrange("p (k c) -> p k c", c=c)
        K.dma("pool", dst, src)
        piece_dst[idx] = dst

    def next_piece(expect):
        idx = wp_rr[0]
        wp_rr[0] += 1
        assert piece_list[idx] == expect, (piece_list[idx], expect)
        while issued[0] < min(len(piece_list), idx + PF + 1):
            _issue(issued[0])
            issued[0] += 1
        return piece_dst.pop(idx)

    def load_piece(l, key):
        return next_piece(("w", l, key))

    K.memset("pool", onesf[:], 1.0)
    K.memset("pool", ones16[:], 1.0)
    K.memset("pool", bd16[:], 0.0)
    K.memset("pool", bd16[0:64, 0:64], 1.0)
    K.memset("pool", bd16[64:128, 64:128], 1.0)

    def asel(out, in_, pattern, op, base, cm):
        K.S.op("pool", lambda e: e.affine_select(out=out, in_=in_, pattern=pattern, compare_op=op, fill=0.0, base=base, channel_multiplier=cm),
               [_tok(in_)], [_tok(out)], dur=0.3)
    asel(ident16[:], ones16[:], [[1, 128]], ALU.is_equal, 0, -1)
    for j in range(4):
        asel(mask512[:, j * 128:(j + 1) * 128], onesf[:, 0:128], [[1, 128]], ALU.is_gt if j % 2 == 0 else ALU.is_ge, 0, -1)
    asel(masksl[:], onesf[:, 0:128], [[-1, 128]], ALU.is_gt, 0, 1)
    K.memset("pool", rst[:], 1.0)
    for j in range(NCH):
        K.memset("pool", rst[:, j * C:j * C + 1], 0.0)
    for l in range(L):
        K.dma("sp", vecs[l][:], vecs_d[l])
        K.memset("pool", S32[l][:], 0.0)
        K.memset("pool", S16[l][:], 0.0)
        K.memset("pool", ctail[l][:], 0.0)
        K.memset("pool", carry[l][:], 0.0)
    K.dma("sp", c32[:], cT)
    K.memset("pool", epsc[:, 0:1], RMS_EPS)
    K.memset("pool", epsc[:, 1:2], LN_EPS)
    K.memset("pool", epsc[:, 2:3], GN_EPS)
    K.memset("pool", epsc[:, 3:4], 1e-24)
    K.act(c16[:], c32[:], AF.Silu)

    for l in range(K.nlayers):
        pmod = bank()
        adav = ada[l].rearrange("(k p) c -> p k c", p=128)
        for pc in range(12):
            dst = next_piece(("ada", l, pc))
            for m in range(4):
                j = pc * 4 + m
                K.mm(pmod[:, j:j + 1], [(dst[:, k, m * 128:(m + 1) * 128], c16[:, k:k + 1]) for k in range(8)])
        K.tt("dve", dv[l][:, DV_MOD:DV_MOD + 48], pmod[:, 0:48], vecs[l][:, V_ADAB:V_ADAB + 48], ALU.add)
        K.stt(dv[l][:, DV_GSCM:DV_GSCM + 8], dv[l][:, DV_MOD + M_SCM:DV_MOD + M_SCM + 8], 1.0, vecs[l][:, V_GMIX:V_GMIX + 8], ALU.add, ALU.mult)
        K.stt(dv[l][:, DV_GSCF:DV_GSCF + 8], dv[l][:, DV_MOD + M_SCF:DV_MOD + M_SCF + 8], 1.0, vecs[l][:, V_GFFN:V_GFFN + 8], ALU.add, ALU.mult)
        K.ts("dve", dv[l][:, DV_OMU:DV_OMU + 27], vecs[l][:, V_MU:V_MU + 27], -1.0, 1.0, ALU.mult, ALU.add)
        K.dump("dv%d" % l, dv[l][:], [128, ND])

    def rmsnorm_to_ht(l, gsc_col, sh_col):
        for k in range(NB):
            K.act(sq16[k][:], xt[k][:], AF.Square)
        p = bank()
        K.mm(p[:], [(ones16[:], sq16[k][:]) for k in range(NB)])
        K.act(rs32[:], p[:], AF.Ln, scale=1.0 / D, bias=epsc[:, 0:1])
        K.act(rstd[:], rs32[:], AF.Exp, scale=-0.5)
        for k in range(NB):
            t = t32[k % 2]
            K.stt(t[:], xt[k][:], dv[l][:, gsc_col + k:gsc_col + k + 1], rstd[:], ALU.mult, ALU.mult)
            K.act(ht[k][:], t[:], AF.Identity, bias=dv[l][:, sh_col + k:sh_col + k + 1])

    def proj(wp, j0, M=128):
        p = bank()
        K.mm(p[0:M, :], [(wp[:, k, j0:j0 + M], ht[k][:]) for k in range(NB)])
        return p

    def shift(l, p, mucol, dst, M=128):
        mu = vecs[l][0:M, V_MU + mucol:V_MU + mucol + 1]
        omu = dv[l][0:M, DV_OMU + mucol:DV_OMU + mucol + 1]
        cy = carry[l][0:M, mucol:mucol + 1]
        K.act(dst[0:M, :], p[0:M, :], AF.Identity, scale=omu)
        K.stt(dst[0:M, 1:TT], p[0:M, 0:TT - 1], mu, dst[0:M, 1:TT], ALU.mult, ALU.add)
        K.stt(dst[0:M, 0:1], cy, mu, dst[0:M, 0:1], ALU.mult, ALU.add)
        K.copy("dve", cy, p[0:M, TT - 1:TT])

    xTv = xT.rearrange("(k p) t -> k p t", p=128)
    oTv = outT.rearrange("(k p) t -> k p t", p=128)
    for it in range(K.ntiles):
        t0 = it * TT
        for k in range(NB):
            K.dma("sp", xt[k][:], xTv[k, :, t0:t0 + TT])
        for l in range(K.nlayers):
            V = vecs[l]
            DVl = dv[l]
            rmsnorm_to_ht(l, DV_GSCM, DV_MOD + M_SHM)
            if it == 0 and l == 0:
                for k in (0, 7):
                    K.dump("ht%d" % k, ht[k][:], [128, TT])
            wlo = load_piece(l, "lo")
            for j in range(2):
                p = proj(wlo, j * 128)
                shift(l, p, 24 + j, lo32[j])
            K.act(lo16[0][0:64, :], lo32[0][0:64, :], AF.Tanh)
            K.copy("act", lo16[0][64:128, :], lo32[0][64:128, :])
            K.act(lo16[1][:], lo32[1][:], AF.Sigmoid)
            if l >= 1:
                p = proj(wlo, 256, M=32)
                shift(l, p, 26, vlo32, M=32)
                K.copy("act", vlo16[:], vlo32[0:32, :])
            if K.stage < 1:
                continue
            def front(hp, par):
                AR = ARp[par]; KBtok = KBtokp[par]; Vtok = Vtokp[par]
                g32 = g32p[par]; bon32 = bon32p[par]; gA16 = gA16p[par]; kt16 = kt16p[par]; bt16 = bt16p[par]; WC = WCp[par]
                whp, wsl = load_piece(l, "hp%d" % hp)
                p = proj(whp, 0);   shift(l, p, hp, W["r32"])
                yield
                p = proj(whp, 128); shift(l, p, 8 + hp, W["k32"])
                yield
                p = proj(whp, 256); shift(l, p, 16 + hp, W["v32"])
                yield
                p = proj(whp, 384); K.act(gA16[:], p[:], AF.Sigmoid)
                p = bank(); K.mm(p[:], [(wsl[0:64, 0, :], lo16[0][0:64, :])])
                K.act(W["sg32"][:], p[:], AF.Sigmoid, bias=V[:, V_W0 + hp:V_W0 + hp + 1])
                yield
                p = bank(); K.mm(p[:], [(wsl[64:128, 0, :], lo16[0][64:128, :])])
                K.act(W["a32"][:], p[:], AF.Sigmoid, bias=V[:, V_A0 + hp:V_A0 + hp + 1])
                p = bank(); K.mm(p[:], [(wsl[:, 1, :], lo16[1][:])])
                K.copy("act", g32[:], p[:])
                yield
                if l >= 1:
                    p = bank(); K.mm(p[:], [(wsl[0:32, 2, :], vlo16[:])])
                    K.act(W["sv32"][:], p[:], AF.Sigmoid, bias=V[:, V_V0 + hp:V_V0 + hp + 1])
                    K.tt("pool", W["dd32"][:], vf[hp][:], W["v32"][:], ALU.subtract)
                    K.tt("pool", W["dd32"][:], W["dd32"][:], W["sv32"][:], ALU.mult)
                    K.tt("pool", W["v32"][:], W["v32"][:], W["dd32"][:], ALU.add)
                else:
                    K.copy("pool", vf[hp][:], W["v32"][:])
                yield
                K.scan(W["cs32"][:], rst[:], W["sg32"][:])
                K.act(W["E1"][:], W["cs32"][:], AF.Exp, scale=CEXP)
                K.act(W["E3"][:], W["cs32"][:], AF.Exp, scale=-CEXP)
                K.tt("pool", W["d2"][:], W["cs32"][:], W["sg32"][:], ALU.subtract)
                K.act(W["E2"][:], W["d2"][:], AF.Exp, scale=CEXP)
                yield
                cs3 = W["cs32"][:].rearrange("p (c t) -> p c t", t=C)
                K.tt("pool", W["d4"][:].rearrange("p (c t) -> p c t", t=C), cs3, cs3[:, :, C - 1:C].to_broadcast([128, NCH, C]), ALU.subtract)
                K.act(W["E4"][:], W["d4"][:], AF.Exp, scale=-CEXP)
                K.copy("dve", WC[:], W["E1"][:].rearrange("p (c t) -> p c t", t=C)[:, :, C - 1])
                yield
                K.act(W["kk32"][:], W["k32"][:], AF.Identity, scale=V[:, V_KK + hp:V_KK + hp + 1])
                K.act(H["kk2"][:], W["kk32"][:], AF.Square)
                p = bank(); K.mm(p[:], [(bd16[:], H["kk2"][:])])
                K.act(W["nrm"][:], p[:], AF.Ln, bias=epsc[:, 3:4])
                K.act(W["rn"][:], W["nrm"][:], AF.Exp, scale=-0.5)
                yield
                K.tt("pool", W["kkn"][:], W["kk32"][:], W["rn"][:], ALU.mult)
                K.ts("dve", W["t1"][:], W["a32"][:], -1.0, V[:, V_KA + hp:V_KA + hp + 1], ALU.add, ALU.mult)
                K.stt(W["km"][:], W["t1"][:], 1.0, W["k32"][:], ALU.add, ALU.mult)
                K.tt("pool", W["b32"][:], W["kkn"][:], W["a32"][:], ALU.mult)
                yield
                K.tt("dve", AR[:, :, 1, :], W["r32"][:].rearrange("p (c t) -> p c t", t=C), W["E1"][:].rearrange("p (c t) -> p c t", t=C), ALU.mult)
                K.stt(AR[:, :, 0, :], W["kkn"][:].rearrange("p (c t) -> p c t", t=C), -1.0, W["E2"][:].rearrange("p (c t) -> p c t", t=C), ALU.mult, ALU.mult)
                K.tt("pool", kt16[:], W["km"][:], W["E3"][:], ALU.mult)
                K.tt("pool", bt16[:], W["b32"][:], W["E3"][:], ALU.mult)
                yield
                K.tt("pool", H["Kh16"][:], W["km"][:], W["E4"][:], ALU.mult)
                K.tt("dve", H["Bh16"][:], W["b32"][:], W["E4"][:], ALU.mult)
                K.copy("pool", H["v16"][:], W["v32"][:])
                yield
                K.stt(H["rk16"][:], W["r32"][:], V[:, V_RK + hp:V_RK + hp + 1], W["km"][:], ALU.mult, ALU.mult)
                p = bank(); K.mm(p[:], [(bd16[:], H["rk16"][:])])
                K.tt("dve", bon32[:], p[:], W["v32"][:], ALU.mult)
                yield
                trb = bank()
                trv = trb[:].bitcast(BF16).rearrange("p (a t) -> p a t", t=128)
                for c in range(NCH):
                    K.tr(trv[:, c, :], H["Kh16"][:, c * C:(c + 1) * C], ident16[:])
                    K.tr(trv[:, 4 + c, :], H["Bh16"][:, c * C:(c + 1) * C], ident16[:])
                K.copy("act", KBtok[:], trv)
                yield
                trb = bank()
                trv = trb[:].bitcast(BF16).rearrange("p (a t) -> p a t", t=128)
                for c in range(NCH):
                    K.tr(trv[:, c, :], H["v16"][:, c * C:(c + 1) * C], ident16[:])
                K.copy("dve", Vtok[:], trv[:, 0:4, :])
                yield

            def back(hp, par):
                Amat = Amatp[par]; Tfin = Tfinp[par]
                AR = ARp[par]; KBtok = KBtokp[par]; Vtok = Vtokp[par]
                g32 = g32p[par]; bon32 = bon32p[par]; gA16 = gA16p[par]; kt16 = kt16p[par]; bt16 = bt16p[par]; WC = WCp[par]
                for h in range(2):
                    hs = slice(h * 64, h * 64 + 64)
                    for c in range(NCH):
                        ck = slice(c * C, (c + 1) * C)
                        bk = dblb[c]
                        ar = AR[hs, c, :, :].rearrange("p a t -> p (a t)")
                        K.mm(bk[:, 0:256], [(bt16[hs, ck], ar)])
                        K.mm(bk[:, 256:512], [(kt16[hs, ck], ar)])
                        K.tt("dve", Amat[h][c][:], bk[:], mask512[:], ALU.mult)
                        K.mm(bk[:, 0:128], [(AR[hs, c, 0, :], bt16[hs, ck])])
                        K.tt("dve", Q0[c][:], bk[:, 0:128], masksl[:], ALU.mult)
                        K.tt("pool", PQT[c][1][:, 1, :], Amat[h][c][:, 0:128], ident16[:], ALU.add)
                        yield
                    for c in range(NCH):
                        bk = dblb[c]
                        P0 = Amat[h][c][:, 0:128]
                        K.mm(bk[:, 0:128], [(Q0[c][:], P0)])
                        K.mm(bk[:, 256:384], [(P0, Q0[c][:])])
                        K.copy("act" if c % 2 == 0 else "dve", PQT[c][1][:, 0:3:2, :], bk[:, 0:384].rearrange("p (a t) -> p a t", t=128)[:, 0:3:2, :])
                    yield
                    for lv in range(1, 7):
                        cur, nxt = lv % 2, (lv + 1) % 2
                        for c in range(NCH):
                            bk = dblb[c]
                            Pk = PQT[c][cur][:, 0, :]
                            Tk = PQT[c][cur][:, 1, :]
                            Qk = PQT[c][cur][:, 2, :]
                            PTk = PQT[c][cur][:, 0:2, :].rearrange("p a t -> p (a t)")
                            eng = "dve" if (c + lv) % 4 == 0 else "act"
                            if lv < 6:
                                K.mmx([(bk[:, 256:384], Pk, Qk, True, True),
                                       (bk[:, 0:256], Qk, PTk, True, False),
                                       (bk[:, 128:256], ident16[:], Tk, False, True)])
                                K.copy(eng, PQT[c][nxt][:], bk[:, 0:384].rearrange("p (a t) -> p a t", t=128))
                            else:
                                K.mm(bk[:, 128:256], [(Qk, Tk), (ident16[:], Tk)])
                                K.copy(eng, Tfin[h][c][:], bk[:, 128:256])
                        yield
                for c in range(NCH):
                    for h in range(2):
                        hs = slice(h * 64, h * 64 + 64)
                        K.mm(seqb[:, h * 64:(h + 1) * 64], [(AR[hs, c, 0, :], S16[l][hs, hp, :]),
                                                             (Amat[h][c][:, 256:384], Vtok[:, c, hs])])
                    K.copy("act", XT16[:], seqb[:, 0:128])
                    yield
                    for h in range(2):
                        hs = slice(h * 64, h * 64 + 64)
                        K.mm(seqb[:, 128 + h * 64:128 + (h + 1) * 64], [(Tfin[h][c][:], XT16[:, hs])])
                    K.copy("dve", UT16[:], seqb[:, 128:256])
                    yield
                    for h in range(2):
                        hs = slice(h * 64, h * 64 + 64)
                        K.mm(yb[hs, c * C:(c + 1) * C], [(S16[l][hs, hp, :], AR[hs, c, 1, :]),
                                                          (UT16[:, hs], Amat[h][c][:, 128:256]),
                                                          (Vtok[:, c, hs], Amat[h][c][:, 384:512])])
                        K.mm(seqb[hs, 256:320], [(KBtok[:, 4 + c, hs], UT16[:, hs]),
                                                  (KBtok[:, c, hs], Vtok[:, c, hs])])
                    K.stt(S32[l][:, hp, :], S32[l][:, hp, :], WC[:, c:c + 1], seqb[:, 256:320], ALU.mult, ALU.add)
                    K.copy("act", S16[l][:, hp, :], S32[l][:, hp, :])
                    yield
                K.copy("act", W["y32"][:], yb[:])
                K.copy("dve", H["y16"][:], yb[:])
                if it == 0 and l == 0 and hp == 0:
                    K.dump("y32", W["y32"][:], [128, TT])
                yield
                p = bank(); K.mm(p[:], [(bd16[:], H["y16"][:])])
                K.stt(W["yc"][:], p[:], -1.0 / 64, W["y32"][:], ALU.mult, ALU.add)
                yield
                K.act(H["yc2"][:], W["yc"][:], AF.Square)
                p = bank(); K.mm(p[:], [(bd16[:], H["yc2"][:])])
                K.act(W["sd"][:], p[:], AF.Ln, scale=1.0 / 64, bias=epsc[:, 2:3])
                yield
                K.act(W["rsd"][:], W["sd"][:], AF.Exp, scale=-0.5)
                K.tt("pool", W["yn"][:], W["yc"][:], W["rsd"][:], ALU.mult)
                K.act(W["yg"][:], W["yn"][:], AF.Identity, scale=V[:, V_GNG + hp:V_GNG + hp + 1], bias=V[:, V_GNB + hp:V_GNB + hp + 1])
                yield
                K.tt("pool", W["o1"][:], W["yg"][:], bon32[:], ALU.add)
                K.tt("pool", W["o2"][:], W["o1"][:], g32[:], ALU.mult)
                K.tt("pool", merged[hp][:], W["o2"][:], gA16[:], ALU.mult)
                if it == 0 and l == 0 and hp in (0, 7):
                    K.dump("mg%d" % hp, merged[hp][:], [128, TT], BF16)
                yield

            def drain(g):
                for _ in g:
                    pass

            def interleave(g1, g2):
                a1 = a2 = True
                while a1 or a2:
                    if a1:
                        try:
                            next(g1)
                        except StopIteration:
                            a1 = False
                    if a2:
                        try:
                            next(g2)
                        except StopIteration:
                            a2 = False

            def conv(cb):
                wcb = load_piece(l, "cb%d" % cb)
                pb = proj(wcb, 128)
                K.act(sgb[:], pb[:], AF.Sigmoid)
                pa = proj(wcb, 0)
                K.copy("pool", gbuf[:, 0:30], ctail[l][:, cb, :])
                K.tt("dve", gbuf[:, 30:30 + TT], pa[:], sgb[:], ALU.mult)
                K.copy("pool", ctail[l][:, cb, :], gbuf[:, TT:TT + 30])
                cw = lambda k: V[:, V_CW + cb * 31 + k:V_CW + cb * 31 + k + 1]
                NT_DVE = 22
                K.ts("dve", cacc[0][:], gbuf[:, 0:TT], cw(0), V[:, V_CB + cb:V_CB + cb + 1], ALU.mult, ALU.add)
                for k in range(1, NT_DVE):
                    K.stt(cacc[0][:], gbuf[:, k:k + TT], cw(k), cacc[0][:], ALU.mult, ALU.add)
                K.act(cacc[1][:], gbuf[:, NT_DVE:NT_DVE + TT], AF.Identity, scale=cw(NT_DVE))
                for k in range(NT_DVE + 1, 31):
                    tp_ = ctmp[k % 2]
                    K.act(tp_[:], gbuf[:, k:k + TT], AF.Identity, scale=cw(k))
                    K.tt("pool", cacc[1][:], cacc[1][:], tp_[:], ALU.add)
                K.tt("pool", z16[cb][:], cacc[0][:], cacc[1][:], ALU.add)

            if K.stage < 3:
                for hp in range(8):
                    drain(front(hp, hp % 2))
            else:
                drain(front(0, 0))
                for hp in range(8):
                    if hp + 1 < 8:
                        interleave(back(hp, hp % 2), front(hp + 1, (hp + 1) % 2))
                    else:
                        drain(back(hp, hp % 2))
                    if K.stage >= 4:
                        conv(hp)
            if K.stage < 4:
                continue
            if it == 0 and l == 0:
                K.dump("z0", z16[0][:], [128, TT], BF16)
            p = bank(); K.mm(p[:], [(ones16[:], z16[cb][:]) for cb in range(8)])
            K.act(W["nrm"][:], p[:], AF.Identity, scale=-1.0 / D)
            for cb in range(8):
                t = t32[cb % 2]
                K.tt("dve", t[:], z16[cb][:], W["nrm"][:], ALU.add)
                K.act(sq16[cb][:], t[:], AF.Square)
            p = bank(); K.mm(p[:], [(ones16[:], sq16[cb][:]) for cb in range(8)])
            K.act(W["sd"][:], p[:], AF.Ln, scale=1.0 / D, bias=epsc[:, 1:2])
            K.act(W["sd"][:], W["sd"][:], AF.Exp, scale=-0.5)
            for cb in range(8):
                t = t32[cb % 2]
                wgb = load_piece(l, "gb%d" % cb)
                pg = proj(wgb, 0)
                K.act(sgb[:], pg[:], AF.Sigmoid)
                K.tt("dve", t[:], z16[cb][:], W["nrm"][:], ALU.add)
                K.tt("pool", t[:], t[:], W["sd"][:], ALU.mult)
                K.act(t[:], t[:], AF.Silu, scale=V[:, V_LNG + cb:V_LNG + cb + 1], bias=V[:, V_LNB + cb:V_LNB + cb + 1])
                K.tt("pool", merged[8 + cb][:], t[:], sgb[:], ALU.mult)
            if it == 0 and l == 0:
                K.dump("mg8", merged[8][:], [128, TT], BF16)
            for g in range(4):
                wo = load_piece(l, "wo%d" % g)
                for o2 in range(2):
                    ob = g * 2 + o2
                    p = bank()
                    K.mm(p[:], [(wo[:, o2 * 16 + kc, :], merged[kc][:]) for kc in range(16)])
                    K.stt(xt[ob][:], p[:], DVl[:, DV_MOD + M_GTM + ob:DV_MOD + M_GTM + ob + 1], xt[ob][:], ALU.mult, ALU.add)
            if it == 0 and l == 0:
                K.dump("xmix0", xt[0][:], [128, TT])
            if K.stage < 5:
                continue
            rmsnorm_to_ht(l, DV_GSCF, DV_MOD + M_SHF)
            for pc in range(8):
                w1 = load_piece(l, "w1%d" % pc)
                for j in range(4):
                    hb = pc * 4 + j
                    p = proj(w1, j * 128)
                    t = t32[hb % 2]
                    K.act(t[:], p[:], AF.Relu)
                    K.tt("pool", hidv[hb], t[:], t[:], ALU.mult)
            for ob in range(8):
                w2 = load_piece(l, "w2%d" % ob)
                p = bank()
                K.mm(p[:], [(w2[:, hb, :], hidv[hb]) for hb in range(32)])
                K.stt(xt[ob][:], p[:], DVl[:, DV_MOD + M_GTF + ob:DV_MOD + M_GTF + ob + 1], xt[ob][:], ALU.mult, ALU.add)
            if it == 0 and l == 0:
                K.dump("xffn0", xt[0][:], [128, TT])
        for k in range(NB):
            K.act(sq16[k][:], xt[k][:], AF.Square)
        p = bank()
        K.mm(p[:], [(ones16[:], sq16[k][:]) for k in range(NB)])
        K.act(rs32[:], p[:], AF.Ln, scale=1.0 / D, bias=epsc[:, 0:1])
        K.act(rstd[:], rs32[:], AF.Exp, scale=-0.5)
        for k in range(NB):
            t = t32[k % 2]
            K.stt(t[:], xt[k][:], vecs[0][:, V_FG + k:V_FG + k + 1], rstd[:], ALU.mult, ALU.mult)
            K.dma("sp", oTv[k, :, t0:t0 + TT], t[:], w=["outT%d_%d" % (it, k)])
            K.out_tokens.append("outT%d_%d" % (it, k))
    K.S.wait_all("sp", K.out_tokens)
    K.S.emit(K.st)
    K.st.close()
    return K


def dblb_view(dblb, c):
    col = (c % 2) * 256
    return dblb[c // 2][:, col:col + 256].rearrange("p (a t) -> p a t", t=128)


def _fm(v):
    v = np.asarray(v, np.float32)
    return np.ascontiguousarray(v.reshape(-1, 128).T)


def prep_shared(inp):
    shared = {}
    for l in range(L):
        vec = np.zeros((128, NV), np.float32)
        vec[:, V_GMIX:V_GMIX + 8] = _fm(inp["norm_mix_gain"][l])
        vec[:, V_GFFN:V_GFFN + 8] = _fm(inp["norm_ffn_gain"][l])
        vec[:, V_MU:V_MU + 26] = _fm(inp["mu_shift"][l])
        if l >= 1:
            vec[0:32, V_MUV] = inp["mu_vres"][l - 1]
            vec[:, V_V0:V_V0 + 8] = _fm(inp["v0"][l - 1])
        vec[:, V_W0:V_W0 + 8] = _fm(inp["w0"][l])
        vec[:, V_A0:V_A0 + 8] = _fm(inp["a0"][l])
        vec[:, V_KK:V_KK + 8] = _fm(inp["k_k"][l])
        vec[:, V_KA:V_KA + 8] = _fm(inp["k_a"][l])
        vec[:, V_RK:V_RK + 8] = _fm(inp["r_k"][l].reshape(-1))
        vec[:, V_GNG:V_GNG + 8] = _fm(inp["gn_gain"][l])
        vec[:, V_GNB:V_GNB + 8] = _fm(inp["gn_bias"][l])
        vec[:, V_CB:V_CB + 8] = _fm(inp["conv_b"][l])
        vec[:, V_LNG:V_LNG + 8] = _fm(inp["conv_ln_gain"][l])
        vec[:, V_LNB:V_LNB + 8] = _fm(inp["conv_ln_bias"][l])
        vec[:, V_FG:V_FG + 8] = _fm(inp["final_gain"])
        vec[:, V_ADAB:V_ADAB + 48] = _fm(inp["ada_b"][l])
        cw = np.asarray(inp["conv_w"][l], np.float32)
        vec[:, V_CW:V_CW + 248] = cw.reshape(31, 8, 128).transpose(2, 1, 0).reshape(128, 248)
        shared["vecs%d" % l] = vec
        wsm = np.zeros((128, 3, D), np.float32)
        wsm[0:64, 0] = inp["w_decay_up"][l]
        wsm[64:128, 0] = inp["w_aaa_up"][l]
        wsm[:, 1] = inp["w_gate_up"][l]
        if l >= 1:
            wsm[0:32, 2] = inp["w_vres_up"][l - 1]
        shared["ada%d" % l] = np.ascontiguousarray(inp["ada_w"][l], dtype=np.float32)
        lay, tot = _wbig_layout(l)
        wb = np.empty((128, tot), np.float32)
        win = np.asarray(inp["w_in"][l], np.float32)
        if l >= 1:
            win = np.concatenate([win, np.asarray(inp["w_in_vres"][l - 1], np.float32)], axis=1)

        def put(key, cols_matrix):
            off, k, c = lay[key]
            wb[:, off:off + k * c] = cols_matrix.reshape(k, 128, c).transpose(1, 0, 2).reshape(128, k * c)
        lo_idx = list(range(3072, 3328)) + (list(range(N_COLS, N_COLS + 32)) if l >= 1 else [])
        put("lo", win[:, lo_idx])
        for hp in range(8):
            idx = np.concatenate([np.arange(hp * 128, hp * 128 + 128), 1024 + np.arange(hp * 128, hp * 128 + 128),
                                  2048 + np.arange(hp * 128, hp * 128 + 128), 5376 + np.arange(hp * 128, hp * 128 + 128)])
            put("hp%d" % hp, win[:, idx])
            off_, k_, c_ = lay["hp%d" % hp]
            wb[:, off_ + k_ * c_: off_ + k_ * c_ + 384] = wsm[:, :, hp * 128:(hp + 1) * 128].reshape(128, 384)
        for cb in range(8):
            idx = np.concatenate([3328 + np.arange(cb * 128, cb * 128 + 128), 4352 + np.arange(cb * 128, cb * 128 + 128)])
            put("cb%d" % cb, win[:, idx])
            put("gb%d" % cb, win[:, 6400 + np.arange(cb * 128, cb * 128 + 128)])
        wo = np.asarray(inp["w_out"][l], np.float32)
        for g in range(4):
            off, k, c = lay["wo%d" % g]
            blk = wo[:, g * 256:(g + 1) * 256].reshape(16, 128, 2, 128)
            wb[:, off:off + k * c] = blk.transpose(1, 2, 0, 3).reshape(128, 32 * 128)
        w1 = np.asarray(inp["w_ff_in"][l], np.float32)
        for pc in range(8):
            put("w1%d" % pc, w1[:, pc * 512:(pc + 1) * 512])
        w2 = np.asarray(inp["w_ff_out"][l], np.float32)
        for ob in range(8):
            put("w2%d" % ob, w2[:, ob * 128:(ob + 1) * 128])
        shared["wbig%d" % l] = wb
    return shared


def prep_core(inp, b):
    return {"xT": np.ascontiguousarray(np.asarray(inp["x"][b], np.float32).T),
            "cT": _fm(inp["c"][b])}


_CACHE = {}


def kernel(**inputs):
    inp = {k: np.asarray(v) for k, v in inputs.items()}
    if "K" not in _CACHE:
        _CACHE["K"] = build()
    K = _CACHE["K"]
    shared = prep_shared(inp)
    in_maps = []
    for b in range(NCORES):
        m = dict(shared)
        m.update(prep_core(inp, b))
        in_maps.append(m)
    res = run_bass_kernel_spmd(K.nc, in_maps, core_ids=list(range(NCORES)))
    out = np.stack([np.ascontiguousarray(res.results[b]["outT"].T) for b in range(NCORES)], axis=0)
    return out.astype(np.float32)
```

```python
import contextlib
import math
import numpy as np
import concourse.bass as bass
import concourse.mybir as mybir
from concourse.bass_utils import run_bass_kernel_spmd

F32 = mybir.dt.float32
BF16 = mybir.dt.bfloat16
AF = mybir.ActivationFunctionType
ALU = mybir.AluOpType

D = 1024
T = 2048
NB = 8
TT = 512
C = 128
NCH = TT // C
L = 2
NCORES = 8
N_SHIFT = 3328
N_COLS = 7424
RMS_EPS = 1e-6
LN_EPS = 1e-5
GN_EPS = 64e-5
CEXP = -math.exp(-0.5)

V_GMIX, V_GFFN, V_MU, V_MUV, V_W0, V_A0, V_KK, V_KA, V_RK, V_GNG, V_GNB, V_V0, V_CB, V_LNG, V_LNB, V_FG, V_ADAB, V_CW = (
    0, 8, 16, 42, 43, 51, 59, 67, 75, 83, 91, 99, 107, 115, 123, 131, 139, 187)
NV = 187 + 8 * 31
DV_MOD, DV_GSCM, DV_GSCF, DV_OMU = 0, 48, 56, 64
ND = 64 + 27
M_SHM, M_SCM, M_GTM, M_SHF, M_SCF, M_GTF = 0, 8, 16, 24, 32, 40

def _wbig_layout(l):
    off = 0
    lay = {}
    nlo = 256 + (32 if l >= 1 else 0)
    lay["lo"] = (off, 8, nlo); off += 8 * nlo
    for hp in range(8):
        lay["hp%d" % hp] = (off, 8, 512); off += 8 * 512 + 3 * 128
    for cb in range(8):
        lay["cb%d" % cb] = (off, 8, 256); off += 8 * 256
    for cb in range(8):
        lay["gb%d" % cb] = (off, 8, 128); off += 8 * 128
    for g in range(4):
        lay["wo%d" % g] = (off, 32, 128); off += 32 * 128
    for pc in range(8):
        lay["w1%d" % pc] = (off, 8, 512); off += 8 * 512
    for ob in range(8):
        lay["w2%d" % ob] = (off, 32, 128); off += 32 * 128
    return lay, off


ENGS = ("pe", "act", "dve", "pool", "sp")
DMA_SEMS = {"sp": 8, "pool": 16, "act": 4}
SEM_LAT = 0.4
NT_DVE_G = 22
EVAC_MOD = 4
PRIO_MODE = 0
NQ_AMAT = 1
SIM_ONLY = False
ACT_WINDOW = 0.3
LIST_SCHED = True


class Sched:
    def __init__(self, nc):
        self.nc = nc
        self.recs = []
        self.last_w = {}
        self.readers = {}

    def _deps(self, reads, writes):
        deps = set()
        for t in reads:
            w = self.last_w.get(t)
            if w is not None:
                deps.add(w)
        for t in writes:
            w = self.last_w.get(t)
            if w is not None:
                deps.add(w)
            deps.update(self.readers.get(t, ()))
        return deps

    def _commit(self, reads, writes, me):
        for t in reads:
            self.readers.setdefault(t, []).append(me)
        for t in writes:
            self.last_w[t] = me
            self.readers[t] = []

    def op(self, eng, fn, reads=(), writes=(), dur=0.5, tbl=0):
        deps = self._deps(reads, writes)
        me = len(self.recs)
        self.recs.append([eng, fn, deps, dur, False, dur, (list(writes) or ["?"])[0], tbl])
        self._commit(reads, writes, me)

    def dma(self, queue, fn, reads=(), writes=(), nbytes=0):
        deps = self._deps(reads, writes)
        me = len(self.recs)
        self.recs.append([queue, fn, deps, 0.6 if queue == "pool" else 0.15, True, 2.0 + nbytes / 340e3, (list(writes) or ["?"])[0]])
        self._commit(reads, writes, me)

    def wait_all(self, eng, tokens):
        deps = set(self.last_w[t] for t in tokens if t in self.last_w)
        me = len(self.recs)
        self.recs.append([eng, None, deps, 0.01, False, 0.01, "final"])
        self.final = me

    def finalize(self):
        recs = self.recs
        n = len(recs)
        order = {e: [] for e in ENGS}
        if not LIST_SCHED:
            for i, r in enumerate(recs):
                order[r[0]].append(i)
            self.order = order
            return
        succs = [[] for _ in range(n)]
        indeg = [0] * n
        for i, r in enumerate(recs):
            for d in r[2]:
                succs[d].append(i)
            indeg[i] = len(r[2])
        prio = [0.0] * n
        for i in range(n - 1, -1, -1):
            m = 0.0
            for sx in succs[i]:
                if prio[sx] > m:
                    m = prio[sx]
            prio[i] = m + recs[i][5] + SEM_LAT
        import heapq
        ready = {e: [] for e in ENGS}
        finish = [0.0] * n
        self.sim_start = [0.0] * n
        self.sim_finish = finish
        rdy_t = [0.0] * n
        for i in range(n):
            if indeg[i] == 0:
                heapq.heappush(ready[recs[i][0]], (0.0, -prio[i], i))
        efree = {e: 0.0 for e in ENGS}
        done = 0
        WINDOW = 0.3
        cur_tbl = 0
        dma_free = 0.0
        self.n_tbl_switch = 0
        while done < n:
            best = None
            for e in ENGS:
                h = ready[e]
                if not h:
                    continue
                st = max(efree[e], h[0][0])
                if best is None or st < best[0]:
                    best = (st, e)
            st, e = best
            h = ready[e]
            cands = []
            win = ACT_WINDOW if e == "act" else WINDOW
            while h and h[0][0] <= st + win and len(cands) < 32:
                cands.append(heapq.heappop(h))
            cands.sort(key=lambda c: c[1])
            pick = cands[0]
            pen = 0.0
            if e == "act":
                ok = [c for c in cands if len(recs[c[2]]) < 8 or recs[c[2]][7] in (0, cur_tbl)]
                if ok:
                    pick = ok[0]
                else:
                    pen = 1.3
                    self.n_tbl_switch += 1
                t_ = recs[pick[2]][7] if len(recs[pick[2]]) >= 8 else 0
                if t_:
                    cur_tbl = t_
            for c in cands:
                if c is not pick:
                    heapq.heappush(h, c)
            i = pick[2]
            start = max(efree[e], pick[0]) + pen
            efree[e] = start + recs[i][3]
            if recs[i][4]:
                xs = max(start + recs[i][3], dma_free)
                dma_free = xs + (recs[i][5] - 2.0)
                finish[i] = dma_free + 2.0
            else:
                finish[i] = start + recs[i][5]
            self.sim_start[i] = start
            order[e].append(i)
            done += 1
            for sx in succs[i]:
                t = finish[i] + SEM_LAT
                if t > rdy_t[sx]:
                    rdy_t[sx] = t
                indeg[sx] -= 1
                if indeg[sx] == 0:
                    heapq.heappush(ready[recs[sx][0]], (rdy_t[sx], -prio[sx], sx))
        self.order = order
        self.sim_time = max(finish)

    def emit(self, st):
        nc = self.nc
        recs = self.recs
        self.finalize()
        if SIM_ONLY:
            return
        sems = {}
        for e in ENGS:
            sems[e] = st.enter_context(nc.semaphore("s_" + e))
        for q, nq in DMA_SEMS.items():
            for k in range(nq):
                sems[("dma", q, k)] = st.enter_context(nc.semaphore("s_dma_%s%d" % (q, k)))
        comp = [None] * len(recs)
        prev_same_sem = {}
        for e in ENGS:
            cc = 0
            dk = 0
            for i in self.order[e]:
                r = recs[i]
                if r[1] is None:
                    continue
                if r[4]:
                    nq = DMA_SEMS[e]
                    key = ("dma", e, dk % nq)
                    val = 16 * (dk // nq + 1)
                    comp[i] = (key, val)
                    if dk >= nq:
                        prev_same_sem[i] = (key, val - 16)
                    dk += 1
                else:
                    cc += 1
                    comp[i] = (e, cc)
        block = st.enter_context(nc.Block())

        def run(eng_name):
            def body(engine):
                known = {}
                for i in self.order[eng_name]:
                    r = recs[i]
                    need = {}
                    for d in r[2]:
                        k, v = comp[d]
                        if eng_name == "pe" and k == "pe":
                            continue
                        if v > need.get(k, 0):
                            need[k] = v
                    if i in prev_same_sem:
                        k, v = prev_same_sem[i]
                        if v > need.get(k, 0):
                            need[k] = v
                    for k, v in need.items():
                        if known.get(k, 0) >= v:
                            continue
                        known[k] = v
                        engine.wait_ge(sems[k], v)
                    if r[1] is None:
                        continue
                    ins = r[1](engine)
                    k, v = comp[i]
                    ins.then_inc(sems[k], 16 if r[4] else 1)
            return body

        block.tensor(run("pe"))
        block.scalar(run("act"))
        block.vector(run("dve"))
        block.gpsimd(run("pool"))
        block.sync(run("sp"))


def _tok(ap):
    return ap.name


def _n(ap):
    n = 1
    for d in ap.shape[1:]:
        n *= int(d)
    return n


def _bytes(ap):
    n = int(ap.shape[0]) * _n(ap)
    return n * (2 if ap.dtype == BF16 else 4)


class KB:
    def __init__(self, ntiles=4, nlayers=2, dbg=None, stage=99):
        self.ntiles = ntiles
        self.nlayers = nlayers
        self.dbg_names = dbg or []
        self.stage = stage
        self.nc = bass.Bass("TRN2", target_bir_lowering=False)
        self.st = contextlib.ExitStack()
        self.S = Sched(self.nc)
        self.dbg_out = {}
        self.out_tokens = []
        self.psum_names = set()

    def sb(self, name, shape, dt):
        if SIM_ONLY:
            return self.nc.dram_tensor(name, shape, dt, kind="Internal")
        return self.st.enter_context(self.nc.sbuf_tensor(name, shape, dt))

    def ps(self, name, shape, dt):
        self.psum_names.add(name)
        return self.st.enter_context(self.nc.psum_tensor(name, shape, dt))

    def dram_in(self, name, shape, dt=F32):
        return self.nc.dram_tensor(name, shape, dt, kind="ExternalInput").ap()

    def dram_out(self, name, shape, dt=F32):
        return self.nc.dram_tensor(name, shape, dt, kind="ExternalOutput").ap()

    def _rw(self, outs, ins, r, w):
        reads = list(r) if r is not None else [_tok(a) for a in ins if hasattr(a, "name")]
        writes = list(w) if w is not None else [_tok(a) for a in outs]
        ex = [t for t in reads if t in self.psum_names]
        if ex:
            reads = [t for t in reads if t not in self.psum_names]
            writes = writes + [t for t in ex if t not in writes]
        return reads, writes

    def act(self, out, in_, func, scale=1.0, bias=0.0, r=None, w=None):
        extra = [a for a in (scale, bias) if hasattr(a, "name")]
        reads, writes = self._rw([out], [in_] + extra, r, w)
        tbl = {AF.Exp: 1, AF.Ln: 1, AF.Sigmoid: 2, AF.Tanh: 2, AF.Silu: 3}.get(func, 0)
        self.S.op("act", lambda e: e.activation(out=out, in_=in_, func=func, scale=scale, bias=bias), reads, writes, dur=0.25 + _n(out) / 1400.0, tbl=tbl)

    def tt(self, eng, out, in0, in1, op, r=None, w=None):
        reads, writes = self._rw([out], [in0, in1], r, w)
        self.S.op(eng, lambda e: e.tensor_tensor(out=out, in0=in0, in1=in1, op=op), reads, writes, dur=self._vdur(eng, out))

    def ts(self, eng, out, in0, s1, s2, op0, op1=None, r=None, w=None):
        extra = [a for a in (s1, s2) if hasattr(a, "name")]
        reads, writes = self._rw([out], [in0] + extra, r, w)
        if op1 is None:
            self.S.op(eng, lambda e: e.tensor_scalar(out=out, in0=in0, scalar1=s1, scalar2=None, op0=op0), reads, writes, dur=self._vdur(eng, out))
        else:
            self.S.op(eng, lambda e: e.tensor_scalar(out=out, in0=in0, scalar1=s1, scalar2=s2, op0=op0, op1=op1), reads, writes, dur=self._vdur(eng, out))

    def stt(self, out, in0, scalar, in1, op0, op1, r=None, w=None):
        extra = [scalar] if hasattr(scalar, "name") else []
        reads, writes = self._rw([out], [in0, in1] + extra, r, w)
        self.S.op("dve", lambda e: e.scalar_tensor_tensor(out=out, in0=in0, scalar=scalar, in1=in1, op0=op0, op1=op1), reads, writes, dur=self._vdur("dve", out))

    def copy(self, eng, out, in_, r=None, w=None):
        reads, writes = self._rw([out], [in_], r, w)
        if eng == "act":
            self.S.op("act", lambda e: e.copy(out=out, in_=in_), reads, writes, dur=0.25 + _n(out) / 1400.0)
        else:
            self.S.op(eng, lambda e: e.tensor_copy(out=out, in_=in_), reads, writes,
                      dur=(0.3 + _n(out) / 300.0) if eng == "pool" else (0.1 + _n(out) / 1250.0))

    def recip(self, out, in_, r=None, w=None):
        reads, writes = self._rw([out], [in_], r, w)
        self.S.op("dve", lambda e: e.reciprocal(out=out, in_=in_), reads, writes, dur=0.1 + _n(out) / 155.0)

    def scan(self, out, d0, d1, r=None, w=None):
        reads, writes = self._rw([out], [d0, d1], r, w)
        self.S.op("dve", lambda e: e.tensor_tensor_scan(out=out, data0=d0, data1=d1, initial=0.0, op0=ALU.mult, op1=ALU.add), reads, writes, dur=0.12 + _n(out) / 480.0)

    def memset(self, eng, out, val, r=None, w=None):
        reads, writes = self._rw([out], [], r, w)
        self.S.op(eng, lambda e: e.memset(out, val), reads, writes, dur=self._vdur(eng, out))

    def mm(self, out, pairs, r=None, w=None):
        ins = []
        for a, b in pairs:
            ins += [a, b]
        reads, writes = self._rw([out], ins, r, w)
        n = len(pairs)

        def fn(e):
            last = None
            for i, (a, b) in enumerate(pairs):
                last = e.matmul(out, lhsT=a, rhs=b, start=(i == 0), stop=(i == n - 1))
            return last
        self.S.op("pe", fn, reads, writes, dur=sum(0.035 + max(_n(b_), 64) / 2000.0 for a_, b_ in pairs))

    def mmx(self, items):
        ins = []
        outs = []
        for o, a, b, st_, sp_ in items:
            ins += [a, b]
            outs.append(o)
        reads, writes = self._rw(outs[:1], ins, None, None)

        def fn(e):
            last = None
            for o, a, b, st_, sp_ in items:
                last = e.matmul(o, lhsT=a, rhs=b, start=st_, stop=sp_)
            return last
        self.S.op("pe", fn, reads, writes, dur=sum(0.035 + max(_n(b_), 64) / 2000.0 for o_, a_, b_, s1, s2 in items))

    def _vdur(self, eng, out):
        if eng == "pool":
            return 0.3 + _n(out) / 520.0
        return 0.15 + _n(out) / 900.0

    def tr(self, out, in_, ident, r=None, w=None):
        reads, writes = self._rw([out], [in_, ident], r, w)
        self.S.op("pe", lambda e: e.transpose(out=out, in_=in_, identity=ident), reads, writes, dur=0.1)

    def dma(self, q, out, in_, r=None, w=None):
        reads, writes = self._rw([out], [in_], r, w)
        self.S.dma(q, lambda e: e.dma_start(out=out, in_=in_), reads, writes, nbytes=max(_bytes(out), _bytes(in_)))

    def dump(self, name, ap, shape, dt=F32):
        if name not in self.dbg_names:
            return
        if dt != F32 or ap.dtype != F32:
            if not hasattr(self, "_dbgtmp"):
                self._dbgtmp = self.sb("dbgtmp", [128, 1024], F32)
            tmp = self._dbgtmp[0:shape[0], 0:shape[1]]
            self.copy("dve", tmp, ap)
            ap = tmp
        o = self.dram_out("dbg_" + name, list(shape))
        self.dma("sp", o, ap)
        self.out_tokens.append(_tok(o))
        self.dbg_out[name] = "dbg_" + name


def build(ntiles=4, nlayers=2, dbg=None, stage=99):
    K = KB(ntiles, nlayers, dbg, stage)
    nc = K.nc
    xT = K.dram_in("xT", [D, T])
    cT = K.dram_in("cT", [128, 8])
    ada = [K.dram_in("ada%d" % l, [D, 6 * D]) for l in range(L)]
    vecs_d = [K.dram_in("vecs%d" % l, [128, NV]) for l in range(L)]
    lay = [_wbig_layout(l) for l in range(L)]
    wbig = [K.dram_in("wbig%d" % l, [128, lay[l][1]]) for l in range(L)]
    outT = K.dram_out("outT", [D, T])

    sb, ps = K.sb, K.ps
    ident16 = sb("ident16", [128, 128], BF16)
    ones16 = sb("ones16", [128, 128], BF16)
    bd16 = sb("bd16", [128, 128], BF16)
    onesf = sb("onesf", [128, 128], F32)
    mask512 = sb("mask512", [128, 512], BF16)
    masksl = sb("masksl", [128, 128], BF16)
    rst = sb("rst", [128, 512], BF16)
    vecs = [sb("vecs_s%d" % l, [128, NV], F32) for l in range(L)]
    dv = [sb("dv%d" % l, [128, ND], F32) for l in range(L)]
    c32 = sb("c32", [128, 8], F32)
    epsc = sb("epsc", [128, 4], F32)
    c16 = sb("c16", [128, 8], BF16)
    S32 = [sb("S32_%d" % l, [128, 8, 64], F32) for l in range(L)]
    S16 = [sb("S16_%d" % l, [128, 8, 64], BF16) for l in range(L)]
    ctail = [sb("ctail%d" % l, [128, 8, 30], F32) for l in range(L)]
    carry = [sb("carry%d" % l, [128, 27], F32) for l in range(L)]
    xt = [sb("xt%d" % k, [128, TT], F32) for k in range(NB)]
    ht = [sb("ht%d" % k, [128, TT], BF16) for k in range(NB)]
    merged = [sb("mg%d" % k, [128, TT], BF16) for k in range(16)]
    vf = [sb("vf%d" % k, [128, TT], BF16) for k in range(NB)]
    NWP = 3
    WPN = 4480
    wpool = [sb("wp%d" % i, [128, WPN], BF16) for i in range(NWP)]
    lo16 = [sb("lo16_%d" % j, [128, TT], BF16) for j in range(2)]
    vlo16 = sb("vlo16", [32, TT], BF16)
    names32 = ["r32", "k32", "v32", "sg32", "a32", "sv32", "dd32", "cs32", "E1", "E3",
               "kk32", "nrm", "km", "b32", "y32", "yc", "sd"]
    W = {n: sb(n, [128, TT], F32) for n in names32}
    W["d2"] = W["dd32"]; W["E2"] = W["dd32"]; W["d4"] = W["sv32"]; W["E4"] = W["sv32"]; W["rn"] = W["nrm"]
    W["kkn"] = W["kk32"]; W["t1"] = W["km"]; W["rsd"] = W["sd"]
    W["yn"] = W["yc"]; W["yg"] = W["yc"]; W["o1"] = W["yc"]; W["o2"] = W["yc"]
    names16 = ["kk2", "Kh16", "Bh16", "v16", "rk16", "y16", "yc2", "sqa", "sqb", "sqc"]
    H = {n: sb(n, [128, TT], BF16) for n in names16}
    sq16 = [H[n] for n in ("kk2", "Kh16", "Bh16", "v16", "rk16", "sqa", "sqb", "sqc")]
    g32p = [sb("g32_%d" % i, [128, TT], F32) for i in range(2)]
    bon32p = [sb("bon32_%d" % i, [128, TT], F32) for i in range(2)]
    gA16p = [sb("gA16_%d" % i, [128, TT], BF16) for i in range(2)]
    kt16p = [sb("kt16_%d" % i, [128, TT], BF16) for i in range(2)]
    bt16p = [sb("bt16_%d" % i, [128, TT], BF16) for i in range(2)]
    WCp = [sb("WC_%d" % i, [128, NCH], F32) for i in range(2)]
    lo32 = [W["yc"], W["y32"]]
    vlo32 = bon32p[0]
    rs32 = W["nrm"]; rstd = W["sd"]; t32 = [W["yc"], W["y32"]]
    ARp = [sb("AR_%d" % i, [128, NCH, 2, C], BF16) for i in range(2)]
    KBtokp = [sb("KBtok_%d" % i, [128, 8, 128], BF16) for i in range(2)]
    Vtokp = [sb("Vtok_%d" % i, [128, 4, 128], BF16) for i in range(2)]
    Amatp = [[[sb("Amat%d_%d_%d" % (q, h, c), [128, 512], BF16) for c in range(NCH)] for h in range(2)] for q in range(NQ_AMAT)] * (2 // NQ_AMAT)
    Tfinp = [[[sb("Tfin%d_%d_%d" % (q, h, c), [128, 128], BF16) for c in range(NCH)] for h in range(2)] for q in range(NQ_AMAT)] * (2 // NQ_AMAT)
    Q0 = [sb("Q0_%d" % i, [128, 128], BF16) for i in range(4)]
    PQT = [[sb("PQT%d_%d" % (i, j), [128, 3, 128], BF16) for j in range(2)] for i in range(4)]
    XT16 = sb("XT16", [128, 128], BF16)
    UT16 = sb("UT16", [128, 128], BF16)
    sgb = sb("sgb16", [128, TT], BF16)
    sgb2 = [sb("sgbB%d" % i, [128, TT], BF16) for i in range(2)]
    gbuf = sb("gbuf", [128, 30 + TT], F32)
    cacc = [sb("cacc%d" % i, [128, TT], F32) for i in range(2)]
    ctmp = [sb("ctmp%d" % i, [128, TT], F32) for i in range(2)]
    z16 = [sb("z16_%d" % k, [128, TT], BF16) for k in range(NB)]

    def halves(t):
        v = t[:].bitcast(BF16)
        return [v[:, 0:TT], v[:, TT:2 * TT]]
    hidv = []
    for t_ in [W[n] for n in ("r32", "k32", "v32", "sg32", "a32", "sv32", "dd32", "kk32", "cs32", "E1", "E3", "km", "b32")] + [g32p[0], g32p[1], bon32p[0]]:
        hidv += halves(t_)
    mmb = [ps("mmb%d" % i, [128, 512], F32) for i in range(2)]
    dblb = [ps("dblb%d" % i, [128, 512], F32) for i in range(4)]
    seqb = ps("seqb", [128, 512], F32)
    yb = ps("yb", [128, 512], F32)
    mm_rr = [0]

    def bank():
        b = mmb[mm_rr[0] % 2]
        mm_rr[0] += 1
        return b

    def dbl_region(i, j):
        if j < 2:
            col = (i % 2) * 256 + j * 128
            return dblb[i // 2][:, col:col + 128]
        return dblb[2][:, i * 128:(i + 1) * 128]

    wp_rr = [0]
    PF = 2
    piece_list = []
    for pc in range(12):
        piece_list.append(("ada", 0, pc))
    ada1_left = list(range(12)) if K.nlayers > 1 else []
    for it_ in range(K.ntiles):
        for l_ in range(K.nlayers):
            if l_ == 1:
                while ada1_left:
                    piece_list.append(("ada", 1, ada1_left.pop(0)))
            if K.stage >= 4:
                keys = ["lo", "hp0"]
                for i in range(8):
                    keys.append("cb%d" % i)
                    if i + 1 < 8:
                        keys.append("hp%d" % (i + 1))
                keys += ["gb%d" % i for i in range(8)] + ["wo%d" % i for i in range(4)]
            else:
                keys = ["lo"] + ["hp%d" % i for i in range(8)]
            if K.stage >= 5:
                keys += ["w1%d" % i for i in range(8)] + ["w2%d" % i for i in range(8)]
            for n_, key in enumerate(keys):
                piece_list.append(("w", l_, key))
                if it_ == 0 and l_ == 0 and n_ % 3 == 2 and ada1_left:
                    piece_list.append(("ada", 1, ada1_left.pop(0)))
    issued = [0]
    piece_dst = {}

    def _issue(idx):
        kind, l_, key = piece_list[idx]
        buf = wpool[idx % NWP]
        if kind == "ada":
            src = ada[l_].rearrange("(k p) c -> p k c", p=128)[:, :, key * 512:(key + 1) * 512]
            dst = buf[:, 0:4096].rearrange("p (k c) -> p k c", c=512)
        elif key.startswith("hp"):
            off, k, c = lay[l_][0][key]
            src = wbig[l_][:, off:off + 4480].rearrange("p (a b) -> p a b", b=640)
            K.dma("pool", buf[:, 0:4480].rearrange("p (a b) -> p a b", b=640), src)
            piece_dst[idx] = (buf[:, 0:4096].rearrange("p (k c) -> p k c", c=512), buf[:, 4096:4480].rearrange("p (a c) -> p a c", c=128))
            return
        else:
            off, k, c = lay[l_][0][key]
            src = wbig[l_][:, off:off + k * c].rearrange("p (k c) -> p k c", c=c)
            dst = buf[:, 0:k * c].rearrange("p (k c) -> p k c", c=c)
        K.dma("pool", dst, src)
        piece_dst[idx] = dst

    def next_piece(expect):
        idx = wp_rr[0]
        wp_rr[0] += 1
        assert piece_list[idx] == expect, (piece_list[idx], expect)
        while issued[0] < min(len(piece_list), idx + PF + 1):
            _issue(issued[0])
            issued[0] += 1
        return piece_dst.pop(idx)

    def ada_piece():
        kind, l_, pc = piece_list[wp_rr[0]]
        dst = next_piece((kind, l_, pc))
        p = bank()
        for m in range(4):
            K.mm(p[:, m:m + 1], [(dst[:, k, m * 128:(m + 1) * 128], c16[:, k:k + 1]) for k in range(8)])
        K.tt("dve", dv[l_][:, DV_MOD + 4 * pc:DV_MOD + 4 * pc + 4], p[:, 0:4], vecs[l_][:, V_ADAB + 4 * pc:V_ADAB + 4 * pc + 4], ALU.add)
        if pc == 11:
            K.stt(dv[l_][:, DV_GSCM:DV_GSCM + 8], dv[l_][:, DV_MOD + M_SCM:DV_MOD + M_SCM + 8], 1.0, vecs[l_][:, V_GMIX:V_GMIX + 8], ALU.add, ALU.mult)
            K.stt(dv[l_][:, DV_GSCF:DV_GSCF + 8], dv[l_][:, DV_MOD + M_SCF:DV_MOD + M_SCF + 8], 1.0, vecs[l_][:, V_GFFN:V_GFFN + 8], ALU.add, ALU.mult)
            K.ts("dve", dv[l_][:, DV_OMU:DV_OMU + 27], vecs[l_][:, V_MU:V_MU + 27], -1.0, 1.0, ALU.mult, ALU.add)
            K.dump("dv%d" % l_, dv[l_][:], [128, ND])

    def load_piece(l, key):
        while wp_rr[0] < len(piece_list) and piece_list[wp_rr[0]][0] == "ada":
            ada_piece()
        return next_piece(("w", l, key))

    K.memset("pool", onesf[:], 1.0)
    K.memset("pool", ones16[:], 1.0)
    K.memset("pool", bd16[:], 0.0)
    K.memset("pool", bd16[0:64, 0:64], 1.0)
    K.memset("pool", bd16[64:128, 64:128], 1.0)

    def asel(out, in_, pattern, op, base, cm):
        K.S.op("pool", lambda e: e.affine_select(out=out, in_=in_, pattern=pattern, compare_op=op, fill=0.0, base=base, channel_multiplier=cm),
               [_tok(in_)], [_tok(out)], dur=0.3)
    asel(ident16[:], ones16[:], [[1, 128]], ALU.is_equal, 0, -1)
    for j in range(4):
        asel(mask512[:, j * 128:(j + 1) * 128], onesf[:, 0:128], [[1, 128]], ALU.is_gt if j % 2 == 0 else ALU.is_ge, 0, -1)
    asel(masksl[:], onesf[:, 0:128], [[-1, 128]], ALU.is_gt, 0, 1)
    K.memset("pool", rst[:], 1.0)
    for j in range(NCH):
        K.memset("pool", rst[:, j * C:j * C + 1], 0.0)
    for l in range(L):
        K.dma("sp", vecs[l][:], vecs_d[l])
        K.memset("pool", S32[l][:], 0.0)
        K.memset("pool", S16[l][:], 0.0)
        K.memset("pool", ctail[l][:], 0.0)
        K.memset("pool", carry[l][:], 0.0)
    K.dma("sp", c32[:], cT)
    K.memset("pool", epsc[:, 0:1], RMS_EPS)
    K.memset("pool", epsc[:, 1:2], LN_EPS)
    K.memset("pool", epsc[:, 2:3], GN_EPS)
    K.memset("pool", epsc[:, 3:4], 1e-24)
    K.act(c16[:], c32[:], AF.Silu)

    for pc in range(12):
        ada_piece()

    def rmsnorm_to_ht(l, gsc_col, sh_col):
        for k in range(NB):
            K.act(sq16[k][:], xt[k][:], AF.Square)
        p = bank()
        K.mm(p[:], [(ones16[:], sq16[k][:]) for k in range(NB)])
        K.act(rs32[:], p[:], AF.Ln, scale=1.0 / D, bias=epsc[:, 0:1])
        K.act(rstd[:], rs32[:], AF.Exp, scale=-0.5)
        for k in range(NB):
            t = t32[k % 2]
            K.stt(t[:], xt[k][:], dv[l][:, gsc_col + k:gsc_col + k + 1], rstd[:], ALU.mult, ALU.mult)
            K.act(ht[k][:], t[:], AF.Identity, bias=dv[l][:, sh_col + k:sh_col + k + 1])

    def proj(wp, j0, M=128):
        p = bank()
        K.mm(p[0:M, :], [(wp[:, k, j0:j0 + M], ht[k][:]) for k in range(NB)])
        return p

    def shift(l, p, mucol, dst, M=128):
        mu = vecs[l][0:M, V_MU + mucol:V_MU + mucol + 1]
        omu = dv[l][0:M, DV_OMU + mucol:DV_OMU + mucol + 1]
        cy = carry[l][0:M, mucol:mucol + 1]
        K.act(dst[0:M, :], p[0:M, :], AF.Identity, scale=omu)
        K.stt(dst[0:M, 1:TT], p[0:M, 0:TT - 1], mu, dst[0:M, 1:TT], ALU.mult, ALU.add)
        K.stt(dst[0:M, 0:1], cy, mu, dst[0:M, 0:1], ALU.mult, ALU.add)
        K.copy("dve", cy, p[0:M, TT - 1:TT])

    xTv = xT.rearrange("(k p) t -> k p t", p=128)
    oTv = outT.rearrange("(k p) t -> k p t", p=128)
    for it in range(K.ntiles):
        t0 = it * TT
        for k in range(NB):
            K.dma("sp", xt[k][:], xTv[k, :, t0:t0 + TT])
        for l in range(K.nlayers):
            V = vecs[l]
            DVl = dv[l]
            rmsnorm_to_ht(l, DV_GSCM, DV_MOD + M_SHM)
            if it == 0 and l == 0:
                for k in (0, 7):
                    K.dump("ht%d" % k, ht[k][:], [128, TT])
            wlo = load_piece(l, "lo")
            for j in range(2):
                p = proj(wlo, j * 128)
                shift(l, p, 24 + j, lo32[j])
            K.act(lo16[0][0:64, :], lo32[0][0:64, :], AF.Tanh)
            K.copy("act", lo16[0][64:128, :], lo32[0][64:128, :])
            K.act(lo16[1][:], lo32[1][:], AF.Sigmoid)
            if l >= 1:
                p = proj(wlo, 256, M=32)
                shift(l, p, 26, vlo32, M=32)
                K.copy("act", vlo16[:], vlo32[0:32, :])
            if K.stage < 1:
                continue
            def front(hp, par):
                AR = ARp[par]; KBtok = KBtokp[par]; Vtok = Vtokp[par]
                g32 = g32p[par]; bon32 = bon32p[par]; gA16 = gA16p[par]; kt16 = kt16p[par]; bt16 = bt16p[par]; WC = WCp[par]
                whp, wsl = load_piece(l, "hp%d" % hp)
                p = proj(whp, 0);   shift(l, p, hp, W["r32"])
                yield
                p = proj(whp, 128); shift(l, p, 8 + hp, W["k32"])
                yield
                p = proj(whp, 256); shift(l, p, 16 + hp, W["v32"])
                yield
                p = proj(whp, 384); K.act(gA16[:], p[:], AF.Sigmoid)
                p = bank(); K.mm(p[:], [(wsl[0:64, 0, :], lo16[0][0:64, :])])
                K.act(W["sg32"][:], p[:], AF.Sigmoid, bias=V[:, V_W0 + hp:V_W0 + hp + 1])
                yield
                p = bank(); K.mm(p[:], [(wsl[64:128, 0, :], lo16[0][64:128, :])])
                K.act(W["a32"][:], p[:], AF.Sigmoid, bias=V[:, V_A0 + hp:V_A0 + hp + 1])
                p = bank(); K.mm(p[:], [(wsl[:, 1, :], lo16[1][:])])
                K.copy("act", g32[:], p[:])
                yield
                if l >= 1:
                    p = bank(); K.mm(p[:], [(wsl[0:32, 2, :], vlo16[:])])
                    K.act(W["sv32"][:], p[:], AF.Sigmoid, bias=V[:, V_V0 + hp:V_V0 + hp + 1])
                    K.tt("pool", W["dd32"][:], vf[hp][:], W["v32"][:], ALU.subtract)
                    K.tt("pool", W["dd32"][:], W["dd32"][:], W["sv32"][:], ALU.mult)
                    K.tt("pool", W["v32"][:], W["v32"][:], W["dd32"][:], ALU.add)
                else:
                    K.copy("pool", vf[hp][:], W["v32"][:])
                yield
                K.scan(W["cs32"][:], rst[:], W["sg32"][:])
                K.act(W["E1"][:], W["cs32"][:], AF.Exp, scale=CEXP)
                K.act(W["E3"][:], W["cs32"][:], AF.Exp, scale=-CEXP)
                K.tt("pool", W["d2"][:], W["cs32"][:], W["sg32"][:], ALU.subtract)
                K.act(W["E2"][:], W["d2"][:], AF.Exp, scale=CEXP)
                yield
                cs3 = W["cs32"][:].rearrange("p (c t) -> p c t", t=C)
                K.tt("pool", W["d4"][:].rearrange("p (c t) -> p c t", t=C), cs3, cs3[:, :, C - 1:C].to_broadcast([128, NCH, C]), ALU.subtract)
                K.act(W["E4"][:], W["d4"][:], AF.Exp, scale=-CEXP)
                K.copy("dve", WC[:], W["E1"][:].rearrange("p (c t) -> p c t", t=C)[:, :, C - 1])
                yield
                K.act(W["kk32"][:], W["k32"][:], AF.Identity, scale=V[:, V_KK + hp:V_KK + hp + 1])
                K.act(H["kk2"][:], W["kk32"][:], AF.Square)
                p = bank(); K.mm(p[:], [(bd16[:], H["kk2"][:])])
                K.act(W["nrm"][:], p[:], AF.Ln, bias=epsc[:, 3:4])
                K.act(W["rn"][:], W["nrm"][:], AF.Exp, scale=-0.5)
                yield
                K.tt("pool", W["kkn"][:], W["kk32"][:], W["rn"][:], ALU.mult)
                K.ts("dve", W["t1"][:], W["a32"][:], -1.0, V[:, V_KA + hp:V_KA + hp + 1], ALU.add, ALU.mult)
                K.stt(W["km"][:], W["t1"][:], 1.0, W["k32"][:], ALU.add, ALU.mult)
                K.tt("pool", W["b32"][:], W["kkn"][:], W["a32"][:], ALU.mult)
                yield
                K.tt("dve", AR[:, :, 1, :], W["r32"][:].rearrange("p (c t) -> p c t", t=C), W["E1"][:].rearrange("p (c t) -> p c t", t=C), ALU.mult)
                K.stt(AR[:, :, 0, :], W["kkn"][:].rearrange("p (c t) -> p c t", t=C), -1.0, W["E2"][:].rearrange("p (c t) -> p c t", t=C), ALU.mult, ALU.mult)
                K.tt("pool", kt16[:], W["km"][:], W["E3"][:], ALU.mult)
                K.tt("pool", bt16[:], W["b32"][:], W["E3"][:], ALU.mult)
                yield
                K.tt("pool", H["Kh16"][:], W["km"][:], W["E4"][:], ALU.mult)
                K.tt("dve", H["Bh16"][:], W["b32"][:], W["E4"][:], ALU.mult)
                K.copy("pool", H["v16"][:], W["v32"][:])
                yield
                K.stt(H["rk16"][:], W["r32"][:], V[:, V_RK + hp:V_RK + hp + 1], W["km"][:], ALU.mult, ALU.mult)
                p = bank(); K.mm(p[:], [(bd16[:], H["rk16"][:])])
                K.tt("dve", bon32[:], p[:], W["v32"][:], ALU.mult)
                yield
                trb = bank()
                trv = trb[:].bitcast(BF16).rearrange("p (a t) -> p a t", t=128)
                for c in range(NCH):
                    K.tr(trv[:, c, :], H["Kh16"][:, c * C:(c + 1) * C], ident16[:])
                    K.tr(trv[:, 4 + c, :], H["Bh16"][:, c * C:(c + 1) * C], ident16[:])
                K.copy("act", KBtok[:], trv)
                yield
                trb = bank()
                trv = trb[:].bitcast(BF16).rearrange("p (a t) -> p a t", t=128)
                for c in range(NCH):
                    K.tr(trv[:, c, :], H["v16"][:, c * C:(c + 1) * C], ident16[:])
                K.copy("dve", Vtok[:], trv[:, 0:4, :])
                yield

            def back(hp, par):
                Amat = Amatp[par]; Tfin = Tfinp[par]
                AR = ARp[par]; KBtok = KBtokp[par]; Vtok = Vtokp[par]
                g32 = g32p[par]; bon32 = bon32p[par]; gA16 = gA16p[par]; kt16 = kt16p[par]; bt16 = bt16p[par]; WC = WCp[par]
                for h in range(2):
                    hs = slice(h * 64, h * 64 + 64)
                    for c in range(NCH):
                        ck = slice(c * C, (c + 1) * C)
                        bk = dblb[c]
                        ar = AR[hs, c, :, :].rearrange("p a t -> p (a t)")
                        K.mm(bk[:, 0:256], [(bt16[hs, ck], ar)])
                        K.mm(bk[:, 256:512], [(kt16[hs, ck], ar)])
                        K.tt("dve", Amat[h][c][:], bk[:], mask512[:], ALU.mult)
                        K.mm(bk[:, 0:128], [(AR[hs, c, 0, :], bt16[hs, ck])])
                        K.tt("dve", Q0[c][:], bk[:, 0:128], masksl[:], ALU.mult)
                        K.tt("pool", PQT[c][1][:, 1, :], Amat[h][c][:, 0:128], ident16[:], ALU.add)
                        yield
                    for c in range(NCH):
                        bk = dblb[c]
                        P0 = Amat[h][c][:, 0:128]
                        K.mm(bk[:, 0:128], [(Q0[c][:], P0)])
                        K.mm(bk[:, 256:384], [(P0, Q0[c][:])])
                        K.copy("act" if c % 2 == 0 else "dve", PQT[c][1][:, 0:3:2, :], bk[:, 0:384].rearrange("p (a t) -> p a t", t=128)[:, 0:3:2, :])
                    yield
                    for lv in range(1, 7):
                        cur, nxt = lv % 2, (lv + 1) % 2
                        for c in range(NCH):
                            bk = dblb[c]
                            Pk = PQT[c][cur][:, 0, :]
                            Tk = PQT[c][cur][:, 1, :]
                            Qk = PQT[c][cur][:, 2, :]
                            PTk = PQT[c][cur][:, 0:2, :].rearrange("p a t -> p (a t)")
                            eng = "dve" if (c + lv) % EVAC_MOD == 0 else "act"
                            if lv < 6:
                                K.mmx([(bk[:, 256:384], Pk, Qk, True, True),
                                       (bk[:, 0:256], Qk, PTk, True, False),
                                       (bk[:, 128:256], ident16[:], Tk, False, True)])
                                K.copy(eng, PQT[c][nxt][:], bk[:, 0:384].rearrange("p (a t) -> p a t", t=128))
                            else:
                                K.mm(bk[:, 128:256], [(Qk, Tk), (ident16[:], Tk)])
                                K.copy(eng, Tfin[h][c][:], bk[:, 128:256])
                        yield
                for c in range(NCH):
                    for h in range(2):
                        hs = slice(h * 64, h * 64 + 64)
                        K.mm(seqb[:, h * 64:(h + 1) * 64], [(AR[hs, c, 0, :], S16[l][hs, hp, :]),
                                                             (Amat[h][c][:, 256:384], Vtok[:, c, hs])])
                    K.copy("act", XT16[:], seqb[:, 0:128])
                    yield
                    for h in range(2):
                        hs = slice(h * 64, h * 64 + 64)
                        K.mm(seqb[:, 128 + h * 64:128 + (h + 1) * 64], [(Tfin[h][c][:], XT16[:, hs])])
                    K.copy("dve", UT16[:], seqb[:, 128:256])
                    yield
                    for h in range(2):
                        hs = slice(h * 64, h * 64 + 64)
                        K.mm(yb[hs, c * C:(c + 1) * C], [(S16[l][hs, hp, :], AR[hs, c, 1, :]),
                                                          (UT16[:, hs], Amat[h][c][:, 128:256]),
                                                          (Vtok[:, c, hs], Amat[h][c][:, 384:512])])
                        K.mm(seqb[hs, 256:320], [(KBtok[:, 4 + c, hs], UT16[:, hs]),
                                                  (KBtok[:, c, hs], Vtok[:, c, hs])])
                    K.stt(S32[l][:, hp, :], S32[l][:, hp, :], WC[:, c:c + 1], seqb[:, 256:320], ALU.mult, ALU.add)
                    K.copy("act", S16[l][:, hp, :], S32[l][:, hp, :])
                    yield
                K.copy("act", W["y32"][:], yb[:])
                K.copy("dve", H["y16"][:], yb[:])
                if it == 0 and l == 0 and hp == 0:
                    K.dump("y32", W["y32"][:], [128, TT])
                yield
                p = bank(); K.mm(p[:], [(bd16[:], H["y16"][:])])
                K.stt(W["yc"][:], p[:], -1.0 / 64, W["y32"][:], ALU.mult, ALU.add)
                yield
                K.act(H["yc2"][:], W["yc"][:], AF.Square)
                p = bank(); K.mm(p[:], [(bd16[:], H["yc2"][:])])
                K.act(W["sd"][:], p[:], AF.Ln, scale=1.0 / 64, bias=epsc[:, 2:3])
                yield
                K.act(W["rsd"][:], W["sd"][:], AF.Exp, scale=-0.5)
                K.tt("pool", W["yn"][:], W["yc"][:], W["rsd"][:], ALU.mult)
                K.act(W["yg"][:], W["yn"][:], AF.Identity, scale=V[:, V_GNG + hp:V_GNG + hp + 1], bias=V[:, V_GNB + hp:V_GNB + hp + 1])
                yield
                K.tt("pool", W["o1"][:], W["yg"][:], bon32[:], ALU.add)
                K.tt("pool", W["o2"][:], W["o1"][:], g32[:], ALU.mult)
                K.tt("pool", merged[hp][:], W["o2"][:], gA16[:], ALU.mult)
                if it == 0 and l == 0 and hp in (0, 7):
                    K.dump("mg%d" % hp, merged[hp][:], [128, TT], BF16)
                yield

            def drain(g):
                for _ in g:
                    pass

            def interleave(g1, g2):
                a1 = a2 = True
                while a1 or a2:
                    if a1:
                        try:
                            next(g1)
                        except StopIteration:
                            a1 = False
                    if a2:
                        try:
                            next(g2)
                        except StopIteration:
                            a2 = False

            def conv(cb):
                wcb = load_piece(l, "cb%d" % cb)
                pb = proj(wcb, 128)
                K.act(sgb[:], pb[:], AF.Sigmoid)
                pa = proj(wcb, 0)
                K.copy("pool", gbuf[:, 0:30], ctail[l][:, cb, :])
                K.tt("dve", gbuf[:, 30:30 + TT], pa[:], sgb[:], ALU.mult)
                K.copy("pool", ctail[l][:, cb, :], gbuf[:, TT:TT + 30])
                cw = lambda k: V[:, V_CW + cb * 31 + k:V_CW + cb * 31 + k + 1]
                NT_DVE = NT_DVE_G
                K.ts("dve", cacc[0][:], gbuf[:, 0:TT], cw(0), V[:, V_CB + cb:V_CB + cb + 1], ALU.mult, ALU.add)
                for k in range(1, NT_DVE):
                    K.stt(cacc[0][:], gbuf[:, k:k + TT], cw(k), cacc[0][:], ALU.mult, ALU.add)
                K.act(cacc[1][:], gbuf[:, NT_DVE:NT_DVE + TT], AF.Identity, scale=cw(NT_DVE))
                for k in range(NT_DVE + 1, 31):
                    tp_ = ctmp[k % 2]
                    K.act(tp_[:], gbuf[:, k:k + TT], AF.Identity, scale=cw(k))
                    K.tt("pool", cacc[1][:], cacc[1][:], tp_[:], ALU.add)
                K.tt("pool", z16[cb][:], cacc[0][:], cacc[1][:], ALU.add)

            if K.stage < 3:
                for hp in range(8):
                    drain(front(hp, hp % 2))
            else:
                drain(front(0, 0))
                for hp in range(8):
                    if K.stage >= 4:
                        conv(hp)
                    if hp + 1 < 8:
                        interleave(back(hp, hp % 2), front(hp + 1, (hp + 1) % 2))
                    else:
                        drain(back(hp, hp % 2))
            if K.stage < 4:
                continue
            if it == 0 and l == 0:
                K.dump("z0", z16[0][:], [128, TT], BF16)
            p = bank(); K.mm(p[:], [(ones16[:], z16[cb][:]) for cb in range(8)])
            K.act(W["nrm"][:], p[:], AF.Identity, scale=-1.0 / D)
            for cb in range(8):
                t = t32[cb % 2]
                K.tt("dve", t[:], z16[cb][:], W["nrm"][:], ALU.add)
                K.act(sq16[cb][:], t[:], AF.Square)
            p = bank(); K.mm(p[:], [(ones16[:], sq16[cb][:]) for cb in range(8)])
            K.act(W["sd"][:], p[:], AF.Ln, scale=1.0 / D, bias=epsc[:, 1:2])
            K.act(W["sd"][:], W["sd"][:], AF.Exp, scale=-0.5)
            for cb in range(8):
                t = t32[cb % 2]
                wgb = load_piece(l, "gb%d" % cb)
                pg = proj(wgb, 0)
                sg_ = sgb2[cb % 2]
                K.act(sg_[:], pg[:], AF.Sigmoid)
                K.tt("dve", t[:], z16[cb][:], W["nrm"][:], ALU.add)
                K.tt("pool", t[:], t[:], W["sd"][:], ALU.mult)
                K.act(t[:], t[:], AF.Silu, scale=V[:, V_LNG + cb:V_LNG + cb + 1], bias=V[:, V_LNB + cb:V_LNB + cb + 1])
                K.tt("pool", merged[8 + cb][:], t[:], sg_[:], ALU.mult)
            if it == 0 and l == 0:
                K.dump("mg8", merged[8][:], [128, TT], BF16)
            for g in range(4):
                wo = load_piece(l, "wo%d" % g)
                for o2 in range(2):
                    ob = g * 2 + o2
                    p = bank()
                    K.mm(p[:], [(wo[:, o2 * 16 + kc, :], merged[kc][:]) for kc in range(16)])
                    K.stt(xt[ob][:], p[:], DVl[:, DV_MOD + M_GTM + ob:DV_MOD + M_GTM + ob + 1], xt[ob][:], ALU.mult, ALU.add)
            if it == 0 and l == 0:
                K.dump("xmix0", xt[0][:], [128, TT])
            if K.stage < 5:
                continue
            rmsnorm_to_ht(l, DV_GSCF, DV_MOD + M_SHF)
            for pc in range(8):
                w1 = load_piece(l, "w1%d" % pc)
                for j in range(4):
                    hb = pc * 4 + j
                    p = proj(w1, j * 128)
                    t = t32[hb % 2]
                    K.act(t[:], p[:], AF.Relu)
                    K.tt("pool", hidv[hb], t[:], t[:], ALU.mult)
            for ob in range(8):
                w2 = load_piece(l, "w2%d" % ob)
                p = bank()
                K.mm(p[:], [(w2[:, hb, :], hidv[hb]) for hb in range(32)])
                K.stt(xt[ob][:], p[:], DVl[:, DV_MOD + M_GTF + ob:DV_MOD + M_GTF + ob + 1], xt[ob][:], ALU.mult, ALU.add)
            if it == 0 and l == 0:
                K.dump("xffn0", xt[0][:], [128, TT])
        for k in range(NB):
            K.act(sq16[k][:], xt[k][:], AF.Square)
        p = bank()
        K.mm(p[:], [(ones16[:], sq16[k][:]) for k in range(NB)])
        K.act(rs32[:], p[:], AF.Ln, scale=1.0 / D, bias=epsc[:, 0:1])
        K.act(rstd[:], rs32[:], AF.Exp, scale=-0.5)
        for k in range(NB):
            t = t32[k % 2]
            K.stt(t[:], xt[k][:], vecs[0][:, V_FG + k:V_FG + k + 1], rstd[:], ALU.mult, ALU.mult)
            K.dma("sp", oTv[k, :, t0:t0 + TT], t[:], w=["outT%d_%d" % (it, k)])
            K.out_tokens.append("outT%d_%d" % (it, k))
    K.S.wait_all("sp", K.out_tokens)
    K.S.emit(K.st)
    K.st.close()
    return K


def dblb_view(dblb, c):
    col = (c % 2) * 256
    return dblb[c // 2][:, col:col + 256].rearrange("p (a t) -> p a t", t=128)


def _fm(v):
    v = np.asarray(v, np.float32)
    return np.ascontiguousarray(v.reshape(-1, 128).T)


def prep_shared(inp):
    shared = {}
    for l in range(L):
        vec = np.zeros((128, NV), np.float32)
        vec[:, V_GMIX:V_GMIX + 8] = _fm(inp["norm_mix_gain"][l])
        vec[:, V_GFFN:V_GFFN + 8] = _fm(inp["norm_ffn_gain"][l])
        vec[:, V_MU:V_MU + 26] = _fm(inp["mu_shift"][l])
        if l >= 1:
            vec[0:32, V_MUV] = inp["mu_vres"][l - 1]
            vec[:, V_V0:V_V0 + 8] = _fm(inp["v0"][l - 1])
        vec[:, V_W0:V_W0 + 8] = _fm(inp["w0"][l])
        vec[:, V_A0:V_A0 + 8] = _fm(inp["a0"][l])
        vec[:, V_KK:V_KK + 8] = _fm(inp["k_k"][l])
        vec[:, V_KA:V_KA + 8] = _fm(inp["k_a"][l])
        vec[:, V_RK:V_RK + 8] = _fm(inp["r_k"][l].reshape(-1))
        vec[:, V_GNG:V_GNG + 8] = _fm(inp["gn_gain"][l])
        vec[:, V_GNB:V_GNB + 8] = _fm(inp["gn_bias"][l])
        vec[:, V_CB:V_CB + 8] = _fm(inp["conv_b"][l])
        vec[:, V_LNG:V_LNG + 8] = _fm(inp["conv_ln_gain"][l])
        vec[:, V_LNB:V_LNB + 8] = _fm(inp["conv_ln_bias"][l])
        vec[:, V_FG:V_FG + 8] = _fm(inp["final_gain"])
        vec[:, V_ADAB:V_ADAB + 48] = _fm(inp["ada_b"][l])
        cw = np.asarray(inp["conv_w"][l], np.float32)
        vec[:, V_CW:V_CW + 248] = cw.reshape(31, 8, 128).transpose(2, 1, 0).reshape(128, 248)
        shared["vecs%d" % l] = vec
        wsm = np.zeros((128, 3, D), np.float32)
        wsm[0:64, 0] = inp["w_decay_up"][l]
        wsm[64:128, 0] = inp["w_aaa_up"][l]
        wsm[:, 1] = inp["w_gate_up"][l]
        if l >= 1:
            wsm[0:32, 2] = inp["w_vres_up"][l - 1]
        shared["ada%d" % l] = np.ascontiguousarray(inp["ada_w"][l], dtype=np.float32)
        lay, tot = _wbig_layout(l)
        wb = np.empty((128, tot), np.float32)
        win = np.asarray(inp["w_in"][l], np.float32)
        if l >= 1:
            win = np.concatenate([win, np.asarray(inp["w_in_vres"][l - 1], np.float32)], axis=1)

        def put(key, cols_matrix):
            off, k, c = lay[key]
            wb[:, off:off + k * c] = cols_matrix.reshape(k, 128, c).transpose(1, 0, 2).reshape(128, k * c)
        lo_idx = list(range(3072, 3328)) + (list(range(N_COLS, N_COLS + 32)) if l >= 1 else [])
        put("lo", win[:, lo_idx])
        for hp in range(8):
            idx = np.concatenate([np.arange(hp * 128, hp * 128 + 128), 1024 + np.arange(hp * 128, hp * 128 + 128),
                                  2048 + np.arange(hp * 128, hp * 128 + 128), 5376 + np.arange(hp * 128, hp * 128 + 128)])
            put("hp%d" % hp, win[:, idx])
            off_, k_, c_ = lay["hp%d" % hp]
            wb[:, off_ + k_ * c_: off_ + k_ * c_ + 384] = wsm[:, :, hp * 128:(hp + 1) * 128].reshape(128, 384)
        for cb in range(8):
            idx = np.concatenate([3328 + np.arange(cb * 128, cb * 128 + 128), 4352 + np.arange(cb * 128, cb * 128 + 128)])
            put("cb%d" % cb, win[:, idx])
            put("gb%d" % cb, win[:, 6400 + np.arange(cb * 128, cb * 128 + 128)])
        wo = np.asarray(inp["w_out"][l], np.float32)
        for g in range(4):
            off, k, c = lay["wo%d" % g]
            blk = wo[:, g * 256:(g + 1) * 256].reshape(16, 128, 2, 128)
            wb[:, off:off + k * c] = blk.transpose(1, 2, 0, 3).reshape(128, 32 * 128)
        w1 = np.asarray(inp["w_ff_in"][l], np.float32)
        for pc in range(8):
            put("w1%d" % pc, w1[:, pc * 512:(pc + 1) * 512])
        w2 = np.asarray(inp["w_ff_out"][l], np.float32)
        for ob in range(8):
            put("w2%d" % ob, w2[:, ob * 128:(ob + 1) * 128])
        shared["wbig%d" % l] = wb
    return shared


def prep_core(inp, b):
    return {"xT": np.ascontiguousarray(np.asarray(inp["x"][b], np.float32).T),
            "cT": _fm(inp["c"][b])}


_CACHE = {}


def kernel(**inputs):
    inp = {k: np.asarray(v) for k, v in inputs.items()}
    if "K" not in _CACHE:
        _CACHE["K"] = build()
    K = _CACHE["K"]
    shared = prep_shared(inp)
    in_maps = []
    for b in range(NCORES):
        m = dict(shared)
        m.update(prep_core(inp, b))
        in_maps.append(m)
    res = run_bass_kernel_spmd(K.nc, in_maps, core_ids=list(range(NCORES)))
    out = np.stack([np.ascontiguousarray(res.results[b]["outT"].T) for b in range(NCORES)], axis=0)
    return out.astype(np.float32)
```

```python
import contextlib
import math
import numpy as np
import concourse.bass as bass
import concourse.mybir as mybir
from concourse.bass_utils import run_bass_kernel_spmd

F32 = mybir.dt.float32
BF16 = mybir.dt.bfloat16
AF = mybir.ActivationFunctionType
ALU = mybir.AluOpType

D = 1024
T = 2048
NB = 8
TT = 512
C = 128
NCH = TT // C
L = 2
NCORES = 8
N_SHIFT = 3328
N_COLS = 7424
RMS_EPS = 1e-6
LN_EPS = 1e-5
GN_EPS = 64e-5
CEXP = -math.exp(-0.5)

V_GMIX, V_GFFN, V_MU, V_MUV, V_W0, V_A0, V_KK, V_KA, V_RK, V_GNG, V_GNB, V_V0, V_CB, V_LNG, V_LNB, V_FG, V_ADAB, V_CW = (
    0, 8, 16, 42, 43, 51, 59, 67, 75, 83, 91, 99, 107, 115, 123, 131, 139, 187)
NV = 187 + 8 * 31
DV_MOD, DV_GSCM, DV_GSCF, DV_OMU = 0, 48, 56, 64
ND = 64 + 27
M_SHM, M_SCM, M_GTM, M_SHF, M_SCF, M_GTF = 0, 8, 16, 24, 32, 40

def _wbig_layout(l):
    off = 0
    lay = {}
    nlo = 256 + (32 if l >= 1 else 0)
    lay["lo"] = (off, 8, nlo); off += 8 * nlo
    for hp in range(8):
        lay["hp%d" % hp] = (off, 8, 512); off += 8 * 512 + 3 * 128
    for cb in range(8):
        lay["cb%d" % cb] = (off, 8, 256); off += 8 * 256
    for cb in range(8):
        lay["gb%d" % cb] = (off, 8, 128); off += 8 * 128
    for g in range(4):
        lay["wo%d" % g] = (off, 32, 128); off += 32 * 128
    for pc in range(8):
        lay["w1%d" % pc] = (off, 8, 512); off += 8 * 512
    for ob in range(8):
        lay["w2%d" % ob] = (off, 32, 128); off += 32 * 128
    return lay, off


ENGS = ("pe", "act", "dve", "pool", "sp")
DMA_SEMS = {"sp": 8, "pool": 16, "act": 4}
SEM_LAT = 0.4
NT_DVE_G = 22
EVAC_MOD = 4
PRIO_MODE = 0
NQ_AMAT = 1
NHO = 2
SIM_ONLY = False
ACT_WINDOW = 0.3
WINDOW_G = 0.3
LIST_SCHED = True


class Sched:
    def __init__(self, nc):
        self.nc = nc
        self.recs = []
        self.last_w = {}
        self.readers = {}

    def _deps(self, reads, writes):
        deps = set()
        for t in reads:
            w = self.last_w.get(t)
            if w is not None:
                deps.add(w)
        for t in writes:
            w = self.last_w.get(t)
            if w is not None:
                deps.add(w)
            deps.update(self.readers.get(t, ()))
        return deps

    def _commit(self, reads, writes, me):
        for t in reads:
            self.readers.setdefault(t, []).append(me)
        for t in writes:
            self.last_w[t] = me
            self.readers[t] = []

    def op(self, eng, fn, reads=(), writes=(), dur=0.5, tbl=0):
        deps = self._deps(reads, writes)
        me = len(self.recs)
        self.recs.append([eng, fn, deps, dur, False, dur, (list(writes) or ["?"])[0], tbl])
        self._commit(reads, writes, me)

    def dma(self, queue, fn, reads=(), writes=(), nbytes=0):
        deps = self._deps(reads, writes)
        me = len(self.recs)
        self.recs.append([queue, fn, deps, 0.6 if queue == "pool" else 0.15, True, 2.0 + nbytes / 340e3, (list(writes) or ["?"])[0]])
        self._commit(reads, writes, me)

    def wait_all(self, eng, tokens):
        deps = set(self.last_w[t] for t in tokens if t in self.last_w)
        me = len(self.recs)
        self.recs.append([eng, None, deps, 0.01, False, 0.01, "final"])
        self.final = me

    def finalize(self):
        recs = self.recs
        n = len(recs)
        order = {e: [] for e in ENGS}
        if not LIST_SCHED:
            for i, r in enumerate(recs):
                order[r[0]].append(i)
            self.order = order
            return
        succs = [[] for _ in range(n)]
        indeg = [0] * n
        for i, r in enumerate(recs):
            for d in r[2]:
                succs[d].append(i)
            indeg[i] = len(r[2])
        prio = [0.0] * n
        for i in range(n - 1, -1, -1):
            m = 0.0
            for sx in succs[i]:
                if prio[sx] > m:
                    m = prio[sx]
            prio[i] = m + recs[i][5] + SEM_LAT
        import heapq
        ready = {e: [] for e in ENGS}
        finish = [0.0] * n
        self.sim_start = [0.0] * n
        self.sim_finish = finish
        rdy_t = [0.0] * n
        for i in range(n):
            if indeg[i] == 0:
                heapq.heappush(ready[recs[i][0]], (0.0, -prio[i], i))
        efree = {e: 0.0 for e in ENGS}
        done = 0
        WINDOW = WINDOW_G
        cur_tbl = 0
        dma_free = 0.0
        self.n_tbl_switch = 0
        while done < n:
            best = None
            for e in ENGS:
                h = ready[e]
                if not h:
                    continue
                st = max(efree[e], h[0][0])
                if best is None or st < best[0]:
                    best = (st, e)
            st, e = best
            h = ready[e]
            cands = []
            win = ACT_WINDOW if e == "act" else WINDOW
            while h and h[0][0] <= st + win and len(cands) < 32:
                cands.append(heapq.heappop(h))
            cands.sort(key=lambda c: c[1])
            pick = cands[0]
            pen = 0.0
            if e == "act":
                ok = [c for c in cands if len(recs[c[2]]) < 8 or recs[c[2]][7] in (0, cur_tbl)]
                if ok:
                    pick = ok[0]
                else:
                    pen = 1.3
                    self.n_tbl_switch += 1
                t_ = recs[pick[2]][7] if len(recs[pick[2]]) >= 8 else 0
                if t_:
                    cur_tbl = t_
            for c in cands:
                if c is not pick:
                    heapq.heappush(h, c)
            i = pick[2]
            start = max(efree[e], pick[0]) + pen
            efree[e] = start + recs[i][3]
            if recs[i][4]:
                xs = max(start + recs[i][3], dma_free)
                dma_free = xs + (recs[i][5] - 2.0)
                finish[i] = dma_free + 2.0
            else:
                finish[i] = start + recs[i][5]
            self.sim_start[i] = start
            order[e].append(i)
            done += 1
            for sx in succs[i]:
                t = finish[i] + SEM_LAT
                if t > rdy_t[sx]:
                    rdy_t[sx] = t
                indeg[sx] -= 1
                if indeg[sx] == 0:
                    heapq.heappush(ready[recs[sx][0]], (rdy_t[sx], -prio[sx], sx))
        self.order = order
        self.sim_time = max(finish)

    def emit(self, st):
        nc = self.nc
        recs = self.recs
        self.finalize()
        if SIM_ONLY:
            return
        sems = {}
        for e in ENGS:
            sems[e] = st.enter_context(nc.semaphore("s_" + e))
        for q, nq in DMA_SEMS.items():
            for k in range(nq):
                sems[("dma", q, k)] = st.enter_context(nc.semaphore("s_dma_%s%d" % (q, k)))
        comp = [None] * len(recs)
        prev_same_sem = {}
        for e in ENGS:
            cc = 0
            dk = 0
            for i in self.order[e]:
                r = recs[i]
                if r[1] is None:
                    continue
                if r[4]:
                    nq = DMA_SEMS[e]
                    key = ("dma", e, dk % nq)
                    val = 16 * (dk // nq + 1)
                    comp[i] = (key, val)
                    if dk >= nq:
                        prev_same_sem[i] = (key, val - 16)
                    dk += 1
                else:
                    cc += 1
                    comp[i] = (e, cc)
        block = st.enter_context(nc.Block())

        def run(eng_name):
            def body(engine):
                known = {}
                for i in self.order[eng_name]:
                    r = recs[i]
                    need = {}
                    for d in r[2]:
                        k, v = comp[d]
                        if eng_name == "pe" and k == "pe":
                            continue
                        if v > need.get(k, 0):
                            need[k] = v
                    if i in prev_same_sem:
                        k, v = prev_same_sem[i]
                        if v > need.get(k, 0):
                            need[k] = v
                    for k, v in need.items():
                        if known.get(k, 0) >= v:
                            continue
                        known[k] = v
                        engine.wait_ge(sems[k], v)
                    if r[1] is None:
                        continue
                    ins = r[1](engine)
                    k, v = comp[i]
                    ins.then_inc(sems[k], 16 if r[4] else 1)
            return body

        block.tensor(run("pe"))
        block.scalar(run("act"))
        block.vector(run("dve"))
        block.gpsimd(run("pool"))
        block.sync(run("sp"))


def _tok(ap):
    return ap.name


def _n(ap):
    n = 1
    for d in ap.shape[1:]:
        n *= int(d)
    return n


def _bytes(ap):
    n = int(ap.shape[0]) * _n(ap)
    return n * (2 if ap.dtype == BF16 else 4)


class KB:
    def __init__(self, ntiles=4, nlayers=2, dbg=None, stage=99):
        self.ntiles = ntiles
        self.nlayers = nlayers
        self.dbg_names = dbg or []
        self.stage = stage
        self.nc = bass.Bass("TRN2", target_bir_lowering=False)
        self.st = contextlib.ExitStack()
        self.S = Sched(self.nc)
        self.dbg_out = {}
        self.out_tokens = []
        self.psum_names = set()

    def sb(self, name, shape, dt):
        if SIM_ONLY:
            return self.nc.dram_tensor(name, shape, dt, kind="Internal")
        return self.st.enter_context(self.nc.sbuf_tensor(name, shape, dt))

    def ps(self, name, shape, dt):
        self.psum_names.add(name)
        return self.st.enter_context(self.nc.psum_tensor(name, shape, dt))

    def dram_in(self, name, shape, dt=F32):
        return self.nc.dram_tensor(name, shape, dt, kind="ExternalInput").ap()

    def dram_out(self, name, shape, dt=F32):
        return self.nc.dram_tensor(name, shape, dt, kind="ExternalOutput").ap()

    def _rw(self, outs, ins, r, w):
        reads = list(r) if r is not None else [_tok(a) for a in ins if hasattr(a, "name")]
        writes = list(w) if w is not None else [_tok(a) for a in outs]
        ex = [t for t in reads if t in self.psum_names]
        if ex:
            reads = [t for t in reads if t not in self.psum_names]
            writes = writes + [t for t in ex if t not in writes]
        return reads, writes

    def act(self, out, in_, func, scale=1.0, bias=0.0, r=None, w=None):
        extra = [a for a in (scale, bias) if hasattr(a, "name")]
        reads, writes = self._rw([out], [in_] + extra, r, w)
        tbl = {AF.Exp: 1, AF.Ln: 1, AF.Sigmoid: 2, AF.Tanh: 2, AF.Silu: 3}.get(func, 0)
        self.S.op("act", lambda e: e.activation(out=out, in_=in_, func=func, scale=scale, bias=bias), reads, writes, dur=0.25 + _n(out) / 1400.0, tbl=tbl)

    def tt(self, eng, out, in0, in1, op, r=None, w=None):
        reads, writes = self._rw([out], [in0, in1], r, w)
        self.S.op(eng, lambda e: e.tensor_tensor(out=out, in0=in0, in1=in1, op=op), reads, writes, dur=self._vdur(eng, out))

    def ts(self, eng, out, in0, s1, s2, op0, op1=None, r=None, w=None):
        extra = [a for a in (s1, s2) if hasattr(a, "name")]
        reads, writes = self._rw([out], [in0] + extra, r, w)
        if op1 is None:
            self.S.op(eng, lambda e: e.tensor_scalar(out=out, in0=in0, scalar1=s1, scalar2=None, op0=op0), reads, writes, dur=self._vdur(eng, out))
        else:
            self.S.op(eng, lambda e: e.tensor_scalar(out=out, in0=in0, scalar1=s1, scalar2=s2, op0=op0, op1=op1), reads, writes, dur=self._vdur(eng, out))

    def stt(self, out, in0, scalar, in1, op0, op1, r=None, w=None):
        extra = [scalar] if hasattr(scalar, "name") else []
        reads, writes = self._rw([out], [in0, in1] + extra, r, w)
        self.S.op("dve", lambda e: e.scalar_tensor_tensor(out=out, in0=in0, scalar=scalar, in1=in1, op0=op0, op1=op1), reads, writes, dur=self._vdur("dve", out))

    def copy(self, eng, out, in_, r=None, w=None):
        reads, writes = self._rw([out], [in_], r, w)
        if eng == "act":
            self.S.op("act", lambda e: e.copy(out=out, in_=in_), reads, writes, dur=0.25 + _n(out) / 1400.0)
        else:
            self.S.op(eng, lambda e: e.tensor_copy(out=out, in_=in_), reads, writes,
                      dur=(0.3 + _n(out) / 300.0) if eng == "pool" else (0.1 + _n(out) / 1250.0))

    def recip(self, out, in_, r=None, w=None):
        reads, writes = self._rw([out], [in_], r, w)
        self.S.op("dve", lambda e: e.reciprocal(out=out, in_=in_), reads, writes, dur=0.1 + _n(out) / 155.0)

    def scan(self, out, d0, d1, r=None, w=None):
        reads, writes = self._rw([out], [d0, d1], r, w)
        self.S.op("dve", lambda e: e.tensor_tensor_scan(out=out, data0=d0, data1=d1, initial=0.0, op0=ALU.mult, op1=ALU.add), reads, writes, dur=0.12 + _n(out) / 480.0)

    def memset(self, eng, out, val, r=None, w=None):
        reads, writes = self._rw([out], [], r, w)
        self.S.op(eng, lambda e: e.memset(out, val), reads, writes, dur=self._vdur(eng, out))

    def mm(self, out, pairs, r=None, w=None):
        ins = []
        for a, b in pairs:
            ins += [a, b]
        reads, writes = self._rw([out], ins, r, w)
        n = len(pairs)

        def fn(e):
            last = None
            for i, (a, b) in enumerate(pairs):
                last = e.matmul(out, lhsT=a, rhs=b, start=(i == 0), stop=(i == n - 1))
            return last
        self.S.op("pe", fn, reads, writes, dur=sum(0.035 + max(_n(b_), 64) / 2000.0 for a_, b_ in pairs))

    def mmx(self, items):
        ins = []
        outs = []
        for o, a, b, st_, sp_ in items:
            ins += [a, b]
            outs.append(o)
        reads, writes = self._rw(outs[:1], ins, None, None)

        def fn(e):
            last = None
            for o, a, b, st_, sp_ in items:
                last = e.matmul(o, lhsT=a, rhs=b, start=st_, stop=sp_)
            return last
        self.S.op("pe", fn, reads, writes, dur=sum(0.035 + max(_n(b_), 64) / 2000.0 for o_, a_, b_, s1, s2 in items))

    def _vdur(self, eng, out):
        if eng == "pool":
            return 0.3 + _n(out) / 520.0
        return 0.15 + _n(out) / 900.0

    def tr(self, out, in_, ident, r=None, w=None):
        reads, writes = self._rw([out], [in_, ident], r, w)
        self.S.op("pe", lambda e: e.transpose(out=out, in_=in_, identity=ident), reads, writes, dur=0.1)

    def dma(self, q, out, in_, r=None, w=None):
        reads, writes = self._rw([out], [in_], r, w)
        self.S.dma(q, lambda e: e.dma_start(out=out, in_=in_), reads, writes, nbytes=max(_bytes(out), _bytes(in_)))

    def dump(self, name, ap, shape, dt=F32):
        if name not in self.dbg_names:
            return
        if dt != F32 or ap.dtype != F32:
            if not hasattr(self, "_dbgtmp"):
                self._dbgtmp = self.sb("dbgtmp", [128, 1024], F32)
            tmp = self._dbgtmp[0:shape[0], 0:shape[1]]
            self.copy("dve", tmp, ap)
            ap = tmp
        o = self.dram_out("dbg_" + name, list(shape))
        self.dma("sp", o, ap)
        self.out_tokens.append(_tok(o))
        self.dbg_out[name] = "dbg_" + name


def build(ntiles=4, nlayers=2, dbg=None, stage=99):
    K = KB(ntiles, nlayers, dbg, stage)
    nc = K.nc
    xT = K.dram_in("xT", [D, T])
    cT = K.dram_in("cT", [128, 8])
    ada = [K.dram_in("ada%d" % l, [D, 6 * D]) for l in range(L)]
    vecs_d = [K.dram_in("vecs%d" % l, [128, NV]) for l in range(L)]
    lay = [_wbig_layout(l) for l in range(L)]
    wbig = [K.dram_in("wbig%d" % l, [128, lay[l][1]]) for l in range(L)]
    outT = K.dram_out("outT", [D, T])

    sb, ps = K.sb, K.ps
    ident16 = sb("ident16", [128, 128], BF16)
    ones16 = sb("ones16", [128, 128], BF16)
    bd16 = sb("bd16", [128, 128], BF16)
    onesf = sb("onesf", [128, 128], F32)
    mask512 = sb("mask512", [128, 512], BF16)
    masksl = sb("masksl", [128, 128], BF16)
    rst = sb("rst", [128, 512], BF16)
    vecs = [sb("vecs_s%d" % l, [128, NV], F32) for l in range(L)]
    dv = [sb("dv%d" % l, [128, ND], F32) for l in range(L)]
    c32 = sb("c32", [128, 8], F32)
    epsc = sb("epsc", [128, 4], F32)
    c16 = sb("c16", [128, 8], BF16)
    S32 = [sb("S32_%d" % l, [128, 8, 64], F32) for l in range(L)]
    S16 = [sb("S16_%d" % l, [128, 8, 64], BF16) for l in range(L)]
    ctail = [sb("ctail%d" % l, [128, 8, 30], F32) for l in range(L)]
    carry = [sb("carry%d" % l, [128, 27], F32) for l in range(L)]
    xt = [sb("xt%d" % k, [128, TT], F32) for k in range(NB)]
    ht = [sb("ht%d" % k, [128, TT], BF16) for k in range(NB)]
    merged = [sb("mg%d" % k, [128, TT], BF16) for k in range(16)]
    vf = [sb("vf%d" % k, [128, TT], BF16) for k in range(NB)]
    NWP = 3
    WPN = 4480
    wpool = [sb("wp%d" % i, [128, WPN], BF16) for i in range(NWP)]
    lo16 = [sb("lo16_%d" % j, [128, TT], BF16) for j in range(2)]
    vlo16 = sb("vlo16", [32, TT], BF16)
    names32 = ["r32", "k32", "v32", "sg32", "a32", "sv32", "dd32", "cs32", "E1", "E3",
               "kk32", "nrm", "km", "b32", "y32", "yc", "sd"]
    W = {n: sb(n, [128, TT], F32) for n in names32}
    W["d2"] = W["dd32"]; W["E2"] = W["dd32"]; W["d4"] = W["sv32"]; W["E4"] = W["sv32"]; W["rn"] = W["nrm"]
    W["kkn"] = W["kk32"]; W["t1"] = W["km"]; W["rsd"] = W["sd"]
    W["yn"] = W["yc"]; W["yg"] = W["yc"]; W["o1"] = W["yc"]; W["o2"] = W["yc"]
    names16 = ["kk2", "Kh16", "Bh16", "v16", "rk16", "y16", "yc2", "sqa", "sqb", "sqc"]
    H = {n: sb(n, [128, TT], BF16) for n in names16}
    sq16 = [H[n] for n in ("kk2", "Kh16", "Bh16", "v16", "rk16", "sqa", "sqb", "sqc")]
    g32p = [sb("g32_%d" % i, [128, TT], F32) for i in range(NHO)]
    bon32p = [sb("bon32_%d" % i, [128, TT], F32) for i in range(NHO)]
    gA16p = [sb("gA16_%d" % i, [128, TT], BF16) for i in range(NHO)]
    kt16p = [sb("kt16_%d" % i, [128, TT], BF16) for i in range(NHO)]
    bt16p = [sb("bt16_%d" % i, [128, TT], BF16) for i in range(NHO)]
    WCp = [sb("WC_%d" % i, [128, NCH], F32) for i in range(NHO)]
    lo32 = [W["yc"], W["y32"]]
    vlo32 = bon32p[0]
    rs32 = W["nrm"]; rstd = W["sd"]; t32 = [W["yc"], W["y32"]]
    ARp = [sb("AR_%d" % i, [128, NCH, 2, C], BF16) for i in range(NHO)]
    KBtokp = [sb("KBtok_%d" % i, [128, 8, 128], BF16) for i in range(NHO)]
    Vtokp = [sb("Vtok_%d" % i, [128, 4, 128], BF16) for i in range(NHO)]
    Amatp = [[[sb("Amat%d_%d_%d" % (q, h, c), [128, 512], BF16) for c in range(NCH)] for h in range(2)] for q in range(NQ_AMAT)] * (2 // NQ_AMAT)
    Tfinp = [[[sb("Tfin%d_%d_%d" % (q, h, c), [128, 128], BF16) for c in range(NCH)] for h in range(2)] for q in range(NQ_AMAT)] * (2 // NQ_AMAT)
    Q0 = [sb("Q0_%d" % i, [128, 128], BF16) for i in range(4)]
    PQT = [[sb("PQT%d_%d" % (i, j), [128, 3, 128], BF16) for j in range(2)] for i in range(4)]
    XT16 = sb("XT16", [128, 128], BF16)
    UT16 = sb("UT16", [128, 128], BF16)
    sgb = sb("sgb16", [128, TT], BF16)
    sgb2 = [sb("sgbB%d" % i, [128, TT], BF16) for i in range(2)]
    gbuf = sb("gbuf", [128, 30 + TT], F32)
    cacc = [sb("cacc%d" % i, [128, TT], F32) for i in range(2)]
    ctmp = [sb("ctmp%d" % i, [128, TT], F32) for i in range(2)]
    z16 = [sb("z16_%d" % k, [128, TT], BF16) for k in range(NB)]

    def halves(t):
        v = t[:].bitcast(BF16)
        return [v[:, 0:TT], v[:, TT:2 * TT]]
    hidv = []
    for t_ in [W[n] for n in ("r32", "k32", "v32", "sg32", "a32", "sv32", "dd32", "kk32", "cs32", "E1", "E3", "km", "b32")] + [g32p[0], g32p[1], bon32p[0]]:
        hidv += halves(t_)
    mmb = [ps("mmb%d" % i, [128, 512], F32) for i in range(2)]
    dblb = [ps("dblb%d" % i, [128, 512], F32) for i in range(4)]
    seqb = ps("seqb", [128, 512], F32)
    yb = ps("yb", [128, 512], F32)
    mm_rr = [0]

    def bank():
        b = mmb[mm_rr[0] % 2]
        mm_rr[0] += 1
        return b

    def dbl_region(i, j):
        if j < 2:
            col = (i % 2) * 256 + j * 128
            return dblb[i // 2][:, col:col + 128]
        return dblb[2][:, i * 128:(i + 1) * 128]

    wp_rr = [0]
    PF = 2
    piece_list = []
    for pc in range(12):
        piece_list.append(("ada", 0, pc))
    ada1_left = list(range(12)) if K.nlayers > 1 else []
    for it_ in range(K.ntiles):
        for l_ in range(K.nlayers):
            if l_ == 1:
                while ada1_left:
                    piece_list.append(("ada", 1, ada1_left.pop(0)))
            if K.stage >= 4:
                keys = ["lo", "hp0"]
                for i in range(8):
                    keys.append("cb%d" % i)
                    if i + 1 < 8:
                        keys.append("hp%d" % (i + 1))
                keys += ["gb%d" % i for i in range(8)] + ["wo%d" % i for i in range(4)]
            else:
                keys = ["lo"] + ["hp%d" % i for i in range(8)]
            if K.stage >= 5:
                keys += ["w1%d" % i for i in range(8)] + ["w2%d" % i for i in range(8)]
            for n_, key in enumerate(keys):
                piece_list.append(("w", l_, key))
                if it_ == 0 and l_ == 0 and n_ % 3 == 2 and ada1_left:
                    piece_list.append(("ada", 1, ada1_left.pop(0)))
    issued = [0]
    piece_dst = {}

    def _issue(idx):
        kind, l_, key = piece_list[idx]
        buf = wpool[idx % NWP]
        if kind == "ada":
            src = ada[l_].rearrange("(k p) c -> p k c", p=128)[:, :, key * 512:(key + 1) * 512]
            dst = buf[:, 0:4096].rearrange("p (k c) -> p k c", c=512)
        elif key.startswith("hp"):
            off, k, c = lay[l_][0][key]
            src = wbig[l_][:, off:off + 4480].rearrange("p (a b) -> p a b", b=640)
            K.dma("pool", buf[:, 0:4480].rearrange("p (a b) -> p a b", b=640), src)
            piece_dst[idx] = (buf[:, 0:4096].rearrange("p (k c) -> p k c", c=512), buf[:, 4096:4480].rearrange("p (a c) -> p a c", c=128))
            return
        else:
            off, k, c = lay[l_][0][key]
            src = wbig[l_][:, off:off + k * c].rearrange("p (k c) -> p k c", c=c)
            dst = buf[:, 0:k * c].rearrange("p (k c) -> p k c", c=c)
        K.dma("pool", dst, src)
        piece_dst[idx] = dst

    def next_piece(expect):
        idx = wp_rr[0]
        wp_rr[0] += 1
        assert piece_list[idx] == expect, (piece_list[idx], expect)
        while issued[0] < min(len(piece_list), idx + PF + 1):
            _issue(issued[0])
            issued[0] += 1
        return piece_dst.pop(idx)

    def ada_piece():
        kind, l_, pc = piece_list[wp_rr[0]]
        dst = next_piece((kind, l_, pc))
        p = bank()
        for m in range(4):
            K.mm(p[:, m:m + 1], [(dst[:, k, m * 128:(m + 1) * 128], c16[:, k:k + 1]) for k in range(8)])
        K.tt("dve", dv[l_][:, DV_MOD + 4 * pc:DV_MOD + 4 * pc + 4], p[:, 0:4], vecs[l_][:, V_ADAB + 4 * pc:V_ADAB + 4 * pc + 4], ALU.add)
        if pc == 11:
            K.stt(dv[l_][:, DV_GSCM:DV_GSCM + 8], dv[l_][:, DV_MOD + M_SCM:DV_MOD + M_SCM + 8], 1.0, vecs[l_][:, V_GMIX:V_GMIX + 8], ALU.add, ALU.mult)
            K.stt(dv[l_][:, DV_GSCF:DV_GSCF + 8], dv[l_][:, DV_MOD + M_SCF:DV_MOD + M_SCF + 8], 1.0, vecs[l_][:, V_GFFN:V_GFFN + 8], ALU.add, ALU.mult)
            K.ts("dve", dv[l_][:, DV_OMU:DV_OMU + 27], vecs[l_][:, V_MU:V_MU + 27], -1.0, 1.0, ALU.mult, ALU.add)
            K.dump("dv%d" % l_, dv[l_][:], [128, ND])

    def load_piece(l, key):
        while wp_rr[0] < len(piece_list) and piece_list[wp_rr[0]][0] == "ada":
            ada_piece()
        return next_piece(("w", l, key))

    K.memset("pool", onesf[:], 1.0)
    K.memset("pool", ones16[:], 1.0)
    K.memset("pool", bd16[:], 0.0)
    K.memset("pool", bd16[0:64, 0:64], 1.0)
    K.memset("pool", bd16[64:128, 64:128], 1.0)

    def asel(out, in_, pattern, op, base, cm):
        K.S.op("pool", lambda e: e.affine_select(out=out, in_=in_, pattern=pattern, compare_op=op, fill=0.0, base=base, channel_multiplier=cm),
               [_tok(in_)], [_tok(out)], dur=0.3)
    asel(ident16[:], ones16[:], [[1, 128]], ALU.is_equal, 0, -1)
    for j in range(4):
        asel(mask512[:, j * 128:(j + 1) * 128], onesf[:, 0:128], [[1, 128]], ALU.is_gt if j % 2 == 0 else ALU.is_ge, 0, -1)
    asel(masksl[:], onesf[:, 0:128], [[-1, 128]], ALU.is_gt, 0, 1)
    K.memset("pool", rst[:], 1.0)
    for j in range(NCH):
        K.memset("pool", rst[:, j * C:j * C + 1], 0.0)
    for l in range(L):
        K.dma("sp", vecs[l][:], vecs_d[l])
        K.memset("pool", S32[l][:], 0.0)
        K.memset("pool", S16[l][:], 0.0)
        K.memset("pool", ctail[l][:], 0.0)
        K.memset("pool", carry[l][:], 0.0)
    K.dma("sp", c32[:], cT)
    K.memset("pool", epsc[:, 0:1], RMS_EPS)
    K.memset("pool", epsc[:, 1:2], LN_EPS)
    K.memset("pool", epsc[:, 2:3], GN_EPS)
    K.memset("pool", epsc[:, 3:4], 1e-24)
    K.act(c16[:], c32[:], AF.Silu)

    for pc in range(12):
        ada_piece()

    def rmsnorm_to_ht(l, gsc_col, sh_col):
        for k in range(NB):
            K.act(sq16[k][:], xt[k][:], AF.Square)
        p = bank()
        K.mm(p[:], [(ones16[:], sq16[k][:]) for k in range(NB)])
        K.act(rs32[:], p[:], AF.Ln, scale=1.0 / D, bias=epsc[:, 0:1])
        K.act(rstd[:], rs32[:], AF.Exp, scale=-0.5)
        for k in range(NB):
            t = t32[k % 2]
            K.stt(t[:], xt[k][:], dv[l][:, gsc_col + k:gsc_col + k + 1], rstd[:], ALU.mult, ALU.mult)
            K.act(ht[k][:], t[:], AF.Identity, bias=dv[l][:, sh_col + k:sh_col + k + 1])

    def proj(wp, j0, M=128):
        p = bank()
        K.mm(p[0:M, :], [(wp[:, k, j0:j0 + M], ht[k][:]) for k in range(NB)])
        return p

    def shift(l, p, mucol, dst, M=128):
        mu = vecs[l][0:M, V_MU + mucol:V_MU + mucol + 1]
        omu = dv[l][0:M, DV_OMU + mucol:DV_OMU + mucol + 1]
        cy = carry[l][0:M, mucol:mucol + 1]
        K.act(dst[0:M, :], p[0:M, :], AF.Identity, scale=omu)
        K.stt(dst[0:M, 1:TT], p[0:M, 0:TT - 1], mu, dst[0:M, 1:TT], ALU.mult, ALU.add)
        K.stt(dst[0:M, 0:1], cy, mu, dst[0:M, 0:1], ALU.mult, ALU.add)
        K.copy("dve", cy, p[0:M, TT - 1:TT])

    xTv = xT.rearrange("(k p) t -> k p t", p=128)
    oTv = outT.rearrange("(k p) t -> k p t", p=128)
    for it in range(K.ntiles):
        t0 = it * TT
        for k in range(NB):
            K.dma("sp", xt[k][:], xTv[k, :, t0:t0 + TT])
        for l in range(K.nlayers):
            V = vecs[l]
            DVl = dv[l]
            rmsnorm_to_ht(l, DV_GSCM, DV_MOD + M_SHM)
            if it == 0 and l == 0:
                for k in (0, 7):
                    K.dump("ht%d" % k, ht[k][:], [128, TT])
            wlo = load_piece(l, "lo")
            for j in range(2):
                p = proj(wlo, j * 128)
                shift(l, p, 24 + j, lo32[j])
            K.act(lo16[0][0:64, :], lo32[0][0:64, :], AF.Tanh)
            K.copy("act", lo16[0][64:128, :], lo32[0][64:128, :])
            K.act(lo16[1][:], lo32[1][:], AF.Sigmoid)
            if l >= 1:
                p = proj(wlo, 256, M=32)
                shift(l, p, 26, vlo32, M=32)
                K.copy("act", vlo16[:], vlo32[0:32, :])
            if K.stage < 1:
                continue
            def front(hp, par):
                AR = ARp[par]; KBtok = KBtokp[par]; Vtok = Vtokp[par]
                g32 = g32p[par]; bon32 = bon32p[par]; gA16 = gA16p[par]; kt16 = kt16p[par]; bt16 = bt16p[par]; WC = WCp[par]
                whp, wsl = load_piece(l, "hp%d" % hp)
                p = proj(whp, 0);   shift(l, p, hp, W["r32"])
                yield
                p = proj(whp, 128); shift(l, p, 8 + hp, W["k32"])
                yield
                p = proj(whp, 256); shift(l, p, 16 + hp, W["v32"])
                yield
                p = proj(whp, 384); K.act(gA16[:], p[:], AF.Sigmoid)
                p = bank(); K.mm(p[:], [(wsl[0:64, 0, :], lo16[0][0:64, :])])
                K.act(W["sg32"][:], p[:], AF.Sigmoid, bias=V[:, V_W0 + hp:V_W0 + hp + 1])
                yield
                p = bank(); K.mm(p[:], [(wsl[64:128, 0, :], lo16[0][64:128, :])])
                K.act(W["a32"][:], p[:], AF.Sigmoid, bias=V[:, V_A0 + hp:V_A0 + hp + 1])
                p = bank(); K.mm(p[:], [(wsl[:, 1, :], lo16[1][:])])
                K.tt("dve", g32[:], p[:], gA16[:], ALU.mult)
                yield
                if l >= 1:
                    p = bank(); K.mm(p[:], [(wsl[0:32, 2, :], vlo16[:])])
                    K.act(W["sv32"][:], p[:], AF.Sigmoid, bias=V[:, V_V0 + hp:V_V0 + hp + 1])
                    K.tt("pool", W["dd32"][:], vf[hp][:], W["v32"][:], ALU.subtract)
                    K.tt("pool", W["dd32"][:], W["dd32"][:], W["sv32"][:], ALU.mult)
                    K.tt("pool", W["v32"][:], W["v32"][:], W["dd32"][:], ALU.add)
                else:
                    K.copy("pool", vf[hp][:], W["v32"][:])
                yield
                K.scan(W["cs32"][:], rst[:], W["sg32"][:])
                K.act(W["E1"][:], W["cs32"][:], AF.Exp, scale=CEXP)
                K.act(W["E3"][:], W["cs32"][:], AF.Exp, scale=-CEXP)
                K.tt("pool", W["d2"][:], W["cs32"][:], W["sg32"][:], ALU.subtract)
                K.act(W["E2"][:], W["d2"][:], AF.Exp, scale=CEXP)
                yield
                cs3 = W["cs32"][:].rearrange("p (c t) -> p c t", t=C)
                K.tt("pool", W["d4"][:].rearrange("p (c t) -> p c t", t=C), cs3, cs3[:, :, C - 1:C].to_broadcast([128, NCH, C]), ALU.subtract)
                K.act(W["E4"][:], W["d4"][:], AF.Exp, scale=-CEXP)
                K.copy("dve", WC[:], W["E1"][:].rearrange("p (c t) -> p c t", t=C)[:, :, C - 1])
                yield
                K.act(W["kk32"][:], W["k32"][:], AF.Identity, scale=V[:, V_KK + hp:V_KK + hp + 1])
                K.act(H["kk2"][:], W["kk32"][:], AF.Square)
                p = bank(); K.mm(p[:], [(bd16[:], H["kk2"][:])])
                K.act(W["nrm"][:], p[:], AF.Ln, bias=epsc[:, 3:4])
                K.act(W["rn"][:], W["nrm"][:], AF.Exp, scale=-0.5)
                yield
                K.tt("pool", W["kkn"][:], W["kk32"][:], W["rn"][:], ALU.mult)
                K.ts("dve", W["t1"][:], W["a32"][:], -1.0, V[:, V_KA + hp:V_KA + hp + 1], ALU.add, ALU.mult)
                K.stt(W["km"][:], W["t1"][:], 1.0, W["k32"][:], ALU.add, ALU.mult)
                K.tt("pool", W["b32"][:], W["kkn"][:], W["a32"][:], ALU.mult)
                yield
                K.tt("dve", AR[:, :, 1, :], W["r32"][:].rearrange("p (c t) -> p c t", t=C), W["E1"][:].rearrange("p (c t) -> p c t", t=C), ALU.mult)
                K.stt(AR[:, :, 0, :], W["kkn"][:].rearrange("p (c t) -> p c t", t=C), -1.0, W["E2"][:].rearrange("p (c t) -> p c t", t=C), ALU.mult, ALU.mult)
                K.tt("pool", kt16[:], W["km"][:], W["E3"][:], ALU.mult)
                K.tt("pool", bt16[:], W["b32"][:], W["E3"][:], ALU.mult)
                yield
                K.tt("pool", H["Kh16"][:], W["km"][:], W["E4"][:], ALU.mult)
                K.tt("dve", H["Bh16"][:], W["b32"][:], W["E4"][:], ALU.mult)
                K.copy("pool", H["v16"][:], W["v32"][:])
                yield
                K.stt(H["rk16"][:], W["r32"][:], V[:, V_RK + hp:V_RK + hp + 1], W["km"][:], ALU.mult, ALU.mult)
                p = bank(); K.mm(p[:], [(bd16[:], H["rk16"][:])])
                K.tt("dve", bon32[:], p[:], W["v32"][:], ALU.mult)
                yield
                trb = bank()
                trv = trb[:].bitcast(BF16).rearrange("p (a t) -> p a t", t=128)
                for c in range(NCH):
                    K.tr(trv[:, c, :], H["Kh16"][:, c * C:(c + 1) * C], ident16[:])
                    K.tr(trv[:, 4 + c, :], H["Bh16"][:, c * C:(c + 1) * C], ident16[:])
                K.copy("act", KBtok[:], trv)
                yield
                trb = bank()
                trv = trb[:].bitcast(BF16).rearrange("p (a t) -> p a t", t=128)
                for c in range(NCH):
                    K.tr(trv[:, c, :], H["v16"][:, c * C:(c + 1) * C], ident16[:])
                K.copy("dve", Vtok[:], trv[:, 0:4, :])
                yield

            def back(hp, par):
                Amat = Amatp[hp % 2]; Tfin = Tfinp[hp % 2]
                AR = ARp[par]; KBtok = KBtokp[par]; Vtok = Vtokp[par]
                g32 = g32p[par]; bon32 = bon32p[par]; gA16 = gA16p[par]; kt16 = kt16p[par]; bt16 = bt16p[par]; WC = WCp[par]
                for h in range(2):
                    hs = slice(h * 64, h * 64 + 64)
                    for c in range(NCH):
                        ck = slice(c * C, (c + 1) * C)
                        bk = dblb[c]
                        ar = AR[hs, c, :, :].rearrange("p a t -> p (a t)")
                        K.mm(bk[:, 0:256], [(bt16[hs, ck], ar)])
                        K.mm(bk[:, 256:512], [(kt16[hs, ck], ar)])
                        K.tt("dve", Amat[h][c][:], bk[:], mask512[:], ALU.mult)
                        K.mm(bk[:, 0:128], [(AR[hs, c, 0, :], bt16[hs, ck])])
                        K.tt("dve", Q0[c][:], bk[:, 0:128], masksl[:], ALU.mult)
                        K.tt("pool", PQT[c][1][:, 1, :], Amat[h][c][:, 0:128], ident16[:], ALU.add)
                        yield
                    for c in range(NCH):
                        bk = dblb[c]
                        P0 = Amat[h][c][:, 0:128]
                        K.mm(bk[:, 0:128], [(Q0[c][:], P0)])
                        K.mm(bk[:, 256:384], [(P0, Q0[c][:])])
                        K.copy("act" if c % 2 == 0 else "dve", PQT[c][1][:, 0:3:2, :], bk[:, 0:384].rearrange("p (a t) -> p a t", t=128)[:, 0:3:2, :])
                    yield
                    for lv in range(1, 7):
                        cur, nxt = lv % 2, (lv + 1) % 2
                        for c in range(NCH):
                            bk = dblb[c]
                            Pk = PQT[c][cur][:, 0, :]
                            Tk = PQT[c][cur][:, 1, :]
                            Qk = PQT[c][cur][:, 2, :]
                            PTk = PQT[c][cur][:, 0:2, :].rearrange("p a t -> p (a t)")
                            eng = "dve" if (c + lv) % EVAC_MOD == 0 else "act"
                            if lv < 6:
                                K.mmx([(bk[:, 256:384], Pk, Qk, True, True),
                                       (bk[:, 0:256], Qk, PTk, True, False),
                                       (bk[:, 128:256], ident16[:], Tk, False, True)])
                                K.copy(eng, PQT[c][nxt][:], bk[:, 0:384].rearrange("p (a t) -> p a t", t=128))
                            else:
                                K.mm(bk[:, 128:256], [(Qk, Tk), (ident16[:], Tk)])
                                K.copy(eng, Tfin[h][c][:], bk[:, 128:256])
                        yield
                for c in range(NCH):
                    for h in range(2):
                        hs = slice(h * 64, h * 64 + 64)
                        K.mm(seqb[:, h * 64:(h + 1) * 64], [(AR[hs, c, 0, :], S16[l][hs, hp, :]),
                                                             (Amat[h][c][:, 256:384], Vtok[:, c, hs])])
                    K.copy("act", XT16[:], seqb[:, 0:128])
                    yield
                    for h in range(2):
                        hs = slice(h * 64, h * 64 + 64)
                        K.mm(seqb[:, 128 + h * 64:128 + (h + 1) * 64], [(Tfin[h][c][:], XT16[:, hs])])
                    K.copy("dve", UT16[:], seqb[:, 128:256])
                    yield
                    for h in range(2):
                        hs = slice(h * 64, h * 64 + 64)
                        K.mm(yb[hs, c * C:(c + 1) * C], [(S16[l][hs, hp, :], AR[hs, c, 1, :]),
                                                          (UT16[:, hs], Amat[h][c][:, 128:256]),
                                                          (Vtok[:, c, hs], Amat[h][c][:, 384:512])])
                        K.mm(seqb[hs, 256:320], [(KBtok[:, 4 + c, hs], UT16[:, hs]),
                                                  (KBtok[:, c, hs], Vtok[:, c, hs])])
                    K.stt(S32[l][:, hp, :], S32[l][:, hp, :], WC[:, c:c + 1], seqb[:, 256:320], ALU.mult, ALU.add)
                    K.copy("act", S16[l][:, hp, :], S32[l][:, hp, :])
                    yield
                K.copy("act", W["y32"][:], yb[:])
                K.copy("dve", H["y16"][:], yb[:])
                if it == 0 and l == 0 and hp == 0:
                    K.dump("y32", W["y32"][:], [128, TT])
                yield
                p = bank(); K.mm(p[:], [(bd16[:], H["y16"][:])])
                K.stt(W["yc"][:], p[:], -1.0 / 64, W["y32"][:], ALU.mult, ALU.add)
                yield
                K.act(H["yc2"][:], W["yc"][:], AF.Square)
                p = bank(); K.mm(p[:], [(bd16[:], H["yc2"][:])])
                K.act(W["sd"][:], p[:], AF.Ln, scale=1.0 / 64, bias=epsc[:, 2:3])
                yield
                K.act(W["rsd"][:], W["sd"][:], AF.Exp, scale=-0.5)
                K.stt(W["yn"][:], W["yc"][:], V[:, V_GNG + hp:V_GNG + hp + 1], W["rsd"][:], ALU.mult, ALU.mult)
                K.stt(W["o1"][:], W["yn"][:], V[:, V_GNB + hp:V_GNB + hp + 1], bon32[:], ALU.add, ALU.add)
                yield
                K.tt("pool", merged[hp][:], W["o1"][:], g32[:], ALU.mult)
                if it == 0 and l == 0 and hp in (0, 7):
                    K.dump("mg%d" % hp, merged[hp][:], [128, TT], BF16)
                yield

            def drain(g):
                for _ in g:
                    pass

            def interleave(g1, g2):
                a1 = a2 = True
                while a1 or a2:
                    if a1:
                        try:
                            next(g1)
                        except StopIteration:
                            a1 = False
                    if a2:
                        try:
                            next(g2)
                        except StopIteration:
                            a2 = False

            def conv(cb):
                wcb = load_piece(l, "cb%d" % cb)
                pb = proj(wcb, 128)
                K.act(sgb[:], pb[:], AF.Sigmoid)
                pa = proj(wcb, 0)
                K.copy("pool", gbuf[:, 0:30], ctail[l][:, cb, :])
                K.tt("dve", gbuf[:, 30:30 + TT], pa[:], sgb[:], ALU.mult)
                K.copy("pool", ctail[l][:, cb, :], gbuf[:, TT:TT + 30])
                cw = lambda k: V[:, V_CW + cb * 31 + k:V_CW + cb * 31 + k + 1]
                NT_DVE = NT_DVE_G
                K.ts("dve", cacc[0][:], gbuf[:, 0:TT], cw(0), V[:, V_CB + cb:V_CB + cb + 1], ALU.mult, ALU.add)
                for k in range(1, NT_DVE):
                    K.stt(cacc[0][:], gbuf[:, k:k + TT], cw(k), cacc[0][:], ALU.mult, ALU.add)
                K.act(cacc[1][:], gbuf[:, NT_DVE:NT_DVE + TT], AF.Identity, scale=cw(NT_DVE))
                for k in range(NT_DVE + 1, 31):
                    tp_ = ctmp[k % 2]
                    K.act(tp_[:], gbuf[:, k:k + TT], AF.Identity, scale=cw(k))
                    K.tt("pool", cacc[1][:], cacc[1][:], tp_[:], ALU.add)
                K.tt("pool", z16[cb][:], cacc[0][:], cacc[1][:], ALU.add)

            if K.stage < 3:
                for hp in range(8):
                    drain(front(hp, hp % NHO))
            else:
                drain(front(0, 0))
                for hp in range(8):
                    if K.stage >= 4:
                        conv(hp)
                    if hp + 1 < 8:
                        interleave(back(hp, hp % NHO), front(hp + 1, (hp + 1) % NHO))
                    else:
                        drain(back(hp, hp % NHO))
            if K.stage < 4:
                continue
            if it == 0 and l == 0:
                K.dump("z0", z16[0][:], [128, TT], BF16)
            p = bank(); K.mm(p[:], [(ones16[:], z16[cb][:]) for cb in range(8)])
            K.act(W["nrm"][:], p[:], AF.Identity, scale=-1.0 / D)
            for cb in range(8):
                t = t32[cb % 2]
                K.tt("dve", t[:], z16[cb][:], W["nrm"][:], ALU.add)
                K.act(sq16[cb][:], t[:], AF.Square)
            p = bank(); K.mm(p[:], [(ones16[:], sq16[cb][:]) for cb in range(8)])
            K.act(W["sd"][:], p[:], AF.Ln, scale=1.0 / D, bias=epsc[:, 1:2])
            K.act(W["sd"][:], W["sd"][:], AF.Exp, scale=-0.5)
            for cb in range(8):
                t = t32[cb % 2]
                wgb = load_piece(l, "gb%d" % cb)
                pg = proj(wgb, 0)
                sg_ = sgb2[cb % 2]
                K.act(sg_[:], pg[:], AF.Sigmoid)
                K.tt("dve", t[:], z16[cb][:], W["nrm"][:], ALU.add)
                K.tt("pool", t[:], t[:], W["sd"][:], ALU.mult)
                K.act(t[:], t[:], AF.Silu, scale=V[:, V_LNG + cb:V_LNG + cb + 1], bias=V[:, V_LNB + cb:V_LNB + cb + 1])
                K.tt("pool", merged[8 + cb][:], t[:], sg_[:], ALU.mult)
            if it == 0 and l == 0:
                K.dump("mg8", merged[8][:], [128, TT], BF16)
            for g in range(4):
                wo = load_piece(l, "wo%d" % g)
                for o2 in range(2):
                    ob = g * 2 + o2
                    p = bank()
                    K.mm(p[:], [(wo[:, o2 * 16 + kc, :], merged[kc][:]) for kc in range(16)])
                    K.stt(xt[ob][:], p[:], DVl[:, DV_MOD + M_GTM + ob:DV_MOD + M_GTM + ob + 1], xt[ob][:], ALU.mult, ALU.add)
            if it == 0 and l == 0:
                K.dump("xmix0", xt[0][:], [128, TT])
            if K.stage < 5:
                continue
            rmsnorm_to_ht(l, DV_GSCF, DV_MOD + M_SHF)
            for pc in range(8):
                w1 = load_piece(l, "w1%d" % pc)
                for j in range(4):
                    hb = pc * 4 + j
                    p = proj(w1, j * 128)
                    t = t32[hb % 2]
                    K.act(t[:], p[:], AF.Relu)
                    K.tt("pool", hidv[hb], t[:], t[:], ALU.mult)
            for ob in range(8):
                w2 = load_piece(l, "w2%d" % ob)
                p = bank()
                K.mm(p[:], [(w2[:, hb, :], hidv[hb]) for hb in range(32)])
                K.stt(xt[ob][:], p[:], DVl[:, DV_MOD + M_GTF + ob:DV_MOD + M_GTF + ob + 1], xt[ob][:], ALU.mult, ALU.add)
            if it == 0 and l == 0:
                K.dump("xffn0", xt[0][:], [128, TT])
        for k in range(NB):
            K.act(sq16[k][:], xt[k][:], AF.Square)
        p = bank()
        K.mm(p[:], [(ones16[:], sq16[k][:]) for k in range(NB)])
        K.act(rs32[:], p[:], AF.Ln, scale=1.0 / D, bias=epsc[:, 0:1])
        K.act(rstd[:], rs32[:], AF.Exp, scale=-0.5)
        for k in range(NB):
            t = t32[k % 2]
            K.stt(t[:], xt[k][:], vecs[0][:, V_FG + k:V_FG + k + 1], rstd[:], ALU.mult, ALU.mult)
            K.dma("sp", oTv[k, :, t0:t0 + TT], t[:], w=["outT%d_%d" % (it, k)])
            K.out_tokens.append("outT%d_%d" % (it, k))
    K.S.wait_all("sp", K.out_tokens)
    K.S.emit(K.st)
    K.st.close()
    return K


def dblb_view(dblb, c):
    col = (c % 2) * 256
    return dblb[c // 2][:, col:col + 256].rearrange("p (a t) -> p a t", t=128)


def _fm(v):
    v = np.asarray(v, np.float32)
    return np.ascontiguousarray(v.reshape(-1, 128).T)


def prep_shared(inp):
    shared = {}
    for l in range(L):
        vec = np.zeros((128, NV), np.float32)
        vec[:, V_GMIX:V_GMIX + 8] = _fm(inp["norm_mix_gain"][l])
        vec[:, V_GFFN:V_GFFN + 8] = _fm(inp["norm_ffn_gain"][l])
        vec[:, V_MU:V_MU + 26] = _fm(inp["mu_shift"][l])
        if l >= 1:
            vec[0:32, V_MUV] = inp["mu_vres"][l - 1]
            vec[:, V_V0:V_V0 + 8] = _fm(inp["v0"][l - 1])
        vec[:, V_W0:V_W0 + 8] = _fm(inp["w0"][l])
        vec[:, V_A0:V_A0 + 8] = _fm(inp["a0"][l])
        vec[:, V_KK:V_KK + 8] = _fm(inp["k_k"][l])
        vec[:, V_KA:V_KA + 8] = _fm(inp["k_a"][l])
        vec[:, V_RK:V_RK + 8] = _fm(inp["r_k"][l].reshape(-1))
        vec[:, V_GNG:V_GNG + 8] = _fm(inp["gn_gain"][l])
        vec[:, V_GNB:V_GNB + 8] = _fm(inp["gn_bias"][l])
        vec[:, V_CB:V_CB + 8] = _fm(inp["conv_b"][l])
        vec[:, V_LNG:V_LNG + 8] = _fm(inp["conv_ln_gain"][l])
        vec[:, V_LNB:V_LNB + 8] = _fm(inp["conv_ln_bias"][l])
        vec[:, V_FG:V_FG + 8] = _fm(inp["final_gain"])
        vec[:, V_ADAB:V_ADAB + 48] = _fm(inp["ada_b"][l])
        cw = np.asarray(inp["conv_w"][l], np.float32)
        vec[:, V_CW:V_CW + 248] = cw.reshape(31, 8, 128).transpose(2, 1, 0).reshape(128, 248)
        shared["vecs%d" % l] = vec
        wsm = np.zeros((128, 3, D), np.float32)
        wsm[0:64, 0] = inp["w_decay_up"][l]
        wsm[64:128, 0] = inp["w_aaa_up"][l]
        wsm[:, 1] = inp["w_gate_up"][l]
        if l >= 1:
            wsm[0:32, 2] = inp["w_vres_up"][l - 1]
        shared["ada%d" % l] = np.ascontiguousarray(inp["ada_w"][l], dtype=np.float32)
        lay, tot = _wbig_layout(l)
        wb = np.empty((128, tot), np.float32)
        win = np.asarray(inp["w_in"][l], np.float32)
        if l >= 1:
            win = np.concatenate([win, np.asarray(inp["w_in_vres"][l - 1], np.float32)], axis=1)

        def put(key, cols_matrix):
            off, k, c = lay[key]
            wb[:, off:off + k * c] = cols_matrix.reshape(k, 128, c).transpose(1, 0, 2).reshape(128, k * c)
        lo_idx = list(range(3072, 3328)) + (list(range(N_COLS, N_COLS + 32)) if l >= 1 else [])
        put("lo", win[:, lo_idx])
        for hp in range(8):
            idx = np.concatenate([np.arange(hp * 128, hp * 128 + 128), 1024 + np.arange(hp * 128, hp * 128 + 128),
                                  2048 + np.arange(hp * 128, hp * 128 + 128), 5376 + np.arange(hp * 128, hp * 128 + 128)])
            put("hp%d" % hp, win[:, idx])
            off_, k_, c_ = lay["hp%d" % hp]
            wb[:, off_ + k_ * c_: off_ + k_ * c_ + 384] = wsm[:, :, hp * 128:(hp + 1) * 128].reshape(128, 384)
        for cb in range(8):
            idx = np.concatenate([3328 + np.arange(cb * 128, cb * 128 + 128), 4352 + np.arange(cb * 128, cb * 128 + 128)])
            put("cb%d" % cb, win[:, idx])
            put("gb%d" % cb, win[:, 6400 + np.arange(cb * 128, cb * 128 + 128)])
        wo = np.asarray(inp["w_out"][l], np.float32)
        for g in range(4):
            off, k, c = lay["wo%d" % g]
            blk = wo[:, g * 256:(g + 1) * 256].reshape(16, 128, 2, 128)
            wb[:, off:off + k * c] = blk.transpose(1, 2, 0, 3).reshape(128, 32 * 128)
        w1 = np.asarray(inp["w_ff_in"][l], np.float32)
        for pc in range(8):
            put("w1%d" % pc, w1[:, pc * 512:(pc + 1) * 512])
        w2 = np.asarray(inp["w_ff_out"][l], np.float32)
        for ob in range(8):
            put("w2%d" % ob, w2[:, ob * 128:(ob + 1) * 128])
        shared["wbig%d" % l] = wb
    return shared


def prep_core(inp, b):
    return {"xT": np.ascontiguousarray(np.asarray(inp["x"][b], np.float32).T),
            "cT": _fm(inp["c"][b])}


_CACHE = {}


def kernel(**inputs):
    inp = {k: np.asarray(v) for k, v in inputs.items()}
    if "K" not in _CACHE:
        _CACHE["K"] = build()
    K = _CACHE["K"]
    shared = prep_shared(inp)
    in_maps = []
    for b in range(NCORES):
        m = dict(shared)
        m.update(prep_core(inp, b))
        in_maps.append(m)
    res = run_bass_kernel_spmd(K.nc, in_maps, core_ids=list(range(NCORES)))
    out = np.stack([np.ascontiguousarray(res.results[b]["outT"].T) for b in range(NCORES)], axis=0)
    return out.astype(np.float32)
```

```python
import contextlib
import math
import numpy as np
import concourse.bass as bass
import concourse.mybir as mybir
from concourse.bass_utils import run_bass_kernel_spmd

F32 = mybir.dt.float32
BF16 = mybir.dt.bfloat16
AF = mybir.ActivationFunctionType
ALU = mybir.AluOpType

D = 1024
T = 2048
NB = 8
TT = 512
C = 128
NCH = TT // C
L = 2
NCORES = 8
N_SHIFT = 3328
N_COLS = 7424
RMS_EPS = 1e-6
LN_EPS = 1e-5
GN_EPS = 64e-5
CEXP = -math.exp(-0.5)

V_GMIX, V_GFFN, V_MU, V_MUV, V_W0, V_A0, V_KK, V_KA, V_RK, V_GNG, V_GNB, V_V0, V_CB, V_LNG, V_LNB, V_FG, V_ADAB, V_CW = (
    0, 8, 16, 42, 43, 51, 59, 67, 75, 83, 91, 99, 107, 115, 123, 131, 139, 187)
NV = 187 + 8 * 31
DV_MOD, DV_GSCM, DV_GSCF, DV_OMU = 0, 48, 56, 64
ND = 64 + 27
M_SHM, M_SCM, M_GTM, M_SHF, M_SCF, M_GTF = 0, 8, 16, 24, 32, 40

def _wbig_layout(l):
    off = 0
    lay = {}
    nlo = 256 + (32 if l >= 1 else 0)
    lay["lo"] = (off, 8, nlo); off += 8 * nlo
    for hp in range(8):
        lay["hp%d" % hp] = (off, 8, 512); off += 8 * 512 + 4 * 128
    for cb in range(8):
        lay["cb%d" % cb] = (off, 8, 256); off += 8 * 256
    for cb in range(8):
        lay["gb%d" % cb] = (off, 8, 128); off += 8 * 128
    for g in range(4):
        lay["wo%d" % g] = (off, 32, 128); off += 32 * 128
    for pc in range(8):
        lay["w1%d" % pc] = (off, 8, 512); off += 8 * 512
    for ob in range(8):
        lay["w2%d" % ob] = (off, 32, 128); off += 32 * 128
    return lay, off


ENGS = ("pe", "act", "dve", "pool", "sp")
DMA_SEMS = {"sp": 8, "pool": 16, "act": 4}
SEM_LAT = 0.4
NT_DVE_G = 22
EVAC_MOD = 4
PRIO_MODE = 0
NQ_AMAT = 1
NHO = 2
SIM_ONLY = False
ACT_WINDOW = 0.3
WINDOW_G = 0.3
LIST_SCHED = True


class Sched:
    def __init__(self, nc):
        self.nc = nc
        self.recs = []
        self.last_w = {}
        self.readers = {}

    def _deps(self, reads, writes):
        deps = set()
        for t in reads:
            w = self.last_w.get(t)
            if w is not None:
                deps.add(w)
        for t in writes:
            w = self.last_w.get(t)
            if w is not None:
                deps.add(w)
            deps.update(self.readers.get(t, ()))
        return deps

    def _commit(self, reads, writes, me):
        for t in reads:
            self.readers.setdefault(t, []).append(me)
        for t in writes:
            self.last_w[t] = me
            self.readers[t] = []

    def op(self, eng, fn, reads=(), writes=(), dur=0.5, tbl=0):
        deps = self._deps(reads, writes)
        me = len(self.recs)
        self.recs.append([eng, fn, deps, dur, False, dur, (list(writes) or ["?"])[0], tbl])
        self._commit(reads, writes, me)

    def dma(self, queue, fn, reads=(), writes=(), nbytes=0):
        deps = self._deps(reads, writes)
        me = len(self.recs)
        self.recs.append([queue, fn, deps, 0.6 if queue == "pool" else 0.15, True, 2.0 + nbytes / 340e3, (list(writes) or ["?"])[0]])
        self._commit(reads, writes, me)

    def wait_all(self, eng, tokens):
        deps = set(self.last_w[t] for t in tokens if t in self.last_w)
        me = len(self.recs)
        self.recs.append([eng, None, deps, 0.01, False, 0.01, "final"])
        self.final = me

    def finalize(self):
        recs = self.recs
        n = len(recs)
        order = {e: [] for e in ENGS}
        if not LIST_SCHED:
            for i, r in enumerate(recs):
                order[r[0]].append(i)
            self.order = order
            return
        succs = [[] for _ in range(n)]
        indeg = [0] * n
        for i, r in enumerate(recs):
            for d in r[2]:
                succs[d].append(i)
            indeg[i] = len(r[2])
        prio = [0.0] * n
        for i in range(n - 1, -1, -1):
            m = 0.0
            for sx in succs[i]:
                if prio[sx] > m:
                    m = prio[sx]
            prio[i] = m + recs[i][5] + SEM_LAT
        import heapq
        ready = {e: [] for e in ENGS}
        finish = [0.0] * n
        self.sim_start = [0.0] * n
        self.sim_finish = finish
        rdy_t = [0.0] * n
        for i in range(n):
            if indeg[i] == 0:
                heapq.heappush(ready[recs[i][0]], (0.0, -prio[i], i))
        efree = {e: 0.0 for e in ENGS}
        done = 0
        WINDOW = WINDOW_G
        cur_tbl = 0
        dma_free = 0.0
        self.n_tbl_switch = 0
        while done < n:
            best = None
            for e in ENGS:
                h = ready[e]
                if not h:
                    continue
                st = max(efree[e], h[0][0])
                if best is None or st < best[0]:
                    best = (st, e)
            st, e = best
            h = ready[e]
            cands = []
            win = ACT_WINDOW if e == "act" else WINDOW
            while h and h[0][0] <= st + win and len(cands) < 32:
                cands.append(heapq.heappop(h))
            cands.sort(key=lambda c: c[1])
            pick = cands[0]
            pen = 0.0
            if e == "act":
                ok = [c for c in cands if len(recs[c[2]]) < 8 or recs[c[2]][7] in (0, cur_tbl)]
                if ok:
                    pick = ok[0]
                else:
                    pen = 1.3
                    self.n_tbl_switch += 1
                t_ = recs[pick[2]][7] if len(recs[pick[2]]) >= 8 else 0
                if t_:
                    cur_tbl = t_
            for c in cands:
                if c is not pick:
                    heapq.heappush(h, c)
            i = pick[2]
            start = max(efree[e], pick[0]) + pen
            efree[e] = start + recs[i][3]
            if recs[i][4]:
                xs = max(start + recs[i][3], dma_free)
                dma_free = xs + (recs[i][5] - 2.0)
                finish[i] = dma_free + 2.0
            else:
                finish[i] = start + recs[i][5]
            self.sim_start[i] = start
            order[e].append(i)
            done += 1
            for sx in succs[i]:
                t = finish[i] + SEM_LAT
                if t > rdy_t[sx]:
                    rdy_t[sx] = t
                indeg[sx] -= 1
                if indeg[sx] == 0:
                    heapq.heappush(ready[recs[sx][0]], (rdy_t[sx], -prio[sx], sx))
        self.order = order
        self.sim_time = max(finish)

    def emit(self, st):
        nc = self.nc
        recs = self.recs
        self.finalize()
        if SIM_ONLY:
            return
        sems = {}
        for e in ENGS:
            sems[e] = st.enter_context(nc.semaphore("s_" + e))
        for q, nq in DMA_SEMS.items():
            for k in range(nq):
                sems[("dma", q, k)] = st.enter_context(nc.semaphore("s_dma_%s%d" % (q, k)))
        comp = [None] * len(recs)
        prev_same_sem = {}
        for e in ENGS:
            cc = 0
            dk = 0
            for i in self.order[e]:
                r = recs[i]
                if r[1] is None:
                    continue
                if r[4]:
                    nq = DMA_SEMS[e]
                    key = ("dma", e, dk % nq)
                    val = 16 * (dk // nq + 1)
                    comp[i] = (key, val)
                    if dk >= nq:
                        prev_same_sem[i] = (key, val - 16)
                    dk += 1
                else:
                    cc += 1
                    comp[i] = (e, cc)
        block = st.enter_context(nc.Block())

        def run(eng_name):
            def body(engine):
                known = {}
                for i in self.order[eng_name]:
                    r = recs[i]
                    need = {}
                    for d in r[2]:
                        k, v = comp[d]
                        if eng_name == "pe" and k == "pe":
                            continue
                        if v > need.get(k, 0):
                            need[k] = v
                    if i in prev_same_sem:
                        k, v = prev_same_sem[i]
                        if v > need.get(k, 0):
                            need[k] = v
                    for k, v in need.items():
                        if known.get(k, 0) >= v:
                            continue
                        known[k] = v
                        engine.wait_ge(sems[k], v)
                    if r[1] is None:
                        continue
                    ins = r[1](engine)
                    k, v = comp[i]
                    ins.then_inc(sems[k], 16 if r[4] else 1)
            return body

        block.tensor(run("pe"))
        block.scalar(run("act"))
        block.vector(run("dve"))
        block.gpsimd(run("pool"))
        block.sync(run("sp"))


def _tok(ap):
    return ap.name


def _n(ap):
    n = 1
    for d in ap.shape[1:]:
        n *= int(d)
    return n


def _bytes(ap):
    n = int(ap.shape[0]) * _n(ap)
    return n * (2 if ap.dtype == BF16 else 4)


class KB:
    def __init__(self, ntiles=4, nlayers=2, dbg=None, stage=99):
        self.ntiles = ntiles
        self.nlayers = nlayers
        self.dbg_names = dbg or []
        self.stage = stage
        self.nc = bass.Bass("TRN2", target_bir_lowering=False)
        self.st = contextlib.ExitStack()
        self.S = Sched(self.nc)
        self.dbg_out = {}
        self.out_tokens = []
        self.psum_names = set()

    def sb(self, name, shape, dt):
        if SIM_ONLY:
            return self.nc.dram_tensor(name, shape, dt, kind="Internal")
        return self.st.enter_context(self.nc.sbuf_tensor(name, shape, dt))

    def ps(self, name, shape, dt):
        self.psum_names.add(name)
        return self.st.enter_context(self.nc.psum_tensor(name, shape, dt))

    def dram_in(self, name, shape, dt=F32):
        return self.nc.dram_tensor(name, shape, dt, kind="ExternalInput").ap()

    def dram_out(self, name, shape, dt=F32):
        return self.nc.dram_tensor(name, shape, dt, kind="ExternalOutput").ap()

    def _rw(self, outs, ins, r, w):
        reads = list(r) if r is not None else [_tok(a) for a in ins if hasattr(a, "name")]
        writes = list(w) if w is not None else [_tok(a) for a in outs]
        ex = [t for t in reads if t in self.psum_names]
        if ex:
            reads = [t for t in reads if t not in self.psum_names]
            writes = writes + [t for t in ex if t not in writes]
        return reads, writes

    def act(self, out, in_, func, scale=1.0, bias=0.0, r=None, w=None):
        extra = [a for a in (scale, bias) if hasattr(a, "name")]
        reads, writes = self._rw([out], [in_] + extra, r, w)
        tbl = {AF.Exp: 1, AF.Ln: 1, AF.Sigmoid: 2, AF.Tanh: 2, AF.Silu: 3}.get(func, 0)
        self.S.op("act", lambda e: e.activation(out=out, in_=in_, func=func, scale=scale, bias=bias), reads, writes, dur=0.25 + _n(out) / 1400.0, tbl=tbl)

    def tt(self, eng, out, in0, in1, op, r=None, w=None):
        reads, writes = self._rw([out], [in0, in1], r, w)
        self.S.op(eng, lambda e: e.tensor_tensor(out=out, in0=in0, in1=in1, op=op), reads, writes, dur=self._vdur(eng, out))

    def ts(self, eng, out, in0, s1, s2, op0, op1=None, r=None, w=None):
        extra = [a for a in (s1, s2) if hasattr(a, "name")]
        reads, writes = self._rw([out], [in0] + extra, r, w)
        if op1 is None:
            self.S.op(eng, lambda e: e.tensor_scalar(out=out, in0=in0, scalar1=s1, scalar2=None, op0=op0), reads, writes, dur=self._vdur(eng, out))
        else:
            self.S.op(eng, lambda e: e.tensor_scalar(out=out, in0=in0, scalar1=s1, scalar2=s2, op0=op0, op1=op1), reads, writes, dur=self._vdur(eng, out))

    def stt(self, out, in0, scalar, in1, op0, op1, r=None, w=None):
        extra = [scalar] if hasattr(scalar, "name") else []
        reads, writes = self._rw([out], [in0, in1] + extra, r, w)
        self.S.op("dve", lambda e: e.scalar_tensor_tensor(out=out, in0=in0, scalar=scalar, in1=in1, op0=op0, op1=op1), reads, writes, dur=self._vdur("dve", out))

    def copy(self, eng, out, in_, r=None, w=None):
        reads, writes = self._rw([out], [in_], r, w)
        if eng == "act":
            self.S.op("act", lambda e: e.copy(out=out, in_=in_), reads, writes, dur=0.25 + _n(out) / 1400.0)
        else:
            self.S.op(eng, lambda e: e.tensor_copy(out=out, in_=in_), reads, writes,
                      dur=(0.3 + _n(out) / 300.0) if eng == "pool" else (0.1 + _n(out) / 1250.0))

    def recip(self, out, in_, r=None, w=None):
        reads, writes = self._rw([out], [in_], r, w)
        self.S.op("dve", lambda e: e.reciprocal(out=out, in_=in_), reads, writes, dur=0.1 + _n(out) / 155.0)

    def scan(self, out, d0, d1, r=None, w=None):
        reads, writes = self._rw([out], [d0, d1], r, w)
        self.S.op("dve", lambda e: e.tensor_tensor_scan(out=out, data0=d0, data1=d1, initial=0.0, op0=ALU.mult, op1=ALU.add), reads, writes, dur=0.12 + _n(out) / 480.0)

    def memset(self, eng, out, val, r=None, w=None):
        reads, writes = self._rw([out], [], r, w)
        self.S.op(eng, lambda e: e.memset(out, val), reads, writes, dur=self._vdur(eng, out))

    def mm(self, out, pairs, r=None, w=None):
        ins = []
        for a, b in pairs:
            ins += [a, b]
        reads, writes = self._rw([out], ins, r, w)
        n = len(pairs)

        def fn(e):
            last = None
            for i, (a, b) in enumerate(pairs):
                last = e.matmul(out, lhsT=a, rhs=b, start=(i == 0), stop=(i == n - 1))
            return last
        self.S.op("pe", fn, reads, writes, dur=sum(0.035 + max(_n(b_), 64) / 2000.0 for a_, b_ in pairs))

    def mmx(self, items):
        ins = []
        outs = []
        for o, a, b, st_, sp_ in items:
            ins += [a, b]
            outs.append(o)
        reads, writes = self._rw(outs[:1], ins, None, None)

        def fn(e):
            last = None
            for o, a, b, st_, sp_ in items:
                last = e.matmul(o, lhsT=a, rhs=b, start=st_, stop=sp_)
            return last
        self.S.op("pe", fn, reads, writes, dur=sum(0.035 + max(_n(b_), 64) / 2000.0 for o_, a_, b_, s1, s2 in items))

    def _vdur(self, eng, out):
        if eng == "pool":
            return 0.3 + _n(out) / 520.0
        return 0.15 + _n(out) / 900.0

    def tr(self, out, in_, ident, r=None, w=None):
        reads, writes = self._rw([out], [in_, ident], r, w)
        self.S.op("pe", lambda e: e.transpose(out=out, in_=in_, identity=ident), reads, writes, dur=0.1)

    def dma(self, q, out, in_, r=None, w=None):
        reads, writes = self._rw([out], [in_], r, w)
        self.S.dma(q, lambda e: e.dma_start(out=out, in_=in_), reads, writes, nbytes=max(_bytes(out), _bytes(in_)))

    def dump(self, name, ap, shape, dt=F32):
        if name not in self.dbg_names:
            return
        if dt != F32 or ap.dtype != F32:
            if not hasattr(self, "_dbgtmp"):
                self._dbgtmp = self.sb("dbgtmp", [128, 1024], F32)
            tmp = self._dbgtmp[0:shape[0], 0:shape[1]]
            self.copy("dve", tmp, ap)
            ap = tmp
        o = self.dram_out("dbg_" + name, list(shape))
        self.dma("sp", o, ap)
        self.out_tokens.append(_tok(o))
        self.dbg_out[name] = "dbg_" + name


def build(ntiles=4, nlayers=2, dbg=None, stage=99):
    K = KB(ntiles, nlayers, dbg, stage)
    nc = K.nc
    xT = K.dram_in("xT", [D, T])
    cT = K.dram_in("cT", [128, 8])
    ada = [K.dram_in("ada%d" % l, [D, 6 * D]) for l in range(L)]
    vecs_d = [K.dram_in("vecs%d" % l, [128, NV]) for l in range(L)]
    lay = [_wbig_layout(l) for l in range(L)]
    wbig = [K.dram_in("wbig%d" % l, [128, lay[l][1]]) for l in range(L)]
    outT = K.dram_out("outT", [D, T])

    sb, ps = K.sb, K.ps
    ident16 = sb("ident16", [128, 128], BF16)
    ones16 = sb("ones16", [128, 128], BF16)
    bd16 = sb("bd16", [128, 128], BF16)
    onesf = sb("onesf", [128, 128], F32)
    mask512 = sb("mask512", [128, 512], BF16)
    masksl = sb("masksl", [128, 128], BF16)
    rst = sb("rst", [128, 512], BF16)
    vecs = [sb("vecs_s%d" % l, [128, NV], F32) for l in range(L)]
    dv = [sb("dv%d" % l, [128, ND], F32) for l in range(L)]
    c32 = sb("c32", [128, 8], F32)
    epsc = sb("epsc", [128, 4], F32)
    c16 = sb("c16", [128, 8], BF16)
    S32 = [sb("S32_%d" % l, [128, 8, 64], F32) for l in range(L)]
    S16 = [sb("S16_%d" % l, [128, 8, 128], BF16) for l in range(L)]
    ctail = [sb("ctail%d" % l, [128, 8, 30], F32) for l in range(L)]
    carry = [sb("carry%d" % l, [128, 27], F32) for l in range(L)]
    xt = [sb("xt%d" % k, [128, TT], F32) for k in range(NB)]
    ht = [sb("ht%d" % k, [128, TT], BF16) for k in range(NB)]
    merged = [sb("mg%d" % k, [128, TT], BF16) for k in range(16)]
    vf = [sb("vf%d" % k, [128, TT], BF16) for k in range(NB)]
    NWP = 3
    WPN = 4608
    wpool = [sb("wp%d" % i, [128, WPN], BF16) for i in range(NWP)]
    lo16 = [sb("lo16_%d" % j, [128, TT], BF16) for j in range(2)]
    vlo16 = sb("vlo16", [128, TT], BF16)
    names32 = ["r32", "k32", "v32", "sg32", "a32", "sv32", "dd32", "cs32", "E1", "E3",
               "kk32", "nrm", "km", "b32", "y32", "yc", "sd"]
    W = {n: sb(n, [128, TT], F32) for n in names32}
    W["d2"] = W["dd32"]; W["E2"] = W["dd32"]; W["d4"] = W["sv32"]; W["E4"] = W["sv32"]; W["rn"] = W["nrm"]
    W["kkn"] = W["kk32"]; W["t1"] = W["km"]; W["rsd"] = W["sd"]
    W["yn"] = W["yc"]; W["yg"] = W["yc"]; W["o1"] = W["yc"]; W["o2"] = W["yc"]
    names16 = ["kk2", "Kh16", "Bh16", "v16", "rk16", "y16", "yc2", "sqa", "sqb", "sqc"]
    H = {n: sb(n, [128, TT], BF16) for n in names16}
    sq16 = [H[n] for n in ("kk2", "Kh16", "Bh16", "v16", "rk16", "sqa", "sqb", "sqc")]
    g32p = [sb("g32_%d" % i, [128, TT], F32) for i in range(NHO)]
    bon32p = [sb("bon32_%d" % i, [128, TT], F32) for i in range(NHO)]
    gA16p = [sb("gA16_%d" % i, [128, TT], BF16) for i in range(NHO)]
    kt16p = [sb("kt16_%d" % i, [128, 2, TT], BF16) for i in range(NHO)]
    bt16p = [sb("bt16_%d" % i, [128, 2, TT], BF16) for i in range(NHO)]
    hmask = sb("hmask", [128, 2], F32)
    WCp = [sb("WC_%d" % i, [128, NCH], F32) for i in range(NHO)]
    lo32 = [W["yc"], W["y32"]]
    vlo32 = bon32p[0]
    rs32 = W["nrm"]; rstd = W["sd"]; t32 = [W["yc"], W["y32"]]
    ARp = [sb("AR_%d" % i, [128, NCH, 2, C], BF16) for i in range(NHO)]
    KBtokp = [sb("KBtok_%d" % i, [128, 8, 128], BF16) for i in range(NHO)]
    Vtokp = [sb("Vtok_%d" % i, [128, 4, 128], BF16) for i in range(NHO)]
    Amatp = [[[sb("Amat%d_%d_%d" % (q, h, c), [128, 512], BF16) for c in range(NCH)] for h in range(2)] for q in range(NQ_AMAT)] * (2 // NQ_AMAT)
    Tfinp = [[[sb("Tfin%d_%d_%d" % (q, h, c), [128, 128], BF16) for c in range(NCH)] for h in range(2)] for q in range(NQ_AMAT)] * (2 // NQ_AMAT)
    Q0 = [sb("Q0_%d" % i, [128, 128], BF16) for i in range(4)]
    PQT = [[sb("PQT%d_%d" % (i, j), [128, 3, 128], BF16) for j in range(2)] for i in range(4)]
    XT16 = sb("XT16", [128, 128], BF16)
    UT16 = sb("UT16", [128, 128], BF16)
    sgb = sb("sgb16", [128, TT], BF16)
    sgb2 = [sb("sgbB%d" % i, [128, TT], BF16) for i in range(2)]
    gbuf = sb("gbuf", [128, 30 + TT], F32)
    cacc = [sb("cacc%d" % i, [128, TT], F32) for i in range(2)]
    ctmp = [sb("ctmp%d" % i, [128, TT], F32) for i in range(2)]
    z16 = [sb("z16_%d" % k, [128, TT], BF16) for k in range(NB)]

    def halves(t):
        v = t[:].bitcast(BF16)
        return [v[:, 0:TT], v[:, TT:2 * TT]]
    hidv = []
    for t_ in [W[n] for n in ("r32", "k32", "v32", "sg32", "a32", "sv32", "dd32", "kk32", "cs32", "E1", "E3", "km", "b32")] + [g32p[0], g32p[1], bon32p[0]]:
        hidv += halves(t_)
    mmb = [ps("mmb%d" % i, [128, 512], F32) for i in range(2)]
    dblb = [ps("dblb%d" % i, [128, 512], F32) for i in range(4)]
    seqb = ps("seqb", [128, 512], F32)
    yb = ps("yb", [128, 512], F32)
    mm_rr = [0]

    def bank():
        b = mmb[mm_rr[0] % 2]
        mm_rr[0] += 1
        return b

    def dbl_region(i, j):
        if j < 2:
            col = (i % 2) * 256 + j * 128
            return dblb[i // 2][:, col:col + 128]
        return dblb[2][:, i * 128:(i + 1) * 128]

    wp_rr = [0]
    PF = 2
    piece_list = []
    for pc in range(12):
        piece_list.append(("ada", 0, pc))
    ada1_left = list(range(12)) if K.nlayers > 1 else []
    for it_ in range(K.ntiles):
        for l_ in range(K.nlayers):
            if l_ == 1:
                while ada1_left:
                    piece_list.append(("ada", 1, ada1_left.pop(0)))
            if K.stage >= 4:
                keys = ["lo", "hp0"]
                for i in range(8):
                    keys.append("cb%d" % i)
                    if i + 1 < 8:
                        keys.append("hp%d" % (i + 1))
                keys += ["gb%d" % i for i in range(8)] + ["wo%d" % i for i in range(4)]
            else:
                keys = ["lo"] + ["hp%d" % i for i in range(8)]
            if K.stage >= 5:
                keys += ["w1%d" % i for i in range(8)] + ["w2%d" % i for i in range(8)]
            for n_, key in enumerate(keys):
                piece_list.append(("w", l_, key))
                if it_ == 0 and l_ == 0 and n_ % 3 == 2 and ada1_left:
                    piece_list.append(("ada", 1, ada1_left.pop(0)))
    issued = [0]
    piece_dst = {}

    def _issue(idx):
        kind, l_, key = piece_list[idx]
        buf = wpool[idx % NWP]
        if kind == "ada":
            src = ada[l_].rearrange("(k p) c -> p k c", p=128)[:, :, key * 512:(key + 1) * 512]
            dst = buf[:, 0:4096].rearrange("p (k c) -> p k c", c=512)
        elif key.startswith("hp"):
            off, k, c = lay[l_][0][key]
            src = wbig[l_][:, off:off + 4608].rearrange("p (a b) -> p a b", b=512)
            K.dma("pool", buf[:, 0:4608].rearrange("p (a b) -> p a b", b=512), src)
            piece_dst[idx] = (buf[:, 0:4096].rearrange("p (k c) -> p k c", c=512), buf[:, 4096:4608].rearrange("p (a c) -> p a c", c=128))
            return
        else:
            off, k, c = lay[l_][0][key]
            src = wbig[l_][:, off:off + k * c].rearrange("p (k c) -> p k c", c=c)
            dst = buf[:, 0:k * c].rearrange("p (k c) -> p k c", c=c)
        K.dma("pool", dst, src)
        piece_dst[idx] = dst

    def next_piece(expect):
        idx = wp_rr[0]
        wp_rr[0] += 1
        assert piece_list[idx] == expect, (piece_list[idx], expect)
        while issued[0] < min(len(piece_list), idx + PF + 1):
            _issue(issued[0])
            issued[0] += 1
        return piece_dst.pop(idx)

    def ada_piece():
        kind, l_, pc = piece_list[wp_rr[0]]
        dst = next_piece((kind, l_, pc))
        p = bank()
        for m in range(4):
            K.mm(p[:, m:m + 1], [(dst[:, k, m * 128:(m + 1) * 128], c16[:, k:k + 1]) for k in range(8)])
        K.tt("dve", dv[l_][:, DV_MOD + 4 * pc:DV_MOD + 4 * pc + 4], p[:, 0:4], vecs[l_][:, V_ADAB + 4 * pc:V_ADAB + 4 * pc + 4], ALU.add)
        if pc == 11:
            K.stt(dv[l_][:, DV_GSCM:DV_GSCM + 8], dv[l_][:, DV_MOD + M_SCM:DV_MOD + M_SCM + 8], 1.0, vecs[l_][:, V_GMIX:V_GMIX + 8], ALU.add, ALU.mult)
            K.stt(dv[l_][:, DV_GSCF:DV_GSCF + 8], dv[l_][:, DV_MOD + M_SCF:DV_MOD + M_SCF + 8], 1.0, vecs[l_][:, V_GFFN:V_GFFN + 8], ALU.add, ALU.mult)
            K.ts("dve", dv[l_][:, DV_OMU:DV_OMU + 27], vecs[l_][:, V_MU:V_MU + 27], -1.0, 1.0, ALU.mult, ALU.add)
            K.dump("dv%d" % l_, dv[l_][:], [128, ND])

    def load_piece(l, key):
        while wp_rr[0] < len(piece_list) and piece_list[wp_rr[0]][0] == "ada":
            ada_piece()
        return next_piece(("w", l, key))

    K.memset("pool", onesf[:], 1.0)
    K.memset("pool", ones16[:], 1.0)
    K.memset("pool", bd16[:], 0.0)
    K.memset("pool", bd16[0:64, 0:64], 1.0)
    K.memset("pool", bd16[64:128, 64:128], 1.0)

    def asel(out, in_, pattern, op, base, cm):
        K.S.op("pool", lambda e: e.affine_select(out=out, in_=in_, pattern=pattern, compare_op=op, fill=0.0, base=base, channel_multiplier=cm),
               [_tok(in_)], [_tok(out)], dur=0.3)
    asel(ident16[:], ones16[:], [[1, 128]], ALU.is_equal, 0, -1)
    for j in range(4):
        asel(mask512[:, j * 128:(j + 1) * 128], onesf[:, 0:128], [[1, 128]], ALU.is_gt if j % 2 == 0 else ALU.is_ge, 0, -1)
    asel(masksl[:], onesf[:, 0:128], [[-1, 128]], ALU.is_gt, 0, 1)
    K.memset("pool", rst[:], 1.0)
    for j in range(NCH):
        K.memset("pool", rst[:, j * C:j * C + 1], 0.0)
    for l in range(L):
        K.dma("sp", vecs[l][:], vecs_d[l])
        K.memset("pool", S32[l][:], 0.0)
        K.memset("pool", S16[l][:], 0.0)
        K.memset("pool", ctail[l][:], 0.0)
        K.memset("pool", carry[l][:], 0.0)
    K.dma("sp", c32[:], cT)
    K.memset("pool", hmask[:], 0.0)
    K.memset("pool", vlo16[:], 0.0)
    K.memset("pool", hmask[0:64, 0:1], 1.0)
    K.memset("pool", hmask[64:128, 1:2], 1.0)
    K.memset("pool", epsc[:, 0:1], RMS_EPS)
    K.memset("pool", epsc[:, 1:2], LN_EPS)
    K.memset("pool", epsc[:, 2:3], GN_EPS)
    K.memset("pool", epsc[:, 3:4], 1e-24)
    K.act(c16[:], c32[:], AF.Silu)

    for pc in range(12):
        ada_piece()

    def rmsnorm_to_ht(l, gsc_col, sh_col):
        for k in range(NB):
            K.act(sq16[k][:], xt[k][:], AF.Square)
        p = bank()
        K.mm(p[:], [(ones16[:], sq16[k][:]) for k in range(NB)])
        K.act(rs32[:], p[:], AF.Ln, scale=1.0 / D, bias=epsc[:, 0:1])
        K.act(rstd[:], rs32[:], AF.Exp, scale=-0.5)
        for k in range(NB):
            t = t32[k % 2]
            K.stt(t[:], xt[k][:], dv[l][:, gsc_col + k:gsc_col + k + 1], rstd[:], ALU.mult, ALU.mult)
            K.act(ht[k][:], t[:], AF.Identity, bias=dv[l][:, sh_col + k:sh_col + k + 1])

    def proj(wp, j0, M=128):
        p = bank()
        K.mm(p[0:M, :], [(wp[:, k, j0:j0 + M], ht[k][:]) for k in range(NB)])
        return p

    def shift(l, p, mucol, dst, M=128):
        mu = vecs[l][0:M, V_MU + mucol:V_MU + mucol + 1]
        omu = dv[l][0:M, DV_OMU + mucol:DV_OMU + mucol + 1]
        cy = carry[l][0:M, mucol:mucol + 1]
        K.act(dst[0:M, :], p[0:M, :], AF.Identity, scale=omu)
        K.stt(dst[0:M, 1:TT], p[0:M, 0:TT - 1], mu, dst[0:M, 1:TT], ALU.mult, ALU.add)
        K.stt(dst[0:M, 0:1], cy, mu, dst[0:M, 0:1], ALU.mult, ALU.add)
        K.copy("dve", cy, p[0:M, TT - 1:TT])

    xTv = xT.rearrange("(k p) t -> k p t", p=128)
    oTv = outT.rearrange("(k p) t -> k p t", p=128)
    for it in range(K.ntiles):
        t0 = it * TT
        for k in range(NB):
            K.dma("sp", xt[k][:], xTv[k, :, t0:t0 + TT])
        for l in range(K.nlayers):
            V = vecs[l]
            DVl = dv[l]
            rmsnorm_to_ht(l, DV_GSCM, DV_MOD + M_SHM)
            if it == 0 and l == 0:
                for k in (0, 7):
                    K.dump("ht%d" % k, ht[k][:], [128, TT])
            wlo = load_piece(l, "lo")
            for j in range(2):
                p = proj(wlo, j * 128)
                shift(l, p, 24 + j, lo32[j])
            K.act(lo16[0][0:64, :], lo32[0][0:64, :], AF.Tanh)
            K.copy("act", lo16[0][64:128, :], lo32[0][64:128, :])
            K.act(lo16[1][:], lo32[1][:], AF.Sigmoid)
            if l >= 1:
                p = proj(wlo, 256, M=32)
                shift(l, p, 26, vlo32, M=32)
                K.copy("act", vlo16[0:32, :], vlo32[0:32, :])
            if K.stage < 1:
                continue
            def front(hp, par):
                AR = ARp[par]; KBtok = KBtokp[par]; Vtok = Vtokp[par]
                g32 = g32p[par]; bon32 = bon32p[par]; gA16 = gA16p[par]; kt16 = kt16p[par]; bt16 = bt16p[par]; WC = WCp[par]
                whp, wsl = load_piece(l, "hp%d" % hp)
                p = proj(whp, 0);   shift(l, p, hp, W["r32"])
                yield
                p = proj(whp, 128); shift(l, p, 8 + hp, W["k32"])
                yield
                p = proj(whp, 256); shift(l, p, 16 + hp, W["v32"])
                yield
                p = proj(whp, 384); K.act(gA16[:], p[:], AF.Sigmoid)
                p = bank(); K.mm(p[:], [(wsl[:, 0, :], lo16[0][:])])
                K.act(W["sg32"][:], p[:], AF.Sigmoid, bias=V[:, V_W0 + hp:V_W0 + hp + 1])
                yield
                p = bank(); K.mm(p[:], [(wsl[:, 3, :], lo16[0][:])])
                K.act(W["a32"][:], p[:], AF.Sigmoid, bias=V[:, V_A0 + hp:V_A0 + hp + 1])
                p = bank(); K.mm(p[:], [(wsl[:, 1, :], lo16[1][:])])
                K.tt("dve", g32[:], p[:], gA16[:], ALU.mult)
                yield
                if l >= 1:
                    p = bank(); K.mm(p[:], [(wsl[:, 2, :], vlo16[:])])
                    K.act(W["sv32"][:], p[:], AF.Sigmoid, bias=V[:, V_V0 + hp:V_V0 + hp + 1])
                    K.tt("pool", W["dd32"][:], vf[hp][:], W["v32"][:], ALU.subtract)
                    K.tt("pool", W["dd32"][:], W["dd32"][:], W["sv32"][:], ALU.mult)
                    K.tt("pool", W["v32"][:], W["v32"][:], W["dd32"][:], ALU.add)
                else:
                    K.copy("pool", vf[hp][:], W["v32"][:])
                yield
                K.scan(W["cs32"][:], rst[:], W["sg32"][:])
                K.act(W["E1"][:], W["cs32"][:], AF.Exp, scale=CEXP)
                K.act(W["E3"][:], W["cs32"][:], AF.Exp, scale=-CEXP)
                K.tt("pool", W["d2"][:], W["cs32"][:], W["sg32"][:], ALU.subtract)
                K.act(W["E2"][:], W["d2"][:], AF.Exp, scale=CEXP)
                yield
                cs3 = W["cs32"][:].rearrange("p (c t) -> p c t", t=C)
                K.tt("pool", W["d4"][:].rearrange("p (c t) -> p c t", t=C), cs3, cs3[:, :, C - 1:C].to_broadcast([128, NCH, C]), ALU.subtract)
                K.act(W["E4"][:], W["d4"][:], AF.Exp, scale=-CEXP)
                K.copy("dve", WC[:], W["E1"][:].rearrange("p (c t) -> p c t", t=C)[:, :, C - 1])
                yield
                K.act(W["kk32"][:], W["k32"][:], AF.Identity, scale=V[:, V_KK + hp:V_KK + hp + 1])
                K.act(H["kk2"][:], W["kk32"][:], AF.Square)
                p = bank(); K.mm(p[:], [(bd16[:], H["kk2"][:])])
                K.act(W["nrm"][:], p[:], AF.Ln, bias=epsc[:, 3:4])
                K.act(W["rn"][:], W["nrm"][:], AF.Exp, scale=-0.5)
                yield
                K.tt("pool", W["kkn"][:], W["kk32"][:], W["rn"][:], ALU.mult)
                K.ts("dve", W["t1"][:], W["a32"][:], -1.0, V[:, V_KA + hp:V_KA + hp + 1], ALU.add, ALU.mult)
                K.stt(W["km"][:], W["t1"][:], 1.0, W["k32"][:], ALU.add, ALU.mult)
                K.tt("pool", W["b32"][:], W["kkn"][:], W["a32"][:], ALU.mult)
                yield
                K.tt("dve", AR[:, :, 1, :], W["r32"][:].rearrange("p (c t) -> p c t", t=C), W["E1"][:].rearrange("p (c t) -> p c t", t=C), ALU.mult)
                K.stt(AR[:, :, 0, :], W["kkn"][:].rearrange("p (c t) -> p c t", t=C), -1.0, W["E2"][:].rearrange("p (c t) -> p c t", t=C), ALU.mult, ALU.mult)
                for h_ in range(2):
                    K.stt(kt16[:, h_, :], W["km"][:], hmask[:, h_:h_ + 1], W["E3"][:], ALU.mult, ALU.mult)
                    K.stt(bt16[:, h_, :], W["b32"][:], hmask[:, h_:h_ + 1], W["E3"][:], ALU.mult, ALU.mult)
                yield
                K.tt("pool", H["Kh16"][:], W["km"][:], W["E4"][:], ALU.mult)
                K.tt("dve", H["Bh16"][:], W["b32"][:], W["E4"][:], ALU.mult)
                K.copy("pool", H["v16"][:], W["v32"][:])
                yield
                K.stt(H["rk16"][:], W["r32"][:], V[:, V_RK + hp:V_RK + hp + 1], W["km"][:], ALU.mult, ALU.mult)
                p = bank(); K.mm(p[:], [(bd16[:], H["rk16"][:])])
                K.tt("dve", bon32[:], p[:], W["v32"][:], ALU.mult)
                yield
                trb = bank()
                trv = trb[:].bitcast(BF16).rearrange("p (a t) -> p a t", t=128)
                for c in range(NCH):
                    K.tr(trv[:, c, :], H["Kh16"][:, c * C:(c + 1) * C], ident16[:])
                    K.tr(trv[:, 4 + c, :], H["Bh16"][:, c * C:(c + 1) * C], ident16[:])
                K.copy("act", KBtok[:], trv)
                yield
                trb = bank()
                trv = trb[:].bitcast(BF16).rearrange("p (a t) -> p a t", t=128)
                for c in range(NCH):
                    K.tr(trv[:, c, :], H["v16"][:, c * C:(c + 1) * C], ident16[:])
                K.copy("dve", Vtok[:], trv[:, 0:4, :])
                yield

            def back(hp, par):
                Amat = Amatp[hp % 2]; Tfin = Tfinp[hp % 2]
                AR = ARp[par]; KBtok = KBtokp[par]; Vtok = Vtokp[par]
                g32 = g32p[par]; bon32 = bon32p[par]; gA16 = gA16p[par]; kt16 = kt16p[par]; bt16 = bt16p[par]; WC = WCp[par]
                for h in range(2):
                    hs = slice(h * 64, h * 64 + 64)
                    for c in range(NCH):
                        ck = slice(c * C, (c + 1) * C)
                        bk = dblb[c]
                        ar = AR[:, c, :, :].rearrange("p a t -> p (a t)")
                        K.mm(bk[:, 0:256], [(bt16[:, h, ck], ar)])
                        K.mm(bk[:, 256:512], [(kt16[:, h, ck], ar)])
                        K.tt("dve", Amat[h][c][:], bk[:], mask512[:], ALU.mult)
                        K.mm(bk[:, 0:128], [(AR[:, c, 0, :], bt16[:, h, ck])])
                        K.tt("dve", Q0[c][:], bk[:, 0:128], masksl[:], ALU.mult)
                        K.tt("pool", PQT[c][1][:, 1, :], Amat[h][c][:, 0:128], ident16[:], ALU.add)
                        yield
                    for c in range(NCH):
                        bk = dblb[c]
                        P0 = Amat[h][c][:, 0:128]
                        K.mm(bk[:, 0:128], [(Q0[c][:], P0)])
                        K.mm(bk[:, 256:384], [(P0, Q0[c][:])])
                        K.copy("act" if c % 2 == 0 else "dve", PQT[c][1][:, 0:3:2, :], bk[:, 0:384].rearrange("p (a t) -> p a t", t=128)[:, 0:3:2, :])
                    yield
                    for lv in range(1, 7):
                        cur, nxt = lv % 2, (lv + 1) % 2
                        for c in range(NCH):
                            bk = dblb[c]
                            Pk = PQT[c][cur][:, 0, :]
                            Tk = PQT[c][cur][:, 1, :]
                            Qk = PQT[c][cur][:, 2, :]
                            PTk = PQT[c][cur][:, 0:2, :].rearrange("p a t -> p (a t)")
                            eng = "dve" if (c + lv) % EVAC_MOD == 0 else "act"
                            if lv < 6:
                                K.mmx([(bk[:, 256:384], Pk, Qk, True, True),
                                       (bk[:, 0:256], Qk, PTk, True, False),
                                       (bk[:, 128:256], ident16[:], Tk, False, True)])
                                K.copy(eng, PQT[c][nxt][:], bk[:, 0:384].rearrange("p (a t) -> p a t", t=128))
                            else:
                                K.mm(bk[:, 128:256], [(Qk, Tk), (ident16[:], Tk)])
                                K.copy(eng, Tfin[h][c][:], bk[:, 128:256])
                        yield
                h0 = slice(0, 64); h1 = slice(64, 128)
                for c in range(NCH):
                    ck = slice(c * C, (c + 1) * C)
                    K.mmx([(seqb[:, 0:128], AR[:, c, 0, :], S16[l][:, hp, :], True, False),
                           (seqb[:, 0:64], Amat[0][c][:, 256:384], Vtok[:, c, h0], False, False),
                           (seqb[:, 64:128], Amat[1][c][:, 256:384], Vtok[:, c, h1], False, True)])
                    K.copy("act", XT16[:], seqb[:, 0:128])
                    yield
                    for h in range(2):
                        hs = slice(h * 64, h * 64 + 64)
                        K.mm(seqb[:, 128 + h * 64:128 + (h + 1) * 64], [(Tfin[h][c][:], XT16[:, hs])])
                    K.copy("dve", UT16[:], seqb[:, 128:256])
                    yield
                    K.mmx([(yb[:, ck], S16[l][:, hp, :], AR[:, c, 1, :], True, False),
                           (yb[h0, ck], UT16[:, h0], Amat[0][c][:, 128:256], False, False),
                           (yb[h0, ck], Vtok[:, c, h0], Amat[0][c][:, 384:512], False, False),
                           (yb[h1, ck], UT16[:, h1], Amat[1][c][:, 128:256], False, False),
                           (yb[h1, ck], Vtok[:, c, h1], Amat[1][c][:, 384:512], False, True)])
                    for h in range(2):
                        hs = slice(h * 64, h * 64 + 64)
                        K.mm(seqb[hs, 256:320], [(KBtok[:, 4 + c, hs], UT16[:, hs]),
                                                  (KBtok[:, c, hs], Vtok[:, c, hs])])
                    K.stt(S32[l][:, hp, :], S32[l][:, hp, :], WC[:, c:c + 1], seqb[:, 256:320], ALU.mult, ALU.add)
                    K.copy("act", S16[l][h0, hp, 0:64], S32[l][h0, hp, :])
                    K.copy("dve", S16[l][h1, hp, 64:128], S32[l][h1, hp, :])
                    yield
                K.copy("act", W["y32"][:], yb[:])
                K.copy("dve", H["y16"][:], yb[:])
                if it == 0 and l == 0 and hp == 0:
                    K.dump("y32", W["y32"][:], [128, TT])
                yield
                p = bank(); K.mm(p[:], [(bd16[:], H["y16"][:])])
                K.stt(W["yc"][:], p[:], -1.0 / 64, W["y32"][:], ALU.mult, ALU.add)
                yield
                K.act(H["yc2"][:], W["yc"][:], AF.Square)
                p = bank(); K.mm(p[:], [(bd16[:], H["yc2"][:])])
                K.act(W["sd"][:], p[:], AF.Ln, scale=1.0 / 64, bias=epsc[:, 2:3])
                yield
                K.act(W["rsd"][:], W["sd"][:], AF.Exp, scale=-0.5)
                K.stt(W["yn"][:], W["yc"][:], V[:, V_GNG + hp:V_GNG + hp + 1], W["rsd"][:], ALU.mult, ALU.mult)
                K.stt(W["o1"][:], W["yn"][:], V[:, V_GNB + hp:V_GNB + hp + 1], bon32[:], ALU.add, ALU.add)
                yield
                K.tt("pool", merged[hp][:], W["o1"][:], g32[:], ALU.mult)
                if it == 0 and l == 0 and hp in (0, 7):
                    K.dump("mg%d" % hp, merged[hp][:], [128, TT], BF16)
                yield

            def drain(g):
                for _ in g:
                    pass

            def interleave(g1, g2):
                a1 = a2 = True
                while a1 or a2:
                    if a1:
                        try:
                            next(g1)
                        except StopIteration:
                            a1 = False
                    if a2:
                        try:
                            next(g2)
                        except StopIteration:
                            a2 = False

            def conv(cb):
                wcb = load_piece(l, "cb%d" % cb)
                pb = proj(wcb, 128)
                K.act(sgb[:], pb[:], AF.Sigmoid)
                pa = proj(wcb, 0)
                K.copy("pool", gbuf[:, 0:30], ctail[l][:, cb, :])
                K.tt("dve", gbuf[:, 30:30 + TT], pa[:], sgb[:], ALU.mult)
                K.copy("pool", ctail[l][:, cb, :], gbuf[:, TT:TT + 30])
                cw = lambda k: V[:, V_CW + cb * 31 + k:V_CW + cb * 31 + k + 1]
                NT_DVE = NT_DVE_G
                K.ts("dve", cacc[0][:], gbuf[:, 0:TT], cw(0), V[:, V_CB + cb:V_CB + cb + 1], ALU.mult, ALU.add)
                for k in range(1, NT_DVE):
                    K.stt(cacc[0][:], gbuf[:, k:k + TT], cw(k), cacc[0][:], ALU.mult, ALU.add)
                K.act(cacc[1][:], gbuf[:, NT_DVE:NT_DVE + TT], AF.Identity, scale=cw(NT_DVE))
                for k in range(NT_DVE + 1, 31):
                    tp_ = ctmp[k % 2]
                    K.act(tp_[:], gbuf[:, k:k + TT], AF.Identity, scale=cw(k))
                    K.tt("pool", cacc[1][:], cacc[1][:], tp_[:], ALU.add)
                K.tt("pool", z16[cb][:], cacc[0][:], cacc[1][:], ALU.add)

            if K.stage < 3:
                for hp in range(8):
                    drain(front(hp, hp % NHO))
            else:
                drain(front(0, 0))
                for hp in range(8):
                    if K.stage >= 4:
                        conv(hp)
                    if hp + 1 < 8:
                        interleave(back(hp, hp % NHO), front(hp + 1, (hp + 1) % NHO))
                    else:
                        drain(back(hp, hp % NHO))
            if K.stage < 4:
                continue
            if it == 0 and l == 0:
                K.dump("z0", z16[0][:], [128, TT], BF16)
            p = bank(); K.mm(p[:], [(ones16[:], z16[cb][:]) for cb in range(8)])
            K.act(W["nrm"][:], p[:], AF.Identity, scale=-1.0 / D)
            for cb in range(8):
                t = t32[cb % 2]
                K.tt("dve", t[:], z16[cb][:], W["nrm"][:], ALU.add)
                K.act(sq16[cb][:], t[:], AF.Square)
            p = bank(); K.mm(p[:], [(ones16[:], sq16[cb][:]) for cb in range(8)])
            K.act(W["sd"][:], p[:], AF.Ln, scale=1.0 / D, bias=epsc[:, 1:2])
            K.act(W["sd"][:], W["sd"][:], AF.Exp, scale=-0.5)
            for cb in range(8):
                t = t32[cb % 2]
                wgb = load_piece(l, "gb%d" % cb)
                pg = proj(wgb, 0)
                sg_ = sgb2[cb % 2]
                K.act(sg_[:], pg[:], AF.Sigmoid)
                K.tt("dve", t[:], z16[cb][:], W["nrm"][:], ALU.add)
                K.tt("pool", t[:], t[:], W["sd"][:], ALU.mult)
                K.act(t[:], t[:], AF.Silu, scale=V[:, V_LNG + cb:V_LNG + cb + 1], bias=V[:, V_LNB + cb:V_LNB + cb + 1])
                K.tt("pool", merged[8 + cb][:], t[:], sg_[:], ALU.mult)
            if it == 0 and l == 0:
                K.dump("mg8", merged[8][:], [128, TT], BF16)
            for g in range(4):
                wo = load_piece(l, "wo%d" % g)
                for o2 in range(2):
                    ob = g * 2 + o2
                    p = bank()
                    K.mm(p[:], [(wo[:, o2 * 16 + kc, :], merged[kc][:]) for kc in range(16)])
                    K.stt(xt[ob][:], p[:], DVl[:, DV_MOD + M_GTM + ob:DV_MOD + M_GTM + ob + 1], xt[ob][:], ALU.mult, ALU.add)
            if it == 0 and l == 0:
                K.dump("xmix0", xt[0][:], [128, TT])
            if K.stage < 5:
                continue
            rmsnorm_to_ht(l, DV_GSCF, DV_MOD + M_SHF)
            for pc in range(8):
                w1 = load_piece(l, "w1%d" % pc)
                for j in range(4):
                    hb = pc * 4 + j
                    p = proj(w1, j * 128)
                    t = t32[hb % 2]
                    K.act(t[:], p[:], AF.Relu)
                    K.tt("pool", hidv[hb], t[:], t[:], ALU.mult)
            for ob in range(8):
                w2 = load_piece(l, "w2%d" % ob)
                p = bank()
                K.mm(p[:], [(w2[:, hb, :], hidv[hb]) for hb in range(32)])
                K.stt(xt[ob][:], p[:], DVl[:, DV_MOD + M_GTF + ob:DV_MOD + M_GTF + ob + 1], xt[ob][:], ALU.mult, ALU.add)
            if it == 0 and l == 0:
                K.dump("xffn0", xt[0][:], [128, TT])
        for k in range(NB):
            K.act(sq16[k][:], xt[k][:], AF.Square)
        p = bank()
        K.mm(p[:], [(ones16[:], sq16[k][:]) for k in range(NB)])
        K.act(rs32[:], p[:], AF.Ln, scale=1.0 / D, bias=epsc[:, 0:1])
        K.act(rstd[:], rs32[:], AF.Exp, scale=-0.5)
        for k in range(NB):
            t = t32[k % 2]
            K.stt(t[:], xt[k][:], vecs[0][:, V_FG + k:V_FG + k + 1], rstd[:], ALU.mult, ALU.mult)
            K.dma("sp", oTv[k, :, t0:t0 + TT], t[:], w=["outT%d_%d" % (it, k)])
            K.out_tokens.append("outT%d_%d" % (it, k))
    K.S.wait_all("sp", K.out_tokens)
    K.S.emit(K.st)
    K.st.close()
    return K


def dblb_view(dblb, c):
    col = (c % 2) * 256
    return dblb[c // 2][:, col:col + 256].rearrange("p (a t) -> p a t", t=128)


def _fm(v):
    v = np.asarray(v, np.float32)
    return np.ascontiguousarray(v.reshape(-1, 128).T)


def prep_shared(inp):
    shared = {}
    for l in range(L):
        vec = np.zeros((128, NV), np.float32)
        vec[:, V_GMIX:V_GMIX + 8] = _fm(inp["norm_mix_gain"][l])
        vec[:, V_GFFN:V_GFFN + 8] = _fm(inp["norm_ffn_gain"][l])
        vec[:, V_MU:V_MU + 26] = _fm(inp["mu_shift"][l])
        if l >= 1:
            vec[0:32, V_MUV] = inp["mu_vres"][l - 1]
            vec[:, V_V0:V_V0 + 8] = _fm(inp["v0"][l - 1])
        vec[:, V_W0:V_W0 + 8] = _fm(inp["w0"][l])
        vec[:, V_A0:V_A0 + 8] = _fm(inp["a0"][l])
        vec[:, V_KK:V_KK + 8] = _fm(inp["k_k"][l])
        vec[:, V_KA:V_KA + 8] = _fm(inp["k_a"][l])
        vec[:, V_RK:V_RK + 8] = _fm(inp["r_k"][l].reshape(-1))
        vec[:, V_GNG:V_GNG + 8] = _fm(inp["gn_gain"][l])
        vec[:, V_GNB:V_GNB + 8] = _fm(inp["gn_bias"][l])
        vec[:, V_CB:V_CB + 8] = _fm(inp["conv_b"][l])
        vec[:, V_LNG:V_LNG + 8] = _fm(inp["conv_ln_gain"][l])
        vec[:, V_LNB:V_LNB + 8] = _fm(inp["conv_ln_bias"][l])
        vec[:, V_FG:V_FG + 8] = _fm(inp["final_gain"])
        vec[:, V_ADAB:V_ADAB + 48] = _fm(inp["ada_b"][l])
        cw = np.asarray(inp["conv_w"][l], np.float32)
        vec[:, V_CW:V_CW + 248] = cw.reshape(31, 8, 128).transpose(2, 1, 0).reshape(128, 248)
        shared["vecs%d" % l] = vec
        wsm = np.zeros((128, 4, D), np.float32)
        wsm[0:64, 0] = inp["w_decay_up"][l]
        wsm[64:128, 3] = inp["w_aaa_up"][l]
        wsm[:, 1] = inp["w_gate_up"][l]
        if l >= 1:
            wsm[0:32, 2] = inp["w_vres_up"][l - 1]
        shared["ada%d" % l] = np.ascontiguousarray(inp["ada_w"][l], dtype=np.float32)
        lay, tot = _wbig_layout(l)
        wb = np.empty((128, tot), np.float32)
        win = np.asarray(inp["w_in"][l], np.float32)
        if l >= 1:
            win = np.concatenate([win, np.asarray(inp["w_in_vres"][l - 1], np.float32)], axis=1)

        def put(key, cols_matrix):
            off, k, c = lay[key]
            wb[:, off:off + k * c] = cols_matrix.reshape(k, 128, c).transpose(1, 0, 2).reshape(128, k * c)
        lo_idx = list(range(3072, 3328)) + (list(range(N_COLS, N_COLS + 32)) if l >= 1 else [])
        put("lo", win[:, lo_idx])
        for hp in range(8):
            idx = np.concatenate([np.arange(hp * 128, hp * 128 + 128), 1024 + np.arange(hp * 128, hp * 128 + 128),
                                  2048 + np.arange(hp * 128, hp * 128 + 128), 5376 + np.arange(hp * 128, hp * 128 + 128)])
            put("hp%d" % hp, win[:, idx])
            off_, k_, c_ = lay["hp%d" % hp]
            wb[:, off_ + k_ * c_: off_ + k_ * c_ + 512] = wsm[:, :, hp * 128:(hp + 1) * 128].reshape(128, 512)
        for cb in range(8):
            idx = np.concatenate([3328 + np.arange(cb * 128, cb * 128 + 128), 4352 + np.arange(cb * 128, cb * 128 + 128)])
            put("cb%d" % cb, win[:, idx])
            put("gb%d" % cb, win[:, 6400 + np.arange(cb * 128, cb * 128 + 128)])
        wo = np.asarray(inp["w_out"][l], np.float32)
        for g in range(4):
            off, k, c = lay["wo%d" % g]
            blk = wo[:, g * 256:(g + 1) * 256].reshape(16, 128, 2, 128)
            wb[:, off:off + k * c] = blk.transpose(1, 2, 0, 3).reshape(128, 32 * 128)
        w1 = np.asarray(inp["w_ff_in"][l], np.float32)
        for pc in range(8):
            put("w1%d" % pc, w1[:, pc * 512:(pc + 1) * 512])
        w2 = np.asarray(inp["w_ff_out"][l], np.float32)
        for ob in range(8):
            put("w2%d" % ob, w2[:, ob * 128:(ob + 1) * 128])
        shared["wbig%d" % l] = wb
    return shared


def prep_core(inp, b):
    return {"xT": np.ascontiguousarray(np.asarray(inp["x"][b], np.float32).T),
            "cT": _fm(inp["c"][b])}


_CACHE = {}


def kernel(**inputs):
    inp = {k: np.asarray(v) for k, v in inputs.items()}
    if "K" not in _CACHE:
        _CACHE["K"] = build()
    K = _CACHE["K"]
    shared = prep_shared(inp)
    in_maps = []
    for b in range(NCORES):
        m = dict(shared)
        m.update(prep_core(inp, b))
        in_maps.append(m)
    res = run_bass_kernel_spmd(K.nc, in_maps, core_ids=list(range(NCORES)))
    out = np.stack([np.ascontiguousarray(res.results[b]["outT"].T) for b in range(NCORES)], axis=0)
    return out.astype(np.float32)
```

```python
import contextlib
import math
import numpy as np
import concourse.bass as bass
import concourse.mybir as mybir
from concourse.bass_utils import run_bass_kernel_spmd

F32 = mybir.dt.float32
BF16 = mybir.dt.bfloat16
AF = mybir.ActivationFunctionType
ALU = mybir.AluOpType

D = 1024
T = 2048
NB = 8
TT = 512
C = 128
NCH = TT // C
L = 2
NCORES = 8
N_SHIFT = 3328
N_COLS = 7424
RMS_EPS = 1e-6
LN_EPS = 1e-5
GN_EPS = 64e-5
CEXP = -math.exp(-0.5)

V_GMIX, V_GFFN, V_MU, V_MUV, V_W0, V_A0, V_KK, V_KA, V_RK, V_GNG, V_GNB, V_V0, V_CB, V_LNG, V_LNB, V_FG, V_ADAB, V_CW = (
    0, 8, 16, 42, 43, 51, 59, 67, 75, 83, 91, 99, 107, 115, 123, 131, 139, 187)
NV = 187 + 8 * 31
DV_MOD, DV_GSCM, DV_GSCF, DV_OMU = 0, 48, 56, 64
ND = 64 + 27
M_SHM, M_SCM, M_GTM, M_SHF, M_SCF, M_GTF = 0, 8, 16, 24, 32, 40

def _wbig_layout(l):
    off = 0
    lay = {}
    nlo = 256 + (32 if l >= 1 else 0)
    lay["lo"] = (off, 8, nlo); off += 8 * nlo
    for hp in range(8):
        lay["hp%d" % hp] = (off, 8, 512); off += 8 * 512 + 4 * 128
    for cb in range(8):
        lay["cb%d" % cb] = (off, 8, 256); off += 8 * 256
    for cb in range(8):
        lay["gb%d" % cb] = (off, 8, 128); off += 8 * 128
    for g in range(4):
        lay["wo%d" % g] = (off, 32, 128); off += 32 * 128
    for pc in range(8):
        lay["w1%d" % pc] = (off, 8, 512); off += 8 * 512
    for ob in range(8):
        lay["w2%d" % ob] = (off, 32, 128); off += 32 * 128
    return lay, off


ENGS = ("pe", "act", "dve", "pool", "sp")
DMA_SEMS = {"sp": 8, "pool": 16, "act": 4}
SEM_LAT = 0.4
NT_DVE_G = 31
EVAC_MOD = 4
PRIO_MODE = 0
NQ_AMAT = 1
NHO = 2
MASK_POOL = 0
SIM_ONLY = False
ACT_WINDOW = 0.3
WINDOW_G = 0.3
LIST_SCHED = True


class Sched:
    def __init__(self, nc):
        self.nc = nc
        self.recs = []
        self.last_w = {}
        self.readers = {}

    def _deps(self, reads, writes):
        deps = set()
        for t in reads:
            w = self.last_w.get(t)
            if w is not None:
                deps.add(w)
        for t in writes:
            w = self.last_w.get(t)
            if w is not None:
                deps.add(w)
            deps.update(self.readers.get(t, ()))
        return deps

    def _commit(self, reads, writes, me):
        for t in reads:
            self.readers.setdefault(t, []).append(me)
        for t in writes:
            self.last_w[t] = me
            self.readers[t] = []

    def op(self, eng, fn, reads=(), writes=(), dur=0.5, tbl=0):
        deps = self._deps(reads, writes)
        me = len(self.recs)
        self.recs.append([eng, fn, deps, dur, False, dur, (list(writes) or ["?"])[0], tbl])
        self._commit(reads, writes, me)

    def dma(self, queue, fn, reads=(), writes=(), nbytes=0):
        deps = self._deps(reads, writes)
        me = len(self.recs)
        self.recs.append([queue, fn, deps, 0.6 if queue == "pool" else 0.15, True, 2.0 + nbytes / 340e3, (list(writes) or ["?"])[0]])
        self._commit(reads, writes, me)

    def wait_all(self, eng, tokens):
        deps = set(self.last_w[t] for t in tokens if t in self.last_w)
        me = len(self.recs)
        self.recs.append([eng, None, deps, 0.01, False, 0.01, "final"])
        self.final = me

    def finalize(self):
        recs = self.recs
        n = len(recs)
        order = {e: [] for e in ENGS}
        if not LIST_SCHED:
            for i, r in enumerate(recs):
                order[r[0]].append(i)
            self.order = order
            return
        succs = [[] for _ in range(n)]
        indeg = [0] * n
        for i, r in enumerate(recs):
            for d in r[2]:
                succs[d].append(i)
            indeg[i] = len(r[2])
        prio = [0.0] * n
        for i in range(n - 1, -1, -1):
            m = 0.0
            for sx in succs[i]:
                if prio[sx] > m:
                    m = prio[sx]
            prio[i] = m + recs[i][5] + SEM_LAT
        import heapq
        ready = {e: [] for e in ENGS}
        finish = [0.0] * n
        self.sim_start = [0.0] * n
        self.sim_finish = finish
        rdy_t = [0.0] * n
        for i in range(n):
            if indeg[i] == 0:
                heapq.heappush(ready[recs[i][0]], (0.0, -prio[i], i))
        efree = {e: 0.0 for e in ENGS}
        done = 0
        WINDOW = WINDOW_G
        cur_tbl = 0
        dma_free = 0.0
        self.n_tbl_switch = 0
        while done < n:
            best = None
            for e in ENGS:
                h = ready[e]
                if not h:
                    continue
                st = max(efree[e], h[0][0])
                if best is None or st < best[0]:
                    best = (st, e)
            st, e = best
            h = ready[e]
            cands = []
            win = ACT_WINDOW if e == "act" else WINDOW
            while h and h[0][0] <= st + win and len(cands) < 32:
                cands.append(heapq.heappop(h))
            cands.sort(key=lambda c: c[1])
            pick = cands[0]
            pen = 0.0
            if e == "act":
                ok = [c for c in cands if len(recs[c[2]]) < 8 or recs[c[2]][7] in (0, cur_tbl)]
                if ok:
                    pick = ok[0]
                else:
                    pen = 1.3
                    self.n_tbl_switch += 1
                t_ = recs[pick[2]][7] if len(recs[pick[2]]) >= 8 else 0
                if t_:
                    cur_tbl = t_
            for c in cands:
                if c is not pick:
                    heapq.heappush(h, c)
            i = pick[2]
            start = max(efree[e], pick[0]) + pen
            efree[e] = start + recs[i][3]
            if recs[i][4]:
                xs = max(start + recs[i][3], dma_free)
                dma_free = xs + (recs[i][5] - 2.0)
                finish[i] = dma_free + 2.0
            else:
                finish[i] = start + recs[i][5]
            self.sim_start[i] = start
            order[e].append(i)
            done += 1
            for sx in succs[i]:
                t = finish[i] + SEM_LAT
                if t > rdy_t[sx]:
                    rdy_t[sx] = t
                indeg[sx] -= 1
                if indeg[sx] == 0:
                    heapq.heappush(ready[recs[sx][0]], (rdy_t[sx], -prio[sx], sx))
        self.order = order
        self.sim_time = max(finish)

    def emit(self, st):
        nc = self.nc
        recs = self.recs
        self.finalize()
        if SIM_ONLY:
            return
        sems = {}
        for e in ENGS:
            sems[e] = st.enter_context(nc.semaphore("s_" + e))
        for q, nq in DMA_SEMS.items():
            for k in range(nq):
                sems[("dma", q, k)] = st.enter_context(nc.semaphore("s_dma_%s%d" % (q, k)))
        comp = [None] * len(recs)
        prev_same_sem = {}
        for e in ENGS:
            cc = 0
            dk = 0
            for i in self.order[e]:
                r = recs[i]
                if r[1] is None:
                    continue
                if r[4]:
                    nq = DMA_SEMS[e]
                    key = ("dma", e, dk % nq)
                    val = 16 * (dk // nq + 1)
                    comp[i] = (key, val)
                    if dk >= nq:
                        prev_same_sem[i] = (key, val - 16)
                    dk += 1
                else:
                    cc += 1
                    comp[i] = (e, cc)
        block = st.enter_context(nc.Block())

        def run(eng_name):
            def body(engine):
                known = {}
                for i in self.order[eng_name]:
                    r = recs[i]
                    need = {}
                    for d in r[2]:
                        k, v = comp[d]
                        if eng_name == "pe" and k == "pe":
                            continue
                        if v > need.get(k, 0):
                            need[k] = v
                    if i in prev_same_sem:
                        k, v = prev_same_sem[i]
                        if v > need.get(k, 0):
                            need[k] = v
                    for k, v in need.items():
                        if known.get(k, 0) >= v:
                            continue
                        known[k] = v
                        engine.wait_ge(sems[k], v)
                    if r[1] is None:
                        continue
                    ins = r[1](engine)
                    k, v = comp[i]
                    ins.then_inc(sems[k], 16 if r[4] else 1)
            return body

        block.tensor(run("pe"))
        block.scalar(run("act"))
        block.vector(run("dve"))
        block.gpsimd(run("pool"))
        block.sync(run("sp"))


def _tok(ap):
    return ap.name


def _n(ap):
    n = 1
    for d in ap.shape[1:]:
        n *= int(d)
    return n


def _bytes(ap):
    n = int(ap.shape[0]) * _n(ap)
    return n * (2 if ap.dtype == BF16 else 4)


class KB:
    def __init__(self, ntiles=4, nlayers=2, dbg=None, stage=99):
        self.ntiles = ntiles
        self.nlayers = nlayers
        self.dbg_names = dbg or []
        self.stage = stage
        self.nc = bass.Bass("TRN2", target_bir_lowering=False)
        self.st = contextlib.ExitStack()
        self.S = Sched(self.nc)
        self.dbg_out = {}
        self.out_tokens = []
        self.psum_names = set()

    def sb(self, name, shape, dt):
        if SIM_ONLY:
            return self.nc.dram_tensor(name, shape, dt, kind="Internal")
        return self.st.enter_context(self.nc.sbuf_tensor(name, shape, dt))

    def ps(self, name, shape, dt):
        self.psum_names.add(name)
        return self.st.enter_context(self.nc.psum_tensor(name, shape, dt))

    def dram_in(self, name, shape, dt=F32):
        return self.nc.dram_tensor(name, shape, dt, kind="ExternalInput").ap()

    def dram_out(self, name, shape, dt=F32):
        return self.nc.dram_tensor(name, shape, dt, kind="ExternalOutput").ap()

    def _rw(self, outs, ins, r, w):
        reads = list(r) if r is not None else [_tok(a) for a in ins if hasattr(a, "name")]
        writes = list(w) if w is not None else [_tok(a) for a in outs]
        ex = [t for t in reads if t in self.psum_names]
        if ex:
            reads = [t for t in reads if t not in self.psum_names]
            writes = writes + [t for t in ex if t not in writes]
        return reads, writes

    def act(self, out, in_, func, scale=1.0, bias=0.0, r=None, w=None):
        extra = [a for a in (scale, bias) if hasattr(a, "name")]
        reads, writes = self._rw([out], [in_] + extra, r, w)
        tbl = {AF.Exp: 1, AF.Ln: 1, AF.Sigmoid: 2, AF.Tanh: 2, AF.Silu: 3}.get(func, 0)
        self.S.op("act", lambda e: e.activation(out=out, in_=in_, func=func, scale=scale, bias=bias), reads, writes, dur=0.25 + _n(out) / 1400.0, tbl=tbl)

    def tt(self, eng, out, in0, in1, op, r=None, w=None):
        reads, writes = self._rw([out], [in0, in1], r, w)
        self.S.op(eng, lambda e: e.tensor_tensor(out=out, in0=in0, in1=in1, op=op), reads, writes, dur=self._vdur(eng, out))

    def ts(self, eng, out, in0, s1, s2, op0, op1=None, r=None, w=None):
        extra = [a for a in (s1, s2) if hasattr(a, "name")]
        reads, writes = self._rw([out], [in0] + extra, r, w)
        if op1 is None:
            self.S.op(eng, lambda e: e.tensor_scalar(out=out, in0=in0, scalar1=s1, scalar2=None, op0=op0), reads, writes, dur=self._vdur(eng, out))
        else:
            self.S.op(eng, lambda e: e.tensor_scalar(out=out, in0=in0, scalar1=s1, scalar2=s2, op0=op0, op1=op1), reads, writes, dur=self._vdur(eng, out))

    def stt(self, out, in0, scalar, in1, op0, op1, r=None, w=None):
        extra = [scalar] if hasattr(scalar, "name") else []
        reads, writes = self._rw([out], [in0, in1] + extra, r, w)
        self.S.op("dve", lambda e: e.scalar_tensor_tensor(out=out, in0=in0, scalar=scalar, in1=in1, op0=op0, op1=op1), reads, writes, dur=self._vdur("dve", out))

    def copy(self, eng, out, in_, r=None, w=None):
        reads, writes = self._rw([out], [in_], r, w)
        if eng == "act":
            self.S.op("act", lambda e: e.copy(out=out, in_=in_), reads, writes, dur=0.25 + _n(out) / 1400.0)
        else:
            self.S.op(eng, lambda e: e.tensor_copy(out=out, in_=in_), reads, writes,
                      dur=(0.3 + _n(out) / 300.0) if eng == "pool" else (0.1 + _n(out) / 1250.0))

    def recip(self, out, in_, r=None, w=None):
        reads, writes = self._rw([out], [in_], r, w)
        self.S.op("dve", lambda e: e.reciprocal(out=out, in_=in_), reads, writes, dur=0.1 + _n(out) / 155.0)

    def scan(self, out, d0, d1, r=None, w=None):
        reads, writes = self._rw([out], [d0, d1], r, w)
        self.S.op("dve", lambda e: e.tensor_tensor_scan(out=out, data0=d0, data1=d1, initial=0.0, op0=ALU.mult, op1=ALU.add), reads, writes, dur=0.12 + _n(out) / 480.0)

    def memset(self, eng, out, val, r=None, w=None):
        reads, writes = self._rw([out], [], r, w)
        self.S.op(eng, lambda e: e.memset(out, val), reads, writes, dur=self._vdur(eng, out))

    def mm(self, out, pairs, r=None, w=None):
        ins = []
        for a, b in pairs:
            ins += [a, b]
        reads, writes = self._rw([out], ins, r, w)
        n = len(pairs)

        def fn(e):
            last = None
            for i, (a, b) in enumerate(pairs):
                last = e.matmul(out, lhsT=a, rhs=b, start=(i == 0), stop=(i == n - 1))
            return last
        self.S.op("pe", fn, reads, writes, dur=sum(0.035 + max(_n(b_), 64) / 2000.0 for a_, b_ in pairs))

    def mmx(self, items):
        ins = []
        outs = []
        for o, a, b, st_, sp_ in items:
            ins += [a, b]
            outs.append(o)
        reads, writes = self._rw(outs[:1], ins, None, None)

        def fn(e):
            last = None
            for o, a, b, st_, sp_ in items:
                last = e.matmul(o, lhsT=a, rhs=b, start=st_, stop=sp_)
            return last
        self.S.op("pe", fn, reads, writes, dur=sum(0.035 + max(_n(b_), 64) / 2000.0 for o_, a_, b_, s1, s2 in items))

    def _vdur(self, eng, out):
        if eng == "pool":
            return 0.3 + _n(out) / 520.0
        return 0.15 + _n(out) / 900.0

    def tr(self, out, in_, ident, r=None, w=None):
        reads, writes = self._rw([out], [in_, ident], r, w)
        self.S.op("pe", lambda e: e.transpose(out=out, in_=in_, identity=ident), reads, writes, dur=0.1)

    def dma(self, q, out, in_, r=None, w=None):
        reads, writes = self._rw([out], [in_], r, w)
        self.S.dma(q, lambda e: e.dma_start(out=out, in_=in_), reads, writes, nbytes=max(_bytes(out), _bytes(in_)))

    def dump(self, name, ap, shape, dt=F32):
        if name not in self.dbg_names:
            return
        if dt != F32 or ap.dtype != F32:
            if not hasattr(self, "_dbgtmp"):
                self._dbgtmp = self.sb("dbgtmp", [128, 1024], F32)
            tmp = self._dbgtmp[0:shape[0], 0:shape[1]]
            self.copy("dve", tmp, ap)
            ap = tmp
        o = self.dram_out("dbg_" + name, list(shape))
        self.dma("sp", o, ap)
        self.out_tokens.append(_tok(o))
        self.dbg_out[name] = "dbg_" + name


def build(ntiles=4, nlayers=2, dbg=None, stage=99):
    K = KB(ntiles, nlayers, dbg, stage)
    nc = K.nc
    xT = K.dram_in("xT", [D, T])
    cT = K.dram_in("cT", [128, 8])
    ada = [K.dram_in("ada%d" % l, [D, 6 * D]) for l in range(L)]
    vecs_d = [K.dram_in("vecs%d" % l, [128, NV]) for l in range(L)]
    lay = [_wbig_layout(l) for l in range(L)]
    wbig = [K.dram_in("wbig%d" % l, [128, lay[l][1]]) for l in range(L)]
    outT = K.dram_out("outT", [D, T])

    sb, ps = K.sb, K.ps
    ident16 = sb("ident16", [128, 128], BF16)
    ones16 = sb("ones16", [128, 128], BF16)
    bd16 = sb("bd16", [128, 128], BF16)
    onesf = sb("onesf", [128, 128], F32)
    mask512 = sb("mask512", [128, 512], BF16)
    masksl = sb("masksl", [128, 128], BF16)
    rst = sb("rst", [128, 512], BF16)
    vecs = [sb("vecs_s%d" % l, [128, NV], F32) for l in range(L)]
    dv = [sb("dv%d" % l, [128, ND], F32) for l in range(L)]
    c32 = sb("c32", [128, 8], F32)
    epsc = sb("epsc", [128, 4], F32)
    c16 = sb("c16", [128, 8], BF16)
    S32 = [sb("S32_%d" % l, [128, 8, 64], F32) for l in range(L)]
    S16 = [sb("S16_%d" % l, [128, 8, 128], BF16) for l in range(L)]
    ctail = [sb("ctail%d" % l, [128, 8, 30], F32) for l in range(L)]
    carry = [sb("carry%d" % l, [128, 27], F32) for l in range(L)]
    xt = [sb("xt%d" % k, [128, TT], F32) for k in range(NB)]
    ht = [sb("ht%d" % k, [128, TT], BF16) for k in range(NB)]
    merged = [sb("mg%d" % k, [128, TT], BF16) for k in range(16)]
    vf = [sb("vf%d" % k, [128, TT], BF16) for k in range(NB)]
    NWP = 3
    WPN = 4608
    wpool = [sb("wp%d" % i, [128, WPN], BF16) for i in range(NWP)]
    lo16 = [sb("lo16_%d" % j, [128, TT], BF16) for j in range(2)]
    vlo16 = sb("vlo16", [128, TT], BF16)
    names32 = ["r32", "k32", "v32", "sg32", "a32", "sv32", "dd32", "cs32", "E1", "E3",
               "kk32", "nrm", "km", "b32", "y32", "yc", "sd"]
    W = {n: sb(n, [128, TT], F32) for n in names32}
    W["d2"] = W["dd32"]; W["E2"] = W["dd32"]; W["d4"] = W["sv32"]; W["E4"] = W["sv32"]; W["rn"] = W["nrm"]
    W["kkn"] = W["kk32"]; W["t1"] = W["km"]; W["rsd"] = W["sd"]
    W["yn"] = W["yc"]; W["yg"] = W["yc"]; W["o1"] = W["yc"]; W["o2"] = W["yc"]
    names16 = ["kk2", "Kh16", "Bh16", "v16", "rk16", "y16", "yc2", "sqa", "sqb", "sqc"]
    H = {n: sb(n, [128, TT], BF16) for n in names16}
    sq16 = [H[n] for n in ("kk2", "Kh16", "Bh16", "v16", "rk16", "sqa", "sqb", "sqc")]
    g32p = [sb("g32_%d" % i, [128, TT], F32) for i in range(NHO)]
    bon32p = [sb("bon32_%d" % i, [128, TT], F32) for i in range(NHO)]
    gA16p = [sb("gA16_%d" % i, [128, TT], BF16) for i in range(NHO)]
    kt16p = [sb("kt16_%d" % i, [128, 2, TT], BF16) for i in range(NHO)]
    bt16p = [sb("bt16_%d" % i, [128, 2, TT], BF16) for i in range(NHO)]
    hmask = sb("hmask", [128, 2], F32)
    WCp = [sb("WC_%d" % i, [128, NCH], F32) for i in range(NHO)]
    lo32 = [W["yc"], W["y32"]]
    vlo32 = bon32p[0]
    rs32 = W["nrm"]; rstd = W["sd"]; t32 = [W["yc"], W["y32"]]
    ARp = [sb("AR_%d" % i, [128, NCH, 2, C], BF16) for i in range(NHO)]
    KBtokp = [sb("KBtok_%d" % i, [128, 8, 128], BF16) for i in range(NHO)]
    Vtokp = [sb("Vtok_%d" % i, [128, 4, 128], BF16) for i in range(NHO)]
    Amatp = [[[sb("Amat%d_%d_%d" % (q, h, c), [128, 512], BF16) for c in range(NCH)] for h in range(2)] for q in range(NQ_AMAT)] * (2 // NQ_AMAT)
    Tfinp = [[[sb("Tfin%d_%d_%d" % (q, h, c), [128, 128], BF16) for c in range(NCH)] for h in range(2)] for q in range(NQ_AMAT)] * (2 // NQ_AMAT)
    Q0 = [sb("Q0_%d" % i, [128, 128], BF16) for i in range(4)]
    PQT = [[sb("PQT%d_%d" % (i, j), [128, 3, 128], BF16) for j in range(2)] for i in range(4)]
    XT16 = sb("XT16", [128, 128], BF16)
    UT16 = sb("UT16", [128, 128], BF16)
    sgb = sb("sgb16", [128, TT], BF16)
    sgb2 = [sb("sgbB%d" % i, [128, TT], BF16) for i in range(2)]
    gbuf = sb("gbuf", [128, 30 + TT], F32)
    cacc = [sb("cacc%d" % i, [128, TT], F32) for i in range(2)]
    ctmp = [sb("ctmp%d" % i, [128, TT], F32) for i in range(2)]
    z16 = [sb("z16_%d" % k, [128, TT], BF16) for k in range(NB)]

    def halves(t):
        v = t[:].bitcast(BF16)
        return [v[:, 0:TT], v[:, TT:2 * TT]]
    hidv = []
    for t_ in [W[n] for n in ("r32", "k32", "v32", "sg32", "a32", "sv32", "dd32", "kk32", "cs32", "E1", "E3", "km", "b32")] + [g32p[0], g32p[1], bon32p[0]]:
        hidv += halves(t_)
    mmb = [ps("mmb%d" % i, [128, 512], F32) for i in range(2)]
    dblb = [ps("dblb%d" % i, [128, 512], F32) for i in range(4)]
    seqb = ps("seqb", [128, 512], F32)
    yb = ps("yb", [128, 512], F32)
    mm_rr = [0]

    def bank():
        b = mmb[mm_rr[0] % 2]
        mm_rr[0] += 1
        return b

    def dbl_region(i, j):
        if j < 2:
            col = (i % 2) * 256 + j * 128
            return dblb[i // 2][:, col:col + 128]
        return dblb[2][:, i * 128:(i + 1) * 128]

    wp_rr = [0]
    PF = 2
    piece_list = []
    for pc in range(12):
        piece_list.append(("ada", 0, pc))
    ada1_left = list(range(12)) if K.nlayers > 1 else []
    for it_ in range(K.ntiles):
        for l_ in range(K.nlayers):
            if l_ == 1:
                while ada1_left:
                    piece_list.append(("ada", 1, ada1_left.pop(0)))
            if K.stage >= 4:
                keys = ["lo", "hp0"]
                for i in range(8):
                    keys.append("cb%d" % i)
                    if i + 1 < 8:
                        keys.append("hp%d" % (i + 1))
                keys += ["gb%d" % i for i in range(8)] + ["wo%d" % i for i in range(4)]
            else:
                keys = ["lo"] + ["hp%d" % i for i in range(8)]
            if K.stage >= 5:
                keys += ["w1%d" % i for i in range(8)] + ["w2%d" % i for i in range(8)]
            for n_, key in enumerate(keys):
                piece_list.append(("w", l_, key))
                if it_ == 0 and l_ == 0 and n_ % 3 == 2 and ada1_left:
                    piece_list.append(("ada", 1, ada1_left.pop(0)))
    issued = [0]
    piece_dst = {}

    def _issue(idx):
        kind, l_, key = piece_list[idx]
        buf = wpool[idx % NWP]
        if kind == "ada":
            src = ada[l_].rearrange("(k p) c -> p k c", p=128)[:, :, key * 512:(key + 1) * 512]
            dst = buf[:, 0:4096].rearrange("p (k c) -> p k c", c=512)
        elif key.startswith("hp"):
            off, k, c = lay[l_][0][key]
            src = wbig[l_][:, off:off + 4608].rearrange("p (a b) -> p a b", b=512)
            K.dma("pool", buf[:, 0:4608].rearrange("p (a b) -> p a b", b=512), src)
            piece_dst[idx] = (buf[:, 0:4096].rearrange("p (k c) -> p k c", c=512), buf[:, 4096:4608].rearrange("p (a c) -> p a c", c=128))
            return
        else:
            off, k, c = lay[l_][0][key]
            src = wbig[l_][:, off:off + k * c].rearrange("p (k c) -> p k c", c=c)
            dst = buf[:, 0:k * c].rearrange("p (k c) -> p k c", c=c)
        K.dma("pool", dst, src)
        piece_dst[idx] = dst

    def next_piece(expect):
        idx = wp_rr[0]
        wp_rr[0] += 1
        assert piece_list[idx] == expect, (piece_list[idx], expect)
        while issued[0] < min(len(piece_list), idx + PF + 1):
            _issue(issued[0])
            issued[0] += 1
        return piece_dst.pop(idx)

    def ada_piece():
        kind, l_, pc = piece_list[wp_rr[0]]
        dst = next_piece((kind, l_, pc))
        p = bank()
        for m in range(4):
            K.mm(p[:, m:m + 1], [(dst[:, k, m * 128:(m + 1) * 128], c16[:, k:k + 1]) for k in range(8)])
        K.tt("dve", dv[l_][:, DV_MOD + 4 * pc:DV_MOD + 4 * pc + 4], p[:, 0:4], vecs[l_][:, V_ADAB + 4 * pc:V_ADAB + 4 * pc + 4], ALU.add)
        if pc == 11:
            K.stt(dv[l_][:, DV_GSCM:DV_GSCM + 8], dv[l_][:, DV_MOD + M_SCM:DV_MOD + M_SCM + 8], 1.0, vecs[l_][:, V_GMIX:V_GMIX + 8], ALU.add, ALU.mult)
            K.stt(dv[l_][:, DV_GSCF:DV_GSCF + 8], dv[l_][:, DV_MOD + M_SCF:DV_MOD + M_SCF + 8], 1.0, vecs[l_][:, V_GFFN:V_GFFN + 8], ALU.add, ALU.mult)
            K.ts("dve", dv[l_][:, DV_OMU:DV_OMU + 27], vecs[l_][:, V_MU:V_MU + 27], -1.0, 1.0, ALU.mult, ALU.add)
            K.dump("dv%d" % l_, dv[l_][:], [128, ND])

    def load_piece(l, key):
        while wp_rr[0] < len(piece_list) and piece_list[wp_rr[0]][0] == "ada":
            ada_piece()
        return next_piece(("w", l, key))

    K.memset("pool", onesf[:], 1.0)
    K.memset("pool", ones16[:], 1.0)
    K.memset("pool", bd16[:], 0.0)
    K.memset("pool", bd16[0:64, 0:64], 1.0)
    K.memset("pool", bd16[64:128, 64:128], 1.0)

    def asel(out, in_, pattern, op, base, cm):
        K.S.op("pool", lambda e: e.affine_select(out=out, in_=in_, pattern=pattern, compare_op=op, fill=0.0, base=base, channel_multiplier=cm),
               [_tok(in_)], [_tok(out)], dur=0.3)
    asel(ident16[:], ones16[:], [[1, 128]], ALU.is_equal, 0, -1)
    for j in range(4):
        asel(mask512[:, j * 128:(j + 1) * 128], onesf[:, 0:128], [[1, 128]], ALU.is_gt if j % 2 == 0 else ALU.is_ge, 0, -1)
    asel(masksl[:], onesf[:, 0:128], [[-1, 128]], ALU.is_gt, 0, 1)
    K.memset("pool", rst[:], 1.0)
    for j in range(NCH):
        K.memset("pool", rst[:, j * C:j * C + 1], 0.0)
    for l in range(L):
        K.dma("sp", vecs[l][:], vecs_d[l])
        K.memset("pool", S32[l][:], 0.0)
        K.memset("pool", S16[l][:], 0.0)
        K.memset("pool", ctail[l][:], 0.0)
        K.memset("pool", carry[l][:], 0.0)
    K.dma("sp", c32[:], cT)
    K.memset("pool", hmask[:], 0.0)
    K.memset("pool", vlo16[:], 0.0)
    K.memset("pool", hmask[0:64, 0:1], 1.0)
    K.memset("pool", hmask[64:128, 1:2], 1.0)
    K.memset("pool", epsc[:, 0:1], RMS_EPS)
    K.memset("pool", epsc[:, 1:2], LN_EPS)
    K.memset("pool", epsc[:, 2:3], GN_EPS)
    K.memset("pool", epsc[:, 3:4], 1e-24)
    K.act(c16[:], c32[:], AF.Silu)

    for pc in range(12):
        ada_piece()

    def rmsnorm_to_ht(l, gsc_col, sh_col):
        for k in range(NB):
            K.act(sq16[k][:], xt[k][:], AF.Square)
        p = bank()
        K.mm(p[:], [(ones16[:], sq16[k][:]) for k in range(NB)])
        K.act(rs32[:], p[:], AF.Ln, scale=1.0 / D, bias=epsc[:, 0:1])
        K.act(rstd[:], rs32[:], AF.Exp, scale=-0.5)
        for k in range(NB):
            t = t32[k % 2]
            K.stt(t[:], xt[k][:], dv[l][:, gsc_col + k:gsc_col + k + 1], rstd[:], ALU.mult, ALU.mult)
            K.act(ht[k][:], t[:], AF.Identity, bias=dv[l][:, sh_col + k:sh_col + k + 1])

    def proj(wp, j0, M=128):
        p = bank()
        K.mm(p[0:M, :], [(wp[:, k, j0:j0 + M], ht[k][:]) for k in range(NB)])
        return p

    def shift(l, p, mucol, dst, M=128):
        mu = vecs[l][0:M, V_MU + mucol:V_MU + mucol + 1]
        omu = dv[l][0:M, DV_OMU + mucol:DV_OMU + mucol + 1]
        cy = carry[l][0:M, mucol:mucol + 1]
        K.act(dst[0:M, :], p[0:M, :], AF.Identity, scale=omu)
        K.stt(dst[0:M, 1:TT], p[0:M, 0:TT - 1], mu, dst[0:M, 1:TT], ALU.mult, ALU.add)
        K.stt(dst[0:M, 0:1], cy, mu, dst[0:M, 0:1], ALU.mult, ALU.add)
        K.copy("dve", cy, p[0:M, TT - 1:TT])

    xTv = xT.rearrange("(k p) t -> k p t", p=128)
    oTv = outT.rearrange("(k p) t -> k p t", p=128)
    for it in range(K.ntiles):
        t0 = it * TT
        for k in range(NB):
            K.dma("sp", xt[k][:], xTv[k, :, t0:t0 + TT])
        for l in range(K.nlayers):
            V = vecs[l]
            DVl = dv[l]
            rmsnorm_to_ht(l, DV_GSCM, DV_MOD + M_SHM)
            if it == 0 and l == 0:
                for k in (0, 7):
                    K.dump("ht%d" % k, ht[k][:], [128, TT])
            wlo = load_piece(l, "lo")
            for j in range(2):
                p = proj(wlo, j * 128)
                shift(l, p, 24 + j, lo32[j])
            K.act(lo16[0][0:64, :], lo32[0][0:64, :], AF.Tanh)
            K.copy("act", lo16[0][64:128, :], lo32[0][64:128, :])
            K.act(lo16[1][:], lo32[1][:], AF.Sigmoid)
            if l >= 1:
                p = proj(wlo, 256, M=32)
                shift(l, p, 26, vlo32, M=32)
                K.copy("act", vlo16[0:32, :], vlo32[0:32, :])
            if K.stage < 1:
                continue
            def front(hp, par):
                AR = ARp[par]; KBtok = KBtokp[par]; Vtok = Vtokp[par]
                g32 = g32p[par]; bon32 = bon32p[par]; gA16 = gA16p[par]; kt16 = kt16p[par]; bt16 = bt16p[par]; WC = WCp[par]
                whp, wsl = load_piece(l, "hp%d" % hp)
                p = proj(whp, 0);   shift(l, p, hp, W["r32"])
                yield
                p = proj(whp, 128); shift(l, p, 8 + hp, W["k32"])
                yield
                p = proj(whp, 256); shift(l, p, 16 + hp, W["v32"])
                yield
                p = proj(whp, 384); K.act(gA16[:], p[:], AF.Sigmoid)
                p = bank(); K.mm(p[:], [(wsl[:, 0, :], lo16[0][:])])
                K.act(W["sg32"][:], p[:], AF.Sigmoid, bias=V[:, V_W0 + hp:V_W0 + hp + 1])
                yield
                p = bank(); K.mm(p[:], [(wsl[:, 3, :], lo16[0][:])])
                K.act(W["a32"][:], p[:], AF.Sigmoid, bias=V[:, V_A0 + hp:V_A0 + hp + 1])
                p = bank(); K.mm(p[:], [(wsl[:, 1, :], lo16[1][:])])
                K.tt("dve", g32[:], p[:], gA16[:], ALU.mult)
                yield
                if l >= 1:
                    p = bank(); K.mm(p[:], [(wsl[:, 2, :], vlo16[:])])
                    K.act(W["sv32"][:], p[:], AF.Sigmoid, bias=V[:, V_V0 + hp:V_V0 + hp + 1])
                    K.tt("pool", W["dd32"][:], vf[hp][:], W["v32"][:], ALU.subtract)
                    K.tt("pool", W["dd32"][:], W["dd32"][:], W["sv32"][:], ALU.mult)
                    K.tt("pool", W["v32"][:], W["v32"][:], W["dd32"][:], ALU.add)
                else:
                    K.copy("pool", vf[hp][:], W["v32"][:])
                yield
                K.scan(W["cs32"][:], rst[:], W["sg32"][:])
                K.act(W["E1"][:], W["cs32"][:], AF.Exp, scale=CEXP)
                K.act(W["E3"][:], W["cs32"][:], AF.Exp, scale=-CEXP)
                K.tt("pool", W["d2"][:], W["cs32"][:], W["sg32"][:], ALU.subtract)
                K.act(W["E2"][:], W["d2"][:], AF.Exp, scale=CEXP)
                yield
                cs3 = W["cs32"][:].rearrange("p (c t) -> p c t", t=C)
                K.tt("pool", W["d4"][:].rearrange("p (c t) -> p c t", t=C), cs3, cs3[:, :, C - 1:C].to_broadcast([128, NCH, C]), ALU.subtract)
                K.act(W["E4"][:], W["d4"][:], AF.Exp, scale=-CEXP)
                K.copy("dve", WC[:], W["E1"][:].rearrange("p (c t) -> p c t", t=C)[:, :, C - 1])
                yield
                K.act(W["kk32"][:], W["k32"][:], AF.Identity, scale=V[:, V_KK + hp:V_KK + hp + 1])
                K.act(H["kk2"][:], W["kk32"][:], AF.Square)
                p = bank(); K.mm(p[:], [(bd16[:], H["kk2"][:])])
                K.act(W["nrm"][:], p[:], AF.Ln, bias=epsc[:, 3:4])
                K.act(W["rn"][:], W["nrm"][:], AF.Exp, scale=-0.5)
                yield
                K.tt("pool", W["kkn"][:], W["kk32"][:], W["rn"][:], ALU.mult)
                K.ts("dve", W["t1"][:], W["a32"][:], -1.0, V[:, V_KA + hp:V_KA + hp + 1], ALU.add, ALU.mult)
                K.stt(W["km"][:], W["t1"][:], 1.0, W["k32"][:], ALU.add, ALU.mult)
                K.tt("pool", W["b32"][:], W["kkn"][:], W["a32"][:], ALU.mult)
                yield
                K.tt("dve", AR[:, :, 1, :], W["r32"][:].rearrange("p (c t) -> p c t", t=C), W["E1"][:].rearrange("p (c t) -> p c t", t=C), ALU.mult)
                K.stt(AR[:, :, 0, :], W["kkn"][:].rearrange("p (c t) -> p c t", t=C), -1.0, W["E2"][:].rearrange("p (c t) -> p c t", t=C), ALU.mult, ALU.mult)
                for h_ in range(2):
                    K.stt(kt16[:, h_, :], W["km"][:], hmask[:, h_:h_ + 1], W["E3"][:], ALU.mult, ALU.mult)
                    K.stt(bt16[:, h_, :], W["b32"][:], hmask[:, h_:h_ + 1], W["E3"][:], ALU.mult, ALU.mult)
                yield
                K.tt("pool", H["Kh16"][:], W["km"][:], W["E4"][:], ALU.mult)
                K.tt("dve", H["Bh16"][:], W["b32"][:], W["E4"][:], ALU.mult)
                K.copy("pool", H["v16"][:], W["v32"][:])
                yield
                K.stt(H["rk16"][:], W["r32"][:], V[:, V_RK + hp:V_RK + hp + 1], W["km"][:], ALU.mult, ALU.mult)
                p = bank(); K.mm(p[:], [(bd16[:], H["rk16"][:])])
                K.tt("dve", bon32[:], p[:], W["v32"][:], ALU.mult)
                yield
                trb = bank()
                trv = trb[:].bitcast(BF16).rearrange("p (a t) -> p a t", t=128)
                for c in range(NCH):
                    K.tr(trv[:, c, :], H["Kh16"][:, c * C:(c + 1) * C], ident16[:])
                    K.tr(trv[:, 4 + c, :], H["Bh16"][:, c * C:(c + 1) * C], ident16[:])
                K.copy("act", KBtok[:], trv)
                yield
                trb = bank()
                trv = trb[:].bitcast(BF16).rearrange("p (a t) -> p a t", t=128)
                for c in range(NCH):
                    K.tr(trv[:, c, :], H["v16"][:, c * C:(c + 1) * C], ident16[:])
                K.copy("dve", Vtok[:], trv[:, 0:4, :])
                yield

            def back(hp, par):
                Amat = Amatp[hp % 2]; Tfin = Tfinp[hp % 2]
                AR = ARp[par]; KBtok = KBtokp[par]; Vtok = Vtokp[par]
                g32 = g32p[par]; bon32 = bon32p[par]; gA16 = gA16p[par]; kt16 = kt16p[par]; bt16 = bt16p[par]; WC = WCp[par]
                for h in range(2):
                    hs = slice(h * 64, h * 64 + 64)
                    for c in range(NCH):
                        ck = slice(c * C, (c + 1) * C)
                        bk = dblb[c]
                        ar = AR[:, c, :, :].rearrange("p a t -> p (a t)")
                        K.mm(bk[:, 0:256], [(bt16[:, h, ck], ar)])
                        K.mm(bk[:, 256:512], [(kt16[:, h, ck], ar)])
                        if MASK_POOL:
                            K.copy("act", Amat[h][c][:], bk[:])
                            am = Amat[h][c][:].rearrange("p (j i t) -> p j i t", j=2, i=2)
                            K.S.op("pool", (lambda am=am: (lambda e: e.affine_select(out=am, in_=am, pattern=[[0, 2], [1, 2], [1, 128]], compare_op=ALU.is_gt,
                                                                                     fill=0.0, base=0, channel_multiplier=-1)))(),
                                   [_tok(am)], [_tok(am)], dur=0.55)
                        else:
                            K.tt("dve", Amat[h][c][:], bk[:], mask512[:], ALU.mult)
                        K.mm(bk[:, 0:128], [(AR[:, c, 0, :], bt16[:, h, ck])])
                        if MASK_POOL:
                            K.copy("act", Q0[c][:], bk[:, 0:128])
                            q0 = Q0[c][:]
                            K.S.op("pool", (lambda q0=q0: (lambda e: e.affine_select(out=q0, in_=q0, pattern=[[-1, 128]], compare_op=ALU.is_gt,
                                                                                     fill=0.0, base=0, channel_multiplier=1)))(),
                                   [_tok(q0)], [_tok(q0)], dur=0.25)
                        else:
                            K.tt("dve", Q0[c][:], bk[:, 0:128], masksl[:], ALU.mult)
                        K.tt("pool", PQT[c][1][:, 1, :], Amat[h][c][:, 0:128], ident16[:], ALU.add)
                        yield
                    for c in range(NCH):
                        bk = dblb[c]
                        P0 = Amat[h][c][:, 0:128]
                        K.mm(bk[:, 0:128], [(Q0[c][:], P0)])
                        K.mm(bk[:, 256:384], [(P0, Q0[c][:])])
                        K.copy("act" if c % 2 == 0 else "dve", PQT[c][1][:, 0:3:2, :], bk[:, 0:384].rearrange("p (a t) -> p a t", t=128)[:, 0:3:2, :])
                    yield
                    for lv in range(1, 7):
                        cur, nxt = lv % 2, (lv + 1) % 2
                        for c in range(NCH):
                            bk = dblb[c]
                            Pk = PQT[c][cur][:, 0, :]
                            Tk = PQT[c][cur][:, 1, :]
                            Qk = PQT[c][cur][:, 2, :]
                            PTk = PQT[c][cur][:, 0:2, :].rearrange("p a t -> p (a t)")
                            eng = "dve" if (c + lv) % EVAC_MOD == 0 else "act"
                            if lv < 6:
                                K.mmx([(bk[:, 256:384], Pk, Qk, True, True),
                                       (bk[:, 0:256], Qk, PTk, True, False),
                                       (bk[:, 128:256], ident16[:], Tk, False, True)])
                                K.copy(eng, PQT[c][nxt][:], bk[:, 0:384].rearrange("p (a t) -> p a t", t=128))
                            else:
                                K.mm(bk[:, 128:256], [(Qk, Tk), (ident16[:], Tk)])
                                K.copy(eng, Tfin[h][c][:], bk[:, 128:256])
                        yield
                h0 = slice(0, 64); h1 = slice(64, 128)
                for c in range(NCH):
                    ck = slice(c * C, (c + 1) * C)
                    K.mmx([(seqb[:, 0:128], AR[:, c, 0, :], S16[l][:, hp, :], True, False),
                           (seqb[:, 0:64], Amat[0][c][:, 256:384], Vtok[:, c, h0], False, False),
                           (seqb[:, 64:128], Amat[1][c][:, 256:384], Vtok[:, c, h1], False, True)])
                    K.copy("act", XT16[:], seqb[:, 0:128])
                    yield
                    for h in range(2):
                        hs = slice(h * 64, h * 64 + 64)
                        K.mm(seqb[:, 128 + h * 64:128 + (h + 1) * 64], [(Tfin[h][c][:], XT16[:, hs])])
                    K.copy("dve", UT16[:], seqb[:, 128:256])
                    yield
                    K.mmx([(yb[:, ck], S16[l][:, hp, :], AR[:, c, 1, :], True, False),
                           (yb[h0, ck], UT16[:, h0], Amat[0][c][:, 128:256], False, False),
                           (yb[h0, ck], Vtok[:, c, h0], Amat[0][c][:, 384:512], False, False),
                           (yb[h1, ck], UT16[:, h1], Amat[1][c][:, 128:256], False, False),
                           (yb[h1, ck], Vtok[:, c, h1], Amat[1][c][:, 384:512], False, True)])
                    for h in range(2):
                        hs = slice(h * 64, h * 64 + 64)
                        K.mm(seqb[hs, 256:320], [(KBtok[:, 4 + c, hs], UT16[:, hs]),
                                                  (KBtok[:, c, hs], Vtok[:, c, hs])])
                    K.stt(S32[l][:, hp, :], S32[l][:, hp, :], WC[:, c:c + 1], seqb[:, 256:320], ALU.mult, ALU.add)
                    K.copy("act", S16[l][h0, hp, 0:64], S32[l][h0, hp, :])
                    K.copy("dve", S16[l][h1, hp, 64:128], S32[l][h1, hp, :])
                    yield
                K.copy("act", W["y32"][:], yb[:])
                K.copy("dve", H["y16"][:], yb[:])
                if it == 0 and l == 0 and hp == 0:
                    K.dump("y32", W["y32"][:], [128, TT])
                yield
                p = bank(); K.mm(p[:], [(bd16[:], H["y16"][:])])
                K.stt(W["yc"][:], p[:], -1.0 / 64, W["y32"][:], ALU.mult, ALU.add)
                yield
                K.act(H["yc2"][:], W["yc"][:], AF.Square)
                p = bank(); K.mm(p[:], [(bd16[:], H["yc2"][:])])
                K.act(W["sd"][:], p[:], AF.Ln, scale=1.0 / 64, bias=epsc[:, 2:3])
                yield
                K.act(W["rsd"][:], W["sd"][:], AF.Exp, scale=-0.5)
                K.stt(W["yn"][:], W["yc"][:], V[:, V_GNG + hp:V_GNG + hp + 1], W["rsd"][:], ALU.mult, ALU.mult)
                K.stt(W["o1"][:], W["yn"][:], V[:, V_GNB + hp:V_GNB + hp + 1], bon32[:], ALU.add, ALU.add)
                yield
                K.tt("pool", merged[hp][:], W["o1"][:], g32[:], ALU.mult)
                if it == 0 and l == 0 and hp in (0, 7):
                    K.dump("mg%d" % hp, merged[hp][:], [128, TT], BF16)
                yield

            def drain(g):
                for _ in g:
                    pass

            def interleave(g1, g2):
                a1 = a2 = True
                while a1 or a2:
                    if a1:
                        try:
                            next(g1)
                        except StopIteration:
                            a1 = False
                    if a2:
                        try:
                            next(g2)
                        except StopIteration:
                            a2 = False

            def conv(cb):
                wcb = load_piece(l, "cb%d" % cb)
                pb = proj(wcb, 128)
                K.act(sgb[:], pb[:], AF.Sigmoid)
                pa = proj(wcb, 0)
                K.copy("pool", gbuf[:, 0:30], ctail[l][:, cb, :])
                K.tt("dve", gbuf[:, 30:30 + TT], pa[:], sgb[:], ALU.mult)
                K.copy("pool", ctail[l][:, cb, :], gbuf[:, TT:TT + 30])
                cw = lambda k: V[:, V_CW + cb * 31 + k:V_CW + cb * 31 + k + 1]
                NT_DVE = NT_DVE_G
                if NT_DVE >= 31:
                    K.ts("dve", cacc[0][:], gbuf[:, 0:TT], cw(0), V[:, V_CB + cb:V_CB + cb + 1], ALU.mult, ALU.add)
                    K.ts("dve", cacc[1][:], gbuf[:, 1:1 + TT], cw(1), None, ALU.mult)
                    for k in range(2, 31):
                        a_ = cacc[k % 2]
                        K.stt(a_[:], gbuf[:, k:k + TT], cw(k), a_[:], ALU.mult, ALU.add)
                    K.tt("pool", z16[cb][:], cacc[0][:], cacc[1][:], ALU.add)
                    wgb = None
                    return
                K.ts("dve", cacc[0][:], gbuf[:, 0:TT], cw(0), V[:, V_CB + cb:V_CB + cb + 1], ALU.mult, ALU.add)
                for k in range(1, NT_DVE):
                    K.stt(cacc[0][:], gbuf[:, k:k + TT], cw(k), cacc[0][:], ALU.mult, ALU.add)
                K.act(cacc[1][:], gbuf[:, NT_DVE:NT_DVE + TT], AF.Identity, scale=cw(NT_DVE))
                for k in range(NT_DVE + 1, 31):
                    tp_ = ctmp[k % 2]
                    K.act(tp_[:], gbuf[:, k:k + TT], AF.Identity, scale=cw(k))
                    K.tt("pool", cacc[1][:], cacc[1][:], tp_[:], ALU.add)
                K.tt("pool", z16[cb][:], cacc[0][:], cacc[1][:], ALU.add)

            if K.stage < 3:
                for hp in range(8):
                    drain(front(hp, hp % NHO))
            else:
                drain(front(0, 0))
                for hp in range(8):
                    if K.stage >= 4:
                        conv(hp)
                    if hp + 1 < 8:
                        interleave(back(hp, hp % NHO), front(hp + 1, (hp + 1) % NHO))
                    else:
                        drain(back(hp, hp % NHO))
            if K.stage < 4:
                continue
            if it == 0 and l == 0:
                K.dump("z0", z16[0][:], [128, TT], BF16)
            p = bank(); K.mm(p[:], [(ones16[:], z16[cb][:]) for cb in range(8)])
            K.act(W["nrm"][:], p[:], AF.Identity, scale=-1.0 / D)
            for cb in range(8):
                t = t32[cb % 2]
                K.tt("dve", t[:], z16[cb][:], W["nrm"][:], ALU.add)
                K.act(sq16[cb][:], t[:], AF.Square)
            p = bank(); K.mm(p[:], [(ones16[:], sq16[cb][:]) for cb in range(8)])
            K.act(W["sd"][:], p[:], AF.Ln, scale=1.0 / D, bias=epsc[:, 1:2])
            K.act(W["sd"][:], W["sd"][:], AF.Exp, scale=-0.5)
            for cb in range(8):
                t = t32[cb % 2]
                wgb = load_piece(l, "gb%d" % cb)
                pg = proj(wgb, 0)
                sg_ = sgb2[cb % 2]
                K.act(sg_[:], pg[:], AF.Sigmoid)
                K.tt("dve", t[:], z16[cb][:], W["nrm"][:], ALU.add)
                K.tt("pool", t[:], t[:], W["sd"][:], ALU.mult)
                K.act(t[:], t[:], AF.Silu, scale=V[:, V_LNG + cb:V_LNG + cb + 1], bias=V[:, V_LNB + cb:V_LNB + cb + 1])
                K.tt("pool", merged[8 + cb][:], t[:], sg_[:], ALU.mult)
            if it == 0 and l == 0:
                K.dump("mg8", merged[8][:], [128, TT], BF16)
            for g in range(4):
                wo = load_piece(l, "wo%d" % g)
                for o2 in range(2):
                    ob = g * 2 + o2
                    p = bank()
                    K.mm(p[:], [(wo[:, o2 * 16 + kc, :], merged[kc][:]) for kc in range(16)])
                    K.stt(xt[ob][:], p[:], DVl[:, DV_MOD + M_GTM + ob:DV_MOD + M_GTM + ob + 1], xt[ob][:], ALU.mult, ALU.add)
            if it == 0 and l == 0:
                K.dump("xmix0", xt[0][:], [128, TT])
            if K.stage < 5:
                continue
            rmsnorm_to_ht(l, DV_GSCF, DV_MOD + M_SHF)
            for pc in range(8):
                w1 = load_piece(l, "w1%d" % pc)
                for j in range(4):
                    hb = pc * 4 + j
                    p = proj(w1, j * 128)
                    t = t32[hb % 2]
                    K.act(t[:], p[:], AF.Relu)
                    K.tt("pool", hidv[hb], t[:], t[:], ALU.mult)
            for ob in range(8):
                w2 = load_piece(l, "w2%d" % ob)
                p = bank()
                K.mm(p[:], [(w2[:, hb, :], hidv[hb]) for hb in range(32)])
                K.stt(xt[ob][:], p[:], DVl[:, DV_MOD + M_GTF + ob:DV_MOD + M_GTF + ob + 1], xt[ob][:], ALU.mult, ALU.add)
            if it == 0 and l == 0:
                K.dump("xffn0", xt[0][:], [128, TT])
        for k in range(NB):
            K.act(sq16[k][:], xt[k][:], AF.Square)
        p = bank()
        K.mm(p[:], [(ones16[:], sq16[k][:]) for k in range(NB)])
        K.act(rs32[:], p[:], AF.Ln, scale=1.0 / D, bias=epsc[:, 0:1])
        K.act(rstd[:], rs32[:], AF.Exp, scale=-0.5)
        for k in range(NB):
            t = t32[k % 2]
            K.stt(t[:], xt[k][:], vecs[0][:, V_FG + k:V_FG + k + 1], rstd[:], ALU.mult, ALU.mult)
            K.dma("sp", oTv[k, :, t0:t0 + TT], t[:], w=["outT%d_%d" % (it, k)])
            K.out_tokens.append("outT%d_%d" % (it, k))
    K.S.wait_all("sp", K.out_tokens)
    K.S.emit(K.st)
    K.st.close()
    return K


def dblb_view(dblb, c):
    col = (c % 2) * 256
    return dblb[c // 2][:, col:col + 256].rearrange("p (a t) -> p a t", t=128)


def _fm(v):
    v = np.asarray(v, np.float32)
    return np.ascontiguousarray(v.reshape(-1, 128).T)


def prep_shared(inp):
    shared = {}
    for l in range(L):
        vec = np.zeros((128, NV), np.float32)
        vec[:, V_GMIX:V_GMIX + 8] = _fm(inp["norm_mix_gain"][l])
        vec[:, V_GFFN:V_GFFN + 8] = _fm(inp["norm_ffn_gain"][l])
        vec[:, V_MU:V_MU + 26] = _fm(inp["mu_shift"][l])
        if l >= 1:
            vec[0:32, V_MUV] = inp["mu_vres"][l - 1]
            vec[:, V_V0:V_V0 + 8] = _fm(inp["v0"][l - 1])
        vec[:, V_W0:V_W0 + 8] = _fm(inp["w0"][l])
        vec[:, V_A0:V_A0 + 8] = _fm(inp["a0"][l])
        vec[:, V_KK:V_KK + 8] = _fm(inp["k_k"][l])
        vec[:, V_KA:V_KA + 8] = _fm(inp["k_a"][l])
        vec[:, V_RK:V_RK + 8] = _fm(inp["r_k"][l].reshape(-1))
        vec[:, V_GNG:V_GNG + 8] = _fm(inp["gn_gain"][l])
        vec[:, V_GNB:V_GNB + 8] = _fm(inp["gn_bias"][l])
        vec[:, V_CB:V_CB + 8] = _fm(inp["conv_b"][l])
        vec[:, V_LNG:V_LNG + 8] = _fm(inp["conv_ln_gain"][l])
        vec[:, V_LNB:V_LNB + 8] = _fm(inp["conv_ln_bias"][l])
        vec[:, V_FG:V_FG + 8] = _fm(inp["final_gain"])
        vec[:, V_ADAB:V_ADAB + 48] = _fm(inp["ada_b"][l])
        cw = np.asarray(inp["conv_w"][l], np.float32)
        vec[:, V_CW:V_CW + 248] = cw.reshape(31, 8, 128).transpose(2, 1, 0).reshape(128, 248)
        shared["vecs%d" % l] = vec
        wsm = np.zeros((128, 4, D), np.float32)
        wsm[0:64, 0] = inp["w_decay_up"][l]
        wsm[64:128, 3] = inp["w_aaa_up"][l]
        wsm[:, 1] = inp["w_gate_up"][l]
        if l >= 1:
            wsm[0:32, 2] = inp["w_vres_up"][l - 1]
        shared["ada%d" % l] = np.ascontiguousarray(inp["ada_w"][l], dtype=np.float32)
        lay, tot = _wbig_layout(l)
        wb = np.empty((128, tot), np.float32)
        win = np.asarray(inp["w_in"][l], np.float32)
        if l >= 1:
            win = np.concatenate([win, np.asarray(inp["w_in_vres"][l - 1], np.float32)], axis=1)

        def put(key, cols_matrix):
            off, k, c = lay[key]
            wb[:, off:off + k * c] = cols_matrix.reshape(k, 128, c).transpose(1, 0, 2).reshape(128, k * c)
        lo_idx = list(range(3072, 3328)) + (list(range(N_COLS, N_COLS + 32)) if l >= 1 else [])
        put("lo", win[:, lo_idx])
        for hp in range(8):
            idx = np.concatenate([np.arange(hp * 128, hp * 128 + 128), 1024 + np.arange(hp * 128, hp * 128 + 128),
                                  2048 + np.arange(hp * 128, hp * 128 + 128), 5376 + np.arange(hp * 128, hp * 128 + 128)])
            put("hp%d" % hp, win[:, idx])
            off_, k_, c_ = lay["hp%d" % hp]
            wb[:, off_ + k_ * c_: off_ + k_ * c_ + 512] = wsm[:, :, hp * 128:(hp + 1) * 128].reshape(128, 512)
        for cb in range(8):
            idx = np.concatenate([3328 + np.arange(cb * 128, cb * 128 + 128), 4352 + np.arange(cb * 128, cb * 128 + 128)])
            put("cb%d" % cb, win[:, idx])
            put("gb%d" % cb, win[:, 6400 + np.arange(cb * 128, cb * 128 + 128)])
        wo = np.asarray(inp["w_out"][l], np.float32)
        for g in range(4):
            off, k, c = lay["wo%d" % g]
            blk = wo[:, g * 256:(g + 1) * 256].reshape(16, 128, 2, 128)
            wb[:, off:off + k * c] = blk.transpose(1, 2, 0, 3).reshape(128, 32 * 128)
        w1 = np.asarray(inp["w_ff_in"][l], np.float32)
        for pc in range(8):
            put("w1%d" % pc, w1[:, pc * 512:(pc + 1) * 512])
        w2 = np.asarray(inp["w_ff_out"][l], np.float32)
        for ob in range(8):
            put("w2%d" % ob, w2[:, ob * 128:(ob + 1) * 128])
        shared["wbig%d" % l] = wb
    return shared


def prep_core(inp, b):
    return {"xT": np.ascontiguousarray(np.asarray(inp["x"][b], np.float32).T),
            "cT": _fm(inp["c"][b])}


_CACHE = {}


def kernel(**inputs):
    inp = {k: np.asarray(v) for k, v in inputs.items()}
    if "K" not in _CACHE:
        _CACHE["K"] = build()
    K = _CACHE["K"]
    shared = prep_shared(inp)
    in_maps = []
    for b in range(NCORES):
        m = dict(shared)
        m.update(prep_core(inp, b))
        in_maps.append(m)
    res = run_bass_kernel_spmd(K.nc, in_maps, core_ids=list(range(NCORES)))
    out = np.stack([np.ascontiguousarray(res.results[b]["outT"].T) for b in range(NCORES)], axis=0)
    return out.astype(np.float32)
```

```python
import contextlib
import math
import numpy as np
import concourse.bass as bass
import concourse.mybir as mybir
from concourse.bass_utils import run_bass_kernel_spmd

F32 = mybir.dt.float32
BF16 = mybir.dt.bfloat16
AF = mybir.ActivationFunctionType
ALU = mybir.AluOpType

D = 1024
T = 2048
NB = 8
TT = 512
C = 128
NCH = TT // C
L = 2
NCORES = 8
N_SHIFT = 3328
N_COLS = 7424
RMS_EPS = 1e-6
LN_EPS = 1e-5
GN_EPS = 64e-5
CEXP = -math.exp(-0.5)

V_GMIX, V_GFFN, V_MU, V_MUV, V_W0, V_A0, V_KK, V_KA, V_RK, V_GNG, V_GNB, V_V0, V_CB, V_LNG, V_LNB, V_FG, V_ADAB, V_CW = (
    0, 8, 16, 42, 43, 51, 59, 67, 75, 83, 91, 99, 107, 115, 123, 131, 139, 187)
NV = 187 + 8 * 31
DV_MOD, DV_GSCM, DV_GSCF, DV_OMU = 0, 48, 56, 64
ND = 64 + 27
M_SHM, M_SCM, M_GTM, M_SHF, M_SCF, M_GTF = 0, 8, 16, 24, 32, 40

def _wbig_layout(l):
    off = 0
    lay = {}
    nlo = 256 + (32 if l >= 1 else 0)
    lay["lo"] = (off, 8, nlo); off += 8 * nlo
    for hp in range(8):
        lay["hp%d" % hp] = (off, 8, 512); off += 8 * 512 + 4 * 128
    for cb in range(8):
        lay["cb%d" % cb] = (off, 8, 256); off += 8 * 256
    for cb in range(8):
        lay["gb%d" % cb] = (off, 8, 128); off += 8 * 128
    for g in range(4):
        lay["wo%d" % g] = (off, 32, 128); off += 32 * 128
    for pc in range(8):
        lay["w1%d" % pc] = (off, 8, 512); off += 8 * 512
    for ob in range(8):
        lay["w2%d" % ob] = (off, 32, 128); off += 32 * 128
    return lay, off


ENGS = ("pe", "act", "dve", "pool", "sp")
DMA_SEMS = {"sp": 8, "pool": 16, "act": 4}
SEM_LAT = 0.4
NT_DVE_G = 31
EVAC_MOD = 4
PRIO_MODE = 0
NQ_AMAT = 1
NHO = 2
HID_ENG = "dve"
CP_MOVE = 1
MASK_POOL = 0
SIM_ONLY = False
ACT_WINDOW = 0.3
WINDOW_G = 0.3
LIST_SCHED = True


class Sched:
    def __init__(self, nc):
        self.nc = nc
        self.recs = []
        self.last_w = {}
        self.readers = {}

    def _deps(self, reads, writes):
        deps = set()
        for t in reads:
            w = self.last_w.get(t)
            if w is not None:
                deps.add(w)
        for t in writes:
            w = self.last_w.get(t)
            if w is not None:
                deps.add(w)
            deps.update(self.readers.get(t, ()))
        return deps

    def _commit(self, reads, writes, me):
        for t in reads:
            self.readers.setdefault(t, []).append(me)
        for t in writes:
            self.last_w[t] = me
            self.readers[t] = []

    def op(self, eng, fn, reads=(), writes=(), dur=0.5, tbl=0):
        deps = self._deps(reads, writes)
        me = len(self.recs)
        self.recs.append([eng, fn, deps, dur, False, dur, (list(writes) or ["?"])[0], tbl])
        self._commit(reads, writes, me)

    def dma(self, queue, fn, reads=(), writes=(), nbytes=0):
        deps = self._deps(reads, writes)
        me = len(self.recs)
        self.recs.append([queue, fn, deps, 0.6 if queue == "pool" else 0.15, True, 2.0 + nbytes / 340e3, (list(writes) or ["?"])[0]])
        self._commit(reads, writes, me)

    def wait_all(self, eng, tokens):
        deps = set(self.last_w[t] for t in tokens if t in self.last_w)
        me = len(self.recs)
        self.recs.append([eng, None, deps, 0.01, False, 0.01, "final"])
        self.final = me

    def finalize(self):
        recs = self.recs
        n = len(recs)
        order = {e: [] for e in ENGS}
        if not LIST_SCHED:
            for i, r in enumerate(recs):
                order[r[0]].append(i)
            self.order = order
            return
        succs = [[] for _ in range(n)]
        indeg = [0] * n
        for i, r in enumerate(recs):
            for d in r[2]:
                succs[d].append(i)
            indeg[i] = len(r[2])
        prio = [0.0] * n
        for i in range(n - 1, -1, -1):
            m = 0.0
            for sx in succs[i]:
                if prio[sx] > m:
                    m = prio[sx]
            prio[i] = m + recs[i][5] + SEM_LAT
        import heapq
        ready = {e: [] for e in ENGS}
        finish = [0.0] * n
        self.sim_start = [0.0] * n
        self.sim_finish = finish
        rdy_t = [0.0] * n
        for i in range(n):
            if indeg[i] == 0:
                heapq.heappush(ready[recs[i][0]], (0.0, -prio[i], i))
        efree = {e: 0.0 for e in ENGS}
        done = 0
        WINDOW = WINDOW_G
        cur_tbl = 0
        dma_free = 0.0
        self.n_tbl_switch = 0
        while done < n:
            best = None
            for e in ENGS:
                h = ready[e]
                if not h:
                    continue
                st = max(efree[e], h[0][0])
                if best is None or st < best[0]:
                    best = (st, e)
            st, e = best
            h = ready[e]
            cands = []
            win = ACT_WINDOW if e == "act" else WINDOW
            while h and h[0][0] <= st + win and len(cands) < 32:
                cands.append(heapq.heappop(h))
            cands.sort(key=lambda c: c[1])
            pick = cands[0]
            pen = 0.0
            if e == "act":
                ok = [c for c in cands if len(recs[c[2]]) < 8 or recs[c[2]][7] in (0, cur_tbl)]
                if ok:
                    pick = ok[0]
                else:
                    pen = 1.3
                    self.n_tbl_switch += 1
                t_ = recs[pick[2]][7] if len(recs[pick[2]]) >= 8 else 0
                if t_:
                    cur_tbl = t_
            for c in cands:
                if c is not pick:
                    heapq.heappush(h, c)
            i = pick[2]
            start = max(efree[e], pick[0]) + pen
            efree[e] = start + recs[i][3]
            if recs[i][4]:
                xs = max(start + recs[i][3], dma_free)
                dma_free = xs + (recs[i][5] - 2.0)
                finish[i] = dma_free + 2.0
            else:
                finish[i] = start + recs[i][5]
            self.sim_start[i] = start
            order[e].append(i)
            done += 1
            for sx in succs[i]:
                t = finish[i] + SEM_LAT
                if t > rdy_t[sx]:
                    rdy_t[sx] = t
                indeg[sx] -= 1
                if indeg[sx] == 0:
                    heapq.heappush(ready[recs[sx][0]], (rdy_t[sx], -prio[sx], sx))
        self.order = order
        self.sim_time = max(finish)

    def emit(self, st):
        nc = self.nc
        recs = self.recs
        self.finalize()
        if SIM_ONLY:
            return
        sems = {}
        for e in ENGS:
            sems[e] = st.enter_context(nc.semaphore("s_" + e))
        for q, nq in DMA_SEMS.items():
            for k in range(nq):
                sems[("dma", q, k)] = st.enter_context(nc.semaphore("s_dma_%s%d" % (q, k)))
        comp = [None] * len(recs)
        prev_same_sem = {}
        for e in ENGS:
            cc = 0
            dk = 0
            for i in self.order[e]:
                r = recs[i]
                if r[1] is None:
                    continue
                if r[4]:
                    nq = DMA_SEMS[e]
                    key = ("dma", e, dk % nq)
                    val = 16 * (dk // nq + 1)
                    comp[i] = (key, val)
                    if dk >= nq:
                        prev_same_sem[i] = (key, val - 16)
                    dk += 1
                else:
                    cc += 1
                    comp[i] = (e, cc)
        block = st.enter_context(nc.Block())

        def run(eng_name):
            def body(engine):
                known = {}
                for i in self.order[eng_name]:
                    r = recs[i]
                    need = {}
                    for d in r[2]:
                        k, v = comp[d]
                        if eng_name == "pe" and k == "pe":
                            continue
                        if v > need.get(k, 0):
                            need[k] = v
                    if i in prev_same_sem:
                        k, v = prev_same_sem[i]
                        if v > need.get(k, 0):
                            need[k] = v
                    for k, v in need.items():
                        if known.get(k, 0) >= v:
                            continue
                        known[k] = v
                        engine.wait_ge(sems[k], v)
                    if r[1] is None:
                        continue
                    ins = r[1](engine)
                    k, v = comp[i]
                    ins.then_inc(sems[k], 16 if r[4] else 1)
            return body

        block.tensor(run("pe"))
        block.scalar(run("act"))
        block.vector(run("dve"))
        block.gpsimd(run("pool"))
        block.sync(run("sp"))


def _tok(ap):
    return ap.name


def _n(ap):
    n = 1
    for d in ap.shape[1:]:
        n *= int(d)
    return n


def _bytes(ap):
    n = int(ap.shape[0]) * _n(ap)
    return n * (2 if ap.dtype == BF16 else 4)


class KB:
    def __init__(self, ntiles=4, nlayers=2, dbg=None, stage=99):
        self.ntiles = ntiles
        self.nlayers = nlayers
        self.dbg_names = dbg or []
        self.stage = stage
        self.nc = bass.Bass("TRN2", target_bir_lowering=False)
        self.st = contextlib.ExitStack()
        self.S = Sched(self.nc)
        self.dbg_out = {}
        self.out_tokens = []
        self.psum_names = set()

    def sb(self, name, shape, dt):
        if SIM_ONLY:
            return self.nc.dram_tensor(name, shape, dt, kind="Internal")
        return self.st.enter_context(self.nc.sbuf_tensor(name, shape, dt))

    def ps(self, name, shape, dt):
        self.psum_names.add(name)
        return self.st.enter_context(self.nc.psum_tensor(name, shape, dt))

    def dram_in(self, name, shape, dt=F32):
        return self.nc.dram_tensor(name, shape, dt, kind="ExternalInput").ap()

    def dram_out(self, name, shape, dt=F32):
        return self.nc.dram_tensor(name, shape, dt, kind="ExternalOutput").ap()

    def _rw(self, outs, ins, r, w):
        reads = list(r) if r is not None else [_tok(a) for a in ins if hasattr(a, "name")]
        writes = list(w) if w is not None else [_tok(a) for a in outs]
        ex = [t for t in reads if t in self.psum_names]
        if ex:
            reads = [t for t in reads if t not in self.psum_names]
            writes = writes + [t for t in ex if t not in writes]
        return reads, writes

    def act(self, out, in_, func, scale=1.0, bias=0.0, r=None, w=None):
        extra = [a for a in (scale, bias) if hasattr(a, "name")]
        reads, writes = self._rw([out], [in_] + extra, r, w)
        tbl = {AF.Exp: 1, AF.Ln: 1, AF.Sigmoid: 2, AF.Tanh: 2, AF.Silu: 3}.get(func, 0)
        self.S.op("act", lambda e: e.activation(out=out, in_=in_, func=func, scale=scale, bias=bias), reads, writes, dur=0.25 + _n(out) / 1400.0, tbl=tbl)

    def tt(self, eng, out, in0, in1, op, r=None, w=None):
        reads, writes = self._rw([out], [in0, in1], r, w)
        self.S.op(eng, lambda e: e.tensor_tensor(out=out, in0=in0, in1=in1, op=op), reads, writes, dur=self._vdur(eng, out))

    def ts(self, eng, out, in0, s1, s2, op0, op1=None, r=None, w=None):
        extra = [a for a in (s1, s2) if hasattr(a, "name")]
        reads, writes = self._rw([out], [in0] + extra, r, w)
        if op1 is None:
            self.S.op(eng, lambda e: e.tensor_scalar(out=out, in0=in0, scalar1=s1, scalar2=None, op0=op0), reads, writes, dur=self._vdur(eng, out))
        else:
            self.S.op(eng, lambda e: e.tensor_scalar(out=out, in0=in0, scalar1=s1, scalar2=s2, op0=op0, op1=op1), reads, writes, dur=self._vdur(eng, out))

    def stt(self, out, in0, scalar, in1, op0, op1, r=None, w=None):
        extra = [scalar] if hasattr(scalar, "name") else []
        reads, writes = self._rw([out], [in0, in1] + extra, r, w)
        self.S.op("dve", lambda e: e.scalar_tensor_tensor(out=out, in0=in0, scalar=scalar, in1=in1, op0=op0, op1=op1), reads, writes, dur=self._vdur("dve", out))

    def copy(self, eng, out, in_, r=None, w=None):
        reads, writes = self._rw([out], [in_], r, w)
        if eng == "act":
            self.S.op("act", lambda e: e.copy(out=out, in_=in_), reads, writes, dur=0.25 + _n(out) / 1400.0)
        else:
            self.S.op(eng, lambda e: e.tensor_copy(out=out, in_=in_), reads, writes,
                      dur=(0.3 + _n(out) / 300.0) if eng == "pool" else (0.1 + _n(out) / 1250.0))

    def recip(self, out, in_, r=None, w=None):
        reads, writes = self._rw([out], [in_], r, w)
        self.S.op("dve", lambda e: e.reciprocal(out=out, in_=in_), reads, writes, dur=0.1 + _n(out) / 155.0)

    def scan(self, out, d0, d1, r=None, w=None):
        reads, writes = self._rw([out], [d0, d1], r, w)
        self.S.op("dve", lambda e: e.tensor_tensor_scan(out=out, data0=d0, data1=d1, initial=0.0, op0=ALU.mult, op1=ALU.add), reads, writes, dur=0.12 + _n(out) / 480.0)

    def memset(self, eng, out, val, r=None, w=None):
        reads, writes = self._rw([out], [], r, w)
        self.S.op(eng, lambda e: e.memset(out, val), reads, writes, dur=self._vdur(eng, out))

    def mm(self, out, pairs, r=None, w=None):
        ins = []
        for a, b in pairs:
            ins += [a, b]
        reads, writes = self._rw([out], ins, r, w)
        n = len(pairs)

        def fn(e):
            last = None
            for i, (a, b) in enumerate(pairs):
                last = e.matmul(out, lhsT=a, rhs=b, start=(i == 0), stop=(i == n - 1))
            return last
        self.S.op("pe", fn, reads, writes, dur=sum(0.035 + max(_n(b_), 64) / 2000.0 for a_, b_ in pairs))

    def mmx(self, items):
        ins = []
        outs = []
        for o, a, b, st_, sp_ in items:
            ins += [a, b]
            outs.append(o)
        reads, writes = self._rw(outs[:1], ins, None, None)

        def fn(e):
            last = None
            for o, a, b, st_, sp_ in items:
                last = e.matmul(o, lhsT=a, rhs=b, start=st_, stop=sp_)
            return last
        self.S.op("pe", fn, reads, writes, dur=sum(0.035 + max(_n(b_), 64) / 2000.0 for o_, a_, b_, s1, s2 in items))

    def _vdur(self, eng, out):
        if eng == "pool":
            return 0.3 + _n(out) / 520.0
        return 0.15 + _n(out) / 900.0

    def tr(self, out, in_, ident, r=None, w=None):
        reads, writes = self._rw([out], [in_, ident], r, w)
        self.S.op("pe", lambda e: e.transpose(out=out, in_=in_, identity=ident), reads, writes, dur=0.1)

    def dma(self, q, out, in_, r=None, w=None):
        reads, writes = self._rw([out], [in_], r, w)
        self.S.dma(q, lambda e: e.dma_start(out=out, in_=in_), reads, writes, nbytes=max(_bytes(out), _bytes(in_)))

    def dump(self, name, ap, shape, dt=F32):
        if name not in self.dbg_names:
            return
        if dt != F32 or ap.dtype != F32:
            if not hasattr(self, "_dbgtmp"):
                self._dbgtmp = self.sb("dbgtmp", [128, 1024], F32)
            tmp = self._dbgtmp[0:shape[0], 0:shape[1]]
            self.copy("dve", tmp, ap)
            ap = tmp
        o = self.dram_out("dbg_" + name, list(shape))
        self.dma("sp", o, ap)
        self.out_tokens.append(_tok(o))
        self.dbg_out[name] = "dbg_" + name


def build(ntiles=4, nlayers=2, dbg=None, stage=99):
    K = KB(ntiles, nlayers, dbg, stage)
    nc = K.nc
    xT = K.dram_in("xT", [D, T])
    cT = K.dram_in("cT", [128, 8])
    ada = [K.dram_in("ada%d" % l, [D, 6 * D]) for l in range(L)]
    vecs_d = [K.dram_in("vecs%d" % l, [128, NV]) for l in range(L)]
    lay = [_wbig_layout(l) for l in range(L)]
    wbig = [K.dram_in("wbig%d" % l, [128, lay[l][1]]) for l in range(L)]
    outT = K.dram_out("outT", [D, T])

    sb, ps = K.sb, K.ps
    ident16 = sb("ident16", [128, 128], BF16)
    ones16 = sb("ones16", [128, 128], BF16)
    bd16 = sb("bd16", [128, 128], BF16)
    onesf = sb("onesf", [128, 128], F32)
    mask512 = sb("mask512", [128, 512], BF16)
    masksl = sb("masksl", [128, 128], BF16)
    rst = sb("rst", [128, 512], BF16)
    vecs = [sb("vecs_s%d" % l, [128, NV], F32) for l in range(L)]
    dv = [sb("dv%d" % l, [128, ND], F32) for l in range(L)]
    c32 = sb("c32", [128, 8], F32)
    epsc = sb("epsc", [128, 4], F32)
    c16 = sb("c16", [128, 8], BF16)
    S32 = [sb("S32_%d" % l, [128, 8, 64], F32) for l in range(L)]
    S16 = [sb("S16_%d" % l, [128, 8, 128], BF16) for l in range(L)]
    ctail = [sb("ctail%d" % l, [128, 8, 30], F32) for l in range(L)]
    carry = [sb("carry%d" % l, [128, 27], F32) for l in range(L)]
    xt = [sb("xt%d" % k, [128, TT], F32) for k in range(NB)]
    ht = [sb("ht%d" % k, [128, TT], BF16) for k in range(NB)]
    merged = [sb("mg%d" % k, [128, TT], BF16) for k in range(16)]
    vf = [sb("vf%d" % k, [128, TT], BF16) for k in range(NB)]
    NWP = 3
    WPN = 4608
    wpool = [sb("wp%d" % i, [128, WPN], BF16) for i in range(NWP)]
    lo16 = [sb("lo16_%d" % j, [128, TT], BF16) for j in range(2)]
    vlo16 = sb("vlo16", [128, TT], BF16)
    names32 = ["r32", "k32", "v32", "sg32", "a32", "sv32", "dd32", "cs32", "E1", "E3",
               "kk32", "nrm", "km", "b32", "y32", "yc", "sd"]
    W = {n: sb(n, [128, TT], F32) for n in names32}
    W["d2"] = W["dd32"]; W["E2"] = W["dd32"]; W["d4"] = W["sv32"]; W["E4"] = W["sv32"]; W["rn"] = W["nrm"]
    W["kkn"] = W["kk32"]; W["t1"] = W["km"]; W["rsd"] = W["sd"]
    W["yn"] = W["yc"]; W["yg"] = W["yc"]; W["o1"] = W["yc"]; W["o2"] = W["yc"]
    names16 = ["kk2", "Kh16", "Bh16", "v16", "rk16", "y16", "yc2", "sqa", "sqb", "sqc"]
    H = {n: sb(n, [128, TT], BF16) for n in names16}
    sq16 = [H[n] for n in ("kk2", "Kh16", "Bh16", "v16", "rk16", "sqa", "sqb", "sqc")]
    g32p = [sb("g32_%d" % i, [128, TT], F32) for i in range(NHO)]
    bon32p = [sb("bon32_%d" % i, [128, TT], F32) for i in range(NHO)]
    gA16p = [sb("gA16_%d" % i, [128, TT], BF16) for i in range(NHO)]
    kt16p = [sb("kt16_%d" % i, [128, 2, TT], BF16) for i in range(NHO)]
    bt16p = [sb("bt16_%d" % i, [128, 2, TT], BF16) for i in range(NHO)]
    hmask = sb("hmask", [128, 2], F32)
    WCp = [sb("WC_%d" % i, [128, NCH], F32) for i in range(NHO)]
    lo32 = [W["yc"], W["y32"]]
    vlo32 = bon32p[0]
    rs32 = W["nrm"]; rstd = W["sd"]; t32 = [W["yc"], W["y32"]]
    ARp = [sb("AR_%d" % i, [128, NCH, 2, C], BF16) for i in range(NHO)]
    KBtokp = [sb("KBtok_%d" % i, [128, 8, 128], BF16) for i in range(NHO)]
    Vtokp = [sb("Vtok_%d" % i, [128, 4, 128], BF16) for i in range(NHO)]
    Amatp = [[[sb("Amat%d_%d_%d" % (q, h, c), [128, 512], BF16) for c in range(NCH)] for h in range(2)] for q in range(NQ_AMAT)] * (2 // NQ_AMAT)
    Tfinp = [[[sb("Tfin%d_%d_%d" % (q, h, c), [128, 128], BF16) for c in range(NCH)] for h in range(2)] for q in range(NQ_AMAT)] * (2 // NQ_AMAT)
    Q0 = [sb("Q0_%d" % i, [128, 128], BF16) for i in range(4)]
    PQT = [[sb("PQT%d_%d" % (i, j), [128, 3, 128], BF16) for j in range(2)] for i in range(4)]
    XT16 = sb("XT16", [128, 128], BF16)
    UT16 = sb("UT16", [128, 128], BF16)
    sgb = sb("sgb16", [128, TT], BF16)
    sgb2 = [sb("sgbB%d" % i, [128, TT], BF16) for i in range(2)]
    gbuf = sb("gbuf", [128, 30 + TT], F32)
    cacc = [sb("cacc%d" % i, [128, TT], F32) for i in range(2)]
    ctmp = [sb("ctmp%d" % i, [128, TT], F32) for i in range(2)]
    z16 = [sb("z16_%d" % k, [128, TT], BF16) for k in range(NB)]

    def halves(t):
        v = t[:].bitcast(BF16)
        return [v[:, 0:TT], v[:, TT:2 * TT]]
    hidv = []
    for t_ in [W[n] for n in ("r32", "k32", "v32", "sg32", "a32", "sv32", "dd32", "kk32", "cs32", "E1", "E3", "km", "b32")] + [g32p[0], g32p[1], bon32p[0]]:
        hidv += halves(t_)
    mmb = [ps("mmb%d" % i, [128, 512], F32) for i in range(2)]
    dblb = [ps("dblb%d" % i, [128, 512], F32) for i in range(4)]
    seqb = ps("seqb", [128, 512], F32)
    yb = ps("yb", [128, 512], F32)
    mm_rr = [0]

    def bank():
        b = mmb[mm_rr[0] % 2]
        mm_rr[0] += 1
        return b

    def dbl_region(i, j):
        if j < 2:
            col = (i % 2) * 256 + j * 128
            return dblb[i // 2][:, col:col + 128]
        return dblb[2][:, i * 128:(i + 1) * 128]

    wp_rr = [0]
    PF = 2
    piece_list = []
    for pc in range(12):
        piece_list.append(("ada", 0, pc))
    ada1_left = list(range(12)) if K.nlayers > 1 else []
    for it_ in range(K.ntiles):
        for l_ in range(K.nlayers):
            if l_ == 1:
                while ada1_left:
                    piece_list.append(("ada", 1, ada1_left.pop(0)))
            if K.stage >= 4:
                keys = ["lo", "hp0"]
                for i in range(8):
                    keys.append("cb%d" % i)
                    if i + 1 < 8:
                        keys.append("hp%d" % (i + 1))
                keys += ["gb%d" % i for i in range(8)] + ["wo%d" % i for i in range(4)]
            else:
                keys = ["lo"] + ["hp%d" % i for i in range(8)]
            if K.stage >= 5:
                keys += ["w1%d" % i for i in range(8)] + ["w2%d" % i for i in range(8)]
            for n_, key in enumerate(keys):
                piece_list.append(("w", l_, key))
                if it_ == 0 and l_ == 0 and n_ % 3 == 2 and ada1_left:
                    piece_list.append(("ada", 1, ada1_left.pop(0)))
    issued = [0]
    piece_dst = {}

    def _issue(idx):
        kind, l_, key = piece_list[idx]
        buf = wpool[idx % NWP]
        if kind == "ada":
            src = ada[l_].rearrange("(k p) c -> p k c", p=128)[:, :, key * 512:(key + 1) * 512]
            dst = buf[:, 0:4096].rearrange("p (k c) -> p k c", c=512)
        elif key.startswith("hp"):
            off, k, c = lay[l_][0][key]
            src = wbig[l_][:, off:off + 4608].rearrange("p (a b) -> p a b", b=512)
            K.dma("pool", buf[:, 0:4608].rearrange("p (a b) -> p a b", b=512), src)
            piece_dst[idx] = (buf[:, 0:4096].rearrange("p (k c) -> p k c", c=512), buf[:, 4096:4608].rearrange("p (a c) -> p a c", c=128))
            return
        else:
            off, k, c = lay[l_][0][key]
            src = wbig[l_][:, off:off + k * c].rearrange("p (k c) -> p k c", c=c)
            dst = buf[:, 0:k * c].rearrange("p (k c) -> p k c", c=c)
        K.dma("pool", dst, src)
        piece_dst[idx] = dst

    def next_piece(expect):
        idx = wp_rr[0]
        wp_rr[0] += 1
        assert piece_list[idx] == expect, (piece_list[idx], expect)
        while issued[0] < min(len(piece_list), idx + PF + 1):
            _issue(issued[0])
            issued[0] += 1
        return piece_dst.pop(idx)

    def ada_piece():
        kind, l_, pc = piece_list[wp_rr[0]]
        dst = next_piece((kind, l_, pc))
        p = bank()
        for m in range(4):
            K.mm(p[:, m:m + 1], [(dst[:, k, m * 128:(m + 1) * 128], c16[:, k:k + 1]) for k in range(8)])
        K.tt("dve", dv[l_][:, DV_MOD + 4 * pc:DV_MOD + 4 * pc + 4], p[:, 0:4], vecs[l_][:, V_ADAB + 4 * pc:V_ADAB + 4 * pc + 4], ALU.add)
        if pc == 11:
            K.stt(dv[l_][:, DV_GSCM:DV_GSCM + 8], dv[l_][:, DV_MOD + M_SCM:DV_MOD + M_SCM + 8], 1.0, vecs[l_][:, V_GMIX:V_GMIX + 8], ALU.add, ALU.mult)
            K.stt(dv[l_][:, DV_GSCF:DV_GSCF + 8], dv[l_][:, DV_MOD + M_SCF:DV_MOD + M_SCF + 8], 1.0, vecs[l_][:, V_GFFN:V_GFFN + 8], ALU.add, ALU.mult)
            K.ts("dve", dv[l_][:, DV_OMU:DV_OMU + 27], vecs[l_][:, V_MU:V_MU + 27], -1.0, 1.0, ALU.mult, ALU.add)
            K.dump("dv%d" % l_, dv[l_][:], [128, ND])

    def load_piece(l, key):
        while wp_rr[0] < len(piece_list) and piece_list[wp_rr[0]][0] == "ada":
            ada_piece()
        return next_piece(("w", l, key))

    K.memset("pool", onesf[:], 1.0)
    K.memset("pool", ones16[:], 1.0)
    K.memset("pool", bd16[:], 0.0)
    K.memset("pool", bd16[0:64, 0:64], 1.0)
    K.memset("pool", bd16[64:128, 64:128], 1.0)

    def asel(out, in_, pattern, op, base, cm):
        K.S.op("pool", lambda e: e.affine_select(out=out, in_=in_, pattern=pattern, compare_op=op, fill=0.0, base=base, channel_multiplier=cm),
               [_tok(in_)], [_tok(out)], dur=0.3)
    asel(ident16[:], ones16[:], [[1, 128]], ALU.is_equal, 0, -1)
    for j in range(4):
        asel(mask512[:, j * 128:(j + 1) * 128], onesf[:, 0:128], [[1, 128]], ALU.is_gt if j % 2 == 0 else ALU.is_ge, 0, -1)
    asel(masksl[:], onesf[:, 0:128], [[-1, 128]], ALU.is_gt, 0, 1)
    K.memset("pool", rst[:], 1.0)
    for j in range(NCH):
        K.memset("pool", rst[:, j * C:j * C + 1], 0.0)
    for l in range(L):
        K.dma("sp", vecs[l][:], vecs_d[l])
        K.memset("pool", S32[l][:], 0.0)
        K.memset("pool", S16[l][:], 0.0)
        K.memset("pool", ctail[l][:], 0.0)
        K.memset("pool", carry[l][:], 0.0)
    K.dma("sp", c32[:], cT)
    K.memset("pool", hmask[:], 0.0)
    K.memset("pool", vlo16[:], 0.0)
    K.memset("pool", hmask[0:64, 0:1], 1.0)
    K.memset("pool", hmask[64:128, 1:2], 1.0)
    K.memset("pool", epsc[:, 0:1], RMS_EPS)
    K.memset("pool", epsc[:, 1:2], LN_EPS)
    K.memset("pool", epsc[:, 2:3], GN_EPS)
    K.memset("pool", epsc[:, 3:4], 1e-24)
    K.act(c16[:], c32[:], AF.Silu)

    for pc in range(12):
        ada_piece()

    def rmsnorm_to_ht(l, gsc_col, sh_col):
        for k in range(NB):
            K.act(sq16[k][:], xt[k][:], AF.Square)
        p = bank()
        K.mm(p[:], [(ones16[:], sq16[k][:]) for k in range(NB)])
        K.act(rs32[:], p[:], AF.Ln, scale=1.0 / D, bias=epsc[:, 0:1])
        K.act(rstd[:], rs32[:], AF.Exp, scale=-0.5)
        for k in range(NB):
            t = t32[k % 2]
            K.stt(t[:], xt[k][:], dv[l][:, gsc_col + k:gsc_col + k + 1], rstd[:], ALU.mult, ALU.mult)
            K.act(ht[k][:], t[:], AF.Identity, bias=dv[l][:, sh_col + k:sh_col + k + 1])

    def proj(wp, j0, M=128):
        p = bank()
        K.mm(p[0:M, :], [(wp[:, k, j0:j0 + M], ht[k][:]) for k in range(NB)])
        return p

    def shift(l, p, mucol, dst, M=128):
        mu = vecs[l][0:M, V_MU + mucol:V_MU + mucol + 1]
        omu = dv[l][0:M, DV_OMU + mucol:DV_OMU + mucol + 1]
        cy = carry[l][0:M, mucol:mucol + 1]
        K.act(dst[0:M, :], p[0:M, :], AF.Identity, scale=omu)
        K.stt(dst[0:M, 1:TT], p[0:M, 0:TT - 1], mu, dst[0:M, 1:TT], ALU.mult, ALU.add)
        K.stt(dst[0:M, 0:1], cy, mu, dst[0:M, 0:1], ALU.mult, ALU.add)
        K.copy("dve", cy, p[0:M, TT - 1:TT])

    xTv = xT.rearrange("(k p) t -> k p t", p=128)
    oTv = outT.rearrange("(k p) t -> k p t", p=128)
    for it in range(K.ntiles):
        t0 = it * TT
        for k in range(NB):
            K.dma("sp", xt[k][:], xTv[k, :, t0:t0 + TT])
        for l in range(K.nlayers):
            V = vecs[l]
            DVl = dv[l]
            rmsnorm_to_ht(l, DV_GSCM, DV_MOD + M_SHM)
            if it == 0 and l == 0:
                for k in (0, 7):
                    K.dump("ht%d" % k, ht[k][:], [128, TT])
            wlo = load_piece(l, "lo")
            for j in range(2):
                p = proj(wlo, j * 128)
                shift(l, p, 24 + j, lo32[j])
            K.act(lo16[0][0:64, :], lo32[0][0:64, :], AF.Tanh)
            K.copy("act", lo16[0][64:128, :], lo32[0][64:128, :])
            K.act(lo16[1][:], lo32[1][:], AF.Sigmoid)
            if l >= 1:
                p = proj(wlo, 256, M=32)
                shift(l, p, 26, vlo32, M=32)
                K.copy("act", vlo16[0:32, :], vlo32[0:32, :])
            if K.stage < 1:
                continue
            def front(hp, par):
                AR = ARp[par]; KBtok = KBtokp[par]; Vtok = Vtokp[par]
                g32 = g32p[par]; bon32 = bon32p[par]; gA16 = gA16p[par]; kt16 = kt16p[par]; bt16 = bt16p[par]; WC = WCp[par]
                whp, wsl = load_piece(l, "hp%d" % hp)
                p = proj(whp, 0);   shift(l, p, hp, W["r32"])
                yield
                p = proj(whp, 128); shift(l, p, 8 + hp, W["k32"])
                yield
                p = proj(whp, 256); shift(l, p, 16 + hp, W["v32"])
                yield
                p = proj(whp, 384); K.act(gA16[:], p[:], AF.Sigmoid)
                p = bank(); K.mm(p[:], [(wsl[:, 0, :], lo16[0][:])])
                K.act(W["sg32"][:], p[:], AF.Sigmoid, bias=V[:, V_W0 + hp:V_W0 + hp + 1])
                yield
                p = bank(); K.mm(p[:], [(wsl[:, 3, :], lo16[0][:])])
                K.act(W["a32"][:], p[:], AF.Sigmoid, bias=V[:, V_A0 + hp:V_A0 + hp + 1])
                p = bank(); K.mm(p[:], [(wsl[:, 1, :], lo16[1][:])])
                K.tt("dve", g32[:], p[:], gA16[:], ALU.mult)
                yield
                if l >= 1:
                    p = bank(); K.mm(p[:], [(wsl[:, 2, :], vlo16[:])])
                    K.act(W["sv32"][:], p[:], AF.Sigmoid, bias=V[:, V_V0 + hp:V_V0 + hp + 1])
                    K.tt("pool", W["dd32"][:], vf[hp][:], W["v32"][:], ALU.subtract)
                    K.tt("pool", W["dd32"][:], W["dd32"][:], W["sv32"][:], ALU.mult)
                    K.tt("pool", W["v32"][:], W["v32"][:], W["dd32"][:], ALU.add)
                else:
                    K.copy("act" if CP_MOVE else "pool", vf[hp][:], W["v32"][:])
                yield
                r3 = lambda t_: t_[:].rearrange("p (c t) -> p c t", t=C)
                K.scan(W["cs32"][:], rst[:], W["sg32"][:])
                K.act(W["E1"][:], W["cs32"][:], AF.Exp, scale=CEXP)
                K.act(W["E3"][:], W["cs32"][:], AF.Exp, scale=-CEXP)
                K.copy("dve", WC[:], r3(W["E1"])[:, :, C - 1])
                yield
                K.act(H["kk2"][:], W["k32"][:], AF.Square, scale=V[:, V_KK + hp:V_KK + hp + 1])
                p = bank(); K.mm(p[:], [(bd16[:], H["kk2"][:])])
                K.act(W["nrm"][:], p[:], AF.Ln, bias=epsc[:, 3:4])
                K.act(W["rn"][:], W["nrm"][:], AF.Exp, scale=-0.5)
                yield
                K.stt(W["kkn"][:], W["k32"][:], V[:, V_KK + hp:V_KK + hp + 1], W["rn"][:], ALU.mult, ALU.mult)
                K.ts("dve", W["t1"][:], W["a32"][:], -1.0, V[:, V_KA + hp:V_KA + hp + 1], ALU.add, ALU.mult)
                K.stt(W["km"][:], W["t1"][:], 1.0, W["k32"][:], ALU.add, ALU.mult)
                K.tt("dve", W["b32"][:], W["kkn"][:], W["a32"][:], ALU.mult)
                yield
                K.tt("dve", AR[:, :, 1, :], r3(W["r32"]), r3(W["E1"]), ALU.mult)
                K.stt(AR[:, :, 0, 1:C], r3(W["kkn"])[:, :, 1:C], -1.0, r3(W["E1"])[:, :, 0:C - 1], ALU.mult, ALU.mult)
                K.ts("dve", AR[:, :, 0, 0:1], r3(W["kkn"])[:, :, 0:1], -1.0, None, ALU.mult)
                K.tt("dve", W["d4"][:], W["km"][:], W["E3"][:], ALU.mult)
                K.tt("dve", W["d2"][:], W["b32"][:], W["E3"][:], ALU.mult)
                yield
                for h_ in range(2):
                    K.ts("dve", kt16[:, h_, :], W["d4"][:], hmask[:, h_:h_ + 1], None, ALU.mult)
                    K.ts("dve", bt16[:, h_, :], W["d2"][:], hmask[:, h_:h_ + 1], None, ALU.mult)
                wcb = WC[:].unsqueeze(2).to_broadcast([128, NCH, C])
                K.tt("dve", r3(H["Kh16"]), r3(W["d4"]), wcb, ALU.mult)
                K.tt("dve", r3(H["Bh16"]), r3(W["d2"]), wcb, ALU.mult)
                K.copy("dve" if CP_MOVE else "pool", H["v16"][:], W["v32"][:])
                yield
                K.stt(H["rk16"][:], W["r32"][:], V[:, V_RK + hp:V_RK + hp + 1], W["km"][:], ALU.mult, ALU.mult)
                p = bank(); K.mm(p[:], [(bd16[:], H["rk16"][:])])
                K.tt("dve", bon32[:], p[:], W["v32"][:], ALU.mult)
                yield
                trb = bank()
                trv = trb[:].bitcast(BF16).rearrange("p (a t) -> p a t", t=128)
                for c in range(NCH):
                    K.tr(trv[:, c, :], H["Kh16"][:, c * C:(c + 1) * C], ident16[:])
                    K.tr(trv[:, 4 + c, :], H["Bh16"][:, c * C:(c + 1) * C], ident16[:])
                K.copy("act", KBtok[:], trv)
                yield
                trb = bank()
                trv = trb[:].bitcast(BF16).rearrange("p (a t) -> p a t", t=128)
                for c in range(NCH):
                    K.tr(trv[:, c, :], H["v16"][:, c * C:(c + 1) * C], ident16[:])
                K.copy("dve", Vtok[:], trv[:, 0:4, :])
                yield

            def back(hp, par):
                Amat = Amatp[hp % 2]; Tfin = Tfinp[hp % 2]
                AR = ARp[par]; KBtok = KBtokp[par]; Vtok = Vtokp[par]
                g32 = g32p[par]; bon32 = bon32p[par]; gA16 = gA16p[par]; kt16 = kt16p[par]; bt16 = bt16p[par]; WC = WCp[par]
                for h in range(2):
                    hs = slice(h * 64, h * 64 + 64)
                    for c in range(NCH):
                        ck = slice(c * C, (c + 1) * C)
                        bk = dblb[c]
                        ar = AR[:, c, :, :].rearrange("p a t -> p (a t)")
                        K.mm(bk[:, 0:256], [(bt16[:, h, ck], ar)])
                        K.mm(bk[:, 256:512], [(kt16[:, h, ck], ar)])
                        if MASK_POOL:
                            K.copy("act", Amat[h][c][:], bk[:])
                            am = Amat[h][c][:].rearrange("p (j i t) -> p j i t", j=2, i=2)
                            K.S.op("pool", (lambda am=am: (lambda e: e.affine_select(out=am, in_=am, pattern=[[0, 2], [1, 2], [1, 128]], compare_op=ALU.is_gt,
                                                                                     fill=0.0, base=0, channel_multiplier=-1)))(),
                                   [_tok(am)], [_tok(am)], dur=0.55)
                        else:
                            K.tt("dve", Amat[h][c][:], bk[:], mask512[:], ALU.mult)
                        K.mm(bk[:, 0:128], [(AR[:, c, 0, :], bt16[:, h, ck])])
                        if MASK_POOL:
                            K.copy("act", Q0[c][:], bk[:, 0:128])
                            q0 = Q0[c][:]
                            K.S.op("pool", (lambda q0=q0: (lambda e: e.affine_select(out=q0, in_=q0, pattern=[[-1, 128]], compare_op=ALU.is_gt,
                                                                                     fill=0.0, base=0, channel_multiplier=1)))(),
                                   [_tok(q0)], [_tok(q0)], dur=0.25)
                        else:
                            K.tt("dve", Q0[c][:], bk[:, 0:128], masksl[:], ALU.mult)
                        K.tt("pool", PQT[c][1][:, 1, :], Amat[h][c][:, 0:128], ident16[:], ALU.add)
                        yield
                    for c in range(NCH):
                        bk = dblb[c]
                        P0 = Amat[h][c][:, 0:128]
                        K.mm(bk[:, 0:128], [(Q0[c][:], P0)])
                        K.mm(bk[:, 256:384], [(P0, Q0[c][:])])
                        K.copy("act" if c % 2 == 0 else "dve", PQT[c][1][:, 0:3:2, :], bk[:, 0:384].rearrange("p (a t) -> p a t", t=128)[:, 0:3:2, :])
                    yield
                    for lv in range(1, 7):
                        cur, nxt = lv % 2, (lv + 1) % 2
                        for c in range(NCH):
                            bk = dblb[c]
                            Pk = PQT[c][cur][:, 0, :]
                            Tk = PQT[c][cur][:, 1, :]
                            Qk = PQT[c][cur][:, 2, :]
                            PTk = PQT[c][cur][:, 0:2, :].rearrange("p a t -> p (a t)")
                            eng = "dve" if (c + lv) % EVAC_MOD == 0 else "act"
                            if lv < 6:
                                K.mmx([(bk[:, 256:384], Pk, Qk, True, True),
                                       (bk[:, 0:256], Qk, PTk, True, False),
                                       (bk[:, 128:256], ident16[:], Tk, False, True)])
                                K.copy(eng, PQT[c][nxt][:], bk[:, 0:384].rearrange("p (a t) -> p a t", t=128))
                            else:
                                K.mm(bk[:, 128:256], [(Qk, Tk), (ident16[:], Tk)])
                                K.copy(eng, Tfin[h][c][:], bk[:, 128:256])
                        yield
                h0 = slice(0, 64); h1 = slice(64, 128)
                for c in range(NCH):
                    ck = slice(c * C, (c + 1) * C)
                    K.mmx([(seqb[:, 0:128], AR[:, c, 0, :], S16[l][:, hp, :], True, False),
                           (seqb[:, 0:64], Amat[0][c][:, 256:384], Vtok[:, c, h0], False, False),
                           (seqb[:, 64:128], Amat[1][c][:, 256:384], Vtok[:, c, h1], False, True)])
                    K.copy("act", XT16[:], seqb[:, 0:128])
                    yield
                    for h in range(2):
                        hs = slice(h * 64, h * 64 + 64)
                        K.mm(seqb[:, 128 + h * 64:128 + (h + 1) * 64], [(Tfin[h][c][:], XT16[:, hs])])
                    K.copy("dve", UT16[:], seqb[:, 128:256])
                    yield
                    K.mmx([(yb[:, ck], S16[l][:, hp, :], AR[:, c, 1, :], True, False),
                           (yb[h0, ck], UT16[:, h0], Amat[0][c][:, 128:256], False, False),
                           (yb[h0, ck], Vtok[:, c, h0], Amat[0][c][:, 384:512], False, False),
                           (yb[h1, ck], UT16[:, h1], Amat[1][c][:, 128:256], False, False),
                           (yb[h1, ck], Vtok[:, c, h1], Amat[1][c][:, 384:512], False, True)])
                    for h in range(2):
                        hs = slice(h * 64, h * 64 + 64)
                        K.mm(seqb[hs, 256:320], [(KBtok[:, 4 + c, hs], UT16[:, hs]),
                                                  (KBtok[:, c, hs], Vtok[:, c, hs])])
                    K.stt(S32[l][:, hp, :], S32[l][:, hp, :], WC[:, c:c + 1], seqb[:, 256:320], ALU.mult, ALU.add)
                    K.copy("act", S16[l][h0, hp, 0:64], S32[l][h0, hp, :])
                    K.copy("dve", S16[l][h1, hp, 64:128], S32[l][h1, hp, :])
                    yield
                K.copy("act", W["y32"][:], yb[:])
                K.copy("dve", H["y16"][:], yb[:])
                if it == 0 and l == 0 and hp == 0:
                    K.dump("y32", W["y32"][:], [128, TT])
                yield
                p = bank(); K.mm(p[:], [(bd16[:], H["y16"][:])])
                K.stt(W["yc"][:], p[:], -1.0 / 64, W["y32"][:], ALU.mult, ALU.add)
                yield
                K.act(H["yc2"][:], W["yc"][:], AF.Square)
                p = bank(); K.mm(p[:], [(bd16[:], H["yc2"][:])])
                K.act(W["sd"][:], p[:], AF.Ln, scale=1.0 / 64, bias=epsc[:, 2:3])
                yield
                K.act(W["rsd"][:], W["sd"][:], AF.Exp, scale=-0.5)
                K.stt(W["yn"][:], W["yc"][:], V[:, V_GNG + hp:V_GNG + hp + 1], W["rsd"][:], ALU.mult, ALU.mult)
                K.stt(W["o1"][:], W["yn"][:], V[:, V_GNB + hp:V_GNB + hp + 1], bon32[:], ALU.add, ALU.add)
                yield
                K.tt("dve", merged[hp][:], W["o1"][:], g32[:], ALU.mult)
                if it == 0 and l == 0 and hp in (0, 7):
                    K.dump("mg%d" % hp, merged[hp][:], [128, TT], BF16)
                yield

            def drain(g):
                for _ in g:
                    pass

            def interleave(g1, g2):
                a1 = a2 = True
                while a1 or a2:
                    if a1:
                        try:
                            next(g1)
                        except StopIteration:
                            a1 = False
                    if a2:
                        try:
                            next(g2)
                        except StopIteration:
                            a2 = False

            def conv(cb):
                wcb = load_piece(l, "cb%d" % cb)
                pb = proj(wcb, 128)
                K.act(sgb[:], pb[:], AF.Sigmoid)
                pa = proj(wcb, 0)
                K.copy("pool", gbuf[:, 0:30], ctail[l][:, cb, :])
                K.tt("dve", gbuf[:, 30:30 + TT], pa[:], sgb[:], ALU.mult)
                K.copy("pool", ctail[l][:, cb, :], gbuf[:, TT:TT + 30])
                cw = lambda k: V[:, V_CW + cb * 31 + k:V_CW + cb * 31 + k + 1]
                NT_DVE = NT_DVE_G
                if NT_DVE >= 31:
                    K.ts("dve", cacc[0][:], gbuf[:, 0:TT], cw(0), V[:, V_CB + cb:V_CB + cb + 1], ALU.mult, ALU.add)
                    K.ts("dve", cacc[1][:], gbuf[:, 1:1 + TT], cw(1), None, ALU.mult)
                    for k in range(2, 31):
                        a_ = cacc[k % 2]
                        K.stt(a_[:], gbuf[:, k:k + TT], cw(k), a_[:], ALU.mult, ALU.add)
                    K.tt("dve", z16[cb][:], cacc[0][:], cacc[1][:], ALU.add)
                    wgb = None
                    return
                K.ts("dve", cacc[0][:], gbuf[:, 0:TT], cw(0), V[:, V_CB + cb:V_CB + cb + 1], ALU.mult, ALU.add)
                for k in range(1, NT_DVE):
                    K.stt(cacc[0][:], gbuf[:, k:k + TT], cw(k), cacc[0][:], ALU.mult, ALU.add)
                K.act(cacc[1][:], gbuf[:, NT_DVE:NT_DVE + TT], AF.Identity, scale=cw(NT_DVE))
                for k in range(NT_DVE + 1, 31):
                    tp_ = ctmp[k % 2]
                    K.act(tp_[:], gbuf[:, k:k + TT], AF.Identity, scale=cw(k))
                    K.tt("pool", cacc[1][:], cacc[1][:], tp_[:], ALU.add)
                K.tt("pool", z16[cb][:], cacc[0][:], cacc[1][:], ALU.add)

            if K.stage < 3:
                for hp in range(8):
                    drain(front(hp, hp % NHO))
            else:
                drain(front(0, 0))
                for hp in range(8):
                    if K.stage >= 4:
                        conv(hp)
                    if hp + 1 < 8:
                        interleave(back(hp, hp % NHO), front(hp + 1, (hp + 1) % NHO))
                    else:
                        drain(back(hp, hp % NHO))
            if K.stage < 4:
                continue
            if it == 0 and l == 0:
                K.dump("z0", z16[0][:], [128, TT], BF16)
            p = bank(); K.mm(p[:], [(ones16[:], z16[cb][:]) for cb in range(8)])
            K.act(W["nrm"][:], p[:], AF.Identity, scale=-1.0 / D)
            for cb in range(8):
                t = t32[cb % 2]
                K.tt("dve", t[:], z16[cb][:], W["nrm"][:], ALU.add)
                K.act(sq16[cb][:], t[:], AF.Square)
            p = bank(); K.mm(p[:], [(ones16[:], sq16[cb][:]) for cb in range(8)])
            K.act(W["sd"][:], p[:], AF.Ln, scale=1.0 / D, bias=epsc[:, 1:2])
            K.act(W["sd"][:], W["sd"][:], AF.Exp, scale=-0.5)
            for cb in range(8):
                t = t32[cb % 2]
                wgb = load_piece(l, "gb%d" % cb)
                pg = proj(wgb, 0)
                sg_ = sgb2[cb % 2]
                K.act(sg_[:], pg[:], AF.Sigmoid)
                K.tt("dve", t[:], z16[cb][:], W["nrm"][:], ALU.add)
                K.tt("dve", t[:], t[:], W["sd"][:], ALU.mult)
                K.act(t[:], t[:], AF.Silu, scale=V[:, V_LNG + cb:V_LNG + cb + 1], bias=V[:, V_LNB + cb:V_LNB + cb + 1])
                K.tt("pool", merged[8 + cb][:], t[:], sg_[:], ALU.mult)
            if it == 0 and l == 0:
                K.dump("mg8", merged[8][:], [128, TT], BF16)
            for g in range(4):
                wo = load_piece(l, "wo%d" % g)
                for o2 in range(2):
                    ob = g * 2 + o2
                    p = bank()
                    K.mm(p[:], [(wo[:, o2 * 16 + kc, :], merged[kc][:]) for kc in range(16)])
                    K.stt(xt[ob][:], p[:], DVl[:, DV_MOD + M_GTM + ob:DV_MOD + M_GTM + ob + 1], xt[ob][:], ALU.mult, ALU.add)
            if it == 0 and l == 0:
                K.dump("xmix0", xt[0][:], [128, TT])
            if K.stage < 5:
                continue
            rmsnorm_to_ht(l, DV_GSCF, DV_MOD + M_SHF)
            for pc in range(8):
                w1 = load_piece(l, "w1%d" % pc)
                for j in range(4):
                    hb = pc * 4 + j
                    p = proj(w1, j * 128)
                    t = t32[hb % 2]
                    K.act(t[:], p[:], AF.Relu)
                    K.tt(HID_ENG, hidv[hb], t[:], t[:], ALU.mult)
            for ob in range(8):
                w2 = load_piece(l, "w2%d" % ob)
                p = bank()
                K.mm(p[:], [(w2[:, hb, :], hidv[hb]) for hb in range(32)])
                K.stt(xt[ob][:], p[:], DVl[:, DV_MOD + M_GTF + ob:DV_MOD + M_GTF + ob + 1], xt[ob][:], ALU.mult, ALU.add)
            if it == 0 and l == 0:
                K.dump("xffn0", xt[0][:], [128, TT])
        for k in range(NB):
            K.act(sq16[k][:], xt[k][:], AF.Square)
        p = bank()
        K.mm(p[:], [(ones16[:], sq16[k][:]) for k in range(NB)])
        K.act(rs32[:], p[:], AF.Ln, scale=1.0 / D, bias=epsc[:, 0:1])
        K.act(rstd[:], rs32[:], AF.Exp, scale=-0.5)
        for k in range(NB):
            t = t32[k % 2]
            K.stt(t[:], xt[k][:], vecs[0][:, V_FG + k:V_FG + k + 1], rstd[:], ALU.mult, ALU.mult)
            K.dma("sp", oTv[k, :, t0:t0 + TT], t[:], w=["outT%d_%d" % (it, k)])
            K.out_tokens.append("outT%d_%d" % (it, k))
    K.S.wait_all("sp", K.out_tokens)
    K.S.emit(K.st)
    K.st.close()
    return K


def dblb_view(dblb, c):
    col = (c % 2) * 256
    return dblb[c // 2][:, col:col + 256].rearrange("p (a t) -> p a t", t=128)


def _fm(v):
    v = np.asarray(v, np.float32)
    return np.ascontiguousarray(v.reshape(-1, 128).T)


def prep_shared(inp):
    shared = {}
    for l in range(L):
        vec = np.zeros((128, NV), np.float32)
        vec[:, V_GMIX:V_GMIX + 8] = _fm(inp["norm_mix_gain"][l])
        vec[:, V_GFFN:V_GFFN + 8] = _fm(inp["norm_ffn_gain"][l])
        vec[:, V_MU:V_MU + 26] = _fm(inp["mu_shift"][l])
        if l >= 1:
            vec[0:32, V_MUV] = inp["mu_vres"][l - 1]
            vec[:, V_V0:V_V0 + 8] = _fm(inp["v0"][l - 1])
        vec[:, V_W0:V_W0 + 8] = _fm(inp["w0"][l])
        vec[:, V_A0:V_A0 + 8] = _fm(inp["a0"][l])
        vec[:, V_KK:V_KK + 8] = _fm(inp["k_k"][l])
        vec[:, V_KA:V_KA + 8] = _fm(inp["k_a"][l])
        vec[:, V_RK:V_RK + 8] = _fm(inp["r_k"][l].reshape(-1))
        vec[:, V_GNG:V_GNG + 8] = _fm(inp["gn_gain"][l])
        vec[:, V_GNB:V_GNB + 8] = _fm(inp["gn_bias"][l])
        vec[:, V_CB:V_CB + 8] = _fm(inp["conv_b"][l])
        vec[:, V_LNG:V_LNG + 8] = _fm(inp["conv_ln_gain"][l])
        vec[:, V_LNB:V_LNB + 8] = _fm(inp["conv_ln_bias"][l])
        vec[:, V_FG:V_FG + 8] = _fm(inp["final_gain"])
        vec[:, V_ADAB:V_ADAB + 48] = _fm(inp["ada_b"][l])
        cw = np.asarray(inp["conv_w"][l], np.float32)
        vec[:, V_CW:V_CW + 248] = cw.reshape(31, 8, 128).transpose(2, 1, 0).reshape(128, 248)
        shared["vecs%d" % l] = vec
        wsm = np.zeros((128, 4, D), np.float32)
        wsm[0:64, 0] = inp["w_decay_up"][l]
        wsm[64:128, 3] = inp["w_aaa_up"][l]
        wsm[:, 1] = inp["w_gate_up"][l]
        if l >= 1:
            wsm[0:32, 2] = inp["w_vres_up"][l - 1]
        shared["ada%d" % l] = np.ascontiguousarray(inp["ada_w"][l], dtype=np.float32)
        lay, tot = _wbig_layout(l)
        wb = np.empty((128, tot), np.float32)
        win = np.asarray(inp["w_in"][l], np.float32)
        if l >= 1:
            win = np.concatenate([win, np.asarray(inp["w_in_vres"][l - 1], np.float32)], axis=1)

        def put(key, cols_matrix):
            off, k, c = lay[key]
            wb[:, off:off + k * c] = cols_matrix.reshape(k, 128, c).transpose(1, 0, 2).reshape(128, k * c)
        lo_idx = list(range(3072, 3328)) + (list(range(N_COLS, N_COLS + 32)) if l >= 1 else [])
        put("lo", win[:, lo_idx])
        for hp in range(8):
            idx = np.concatenate([np.arange(hp * 128, hp * 128 + 128), 1024 + np.arange(hp * 128, hp * 128 + 128),
                                  2048 + np.arange(hp * 128, hp * 128 + 128), 5376 + np.arange(hp * 128, hp * 128 + 128)])
            put("hp%d" % hp, win[:, idx])
            off_, k_, c_ = lay["hp%d" % hp]
            wb[:, off_ + k_ * c_: off_ + k_ * c_ + 512] = wsm[:, :, hp * 128:(hp + 1) * 128].reshape(128, 512)
        for cb in range(8):
            idx = np.concatenate([3328 + np.arange(cb * 128, cb * 128 + 128), 4352 + np.arange(cb * 128, cb * 128 + 128)])
            put("cb%d" % cb, win[:, idx])
            put("gb%d" % cb, win[:, 6400 + np.arange(cb * 128, cb * 128 + 128)])
        wo = np.asarray(inp["w_out"][l], np.float32)
        for g in range(4):
            off, k, c = lay["wo%d" % g]
            blk = wo[:, g * 256:(g + 1) * 256].reshape(16, 128, 2, 128)
            wb[:, off:off + k * c] = blk.transpose(1, 2, 0, 3).reshape(128, 32 * 128)
        w1 = np.asarray(inp["w_ff_in"][l], np.float32)
        for pc in range(8):
            put("w1%d" % pc, w1[:, pc * 512:(pc + 1) * 512])
        w2 = np.asarray(inp["w_ff_out"][l], np.float32)
        for ob in range(8):
            put("w2%d" % ob, w2[:, ob * 128:(ob + 1) * 128])
        shared["wbig%d" % l] = wb
    return shared


def prep_core(inp, b):
    return {"xT": np.ascontiguousarray(np.asarray(inp["x"][b], np.float32).T),
            "cT": _fm(inp["c"][b])}


_CACHE = {}


def kernel(**inputs):
    inp = {k: np.asarray(v) for k, v in inputs.items()}
    if "K" not in _CACHE:
        _CACHE["K"] = build()
    K = _CACHE["K"]
    shared = prep_shared(inp)
    in_maps = []
    for b in range(NCORES):
        m = dict(shared)
        m.update(prep_core(inp, b))
        in_maps.append(m)
    res = run_bass_kernel_spmd(K.nc, in_maps, core_ids=list(range(NCORES)))
    out = np.stack([np.ascontiguousarray(res.results[b]["outT"].T) for b in range(NCORES)], axis=0)
    return out.astype(np.float32)
```

```python
import contextlib
import math
import numpy as np
import concourse.bass as bass
import concourse.mybir as mybir
from concourse.bass_utils import run_bass_kernel_spmd

F32 = mybir.dt.float32
BF16 = mybir.dt.bfloat16
AF = mybir.ActivationFunctionType
ALU = mybir.AluOpType

D = 1024
T = 2048
NB = 8
TT = 512
C = 128
NCH = TT // C
L = 2
NCORES = 8
N_SHIFT = 3328
N_COLS = 7424
RMS_EPS = 1e-6
LN_EPS = 1e-5
GN_EPS = 64e-5
CEXP = -math.exp(-0.5)

V_GMIX, V_GFFN, V_MU, V_MUV, V_W0, V_A0, V_KK, V_KA, V_RK, V_GNG, V_GNB, V_V0, V_CB, V_LNG, V_LNB, V_FG, V_ADAB, V_CW = (
    0, 8, 16, 42, 43, 51, 59, 67, 75, 83, 91, 99, 107, 115, 123, 131, 139, 187)
NV = 187 + 8 * 31
DV_MOD, DV_GSCM, DV_GSCF, DV_OMU = 0, 48, 56, 64
ND = 64 + 27
M_SHM, M_SCM, M_GTM, M_SHF, M_SCF, M_GTF = 0, 8, 16, 24, 32, 40

def _wbig_layout(l):
    off = 0
    lay = {}
    nlo = 256 + (32 if l >= 1 else 0)
    lay["lo"] = (off, 8, nlo); off += 8 * nlo
    for hp in range(8):
        lay["hp%d" % hp] = (off, 8, 512); off += 8 * 512 + 4 * 128
    for cb in range(8):
        lay["cb%d" % cb] = (off, 8, 256); off += 8 * 256
    for cb in range(8):
        lay["gb%d" % cb] = (off, 8, 128); off += 8 * 128
    for g in range(4):
        lay["wo%d" % g] = (off, 32, 128); off += 32 * 128
    for pc in range(8):
        lay["w1%d" % pc] = (off, 8, 512); off += 8 * 512
    for ob in range(8):
        lay["w2%d" % ob] = (off, 32, 128); off += 32 * 128
    return lay, off


ENGS = ("pe", "act", "dve", "pool", "sp")
DMA_SEMS = {"sp": 8, "pool": 16, "act": 4}
SEM_LAT = 0.4
NT_DVE_G = 31
EVAC_MOD = 1000
PRIO_MODE = 0
NQ_AMAT = 1
NHO = 2
OFF = 7
HID_ENG = "dve"
CP_MOVE = 1
MASK_POOL = 0
SIM_ONLY = False
ACT_WINDOW = 0.3
WINDOW_G = 0.3
LIST_SCHED = True


class Sched:
    def __init__(self, nc):
        self.nc = nc
        self.recs = []
        self.last_w = {}
        self.readers = {}

    def _deps(self, reads, writes):
        deps = set()
        for t in reads:
            w = self.last_w.get(t)
            if w is not None:
                deps.add(w)
        for t in writes:
            w = self.last_w.get(t)
            if w is not None:
                deps.add(w)
            deps.update(self.readers.get(t, ()))
        return deps

    def _commit(self, reads, writes, me):
        for t in reads:
            self.readers.setdefault(t, []).append(me)
        for t in writes:
            self.last_w[t] = me
            self.readers[t] = []

    def op(self, eng, fn, reads=(), writes=(), dur=0.5, tbl=0):
        deps = self._deps(reads, writes)
        me = len(self.recs)
        self.recs.append([eng, fn, deps, dur, False, dur, (list(writes) or ["?"])[0], tbl])
        self._commit(reads, writes, me)

    def dma(self, queue, fn, reads=(), writes=(), nbytes=0):
        deps = self._deps(reads, writes)
        me = len(self.recs)
        self.recs.append([queue, fn, deps, 0.6 if queue == "pool" else 0.15, True, 2.0 + nbytes / 340e3, (list(writes) or ["?"])[0]])
        self._commit(reads, writes, me)

    def wait_all(self, eng, tokens):
        deps = set(self.last_w[t] for t in tokens if t in self.last_w)
        me = len(self.recs)
        self.recs.append([eng, None, deps, 0.01, False, 0.01, "final"])
        self.final = me

    def finalize(self):
        recs = self.recs
        n = len(recs)
        order = {e: [] for e in ENGS}
        if not LIST_SCHED:
            for i, r in enumerate(recs):
                order[r[0]].append(i)
            self.order = order
            return
        succs = [[] for _ in range(n)]
        indeg = [0] * n
        for i, r in enumerate(recs):
            for d in r[2]:
                succs[d].append(i)
            indeg[i] = len(r[2])
        prio = [0.0] * n
        for i in range(n - 1, -1, -1):
            m = 0.0
            for sx in succs[i]:
                if prio[sx] > m:
                    m = prio[sx]
            prio[i] = m + recs[i][5] + SEM_LAT
        import heapq
        ready = {e: [] for e in ENGS}
        finish = [0.0] * n
        self.sim_start = [0.0] * n
        self.sim_finish = finish
        rdy_t = [0.0] * n
        for i in range(n):
            if indeg[i] == 0:
                heapq.heappush(ready[recs[i][0]], (0.0, -prio[i], i))
        efree = {e: 0.0 for e in ENGS}
        done = 0
        WINDOW = WINDOW_G
        cur_tbl = 0
        dma_free = 0.0
        self.n_tbl_switch = 0
        while done < n:
            best = None
            for e in ENGS:
                h = ready[e]
                if not h:
                    continue
                st = max(efree[e], h[0][0])
                if best is None or st < best[0]:
                    best = (st, e)
            st, e = best
            h = ready[e]
            cands = []
            win = ACT_WINDOW if e == "act" else WINDOW
            while h and h[0][0] <= st + win and len(cands) < 32:
                cands.append(heapq.heappop(h))
            cands.sort(key=lambda c: c[1])
            pick = cands[0]
            pen = 0.0
            if e == "act":
                ok = [c for c in cands if len(recs[c[2]]) < 8 or recs[c[2]][7] in (0, cur_tbl)]
                if ok:
                    pick = ok[0]
                else:
                    pen = 1.3
                    self.n_tbl_switch += 1
                t_ = recs[pick[2]][7] if len(recs[pick[2]]) >= 8 else 0
                if t_:
                    cur_tbl = t_
            for c in cands:
                if c is not pick:
                    heapq.heappush(h, c)
            i = pick[2]
            start = max(efree[e], pick[0]) + pen
            efree[e] = start + recs[i][3]
            if recs[i][4]:
                xs = max(start + recs[i][3], dma_free)
                dma_free = xs + (recs[i][5] - 2.0)
                finish[i] = dma_free + 2.0
            else:
                finish[i] = start + recs[i][5]
            self.sim_start[i] = start
            order[e].append(i)
            done += 1
            for sx in succs[i]:
                t = finish[i] + SEM_LAT
                if t > rdy_t[sx]:
                    rdy_t[sx] = t
                indeg[sx] -= 1
                if indeg[sx] == 0:
                    heapq.heappush(ready[recs[sx][0]], (rdy_t[sx], -prio[sx], sx))
        self.order = order
        self.sim_time = max(finish)

    def emit(self, st):
        nc = self.nc
        recs = self.recs
        self.finalize()
        if SIM_ONLY:
            return
        sems = {}
        for e in ENGS:
            sems[e] = st.enter_context(nc.semaphore("s_" + e))
        for q, nq in DMA_SEMS.items():
            for k in range(nq):
                sems[("dma", q, k)] = st.enter_context(nc.semaphore("s_dma_%s%d" % (q, k)))
        comp = [None] * len(recs)
        prev_same_sem = {}
        for e in ENGS:
            cc = 0
            dk = 0
            for i in self.order[e]:
                r = recs[i]
                if r[1] is None:
                    continue
                if r[4]:
                    nq = DMA_SEMS[e]
                    key = ("dma", e, dk % nq)
                    val = 16 * (dk // nq + 1)
                    comp[i] = (key, val)
                    if dk >= nq:
                        prev_same_sem[i] = (key, val - 16)
                    dk += 1
                else:
                    cc += 1
                    comp[i] = (e, cc)
        block = st.enter_context(nc.Block())

        def run(eng_name):
            def body(engine):
                known = {}
                for i in self.order[eng_name]:
                    r = recs[i]
                    need = {}
                    for d in r[2]:
                        k, v = comp[d]
                        if eng_name == "pe" and k == "pe":
                            continue
                        if v > need.get(k, 0):
                            need[k] = v
                    if i in prev_same_sem:
                        k, v = prev_same_sem[i]
                        if v > need.get(k, 0):
                            need[k] = v
                    for k, v in need.items():
                        if known.get(k, 0) >= v:
                            continue
                        known[k] = v
                        engine.wait_ge(sems[k], v)
                    if r[1] is None:
                        continue
                    ins = r[1](engine)
                    k, v = comp[i]
                    ins.then_inc(sems[k], 16 if r[4] else 1)
            return body

        block.tensor(run("pe"))
        block.scalar(run("act"))
        block.vector(run("dve"))
        block.gpsimd(run("pool"))
        block.sync(run("sp"))


def _tok(ap):
    return ap.name


def _n(ap):
    n = 1
    for d in ap.shape[1:]:
        n *= int(d)
    return n


def _bytes(ap):
    n = int(ap.shape[0]) * _n(ap)
    return n * (2 if ap.dtype == BF16 else 4)


class KB:
    def __init__(self, ntiles=4, nlayers=2, dbg=None, stage=99):
        self.ntiles = ntiles
        self.nlayers = nlayers
        self.dbg_names = dbg or []
        self.stage = stage
        self.nc = bass.Bass("TRN2", target_bir_lowering=False)
        self.st = contextlib.ExitStack()
        self.S = Sched(self.nc)
        self.dbg_out = {}
        self.out_tokens = []
        self.psum_names = set()

    def sb(self, name, shape, dt):
        if SIM_ONLY:
            return self.nc.dram_tensor(name, shape, dt, kind="Internal")
        return self.st.enter_context(self.nc.sbuf_tensor(name, shape, dt))

    def ps(self, name, shape, dt):
        self.psum_names.add(name)
        return self.st.enter_context(self.nc.psum_tensor(name, shape, dt))

    def dram_in(self, name, shape, dt=F32):
        return self.nc.dram_tensor(name, shape, dt, kind="ExternalInput").ap()

    def dram_out(self, name, shape, dt=F32):
        return self.nc.dram_tensor(name, shape, dt, kind="ExternalOutput").ap()

    def _rw(self, outs, ins, r, w):
        reads = list(r) if r is not None else [_tok(a) for a in ins if hasattr(a, "name")]
        writes = list(w) if w is not None else [_tok(a) for a in outs]
        ex = [t for t in reads if t in self.psum_names]
        if ex:
            reads = [t for t in reads if t not in self.psum_names]
            writes = writes + [t for t in ex if t not in writes]
        return reads, writes

    def act(self, out, in_, func, scale=1.0, bias=0.0, r=None, w=None):
        extra = [a for a in (scale, bias) if hasattr(a, "name")]
        reads, writes = self._rw([out], [in_] + extra, r, w)
        tbl = {AF.Exp: 1, AF.Ln: 1, AF.Sigmoid: 2, AF.Tanh: 2, AF.Silu: 3}.get(func, 0)
        self.S.op("act", lambda e: e.activation(out=out, in_=in_, func=func, scale=scale, bias=bias), reads, writes, dur=0.25 + _n(out) / 1400.0, tbl=tbl)

    def tt(self, eng, out, in0, in1, op, r=None, w=None):
        reads, writes = self._rw([out], [in0, in1], r, w)
        self.S.op(eng, lambda e: e.tensor_tensor(out=out, in0=in0, in1=in1, op=op), reads, writes, dur=self._vdur(eng, out))

    def ts(self, eng, out, in0, s1, s2, op0, op1=None, r=None, w=None):
        extra = [a for a in (s1, s2) if hasattr(a, "name")]
        reads, writes = self._rw([out], [in0] + extra, r, w)
        if op1 is None:
            self.S.op(eng, lambda e: e.tensor_scalar(out=out, in0=in0, scalar1=s1, scalar2=None, op0=op0), reads, writes, dur=self._vdur(eng, out))
        else:
            self.S.op(eng, lambda e: e.tensor_scalar(out=out, in0=in0, scalar1=s1, scalar2=s2, op0=op0, op1=op1), reads, writes, dur=self._vdur(eng, out))

    def stt(self, out, in0, scalar, in1, op0, op1, r=None, w=None):
        extra = [scalar] if hasattr(scalar, "name") else []
        reads, writes = self._rw([out], [in0, in1] + extra, r, w)
        self.S.op("dve", lambda e: e.scalar_tensor_tensor(out=out, in0=in0, scalar=scalar, in1=in1, op0=op0, op1=op1), reads, writes, dur=self._vdur("dve", out))

    def copy(self, eng, out, in_, r=None, w=None):
        reads, writes = self._rw([out], [in_], r, w)
        if eng == "act":
            self.S.op("act", lambda e: e.copy(out=out, in_=in_), reads, writes, dur=0.25 + _n(out) / 1400.0)
        else:
            self.S.op(eng, lambda e: e.tensor_copy(out=out, in_=in_), reads, writes,
                      dur=(0.3 + _n(out) / 300.0) if eng == "pool" else (0.1 + _n(out) / 1250.0))

    def recip(self, out, in_, r=None, w=None):
        reads, writes = self._rw([out], [in_], r, w)
        self.S.op("dve", lambda e: e.reciprocal(out=out, in_=in_), reads, writes, dur=0.1 + _n(out) / 155.0)

    def scan(self, out, d0, d1, r=None, w=None):
        reads, writes = self._rw([out], [d0, d1], r, w)
        self.S.op("dve", lambda e: e.tensor_tensor_scan(out=out, data0=d0, data1=d1, initial=0.0, op0=ALU.mult, op1=ALU.add), reads, writes, dur=0.12 + _n(out) / 480.0)

    def memset(self, eng, out, val, r=None, w=None):
        reads, writes = self._rw([out], [], r, w)
        self.S.op(eng, lambda e: e.memset(out, val), reads, writes, dur=self._vdur(eng, out))

    def mm(self, out, pairs, r=None, w=None):
        ins = []
        for a, b in pairs:
            ins += [a, b]
        reads, writes = self._rw([out], ins, r, w)
        n = len(pairs)

        def fn(e):
            last = None
            for i, (a, b) in enumerate(pairs):
                last = e.matmul(out, lhsT=a, rhs=b, start=(i == 0), stop=(i == n - 1))
            return last
        self.S.op("pe", fn, reads, writes, dur=sum(0.035 + max(_n(b_), 64) / 2000.0 for a_, b_ in pairs))

    def mmx(self, items):
        ins = []
        outs = []
        for o, a, b, st_, sp_ in items:
            ins += [a, b]
            outs.append(o)
        reads, writes = self._rw(outs[:1], ins, None, None)

        def fn(e):
            last = None
            for o, a, b, st_, sp_ in items:
                last = e.matmul(o, lhsT=a, rhs=b, start=st_, stop=sp_)
            return last
        self.S.op("pe", fn, reads, writes, dur=sum(0.035 + max(_n(b_), 64) / 2000.0 for o_, a_, b_, s1, s2 in items))

    def _vdur(self, eng, out):
        if eng == "pool":
            return 0.3 + _n(out) / 520.0
        return 0.15 + _n(out) / 900.0

    def tr(self, out, in_, ident, r=None, w=None):
        reads, writes = self._rw([out], [in_, ident], r, w)
        self.S.op("pe", lambda e: e.transpose(out=out, in_=in_, identity=ident), reads, writes, dur=0.1)

    def dma(self, q, out, in_, r=None, w=None):
        reads, writes = self._rw([out], [in_], r, w)
        self.S.dma(q, lambda e: e.dma_start(out=out, in_=in_), reads, writes, nbytes=max(_bytes(out), _bytes(in_)))

    def dump(self, name, ap, shape, dt=F32):
        if name not in self.dbg_names:
            return
        if dt != F32 or ap.dtype != F32:
            if not hasattr(self, "_dbgtmp"):
                self._dbgtmp = self.sb("dbgtmp", [128, 1024], F32)
            tmp = self._dbgtmp[0:shape[0], 0:shape[1]]
            self.copy("dve", tmp, ap)
            ap = tmp
        o = self.dram_out("dbg_" + name, list(shape))
        self.dma("sp", o, ap)
        self.out_tokens.append(_tok(o))
        self.dbg_out[name] = "dbg_" + name


def build(ntiles=4, nlayers=2, dbg=None, stage=99):
    K = KB(ntiles, nlayers, dbg, stage)
    nc = K.nc
    xT = K.dram_in("xT", [D, T])
    cT = K.dram_in("cT", [128, 8])
    ada = [K.dram_in("ada%d" % l, [D, 6 * D]) for l in range(L)]
    vecs_d = [K.dram_in("vecs%d" % l, [128, NV]) for l in range(L)]
    lay = [_wbig_layout(l) for l in range(L)]
    wbig = [K.dram_in("wbig%d" % l, [128, lay[l][1]]) for l in range(L)]
    outT = K.dram_out("outT", [D, T])

    sb, ps = K.sb, K.ps
    ident16 = sb("ident16", [128, 128], BF16)
    ones16 = sb("ones16", [128, 128], BF16)
    bd16 = sb("bd16", [128, 128], BF16)
    onesf = sb("onesf", [128, 128], F32)
    mask512 = sb("mask512", [128, 512], BF16)
    masksl = sb("masksl", [128, 128], BF16)
    rst = sb("rst", [128, 512], BF16)
    vecs = [sb("vecs_s%d" % l, [128, NV], F32) for l in range(L)]
    dv = [sb("dv%d" % l, [128, ND], F32) for l in range(L)]
    c32 = sb("c32", [128, 8], F32)
    epsc = sb("epsc", [128, 4], F32)
    c16 = sb("c16", [128, 8], BF16)
    S32 = [sb("S32_%d" % l, [128, 8, 64], F32) for l in range(L)]
    S16 = [sb("S16_%d" % l, [128, 8, 128], BF16) for l in range(L)]
    ctail = [sb("ctail%d" % l, [128, 8, 30], F32) for l in range(L)]
    carry = [sb("carry%d" % l, [128, 27], F32) for l in range(L)]
    xt = [sb("xt%d" % k, [128, TT], F32) for k in range(NB)]
    ht = [sb("ht%d" % k, [128, TT], BF16) for k in range(NB)]
    merged = [sb("mg%d" % k, [128, TT], BF16) for k in range(16)]
    vf = [sb("vf%d" % k, [128, TT], BF16) for k in range(NB)]
    NWP = 3
    WPN = 4608
    wpool = [sb("wp%d" % i, [128, WPN], BF16) for i in range(NWP)]
    lo16 = [sb("lo16_%d" % j, [128, TT], BF16) for j in range(2)]
    vlo16 = sb("vlo16", [128, TT], BF16)
    names32 = ["r32", "k32", "v32", "sg32", "a32", "sv32", "dd32", "cs32", "E1", "E3",
               "kk32", "nrm", "km", "b32", "y32", "yc", "sd"]
    W = {n: sb(n, [128, TT], F32) for n in names32}
    W["d2"] = W["dd32"]; W["E2"] = W["dd32"]; W["d4"] = W["sv32"]; W["E4"] = W["sv32"]; W["rn"] = W["nrm"]
    W["kkn"] = W["kk32"]; W["t1"] = W["km"]; W["rsd"] = W["sd"]
    W["yn"] = W["yc"]; W["yg"] = W["yc"]; W["o1"] = W["yc"]; W["o2"] = W["yc"]
    names16 = ["kk2", "Kh16", "Bh16", "v16", "rk16", "y16", "yc2", "sqa", "sqb", "sqc"]
    H = {n: sb(n, [128, TT], BF16) for n in names16}
    sq16 = [H[n] for n in ("kk2", "Kh16", "Bh16", "v16", "rk16", "sqa", "sqb", "sqc")]
    g32p = [sb("g32_%d" % i, [128, TT], F32) for i in range(NHO)]
    bon32p = [sb("bon32_%d" % i, [128, TT], F32) for i in range(NHO)]
    gA16p = [sb("gA16_%d" % i, [128, TT], BF16) for i in range(NHO)]
    kt16p = [sb("kt16_%d" % i, [128, 2, TT], BF16) for i in range(NHO)]
    bt16p = [sb("bt16_%d" % i, [128, 2, TT], BF16) for i in range(NHO)]
    hmask = sb("hmask", [128, 2], F32)
    WCp = [sb("WC_%d" % i, [128, NCH], F32) for i in range(NHO)]
    lo32 = [W["yc"], W["y32"]]
    vlo32 = bon32p[0]
    rs32 = W["nrm"]; rstd = W["sd"]; t32 = [W["yc"], W["y32"]]
    ARp = [sb("AR_%d" % i, [128, NCH, 2, C], BF16) for i in range(NHO)]
    KBtokp = [sb("KBtok_%d" % i, [128, 8, 128], BF16) for i in range(NHO)]
    Vtokp = [sb("Vtok_%d" % i, [128, 4, 128], BF16) for i in range(NHO)]
    Amatp = [[[sb("Amat%d_%d_%d" % (q, h, c), [128, 512], BF16) for c in range(NCH)] for h in range(2)] for q in range(NQ_AMAT)] * (2 // NQ_AMAT)
    Tfinp = [[[sb("Tfin%d_%d_%d" % (q, h, c), [128, 128], BF16) for c in range(NCH)] for h in range(2)] for q in range(NQ_AMAT)] * (2 // NQ_AMAT)
    Q0 = [sb("Q0_%d" % i, [128, 128], BF16) for i in range(4)]
    PQT = [[sb("PQT%d_%d" % (i, j), [128, 3, 128], BF16) for j in range(2)] for i in range(4)]
    XT16 = sb("XT16", [128, 128], BF16)
    UT16 = sb("UT16", [128, 128], BF16)
    sgb = sb("sgb16", [128, TT], BF16)
    sgb2 = [sb("sgbB%d" % i, [128, TT], BF16) for i in range(2)]
    gbuf = sb("gbuf", [128, 30 + TT], F32)
    cacc = [sb("cacc%d" % i, [128, TT], F32) for i in range(2)]
    ctmp = [sb("ctmp%d" % i, [128, TT], F32) for i in range(2)]
    z16 = [sb("z16_%d" % k, [128, TT], BF16) for k in range(NB)]

    def halves(t):
        v = t[:].bitcast(BF16)
        return [v[:, 0:TT], v[:, TT:2 * TT]]
    hidv = []
    for t_ in [W[n] for n in ("r32", "k32", "v32", "sg32", "a32", "sv32", "dd32", "kk32", "cs32", "E1", "E3", "km", "b32")] + [g32p[0], g32p[1], bon32p[0]]:
        hidv += halves(t_)
    mmb = [ps("mmb%d" % i, [128, 512], F32) for i in range(2)]
    dblb = [ps("dblb%d" % i, [128, 512], F32) for i in range(4)]
    seqb = ps("seqb", [128, 512], F32)
    yb = ps("yb", [128, 512], F32)
    mm_rr = [0]

    def bank():
        b = mmb[mm_rr[0] % 2]
        mm_rr[0] += 1
        return b

    def dbl_region(i, j):
        if j < 2:
            col = (i % 2) * 256 + j * 128
            return dblb[i // 2][:, col:col + 128]
        return dblb[2][:, i * 128:(i + 1) * 128]

    wp_rr = [0]
    PF = 2
    piece_list = []
    for pc in range(12):
        piece_list.append(("ada", 0, pc))
    ada1_left = list(range(12)) if K.nlayers > 1 else []
    for it_ in range(K.ntiles):
        for l_ in range(K.nlayers):
            if l_ == 1:
                while ada1_left:
                    piece_list.append(("ada", 1, ada1_left.pop(0)))
            if K.stage >= 4:
                keys = ["lo", "hp0"]
                for i in range(8):
                    keys.append("cb%d" % i)
                    if i + 1 < 8:
                        keys.append("hp%d" % (i + 1))
                keys += ["gb%d" % i for i in range(8)] + ["wo%d" % i for i in range(4)]
            else:
                keys = ["lo"] + ["hp%d" % i for i in range(8)]
            if K.stage >= 5:
                keys += ["w1%d" % i for i in range(8)] + ["w2%d" % i for i in range(8)]
            for n_, key in enumerate(keys):
                piece_list.append(("w", l_, key))
                if it_ == 0 and l_ == 0 and n_ % 3 == 2 and ada1_left:
                    piece_list.append(("ada", 1, ada1_left.pop(0)))
    issued = [0]
    piece_dst = {}

    def _issue(idx):
        kind, l_, key = piece_list[idx]
        buf = wpool[idx % NWP]
        if kind == "ada":
            src = ada[l_].rearrange("(k p) c -> p k c", p=128)[:, :, key * 512:(key + 1) * 512]
            dst = buf[:, 0:4096].rearrange("p (k c) -> p k c", c=512)
        elif key.startswith("hp"):
            off, k, c = lay[l_][0][key]
            src = wbig[l_][:, off:off + 4608].rearrange("p (a b) -> p a b", b=512)
            K.dma("pool", buf[:, 0:4608].rearrange("p (a b) -> p a b", b=512), src)
            piece_dst[idx] = (buf[:, 0:4096].rearrange("p (k c) -> p k c", c=512), buf[:, 4096:4608].rearrange("p (a c) -> p a c", c=128))
            return
        else:
            off, k, c = lay[l_][0][key]
            src = wbig[l_][:, off:off + k * c].rearrange("p (k c) -> p k c", c=c)
            dst = buf[:, 0:k * c].rearrange("p (k c) -> p k c", c=c)
        K.dma("pool", dst, src)
        piece_dst[idx] = dst

    def next_piece(expect):
        idx = wp_rr[0]
        wp_rr[0] += 1
        assert piece_list[idx] == expect, (piece_list[idx], expect)
        while issued[0] < min(len(piece_list), idx + PF + 1):
            _issue(issued[0])
            issued[0] += 1
        return piece_dst.pop(idx)

    def ada_piece():
        kind, l_, pc = piece_list[wp_rr[0]]
        dst = next_piece((kind, l_, pc))
        p = bank()
        for m in range(4):
            K.mm(p[:, m:m + 1], [(dst[:, k, m * 128:(m + 1) * 128], c16[:, k:k + 1]) for k in range(8)])
        K.tt("dve", dv[l_][:, DV_MOD + 4 * pc:DV_MOD + 4 * pc + 4], p[:, 0:4], vecs[l_][:, V_ADAB + 4 * pc:V_ADAB + 4 * pc + 4], ALU.add)
        if pc == 11:
            K.stt(dv[l_][:, DV_GSCM:DV_GSCM + 8], dv[l_][:, DV_MOD + M_SCM:DV_MOD + M_SCM + 8], 1.0, vecs[l_][:, V_GMIX:V_GMIX + 8], ALU.add, ALU.mult)
            K.stt(dv[l_][:, DV_GSCF:DV_GSCF + 8], dv[l_][:, DV_MOD + M_SCF:DV_MOD + M_SCF + 8], 1.0, vecs[l_][:, V_GFFN:V_GFFN + 8], ALU.add, ALU.mult)
            K.ts("dve", dv[l_][:, DV_OMU:DV_OMU + 27], vecs[l_][:, V_MU:V_MU + 27], -1.0, 1.0, ALU.mult, ALU.add)
            K.dump("dv%d" % l_, dv[l_][:], [128, ND])

    def load_piece(l, key):
        while wp_rr[0] < len(piece_list) and piece_list[wp_rr[0]][0] == "ada":
            ada_piece()
        return next_piece(("w", l, key))

    K.memset("pool", onesf[:], 1.0)
    K.memset("pool", ones16[:], 1.0)
    K.memset("pool", bd16[:], 0.0)
    K.memset("pool", bd16[0:64, 0:64], 1.0)
    K.memset("pool", bd16[64:128, 64:128], 1.0)

    def asel(out, in_, pattern, op, base, cm):
        K.S.op("pool", lambda e: e.affine_select(out=out, in_=in_, pattern=pattern, compare_op=op, fill=0.0, base=base, channel_multiplier=cm),
               [_tok(in_)], [_tok(out)], dur=0.3)
    asel(ident16[:], ones16[:], [[1, 128]], ALU.is_equal, 0, -1)
    for j in range(4):
        asel(mask512[:, j * 128:(j + 1) * 128], onesf[:, 0:128], [[1, 128]], ALU.is_gt if j % 2 == 0 else ALU.is_ge, 0, -1)
    asel(masksl[:], onesf[:, 0:128], [[-1, 128]], ALU.is_gt, 0, 1)
    K.memset("pool", rst[:], 1.0)
    for j in range(NCH):
        K.memset("pool", rst[:, j * C:j * C + 1], 0.0)
    for l in range(L):
        K.dma("sp", vecs[l][:], vecs_d[l])
        K.memset("pool", S32[l][:], 0.0)
        K.memset("pool", S16[l][:], 0.0)
        K.memset("pool", ctail[l][:], 0.0)
        K.memset("pool", carry[l][:], 0.0)
    K.dma("sp", c32[:], cT)
    K.memset("pool", hmask[:], 0.0)
    K.memset("pool", vlo16[:], 0.0)
    K.memset("pool", hmask[0:64, 0:1], 1.0)
    K.memset("pool", hmask[64:128, 1:2], 1.0)
    K.memset("pool", epsc[:, 0:1], RMS_EPS)
    K.memset("pool", epsc[:, 1:2], LN_EPS)
    K.memset("pool", epsc[:, 2:3], GN_EPS)
    K.memset("pool", epsc[:, 3:4], 1e-24)
    K.act(c16[:], c32[:], AF.Silu)

    for pc in range(12):
        ada_piece()

    def rmsnorm_to_ht(l, gsc_col, sh_col):
        for k in range(NB):
            K.act(sq16[k][:], xt[k][:], AF.Square)
        p = bank()
        K.mm(p[:], [(ones16[:], sq16[k][:]) for k in range(NB)])
        K.act(rs32[:], p[:], AF.Ln, scale=1.0 / D, bias=epsc[:, 0:1])
        K.act(rstd[:], rs32[:], AF.Exp, scale=-0.5)
        for k in range(NB):
            t = t32[k % 2]
            K.stt(t[:], xt[k][:], dv[l][:, gsc_col + k:gsc_col + k + 1], rstd[:], ALU.mult, ALU.mult)
            K.act(ht[k][:], t[:], AF.Identity, bias=dv[l][:, sh_col + k:sh_col + k + 1])

    def proj(wp, j0, M=128):
        p = bank()
        K.mm(p[0:M, :], [(wp[:, k, j0:j0 + M], ht[k][:]) for k in range(NB)])
        return p

    def shift(l, p, mucol, dst, M=128):
        mu = vecs[l][0:M, V_MU + mucol:V_MU + mucol + 1]
        omu = dv[l][0:M, DV_OMU + mucol:DV_OMU + mucol + 1]
        cy = carry[l][0:M, mucol:mucol + 1]
        K.act(dst[0:M, :], p[0:M, :], AF.Identity, scale=omu)
        K.stt(dst[0:M, 1:TT], p[0:M, 0:TT - 1], mu, dst[0:M, 1:TT], ALU.mult, ALU.add)
        K.stt(dst[0:M, 0:1], cy, mu, dst[0:M, 0:1], ALU.mult, ALU.add)
        K.copy("dve", cy, p[0:M, TT - 1:TT])

    xTv = xT.rearrange("(k p) t -> k p t", p=128)
    oTv = outT.rearrange("(k p) t -> k p t", p=128)
    for it in range(K.ntiles):
        t0 = it * TT
        for k in range(NB):
            K.dma("sp", xt[k][:], xTv[k, :, t0:t0 + TT])
        for l in range(K.nlayers):
            V = vecs[l]
            DVl = dv[l]
            rmsnorm_to_ht(l, DV_GSCM, DV_MOD + M_SHM)
            if it == 0 and l == 0:
                for k in (0, 7):
                    K.dump("ht%d" % k, ht[k][:], [128, TT])
            wlo = load_piece(l, "lo")
            for j in range(2):
                p = proj(wlo, j * 128)
                shift(l, p, 24 + j, lo32[j])
            K.act(lo16[0][0:64, :], lo32[0][0:64, :], AF.Tanh)
            K.copy("act", lo16[0][64:128, :], lo32[0][64:128, :])
            K.act(lo16[1][:], lo32[1][:], AF.Sigmoid)
            if l >= 1:
                p = proj(wlo, 256, M=32)
                shift(l, p, 26, vlo32, M=32)
                K.copy("act", vlo16[0:32, :], vlo32[0:32, :])
            if K.stage < 1:
                continue
            def front(hp, par):
                AR = ARp[par]; KBtok = KBtokp[par]; Vtok = Vtokp[par]
                g32 = g32p[par]; bon32 = bon32p[par]; gA16 = gA16p[par]; kt16 = kt16p[par]; bt16 = bt16p[par]; WC = WCp[par]
                whp, wsl = load_piece(l, "hp%d" % hp)
                p = proj(whp, 0);   shift(l, p, hp, W["r32"])
                yield
                p = proj(whp, 128); shift(l, p, 8 + hp, W["k32"])
                yield
                p = proj(whp, 256); shift(l, p, 16 + hp, W["v32"])
                yield
                p = proj(whp, 384); K.act(gA16[:], p[:], AF.Sigmoid)
                p = bank(); K.mm(p[:], [(wsl[:, 0, :], lo16[0][:])])
                K.act(W["sg32"][:], p[:], AF.Sigmoid, bias=V[:, V_W0 + hp:V_W0 + hp + 1])
                yield
                p = bank(); K.mm(p[:], [(wsl[:, 3, :], lo16[0][:])])
                K.act(W["a32"][:], p[:], AF.Sigmoid, bias=V[:, V_A0 + hp:V_A0 + hp + 1])
                p = bank(); K.mm(p[:], [(wsl[:, 1, :], lo16[1][:])])
                K.tt("dve", g32[:], p[:], gA16[:], ALU.mult)
                yield
                if l >= 1:
                    p = bank(); K.mm(p[:], [(wsl[:, 2, :], vlo16[:])])
                    K.act(W["sv32"][:], p[:], AF.Sigmoid, bias=V[:, V_V0 + hp:V_V0 + hp + 1])
                    K.tt("pool", W["dd32"][:], vf[hp][:], W["v32"][:], ALU.subtract)
                    K.tt("pool", W["dd32"][:], W["dd32"][:], W["sv32"][:], ALU.mult)
                    K.tt("pool", W["v32"][:], W["v32"][:], W["dd32"][:], ALU.add)
                else:
                    K.copy("act" if CP_MOVE else "pool", vf[hp][:], W["v32"][:])
                yield
                r3 = lambda t_: t_[:].rearrange("p (c t) -> p c t", t=C)
                K.scan(W["cs32"][:], rst[:], W["sg32"][:])
                K.act(W["E1"][:], W["cs32"][:], AF.Exp, scale=CEXP)
                K.act(W["E3"][:], W["cs32"][:], AF.Exp, scale=-CEXP)
                K.copy("dve", WC[:], r3(W["E1"])[:, :, C - 1])
                yield
                K.act(H["kk2"][:], W["k32"][:], AF.Square, scale=V[:, V_KK + hp:V_KK + hp + 1])
                p = bank(); K.mm(p[:], [(bd16[:], H["kk2"][:])])
                K.act(W["nrm"][:], p[:], AF.Ln, bias=epsc[:, 3:4])
                K.act(W["rn"][:], W["nrm"][:], AF.Exp, scale=-0.5)
                yield
                K.stt(W["kkn"][:], W["k32"][:], V[:, V_KK + hp:V_KK + hp + 1], W["rn"][:], ALU.mult, ALU.mult)
                K.ts("dve", W["t1"][:], W["a32"][:], -1.0, V[:, V_KA + hp:V_KA + hp + 1], ALU.add, ALU.mult)
                K.stt(W["km"][:], W["t1"][:], 1.0, W["k32"][:], ALU.add, ALU.mult)
                K.tt("dve", W["b32"][:], W["kkn"][:], W["a32"][:], ALU.mult)
                yield
                K.tt("dve", AR[:, :, 1, :], r3(W["r32"]), r3(W["E1"]), ALU.mult)
                K.stt(AR[:, :, 0, 1:C], r3(W["kkn"])[:, :, 1:C], -1.0, r3(W["E1"])[:, :, 0:C - 1], ALU.mult, ALU.mult)
                K.ts("dve", AR[:, :, 0, 0:1], r3(W["kkn"])[:, :, 0:1], -1.0, None, ALU.mult)
                K.tt("dve", W["d4"][:], W["km"][:], W["E3"][:], ALU.mult)
                K.tt("dve", W["d2"][:], W["b32"][:], W["E3"][:], ALU.mult)
                yield
                for h_ in range(2):
                    if OFF & 8:
                        K.ts("pool", kt16[:, h_, :], W["d4"][:], hmask[:, h_:h_ + 1], None, ALU.mult)
                        K.ts("pool", bt16[:, h_, :], W["d2"][:], hmask[:, h_:h_ + 1], None, ALU.mult)
                    elif OFF & 1:
                        K.act(kt16[:, h_, :], W["d4"][:], AF.Identity, scale=hmask[:, h_:h_ + 1])
                        K.act(bt16[:, h_, :], W["d2"][:], AF.Identity, scale=hmask[:, h_:h_ + 1])
                    else:
                        K.ts("dve", kt16[:, h_, :], W["d4"][:], hmask[:, h_:h_ + 1], None, ALU.mult)
                        K.ts("dve", bt16[:, h_, :], W["d2"][:], hmask[:, h_:h_ + 1], None, ALU.mult)
                wcb = WC[:].unsqueeze(2).to_broadcast([128, NCH, C])
                K.tt("dve", r3(H["Kh16"]), r3(W["d4"]), wcb, ALU.mult)
                K.tt("dve", r3(H["Bh16"]), r3(W["d2"]), wcb, ALU.mult)
                K.copy("dve" if CP_MOVE else "pool", H["v16"][:], W["v32"][:])
                yield
                K.stt(H["rk16"][:], W["r32"][:], V[:, V_RK + hp:V_RK + hp + 1], W["km"][:], ALU.mult, ALU.mult)
                p = bank(); K.mm(p[:], [(bd16[:], H["rk16"][:])])
                K.tt("dve", bon32[:], p[:], W["v32"][:], ALU.mult)
                yield
                trb = bank()
                trv = trb[:].bitcast(BF16).rearrange("p (a t) -> p a t", t=128)
                for c in range(NCH):
                    K.tr(trv[:, c, :], H["Kh16"][:, c * C:(c + 1) * C], ident16[:])
                    K.tr(trv[:, 4 + c, :], H["Bh16"][:, c * C:(c + 1) * C], ident16[:])
                K.copy("act", KBtok[:], trv)
                yield
                trb = bank()
                trv = trb[:].bitcast(BF16).rearrange("p (a t) -> p a t", t=128)
                for c in range(NCH):
                    K.tr(trv[:, c, :], H["v16"][:, c * C:(c + 1) * C], ident16[:])
                K.copy("dve", Vtok[:], trv[:, 0:4, :])
                yield

            def back(hp, par):
                Amat = Amatp[hp % 2]; Tfin = Tfinp[hp % 2]
                AR = ARp[par]; KBtok = KBtokp[par]; Vtok = Vtokp[par]
                g32 = g32p[par]; bon32 = bon32p[par]; gA16 = gA16p[par]; kt16 = kt16p[par]; bt16 = bt16p[par]; WC = WCp[par]
                for h in range(2):
                    hs = slice(h * 64, h * 64 + 64)
                    for c in range(NCH):
                        ck = slice(c * C, (c + 1) * C)
                        bk = dblb[c]
                        ar = AR[:, c, :, :].rearrange("p a t -> p (a t)")
                        K.mm(bk[:, 0:256], [(bt16[:, h, ck], ar)])
                        K.mm(bk[:, 256:512], [(kt16[:, h, ck], ar)])
                        if MASK_POOL:
                            K.copy("act", Amat[h][c][:], bk[:])
                            am = Amat[h][c][:].rearrange("p (j i t) -> p j i t", j=2, i=2)
                            K.S.op("pool", (lambda am=am: (lambda e: e.affine_select(out=am, in_=am, pattern=[[0, 2], [1, 2], [1, 128]], compare_op=ALU.is_gt,
                                                                                     fill=0.0, base=0, channel_multiplier=-1)))(),
                                   [_tok(am)], [_tok(am)], dur=0.55)
                        else:
                            K.tt("dve", Amat[h][c][:], bk[:], mask512[:], ALU.mult)
                        K.mm(bk[:, 0:128], [(AR[:, c, 0, :], bt16[:, h, ck])])
                        if MASK_POOL:
                            K.copy("act", Q0[c][:], bk[:, 0:128])
                            q0 = Q0[c][:]
                            K.S.op("pool", (lambda q0=q0: (lambda e: e.affine_select(out=q0, in_=q0, pattern=[[-1, 128]], compare_op=ALU.is_gt,
                                                                                     fill=0.0, base=0, channel_multiplier=1)))(),
                                   [_tok(q0)], [_tok(q0)], dur=0.25)
                        else:
                            K.tt("dve", Q0[c][:], bk[:, 0:128], masksl[:], ALU.mult)
                        K.tt("pool", PQT[c][1][:, 1, :], Amat[h][c][:, 0:128], ident16[:], ALU.add)
                        yield
                    for c in range(NCH):
                        bk = dblb[c]
                        P0 = Amat[h][c][:, 0:128]
                        K.mm(bk[:, 0:128], [(Q0[c][:], P0)])
                        K.mm(bk[:, 256:384], [(P0, Q0[c][:])])
                        K.copy("act" if c % 2 == 0 else "dve", PQT[c][1][:, 0:3:2, :], bk[:, 0:384].rearrange("p (a t) -> p a t", t=128)[:, 0:3:2, :])
                    yield
                    for lv in range(1, 7):
                        cur, nxt = lv % 2, (lv + 1) % 2
                        for c in range(NCH):
                            bk = dblb[c]
                            Pk = PQT[c][cur][:, 0, :]
                            Tk = PQT[c][cur][:, 1, :]
                            Qk = PQT[c][cur][:, 2, :]
                            PTk = PQT[c][cur][:, 0:2, :].rearrange("p a t -> p (a t)")
                            eng = "dve" if (c + lv) % EVAC_MOD == 0 else "act"
                            if lv < 6:
                                K.mmx([(bk[:, 256:384], Pk, Qk, True, True),
                                       (bk[:, 0:256], Qk, PTk, True, False),
                                       (bk[:, 128:256], ident16[:], Tk, False, True)])
                                K.copy(eng, PQT[c][nxt][:], bk[:, 0:384].rearrange("p (a t) -> p a t", t=128))
                            else:
                                K.mm(bk[:, 128:256], [(Qk, Tk), (ident16[:], Tk)])
                                K.copy(eng, Tfin[h][c][:], bk[:, 128:256])
                        yield
                h0 = slice(0, 64); h1 = slice(64, 128)
                for c in range(NCH):
                    ck = slice(c * C, (c + 1) * C)
                    K.mmx([(seqb[:, 0:128], AR[:, c, 0, :], S16[l][:, hp, :], True, False),
                           (seqb[:, 0:64], Amat[0][c][:, 256:384], Vtok[:, c, h0], False, False),
                           (seqb[:, 64:128], Amat[1][c][:, 256:384], Vtok[:, c, h1], False, True)])
                    K.copy("act", XT16[:], seqb[:, 0:128])
                    yield
                    for h in range(2):
                        hs = slice(h * 64, h * 64 + 64)
                        K.mm(seqb[:, 128 + h * 64:128 + (h + 1) * 64], [(Tfin[h][c][:], XT16[:, hs])])
                    K.copy("act" if OFF & 2 else "dve", UT16[:], seqb[:, 128:256])
                    yield
                    K.mmx([(yb[:, ck], S16[l][:, hp, :], AR[:, c, 1, :], True, False),
                           (yb[h0, ck], UT16[:, h0], Amat[0][c][:, 128:256], False, False),
                           (yb[h0, ck], Vtok[:, c, h0], Amat[0][c][:, 384:512], False, False),
                           (yb[h1, ck], UT16[:, h1], Amat[1][c][:, 128:256], False, False),
                           (yb[h1, ck], Vtok[:, c, h1], Amat[1][c][:, 384:512], False, True)])
                    for h in range(2):
                        hs = slice(h * 64, h * 64 + 64)
                        K.mm(seqb[hs, 256:320], [(KBtok[:, 4 + c, hs], UT16[:, hs]),
                                                  (KBtok[:, c, hs], Vtok[:, c, hs])])
                    K.stt(S32[l][:, hp, :], S32[l][:, hp, :], WC[:, c:c + 1], seqb[:, 256:320], ALU.mult, ALU.add)
                    K.copy("act", S16[l][h0, hp, 0:64], S32[l][h0, hp, :])
                    K.copy("act" if OFF & 2 else "dve", S16[l][h1, hp, 64:128], S32[l][h1, hp, :])
                    yield
                K.copy("act", W["y32"][:], yb[:])
                K.copy("act" if OFF & 2 else "dve", H["y16"][:], yb[:])
                if it == 0 and l == 0 and hp == 0:
                    K.dump("y32", W["y32"][:], [128, TT])
                yield
                p = bank(); K.mm(p[:], [(bd16[:], H["y16"][:])])
                K.stt(W["yc"][:], p[:], -1.0 / 64, W["y32"][:], ALU.mult, ALU.add)
                yield
                K.act(H["yc2"][:], W["yc"][:], AF.Square)
                p = bank(); K.mm(p[:], [(bd16[:], H["yc2"][:])])
                K.act(W["sd"][:], p[:], AF.Ln, scale=1.0 / 64, bias=epsc[:, 2:3])
                yield
                K.act(W["rsd"][:], W["sd"][:], AF.Exp, scale=-0.5)
                K.stt(W["yn"][:], W["yc"][:], V[:, V_GNG + hp:V_GNG + hp + 1], W["rsd"][:], ALU.mult, ALU.mult)
                K.stt(W["o1"][:], W["yn"][:], V[:, V_GNB + hp:V_GNB + hp + 1], bon32[:], ALU.add, ALU.add)
                yield
                K.tt("dve", merged[hp][:], W["o1"][:], g32[:], ALU.mult)
                if it == 0 and l == 0 and hp in (0, 7):
                    K.dump("mg%d" % hp, merged[hp][:], [128, TT], BF16)
                yield

            def drain(g):
                for _ in g:
                    pass

            def interleave(g1, g2):
                a1 = a2 = True
                while a1 or a2:
                    if a1:
                        try:
                            next(g1)
                        except StopIteration:
                            a1 = False
                    if a2:
                        try:
                            next(g2)
                        except StopIteration:
                            a2 = False

            def conv(cb):
                wcb = load_piece(l, "cb%d" % cb)
                pb = proj(wcb, 128)
                K.act(sgb[:], pb[:], AF.Sigmoid)
                pa = proj(wcb, 0)
                K.copy("pool", gbuf[:, 0:30], ctail[l][:, cb, :])
                K.tt("dve", gbuf[:, 30:30 + TT], pa[:], sgb[:], ALU.mult)
                K.copy("pool", ctail[l][:, cb, :], gbuf[:, TT:TT + 30])
                cw = lambda k: V[:, V_CW + cb * 31 + k:V_CW + cb * 31 + k + 1]
                NT_DVE = NT_DVE_G
                if NT_DVE >= 31:
                    if OFF & 4:
                        K.act(cacc[0][:], gbuf[:, 0:TT], AF.Identity, scale=cw(0), bias=V[:, V_CB + cb:V_CB + cb + 1])
                        K.act(cacc[1][:], gbuf[:, 1:1 + TT], AF.Identity, scale=cw(1))
                    else:
                        K.ts("dve", cacc[0][:], gbuf[:, 0:TT], cw(0), V[:, V_CB + cb:V_CB + cb + 1], ALU.mult, ALU.add)
                        K.ts("dve", cacc[1][:], gbuf[:, 1:1 + TT], cw(1), None, ALU.mult)
                    for k in range(2, 31):
                        a_ = cacc[k % 2]
                        K.stt(a_[:], gbuf[:, k:k + TT], cw(k), a_[:], ALU.mult, ALU.add)
                    K.tt("dve", z16[cb][:], cacc[0][:], cacc[1][:], ALU.add)
                    wgb = None
                    return
                K.ts("dve", cacc[0][:], gbuf[:, 0:TT], cw(0), V[:, V_CB + cb:V_CB + cb + 1], ALU.mult, ALU.add)
                for k in range(1, NT_DVE):
                    K.stt(cacc[0][:], gbuf[:, k:k + TT], cw(k), cacc[0][:], ALU.mult, ALU.add)
                K.act(cacc[1][:], gbuf[:, NT_DVE:NT_DVE + TT], AF.Identity, scale=cw(NT_DVE))
                for k in range(NT_DVE + 1, 31):
                    tp_ = ctmp[k % 2]
                    K.act(tp_[:], gbuf[:, k:k + TT], AF.Identity, scale=cw(k))
                    K.tt("pool", cacc[1][:], cacc[1][:], tp_[:], ALU.add)
                K.tt("pool", z16[cb][:], cacc[0][:], cacc[1][:], ALU.add)

            if K.stage < 3:
                for hp in range(8):
                    drain(front(hp, hp % NHO))
            else:
                drain(front(0, 0))
                for hp in range(8):
                    if K.stage >= 4:
                        conv(hp)
                    if hp + 1 < 8:
                        interleave(back(hp, hp % NHO), front(hp + 1, (hp + 1) % NHO))
                    else:
                        drain(back(hp, hp % NHO))
            if K.stage < 4:
                continue
            if it == 0 and l == 0:
                K.dump("z0", z16[0][:], [128, TT], BF16)
            p = bank(); K.mm(p[:], [(ones16[:], z16[cb][:]) for cb in range(8)])
            K.act(W["nrm"][:], p[:], AF.Identity, scale=-1.0 / D)
            for cb in range(8):
                t = t32[cb % 2]
                K.tt("dve", t[:], z16[cb][:], W["nrm"][:], ALU.add)
                K.act(sq16[cb][:], t[:], AF.Square)
            p = bank(); K.mm(p[:], [(ones16[:], sq16[cb][:]) for cb in range(8)])
            K.act(W["sd"][:], p[:], AF.Ln, scale=1.0 / D, bias=epsc[:, 1:2])
            K.act(W["sd"][:], W["sd"][:], AF.Exp, scale=-0.5)
            for cb in range(8):
                t = t32[cb % 2]
                wgb = load_piece(l, "gb%d" % cb)
                pg = proj(wgb, 0)
                sg_ = sgb2[cb % 2]
                K.act(sg_[:], pg[:], AF.Sigmoid)
                K.tt("dve", t[:], z16[cb][:], W["nrm"][:], ALU.add)
                K.tt("dve", t[:], t[:], W["sd"][:], ALU.mult)
                K.act(t[:], t[:], AF.Silu, scale=V[:, V_LNG + cb:V_LNG + cb + 1], bias=V[:, V_LNB + cb:V_LNB + cb + 1])
                K.tt("pool", merged[8 + cb][:], t[:], sg_[:], ALU.mult)
            if it == 0 and l == 0:
                K.dump("mg8", merged[8][:], [128, TT], BF16)
            for g in range(4):
                wo = load_piece(l, "wo%d" % g)
                for o2 in range(2):
                    ob = g * 2 + o2
                    p = bank()
                    K.mm(p[:], [(wo[:, o2 * 16 + kc, :], merged[kc][:]) for kc in range(16)])
                    K.stt(xt[ob][:], p[:], DVl[:, DV_MOD + M_GTM + ob:DV_MOD + M_GTM + ob + 1], xt[ob][:], ALU.mult, ALU.add)
            if it == 0 and l == 0:
                K.dump("xmix0", xt[0][:], [128, TT])
            if K.stage < 5:
                continue
            rmsnorm_to_ht(l, DV_GSCF, DV_MOD + M_SHF)
            for pc in range(8):
                w1 = load_piece(l, "w1%d" % pc)
                for j in range(4):
                    hb = pc * 4 + j
                    p = proj(w1, j * 128)
                    t = t32[hb % 2]
                    K.act(t[:], p[:], AF.Relu)
                    K.tt(HID_ENG, hidv[hb], t[:], t[:], ALU.mult)
            for ob in range(8):
                w2 = load_piece(l, "w2%d" % ob)
                p = bank()
                K.mm(p[:], [(w2[:, hb, :], hidv[hb]) for hb in range(32)])
                K.stt(xt[ob][:], p[:], DVl[:, DV_MOD + M_GTF + ob:DV_MOD + M_GTF + ob + 1], xt[ob][:], ALU.mult, ALU.add)
            if it == 0 and l == 0:
                K.dump("xffn0", xt[0][:], [128, TT])
        for k in range(NB):
            K.act(sq16[k][:], xt[k][:], AF.Square)
        p = bank()
        K.mm(p[:], [(ones16[:], sq16[k][:]) for k in range(NB)])
        K.act(rs32[:], p[:], AF.Ln, scale=1.0 / D, bias=epsc[:, 0:1])
        K.act(rstd[:], rs32[:], AF.Exp, scale=-0.5)
        for k in range(NB):
            t = t32[k % 2]
            K.stt(t[:], xt[k][:], vecs[0][:, V_FG + k:V_FG + k + 1], rstd[:], ALU.mult, ALU.mult)
            K.dma("sp", oTv[k, :, t0:t0 + TT], t[:], w=["outT%d_%d" % (it, k)])
            K.out_tokens.append("outT%d_%d" % (it, k))
    K.S.wait_all("sp", K.out_tokens)
    K.S.emit(K.st)
    K.st.close()
    return K


def dblb_view(dblb, c):
    col = (c % 2) * 256
    return dblb[c // 2][:, col:col + 256].rearrange("p (a t) -> p a t", t=128)


def _fm(v):
    v = np.asarray(v, np.float32)
    return np.ascontiguousarray(v.reshape(-1, 128).T)


def prep_shared(inp):
    shared = {}
    for l in range(L):
        vec = np.zeros((128, NV), np.float32)
        vec[:, V_GMIX:V_GMIX + 8] = _fm(inp["norm_mix_gain"][l])
        vec[:, V_GFFN:V_GFFN + 8] = _fm(inp["norm_ffn_gain"][l])
        vec[:, V_MU:V_MU + 26] = _fm(inp["mu_shift"][l])
        if l >= 1:
            vec[0:32, V_MUV] = inp["mu_vres"][l - 1]
            vec[:, V_V0:V_V0 + 8] = _fm(inp["v0"][l - 1])
        vec[:, V_W0:V_W0 + 8] = _fm(inp["w0"][l])
        vec[:, V_A0:V_A0 + 8] = _fm(inp["a0"][l])
        vec[:, V_KK:V_KK + 8] = _fm(inp["k_k"][l])
        vec[:, V_KA:V_KA + 8] = _fm(inp["k_a"][l])
        vec[:, V_RK:V_RK + 8] = _fm(inp["r_k"][l].reshape(-1))
        vec[:, V_GNG:V_GNG + 8] = _fm(inp["gn_gain"][l])
        vec[:, V_GNB:V_GNB + 8] = _fm(inp["gn_bias"][l])
        vec[:, V_CB:V_CB + 8] = _fm(inp["conv_b"][l])
        vec[:, V_LNG:V_LNG + 8] = _fm(inp["conv_ln_gain"][l])
        vec[:, V_LNB:V_LNB + 8] = _fm(inp["conv_ln_bias"][l])
        vec[:, V_FG:V_FG + 8] = _fm(inp["final_gain"])
        vec[:, V_ADAB:V_ADAB + 48] = _fm(inp["ada_b"][l])
        cw = np.asarray(inp["conv_w"][l], np.float32)
        vec[:, V_CW:V_CW + 248] = cw.reshape(31, 8, 128).transpose(2, 1, 0).reshape(128, 248)
        shared["vecs%d" % l] = vec
        wsm = np.zeros((128, 4, D), np.float32)
        wsm[0:64, 0] = inp["w_decay_up"][l]
        wsm[64:128, 3] = inp["w_aaa_up"][l]
        wsm[:, 1] = inp["w_gate_up"][l]
        if l >= 1:
            wsm[0:32, 2] = inp["w_vres_up"][l - 1]
        shared["ada%d" % l] = np.ascontiguousarray(inp["ada_w"][l], dtype=np.float32)
        lay, tot = _wbig_layout(l)
        wb = np.empty((128, tot), np.float32)
        win = np.asarray(inp["w_in"][l], np.float32)
        if l >= 1:
            win = np.concatenate([win, np.asarray(inp["w_in_vres"][l - 1], np.float32)], axis=1)

        def put(key, cols_matrix):
            off, k, c = lay[key]
            wb[:, off:off + k * c] = cols_matrix.reshape(k, 128, c).transpose(1, 0, 2).reshape(128, k * c)
        lo_idx = list(range(3072, 3328)) + (list(range(N_COLS, N_COLS + 32)) if l >= 1 else [])
        put("lo", win[:, lo_idx])
        for hp in range(8):
            idx = np.concatenate([np.arange(hp * 128, hp * 128 + 128), 1024 + np.arange(hp * 128, hp * 128 + 128),
                                  2048 + np.arange(hp * 128, hp * 128 + 128), 5376 + np.arange(hp * 128, hp * 128 + 128)])
            put("hp%d" % hp, win[:, idx])
            off_, k_, c_ = lay["hp%d" % hp]
            wb[:, off_ + k_ * c_: off_ + k_ * c_ + 512] = wsm[:, :, hp * 128:(hp + 1) * 128].reshape(128, 512)
        for cb in range(8):
            idx = np.concatenate([3328 + np.arange(cb * 128, cb * 128 + 128), 4352 + np.arange(cb * 128, cb * 128 + 128)])
            put("cb%d" % cb, win[:, idx])
            put("gb%d" % cb, win[:, 6400 + np.arange(cb * 128, cb * 128 + 128)])
        wo = np.asarray(inp["w_out"][l], np.float32)
        for g in range(4):
            off, k, c = lay["wo%d" % g]
            blk = wo[:, g * 256:(g + 1) * 256].reshape(16, 128, 2, 128)
            wb[:, off:off + k * c] = blk.transpose(1, 2, 0, 3).reshape(128, 32 * 128)
        w1 = np.asarray(inp["w_ff_in"][l], np.float32)
        for pc in range(8):
            put("w1%d" % pc, w1[:, pc * 512:(pc + 1) * 512])
        w2 = np.asarray(inp["w_ff_out"][l], np.float32)
        for ob in range(8):
            put("w2%d" % ob, w2[:, ob * 128:(ob + 1) * 128])
        shared["wbig%d" % l] = wb
    return shared


def prep_core(inp, b):
    return {"xT": np.ascontiguousarray(np.asarray(inp["x"][b], np.float32).T),
            "cT": _fm(inp["c"][b])}


_CACHE = {}


def kernel(**inputs):
    inp = {k: np.asarray(v) for k, v in inputs.items()}
    if "K" not in _CACHE:
        _CACHE["K"] = build()
    K = _CACHE["K"]
    shared = prep_shared(inp)
    in_maps = []
    for b in range(NCORES):
        m = dict(shared)
        m.update(prep_core(inp, b))
        in_maps.append(m)
    res = run_bass_kernel_spmd(K.nc, in_maps, core_ids=list(range(NCORES)))
    out = np.stack([np.ascontiguousarray(res.results[b]["outT"].T) for b in range(NCORES)], axis=0)
    return out.astype(np.float32)
```

```python
import contextlib
import math
import numpy as np
import concourse.bass as bass
import concourse.mybir as mybir
from concourse.bass_utils import run_bass_kernel_spmd

F32 = mybir.dt.float32
BF16 = mybir.dt.bfloat16
AF = mybir.ActivationFunctionType
ALU = mybir.AluOpType

D = 1024
T = 2048
NB = 8
TT = 512
C = 128
NCH = TT // C
L = 2
NCORES = 8
N_SHIFT = 3328
N_COLS = 7424
RMS_EPS = 1e-6
LN_EPS = 1e-5
GN_EPS = 64e-5
CEXP = -math.exp(-0.5)

V_GMIX, V_GFFN, V_MU, V_MUV, V_W0, V_A0, V_KK, V_KA, V_RK, V_GNG, V_GNB, V_V0, V_CB, V_LNG, V_LNB, V_FG, V_ADAB, V_CW = (
    0, 8, 16, 42, 43, 51, 59, 67, 75, 83, 91, 99, 107, 115, 123, 131, 139, 187)
NV = 187 + 8 * 31
DV_MOD, DV_GSCM, DV_GSCF, DV_OMU = 0, 48, 56, 64
ND = 64 + 27
M_SHM, M_SCM, M_GTM, M_SHF, M_SCF, M_GTF = 0, 8, 16, 24, 32, 40

def _wbig_layout(l):
    off = 0
    lay = {}
    nlo = 256 + (32 if l >= 1 else 0)
    lay["lo"] = (off, 8, nlo); off += 8 * nlo
    for hp in range(8):
        lay["hp%d" % hp] = (off, 8, 512); off += 8 * 512 + 4 * 128
    for cb in range(8):
        lay["cb%d" % cb] = (off, 8, 256); off += 8 * 256
    for cb in range(8):
        lay["gb%d" % cb] = (off, 8, 128); off += 8 * 128
    for g in range(4):
        lay["wo%d" % g] = (off, 32, 128); off += 32 * 128
    for pc in range(8):
        lay["w1%d" % pc] = (off, 8, 512); off += 8 * 512
    for ob in range(8):
        lay["w2%d" % ob] = (off, 32, 128); off += 32 * 128
    return lay, off


ENGS = ("pe", "act", "dve", "pool", "sp")
DMA_SEMS = {"sp": 8, "pool": 16, "act": 4}
SEM_LAT = 0.4
NT_DVE_G = 31
EVAC_MOD = 1000
PRIO_MODE = 0
NQ_AMAT = 1
NHO = 2
OFF = 7
HID_ENG = "dve"
CP_MOVE = 1
MASK_POOL = 0
SIM_ONLY = False
ACT_WINDOW = 0.3
WINDOW_G = 0.3
LIST_SCHED = True


class Sched:
    def __init__(self, nc):
        self.nc = nc
        self.recs = []
        self.last_w = {}
        self.readers = {}

    def _deps(self, reads, writes):
        deps = set()
        for t in reads:
            w = self.last_w.get(t)
            if w is not None:
                deps.add(w)
        for t in writes:
            w = self.last_w.get(t)
            if w is not None:
                deps.add(w)
            deps.update(self.readers.get(t, ()))
        return deps

    def _commit(self, reads, writes, me):
        for t in reads:
            self.readers.setdefault(t, []).append(me)
        for t in writes:
            self.last_w[t] = me
            self.readers[t] = []

    def op(self, eng, fn, reads=(), writes=(), dur=0.5, tbl=0):
        deps = self._deps(reads, writes)
        me = len(self.recs)
        self.recs.append([eng, fn, deps, dur, False, dur, (list(writes) or ["?"])[0], tbl])
        self._commit(reads, writes, me)

    def dma(self, queue, fn, reads=(), writes=(), nbytes=0):
        deps = self._deps(reads, writes)
        me = len(self.recs)
        self.recs.append([queue, fn, deps, 0.6 if queue == "pool" else 0.15, True, 2.0 + nbytes / 340e3, (list(writes) or ["?"])[0]])
        self._commit(reads, writes, me)

    def wait_all(self, eng, tokens):
        deps = set(self.last_w[t] for t in tokens if t in self.last_w)
        me = len(self.recs)
        self.recs.append([eng, None, deps, 0.01, False, 0.01, "final"])
        self.final = me

    def finalize(self):
        recs = self.recs
        n = len(recs)
        order = {e: [] for e in ENGS}
        if not LIST_SCHED:
            for i, r in enumerate(recs):
                order[r[0]].append(i)
            self.order = order
            return
        succs = [[] for _ in range(n)]
        indeg = [0] * n
        for i, r in enumerate(recs):
            for d in r[2]:
                succs[d].append(i)
            indeg[i] = len(r[2])
        prio = [0.0] * n
        for i in range(n - 1, -1, -1):
            m = 0.0
            for sx in succs[i]:
                if prio[sx] > m:
                    m = prio[sx]
            prio[i] = m + recs[i][5] + SEM_LAT
        import heapq
        ready = {e: [] for e in ENGS}
        finish = [0.0] * n
        self.sim_start = [0.0] * n
        self.sim_finish = finish
        rdy_t = [0.0] * n
        for i in range(n):
            if indeg[i] == 0:
                heapq.heappush(ready[recs[i][0]], (0.0, -prio[i], i))
        efree = {e: 0.0 for e in ENGS}
        done = 0
        WINDOW = WINDOW_G
        cur_tbl = 0
        dma_free = 0.0
        self.n_tbl_switch = 0
        while done < n:
            best = None
            for e in ENGS:
                h = ready[e]
                if not h:
                    continue
                st = max(efree[e], h[0][0])
                if best is None or st < best[0]:
                    best = (st, e)
            st, e = best
            h = ready[e]
            cands = []
            win = ACT_WINDOW if e == "act" else WINDOW
            while h and h[0][0] <= st + win and len(cands) < 32:
                cands.append(heapq.heappop(h))
            cands.sort(key=lambda c: c[1])
            pick = cands[0]
            pen = 0.0
            if e == "act":
                ok = [c for c in cands if len(recs[c[2]]) < 8 or recs[c[2]][7] in (0, cur_tbl)]
                if ok:
                    pick = ok[0]
                else:
                    pen = 1.3
                    self.n_tbl_switch += 1
                t_ = recs[pick[2]][7] if len(recs[pick[2]]) >= 8 else 0
                if t_:
                    cur_tbl = t_
            for c in cands:
                if c is not pick:
                    heapq.heappush(h, c)
            i = pick[2]
            start = max(efree[e], pick[0]) + pen
            efree[e] = start + recs[i][3]
            if recs[i][4]:
                xs = max(start + recs[i][3], dma_free)
                dma_free = xs + (recs[i][5] - 2.0)
                finish[i] = dma_free + 2.0
            else:
                finish[i] = start + recs[i][5]
            self.sim_start[i] = start
            order[e].append(i)
            done += 1
            for sx in succs[i]:
                t = finish[i] + SEM_LAT
                if t > rdy_t[sx]:
                    rdy_t[sx] = t
                indeg[sx] -= 1
                if indeg[sx] == 0:
                    heapq.heappush(ready[recs[sx][0]], (rdy_t[sx], -prio[sx], sx))
        self.order = order
        self.sim_time = max(finish)

    def emit(self, st):
        nc = self.nc
        recs = self.recs
        self.finalize()
        if SIM_ONLY:
            return
        sems = {}
        for e in ENGS:
            sems[e] = st.enter_context(nc.semaphore("s_" + e))
        for q, nq in DMA_SEMS.items():
            for k in range(nq):
                sems[("dma", q, k)] = st.enter_context(nc.semaphore("s_dma_%s%d" % (q, k)))
        comp = [None] * len(recs)
        prev_same_sem = {}
        for e in ENGS:
            cc = 0
            dk = 0
            for i in self.order[e]:
                r = recs[i]
                if r[1] is None:
                    continue
                if r[4]:
                    nq = DMA_SEMS[e]
                    key = ("dma", e, dk % nq)
                    val = 16 * (dk // nq + 1)
                    comp[i] = (key, val)
                    if dk >= nq:
                        prev_same_sem[i] = (key, val - 16)
                    dk += 1
                else:
                    cc += 1
                    comp[i] = (e, cc)
        block = st.enter_context(nc.Block())

        def run(eng_name):
            def body(engine):
                known = {}
                for i in self.order[eng_name]:
                    r = recs[i]
                    need = {}
                    for d in r[2]:
                        k, v = comp[d]
                        if eng_name == "pe" and k == "pe":
                            continue
                        if v > need.get(k, 0):
                            need[k] = v
                    if i in prev_same_sem:
                        k, v = prev_same_sem[i]
                        if v > need.get(k, 0):
                            need[k] = v
                    for k, v in need.items():
                        if known.get(k, 0) >= v:
                            continue
                        known[k] = v
                        engine.wait_ge(sems[k], v)
                    if r[1] is None:
                        continue
                    ins = r[1](engine)
                    k, v = comp[i]
                    ins.then_inc(sems[k], 16 if r[4] else 1)
            return body

        block.tensor(run("pe"))
        block.scalar(run("act"))
        block.vector(run("dve"))
        block.gpsimd(run("pool"))
        block.sync(run("sp"))


def _tok(ap):
    return ap.name


def _n(ap):
    n = 1
    for d in ap.shape[1:]:
        n *= int(d)
    return n


def _bytes(ap):
    n = int(ap.shape[0]) * _n(ap)
    return n * (2 if ap.dtype == BF16 else 4)


class KB:
    def __init__(self, ntiles=4, nlayers=2, dbg=None, stage=99):
        self.ntiles = ntiles
        self.nlayers = nlayers
        self.dbg_names = dbg or []
        self.stage = stage
        self.nc = bass.Bass("TRN2", target_bir_lowering=False)
        self.st = contextlib.ExitStack()
        self.S = Sched(self.nc)
        self.dbg_out = {}
        self.out_tokens = []
        self.psum_names = set()

    def sb(self, name, shape, dt):
        if SIM_ONLY:
            return self.nc.dram_tensor(name, shape, dt, kind="Internal")
        return self.st.enter_context(self.nc.sbuf_tensor(name, shape, dt))

    def ps(self, name, shape, dt):
        self.psum_names.add(name)
        return self.st.enter_context(self.nc.psum_tensor(name, shape, dt))

    def dram_in(self, name, shape, dt=F32):
        return self.nc.dram_tensor(name, shape, dt, kind="ExternalInput").ap()

    def dram_out(self, name, shape, dt=F32):
        return self.nc.dram_tensor(name, shape, dt, kind="ExternalOutput").ap()

    def _rw(self, outs, ins, r, w):
        reads = list(r) if r is not None else [_tok(a) for a in ins if hasattr(a, "name")]
        writes = list(w) if w is not None else [_tok(a) for a in outs]
        ex = [t for t in reads if t in self.psum_names]
        if ex:
            reads = [t for t in reads if t not in self.psum_names]
            writes = writes + [t for t in ex if t not in writes]
        return reads, writes

    def act(self, out, in_, func, scale=1.0, bias=0.0, r=None, w=None):
        extra = [a for a in (scale, bias) if hasattr(a, "name")]
        reads, writes = self._rw([out], [in_] + extra, r, w)
        tbl = {AF.Exp: 1, AF.Ln: 1, AF.Sigmoid: 2, AF.Tanh: 2, AF.Silu: 3}.get(func, 0)
        self.S.op("act", lambda e: e.activation(out=out, in_=in_, func=func, scale=scale, bias=bias), reads, writes, dur=0.25 + _n(out) / 1400.0, tbl=tbl)

    def tt(self, eng, out, in0, in1, op, r=None, w=None):
        reads, writes = self._rw([out], [in0, in1], r, w)
        self.S.op(eng, lambda e: e.tensor_tensor(out=out, in0=in0, in1=in1, op=op), reads, writes, dur=self._vdur(eng, out))

    def ts(self, eng, out, in0, s1, s2, op0, op1=None, r=None, w=None):
        extra = [a for a in (s1, s2) if hasattr(a, "name")]
        reads, writes = self._rw([out], [in0] + extra, r, w)
        if op1 is None:
            self.S.op(eng, lambda e: e.tensor_scalar(out=out, in0=in0, scalar1=s1, scalar2=None, op0=op0), reads, writes, dur=self._vdur(eng, out))
        else:
            self.S.op(eng, lambda e: e.tensor_scalar(out=out, in0=in0, scalar1=s1, scalar2=s2, op0=op0, op1=op1), reads, writes, dur=self._vdur(eng, out))

    def stt(self, out, in0, scalar, in1, op0, op1, r=None, w=None):
        extra = [scalar] if hasattr(scalar, "name") else []
        reads, writes = self._rw([out], [in0, in1] + extra, r, w)
        self.S.op("dve", lambda e: e.scalar_tensor_tensor(out=out, in0=in0, scalar=scalar, in1=in1, op0=op0, op1=op1), reads, writes, dur=self._vdur("dve", out))

    def copy(self, eng, out, in_, r=None, w=None):
        reads, writes = self._rw([out], [in_], r, w)
        if eng == "act":
            self.S.op("act", lambda e: e.copy(out=out, in_=in_), reads, writes, dur=0.25 + _n(out) / 1400.0)
        else:
            self.S.op(eng, lambda e: e.tensor_copy(out=out, in_=in_), reads, writes,
                      dur=(0.3 + _n(out) / 300.0) if eng == "pool" else (0.1 + _n(out) / 1250.0))

    def recip(self, out, in_, r=None, w=None):
        reads, writes = self._rw([out], [in_], r, w)
        self.S.op("dve", lambda e: e.reciprocal(out=out, in_=in_), reads, writes, dur=0.1 + _n(out) / 155.0)

    def scan(self, out, d0, d1, r=None, w=None):
        reads, writes = self._rw([out], [d0, d1], r, w)
        self.S.op("dve", lambda e: e.tensor_tensor_scan(out=out, data0=d0, data1=d1, initial=0.0, op0=ALU.mult, op1=ALU.add), reads, writes, dur=0.12 + _n(out) / 480.0)

    def memset(self, eng, out, val, r=None, w=None):
        reads, writes = self._rw([out], [], r, w)
        self.S.op(eng, lambda e: e.memset(out, val), reads, writes, dur=self._vdur(eng, out))

    def mm(self, out, pairs, r=None, w=None):
        ins = []
        for a, b in pairs:
            ins += [a, b]
        reads, writes = self._rw([out], ins, r, w)
        n = len(pairs)

        def fn(e):
            last = None
            for i, (a, b) in enumerate(pairs):
                last = e.matmul(out, lhsT=a, rhs=b, start=(i == 0), stop=(i == n - 1))
            return last
        self.S.op("pe", fn, reads, writes, dur=sum(0.035 + max(_n(b_), 64) / 2000.0 for a_, b_ in pairs))

    def mmx(self, items):
        ins = []
        outs = []
        for o, a, b, st_, sp_ in items:
            ins += [a, b]
            outs.append(o)
        reads, writes = self._rw(outs[:1], ins, None, None)

        def fn(e):
            last = None
            for o, a, b, st_, sp_ in items:
                last = e.matmul(o, lhsT=a, rhs=b, start=st_, stop=sp_)
            return last
        self.S.op("pe", fn, reads, writes, dur=sum(0.035 + max(_n(b_), 64) / 2000.0 for o_, a_, b_, s1, s2 in items))

    def _vdur(self, eng, out):
        if eng == "pool":
            return 0.3 + _n(out) / 520.0
        return 0.15 + _n(out) / 900.0

    def tr(self, out, in_, ident, r=None, w=None):
        reads, writes = self._rw([out], [in_, ident], r, w)
        self.S.op("pe", lambda e: e.transpose(out=out, in_=in_, identity=ident), reads, writes, dur=0.1)

    def dma(self, q, out, in_, r=None, w=None):
        reads, writes = self._rw([out], [in_], r, w)
        self.S.dma(q, lambda e: e.dma_start(out=out, in_=in_), reads, writes, nbytes=max(_bytes(out), _bytes(in_)))

    def dump(self, name, ap, shape, dt=F32):
        if name not in self.dbg_names:
            return
        if dt != F32 or ap.dtype != F32:
            if not hasattr(self, "_dbgtmp"):
                self._dbgtmp = self.sb("dbgtmp", [128, 1024], F32)
            tmp = self._dbgtmp[0:shape[0], 0:shape[1]]
            self.copy("dve", tmp, ap)
            ap = tmp
        o = self.dram_out("dbg_" + name, list(shape))
        self.dma("sp", o, ap)
        self.out_tokens.append(_tok(o))
        self.dbg_out[name] = "dbg_" + name


def build(ntiles=4, nlayers=2, dbg=None, stage=99):
    K = KB(ntiles, nlayers, dbg, stage)
    nc = K.nc
    xT = K.dram_in("xT", [D, T])
    cT = K.dram_in("cT", [128, 8])
    ada = [K.dram_in("ada%d" % l, [D, 6 * D]) for l in range(L)]
    vecs_d = [K.dram_in("vecs%d" % l, [128, NV]) for l in range(L)]
    lay = [_wbig_layout(l) for l in range(L)]
    wbig = [K.dram_in("wbig%d" % l, [128, lay[l][1]]) for l in range(L)]
    outT = K.dram_out("outT", [D, T])

    sb, ps = K.sb, K.ps
    ident16 = sb("ident16", [128, 128], BF16)
    ones16 = sb("ones16", [128, 128], BF16)
    bd16 = sb("bd16", [128, 128], BF16)
    onesf = sb("onesf", [128, 128], F32)
    mask512 = sb("mask512", [128, 512], BF16)
    masksl = sb("masksl", [128, 128], BF16)
    rst = sb("rst", [128, 512], BF16)
    vecs = [sb("vecs_s%d" % l, [128, NV], F32) for l in range(L)]
    dv = [sb("dv%d" % l, [128, ND], F32) for l in range(L)]
    c32 = sb("c32", [128, 8], F32)
    epsc = sb("epsc", [128, 4], F32)
    c16 = sb("c16", [128, 8], BF16)
    S32 = [sb("S32_%d" % l, [128, 8, 64], F32) for l in range(L)]
    S16 = [sb("S16_%d" % l, [128, 8, 128], BF16) for l in range(L)]
    ctail = [sb("ctail%d" % l, [128, 8, 30], F32) for l in range(L)]
    carry = [sb("carry%d" % l, [128, 27], F32) for l in range(L)]
    xt = [sb("xt%d" % k, [128, TT], F32) for k in range(NB)]
    ht = [sb("ht%d" % k, [128, TT], BF16) for k in range(NB)]
    merged = [sb("mg%d" % k, [128, TT], BF16) for k in range(16)]
    vf = [sb("vf%d" % k, [128, TT], BF16) for k in range(NB)]
    NWP = 3
    WPN = 4608
    wpool = [sb("wp%d" % i, [128, WPN], BF16) for i in range(NWP)]
    lo16 = [sb("lo16_%d" % j, [128, TT], BF16) for j in range(2)]
    vlo16 = sb("vlo16", [128, TT], BF16)
    names32 = ["r32", "k32", "v32", "sg32", "a32", "sv32", "dd32", "cs32", "E1", "E3",
               "kk32", "nrm", "km", "b32", "y32", "yc", "sd"]
    W = {n: sb(n, [128, TT], F32) for n in names32}
    W["d2"] = W["dd32"]; W["E2"] = W["dd32"]; W["d4"] = W["sv32"]; W["E4"] = W["sv32"]; W["rn"] = W["nrm"]
    W["kkn"] = W["kk32"]; W["t1"] = W["km"]; W["rsd"] = W["sd"]
    W["yn"] = W["yc"]; W["yg"] = W["yc"]; W["o1"] = W["yc"]; W["o2"] = W["yc"]
    names16 = ["kk2", "Kh16", "Bh16", "v16", "rk16", "y16", "yc2", "sqa", "sqb", "sqc"]
    H = {n: sb(n, [128, TT], BF16) for n in names16}
    sq16 = [H[n] for n in ("kk2", "Kh16", "Bh16", "v16", "rk16", "sqa", "sqb", "sqc")]
    g32p = [sb("g32_%d" % i, [128, TT], F32) for i in range(NHO)]
    bon32p = [sb("bon32_%d" % i, [128, TT], F32) for i in range(NHO)]
    gA16p = [sb("gA16_%d" % i, [128, TT], BF16) for i in range(NHO)]
    kt16p = [sb("kt16_%d" % i, [128, 2, TT], BF16) for i in range(NHO)]
    bt16p = [sb("bt16_%d" % i, [128, 2, TT], BF16) for i in range(NHO)]
    hmask = sb("hmask", [128, 2], F32)
    WCp = [sb("WC_%d" % i, [128, NCH], F32) for i in range(NHO)]
    lo32 = [W["yc"], W["y32"]]
    vlo32 = bon32p[0]
    rs32 = W["nrm"]; rstd = W["sd"]; t32 = [W["yc"], W["y32"]]
    ARp = [sb("AR_%d" % i, [128, NCH, 2, C], BF16) for i in range(NHO)]
    KBtokp = [sb("KBtok_%d" % i, [128, 8, 128], BF16) for i in range(NHO)]
    Vtokp = [sb("Vtok_%d" % i, [128, 4, 128], BF16) for i in range(NHO)]
    Amatp = [[[sb("Amat%d_%d_%d" % (q, h, c), [128, 512], BF16) for c in range(NCH)] for h in range(2)] for q in range(NQ_AMAT)] * (2 // NQ_AMAT)
    Tfinp = [[[sb("Tfin%d_%d_%d" % (q, h, c), [128, 128], BF16) for c in range(NCH)] for h in range(2)] for q in range(NQ_AMAT)] * (2 // NQ_AMAT)
    Q0 = [sb("Q0_%d" % i, [128, 128], BF16) for i in range(4)]
    PQT = [[sb("PQT%d_%d" % (i, j), [128, 3, 128], BF16) for j in range(2)] for i in range(4)]
    XT16 = sb("XT16", [128, 128], BF16)
    UT16 = sb("UT16", [128, 128], BF16)
    sgb = sb("sgb16", [128, TT], BF16)
    sgb2 = [sb("sgbB%d" % i, [128, TT], BF16) for i in range(2)]
    gbuf = sb("gbuf", [128, 30 + TT], F32)
    cacc = [sb("cacc%d" % i, [128, TT], F32) for i in range(2)]
    ctmp = [sb("ctmp%d" % i, [128, TT], F32) for i in range(2)]
    z16 = [sb("z16_%d" % k, [128, TT], BF16) for k in range(NB)]

    def halves(t):
        v = t[:].bitcast(BF16)
        return [v[:, 0:TT], v[:, TT:2 * TT]]
    hidv = []
    for t_ in [W[n] for n in ("r32", "k32", "v32", "sg32", "a32", "sv32", "dd32", "kk32", "cs32", "E1", "E3", "km", "b32")] + [g32p[0], g32p[1], bon32p[0]]:
        hidv += halves(t_)
    mmb = [ps("mmb%d" % i, [128, 512], F32) for i in range(2)]
    dblb = [ps("dblb%d" % i, [128, 512], F32) for i in range(4)]
    seqb = ps("seqb", [128, 512], F32)
    yb = ps("yb", [128, 512], F32)
    mm_rr = [0]

    def bank():
        b = mmb[mm_rr[0] % 2]
        mm_rr[0] += 1
        return b

    def dbl_region(i, j):
        if j < 2:
            col = (i % 2) * 256 + j * 128
            return dblb[i // 2][:, col:col + 128]
        return dblb[2][:, i * 128:(i + 1) * 128]

    wp_rr = [0]
    PF = 2
    piece_list = []
    for pc in range(12):
        piece_list.append(("ada", 0, pc))
    ada1_left = list(range(12)) if K.nlayers > 1 else []
    for it_ in range(K.ntiles):
        for l_ in range(K.nlayers):
            if l_ == 1:
                while ada1_left:
                    piece_list.append(("ada", 1, ada1_left.pop(0)))
            if K.stage >= 4:
                keys = ["lo", "hp0"]
                for i in range(8):
                    keys.append("cb%d" % i)
                    if i + 1 < 8:
                        keys.append("hp%d" % (i + 1))
                keys += ["gb%d" % i for i in range(8)] + ["wo%d" % i for i in range(4)]
            else:
                keys = ["lo"] + ["hp%d" % i for i in range(8)]
            if K.stage >= 5:
                keys += ["w1%d" % i for i in range(8)] + ["w2%d" % i for i in range(8)]
            for n_, key in enumerate(keys):
                piece_list.append(("w", l_, key))
                if it_ == 0 and l_ == 0 and n_ % 3 == 2 and ada1_left:
                    piece_list.append(("ada", 1, ada1_left.pop(0)))
    issued = [0]
    piece_dst = {}

    def _issue(idx):
        kind, l_, key = piece_list[idx]
        buf = wpool[idx % NWP]
        if kind == "ada":
            src = ada[l_].rearrange("(k p) c -> p k c", p=128)[:, :, key * 512:(key + 1) * 512]
            dst = buf[:, 0:4096].rearrange("p (k c) -> p k c", c=512)
        elif key.startswith("hp"):
            off, k, c = lay[l_][0][key]
            src = wbig[l_][:, off:off + 4608].rearrange("p (a b) -> p a b", b=512)
            K.dma("pool", buf[:, 0:4608].rearrange("p (a b) -> p a b", b=512), src)
            piece_dst[idx] = (buf[:, 0:4096].rearrange("p (k c) -> p k c", c=512), buf[:, 4096:4608].rearrange("p (a c) -> p a c", c=128))
            return
        else:
            off, k, c = lay[l_][0][key]
            src = wbig[l_][:, off:off + k * c].rearrange("p (k c) -> p k c", c=c)
            dst = buf[:, 0:k * c].rearrange("p (k c) -> p k c", c=c)
        K.dma("pool", dst, src)
        piece_dst[idx] = dst

    def next_piece(expect):
        idx = wp_rr[0]
        wp_rr[0] += 1
        assert piece_list[idx] == expect, (piece_list[idx], expect)
        while issued[0] < min(len(piece_list), idx + PF + 1):
            _issue(issued[0])
            issued[0] += 1
        return piece_dst.pop(idx)

    def ada_piece():
        kind, l_, pc = piece_list[wp_rr[0]]
        dst = next_piece((kind, l_, pc))
        p = bank()
        for m in range(4):
            K.mm(p[:, m:m + 1], [(dst[:, k, m * 128:(m + 1) * 128], c16[:, k:k + 1]) for k in range(8)])
        K.tt("dve", dv[l_][:, DV_MOD + 4 * pc:DV_MOD + 4 * pc + 4], p[:, 0:4], vecs[l_][:, V_ADAB + 4 * pc:V_ADAB + 4 * pc + 4], ALU.add)
        if pc == 11:
            K.stt(dv[l_][:, DV_GSCM:DV_GSCM + 8], dv[l_][:, DV_MOD + M_SCM:DV_MOD + M_SCM + 8], 1.0, vecs[l_][:, V_GMIX:V_GMIX + 8], ALU.add, ALU.mult)
            K.stt(dv[l_][:, DV_GSCF:DV_GSCF + 8], dv[l_][:, DV_MOD + M_SCF:DV_MOD + M_SCF + 8], 1.0, vecs[l_][:, V_GFFN:V_GFFN + 8], ALU.add, ALU.mult)
            K.ts("dve", dv[l_][:, DV_OMU:DV_OMU + 27], vecs[l_][:, V_MU:V_MU + 27], -1.0, 1.0, ALU.mult, ALU.add)
            K.dump("dv%d" % l_, dv[l_][:], [128, ND])

    def load_piece(l, key):
        while wp_rr[0] < len(piece_list) and piece_list[wp_rr[0]][0] == "ada":
            ada_piece()
        return next_piece(("w", l, key))

    K.memset("pool", onesf[:], 1.0)
    K.memset("pool", ones16[:], 1.0)
    K.memset("pool", bd16[:], 0.0)
    K.memset("pool", bd16[0:64, 0:64], 1.0)
    K.memset("pool", bd16[64:128, 64:128], 1.0)

    def asel(out, in_, pattern, op, base, cm):
        K.S.op("pool", lambda e: e.affine_select(out=out, in_=in_, pattern=pattern, compare_op=op, fill=0.0, base=base, channel_multiplier=cm),
               [_tok(in_)], [_tok(out)], dur=0.3)
    asel(ident16[:], ones16[:], [[1, 128]], ALU.is_equal, 0, -1)
    for j in range(4):
        asel(mask512[:, j * 128:(j + 1) * 128], onesf[:, 0:128], [[1, 128]], ALU.is_gt if j % 2 == 0 else ALU.is_ge, 0, -1)
    asel(masksl[:], onesf[:, 0:128], [[-1, 128]], ALU.is_gt, 0, 1)
    K.memset("pool", rst[:], 1.0)
    for j in range(NCH):
        K.memset("pool", rst[:, j * C:j * C + 1], 0.0)
    for l in range(L):
        K.dma("sp", vecs[l][:], vecs_d[l])
        K.memset("pool", S32[l][:], 0.0)
        K.memset("pool", S16[l][:], 0.0)
        K.memset("pool", ctail[l][:], 0.0)
        K.memset("pool", carry[l][:], 0.0)
    K.dma("sp", c32[:], cT)
    K.memset("pool", hmask[:], 0.0)
    K.memset("pool", vlo16[:], 0.0)
    K.memset("pool", hmask[0:64, 0:1], 1.0)
    K.memset("pool", hmask[64:128, 1:2], 1.0)
    K.memset("pool", epsc[:, 0:1], RMS_EPS)
    K.memset("pool", epsc[:, 1:2], LN_EPS)
    K.memset("pool", epsc[:, 2:3], GN_EPS)
    K.memset("pool", epsc[:, 3:4], 1e-24)
    K.act(c16[:], c32[:], AF.Silu)

    for pc in range(12):
        ada_piece()

    def rmsnorm_to_ht(l, gsc_col, sh_col):
        for k in range(NB):
            K.act(sq16[k][:], xt[k][:], AF.Square)
        p = bank()
        K.mm(p[:], [(ones16[:], sq16[k][:]) for k in range(NB)])
        K.act(rs32[:], p[:], AF.Ln, scale=1.0 / D, bias=epsc[:, 0:1])
        K.act(rstd[:], rs32[:], AF.Exp, scale=-0.5)
        for k in range(NB):
            t = t32[k % 2]
            K.stt(t[:], xt[k][:], dv[l][:, gsc_col + k:gsc_col + k + 1], rstd[:], ALU.mult, ALU.mult)
            K.act(ht[k][:], t[:], AF.Identity, bias=dv[l][:, sh_col + k:sh_col + k + 1])

    def proj(wp, j0, M=128):
        p = bank()
        K.mm(p[0:M, :], [(wp[:, k, j0:j0 + M], ht[k][:]) for k in range(NB)])
        return p

    def shift(l, p, mucol, dst, M=128):
        mu = vecs[l][0:M, V_MU + mucol:V_MU + mucol + 1]
        omu = dv[l][0:M, DV_OMU + mucol:DV_OMU + mucol + 1]
        cy = carry[l][0:M, mucol:mucol + 1]
        K.act(dst[0:M, :], p[0:M, :], AF.Identity, scale=omu)
        K.stt(dst[0:M, 1:TT], p[0:M, 0:TT - 1], mu, dst[0:M, 1:TT], ALU.mult, ALU.add)
        K.stt(dst[0:M, 0:1], cy, mu, dst[0:M, 0:1], ALU.mult, ALU.add)
        K.copy("dve", cy, p[0:M, TT - 1:TT])

    xTv = xT.rearrange("(k p) t -> k p t", p=128)
    oTv = outT.rearrange("(k p) t -> k p t", p=128)
    for it in range(K.ntiles):
        t0 = it * TT
        for k in range(NB):
            K.dma("sp", xt[k][:], xTv[k, :, t0:t0 + TT])
        for l in range(K.nlayers):
            V = vecs[l]
            DVl = dv[l]
            rmsnorm_to_ht(l, DV_GSCM, DV_MOD + M_SHM)
            if it == 0 and l == 0:
                for k in (0, 7):
                    K.dump("ht%d" % k, ht[k][:], [128, TT])
            wlo = load_piece(l, "lo")
            for j in range(2):
                p = proj(wlo, j * 128)
                shift(l, p, 24 + j, lo32[j])
            K.act(lo16[0][0:64, :], lo32[0][0:64, :], AF.Tanh)
            K.copy("act", lo16[0][64:128, :], lo32[0][64:128, :])
            K.act(lo16[1][:], lo32[1][:], AF.Sigmoid)
            if l >= 1:
                p = proj(wlo, 256, M=32)
                shift(l, p, 26, vlo32, M=32)
                K.copy("act", vlo16[0:32, :], vlo32[0:32, :])
            if K.stage < 1:
                continue
            def front(hp, par):
                AR = ARp[par]; KBtok = KBtokp[par]; Vtok = Vtokp[par]
                g32 = g32p[par]; bon32 = bon32p[par]; gA16 = gA16p[par]; kt16 = kt16p[par]; bt16 = bt16p[par]; WC = WCp[par]
                whp, wsl = load_piece(l, "hp%d" % hp)
                p = proj(whp, 0);   shift(l, p, hp, W["r32"])
                yield
                p = proj(whp, 128); shift(l, p, 8 + hp, W["k32"])
                yield
                p = proj(whp, 256); shift(l, p, 16 + hp, W["v32"])
                yield
                p = proj(whp, 384); K.act(gA16[:], p[:], AF.Sigmoid)
                p = bank(); K.mm(p[:], [(wsl[:, 0, :], lo16[0][:])])
                K.act(W["sg32"][:], p[:], AF.Sigmoid, bias=V[:, V_W0 + hp:V_W0 + hp + 1])
                yield
                p = bank(); K.mm(p[:], [(wsl[:, 3, :], lo16[0][:])])
                K.act(W["a32"][:], p[:], AF.Sigmoid, bias=V[:, V_A0 + hp:V_A0 + hp + 1])
                p = bank(); K.mm(p[:], [(wsl[:, 1, :], lo16[1][:])])
                K.tt("dve", g32[:], p[:], gA16[:], ALU.mult)
                yield
                if l >= 1:
                    p = bank(); K.mm(p[:], [(wsl[:, 2, :], vlo16[:])])
                    K.act(W["sv32"][:], p[:], AF.Sigmoid, bias=V[:, V_V0 + hp:V_V0 + hp + 1])
                    K.tt("dve", W["dd32"][:], vf[hp][:], W["v32"][:], ALU.subtract)
                    K.tt("dve", W["dd32"][:], W["dd32"][:], W["sv32"][:], ALU.mult)
                    K.tt("dve", W["v32"][:], W["v32"][:], W["dd32"][:], ALU.add)
                else:
                    K.copy("act" if CP_MOVE else "pool", vf[hp][:], W["v32"][:])
                yield
                r3 = lambda t_: t_[:].rearrange("p (c t) -> p c t", t=C)
                K.scan(W["cs32"][:], rst[:], W["sg32"][:])
                K.act(W["E1"][:], W["cs32"][:], AF.Exp, scale=CEXP)
                K.act(W["E3"][:], W["cs32"][:], AF.Exp, scale=-CEXP)
                K.copy("dve", WC[:], r3(W["E1"])[:, :, C - 1])
                yield
                K.act(H["kk2"][:], W["k32"][:], AF.Square, scale=V[:, V_KK + hp:V_KK + hp + 1])
                p = bank(); K.mm(p[:], [(bd16[:], H["kk2"][:])])
                K.act(W["nrm"][:], p[:], AF.Ln, bias=epsc[:, 3:4])
                K.act(W["rn"][:], W["nrm"][:], AF.Exp, scale=-0.5)
                yield
                K.stt(W["kkn"][:], W["k32"][:], V[:, V_KK + hp:V_KK + hp + 1], W["rn"][:], ALU.mult, ALU.mult)
                K.ts("dve", W["t1"][:], W["a32"][:], -1.0, V[:, V_KA + hp:V_KA + hp + 1], ALU.add, ALU.mult)
                K.stt(W["km"][:], W["t1"][:], 1.0, W["k32"][:], ALU.add, ALU.mult)
                K.tt("dve", W["b32"][:], W["kkn"][:], W["a32"][:], ALU.mult)
                yield
                K.tt("dve", AR[:, :, 1, :], r3(W["r32"]), r3(W["E1"]), ALU.mult)
                K.stt(AR[:, :, 0, 1:C], r3(W["kkn"])[:, :, 1:C], -1.0, r3(W["E1"])[:, :, 0:C - 1], ALU.mult, ALU.mult)
                K.ts("dve", AR[:, :, 0, 0:1], r3(W["kkn"])[:, :, 0:1], -1.0, None, ALU.mult)
                K.tt("dve", W["d4"][:], W["km"][:], W["E3"][:], ALU.mult)
                K.tt("dve", W["d2"][:], W["b32"][:], W["E3"][:], ALU.mult)
                yield
                for h_ in range(2):
                    if OFF & 8:
                        K.ts("pool", kt16[:, h_, :], W["d4"][:], hmask[:, h_:h_ + 1], None, ALU.mult)
                        K.ts("pool", bt16[:, h_, :], W["d2"][:], hmask[:, h_:h_ + 1], None, ALU.mult)
                    elif OFF & 1:
                        K.act(kt16[:, h_, :], W["d4"][:], AF.Identity, scale=hmask[:, h_:h_ + 1])
                        K.act(bt16[:, h_, :], W["d2"][:], AF.Identity, scale=hmask[:, h_:h_ + 1])
                    else:
                        K.ts("dve", kt16[:, h_, :], W["d4"][:], hmask[:, h_:h_ + 1], None, ALU.mult)
                        K.ts("dve", bt16[:, h_, :], W["d2"][:], hmask[:, h_:h_ + 1], None, ALU.mult)
                wcb = WC[:].unsqueeze(2).to_broadcast([128, NCH, C])
                K.tt("dve", r3(H["Kh16"]), r3(W["d4"]), wcb, ALU.mult)
                K.tt("dve", r3(H["Bh16"]), r3(W["d2"]), wcb, ALU.mult)
                K.copy("dve" if CP_MOVE else "pool", H["v16"][:], W["v32"][:])
                yield
                K.stt(H["rk16"][:], W["r32"][:], V[:, V_RK + hp:V_RK + hp + 1], W["km"][:], ALU.mult, ALU.mult)
                p = bank(); K.mm(p[:], [(bd16[:], H["rk16"][:])])
                K.tt("dve", bon32[:], p[:], W["v32"][:], ALU.mult)
                yield
                trb = bank()
                trv = trb[:].bitcast(BF16).rearrange("p (a t) -> p a t", t=128)
                for c in range(NCH):
                    K.tr(trv[:, c, :], H["Kh16"][:, c * C:(c + 1) * C], ident16[:])
                    K.tr(trv[:, 4 + c, :], H["Bh16"][:, c * C:(c + 1) * C], ident16[:])
                K.copy("act", KBtok[:], trv)
                yield
                trb = bank()
                trv = trb[:].bitcast(BF16).rearrange("p (a t) -> p a t", t=128)
                for c in range(NCH):
                    K.tr(trv[:, c, :], H["v16"][:, c * C:(c + 1) * C], ident16[:])
                K.copy("dve", Vtok[:], trv[:, 0:4, :])
                yield

            def back(hp, par):
                Amat = Amatp[hp % 2]; Tfin = Tfinp[hp % 2]
                AR = ARp[par]; KBtok = KBtokp[par]; Vtok = Vtokp[par]
                g32 = g32p[par]; bon32 = bon32p[par]; gA16 = gA16p[par]; kt16 = kt16p[par]; bt16 = bt16p[par]; WC = WCp[par]
                for h in range(2):
                    hs = slice(h * 64, h * 64 + 64)
                    for c in range(NCH):
                        ck = slice(c * C, (c + 1) * C)
                        bk = dblb[c]
                        ar = AR[:, c, :, :].rearrange("p a t -> p (a t)")
                        K.mm(bk[:, 0:256], [(bt16[:, h, ck], ar)])
                        K.mm(bk[:, 256:512], [(kt16[:, h, ck], ar)])
                        if MASK_POOL:
                            K.copy("act", Amat[h][c][:], bk[:])
                            am = Amat[h][c][:].rearrange("p (j i t) -> p j i t", j=2, i=2)
                            K.S.op("pool", (lambda am=am: (lambda e: e.affine_select(out=am, in_=am, pattern=[[0, 2], [1, 2], [1, 128]], compare_op=ALU.is_gt,
                                                                                     fill=0.0, base=0, channel_multiplier=-1)))(),
                                   [_tok(am)], [_tok(am)], dur=0.55)
                        else:
                            K.tt("dve", Amat[h][c][:], bk[:], mask512[:], ALU.mult)
                        K.mm(bk[:, 0:128], [(AR[:, c, 0, :], bt16[:, h, ck])])
                        if MASK_POOL:
                            K.copy("act", Q0[c][:], bk[:, 0:128])
                            q0 = Q0[c][:]
                            K.S.op("pool", (lambda q0=q0: (lambda e: e.affine_select(out=q0, in_=q0, pattern=[[-1, 128]], compare_op=ALU.is_gt,
                                                                                     fill=0.0, base=0, channel_multiplier=1)))(),
                                   [_tok(q0)], [_tok(q0)], dur=0.25)
                        else:
                            K.tt("dve", Q0[c][:], bk[:, 0:128], masksl[:], ALU.mult)
                        K.tt("dve", PQT[c][1][:, 1, :], Amat[h][c][:, 0:128], ident16[:], ALU.add)
                        yield
                    for c in range(NCH):
                        bk = dblb[c]
                        P0 = Amat[h][c][:, 0:128]
                        K.mm(bk[:, 0:128], [(Q0[c][:], P0)])
                        K.mm(bk[:, 256:384], [(P0, Q0[c][:])])
                        K.copy("act" if c % 2 == 0 else "dve", PQT[c][1][:, 0:3:2, :], bk[:, 0:384].rearrange("p (a t) -> p a t", t=128)[:, 0:3:2, :])
                    yield
                    for lv in range(1, 7):
                        cur, nxt = lv % 2, (lv + 1) % 2
                        for c in range(NCH):
                            bk = dblb[c]
                            Pk = PQT[c][cur][:, 0, :]
                            Tk = PQT[c][cur][:, 1, :]
                            Qk = PQT[c][cur][:, 2, :]
                            PTk = PQT[c][cur][:, 0:2, :].rearrange("p a t -> p (a t)")
                            eng = "dve" if (c + lv) % EVAC_MOD == 0 else "act"
                            if lv < 6:
                                K.mmx([(bk[:, 256:384], Pk, Qk, True, True),
                                       (bk[:, 0:256], Qk, PTk, True, False),
                                       (bk[:, 128:256], ident16[:], Tk, False, True)])
                                K.copy(eng, PQT[c][nxt][:], bk[:, 0:384].rearrange("p (a t) -> p a t", t=128))
                            else:
                                K.mm(bk[:, 128:256], [(Qk, Tk), (ident16[:], Tk)])
                                K.copy(eng, Tfin[h][c][:], bk[:, 128:256])
                        yield
                h0 = slice(0, 64); h1 = slice(64, 128)
                for c in range(NCH):
                    ck = slice(c * C, (c + 1) * C)
                    K.mmx([(seqb[:, 0:128], AR[:, c, 0, :], S16[l][:, hp, :], True, False),
                           (seqb[:, 0:64], Amat[0][c][:, 256:384], Vtok[:, c, h0], False, False),
                           (seqb[:, 64:128], Amat[1][c][:, 256:384], Vtok[:, c, h1], False, True)])
                    K.copy("act", XT16[:], seqb[:, 0:128])
                    yield
                    for h in range(2):
                        hs = slice(h * 64, h * 64 + 64)
                        K.mm(seqb[:, 128 + h * 64:128 + (h + 1) * 64], [(Tfin[h][c][:], XT16[:, hs])])
                    K.copy("act" if OFF & 2 else "dve", UT16[:], seqb[:, 128:256])
                    yield
                    K.mmx([(yb[:, ck], S16[l][:, hp, :], AR[:, c, 1, :], True, False),
                           (yb[h0, ck], UT16[:, h0], Amat[0][c][:, 128:256], False, False),
                           (yb[h0, ck], Vtok[:, c, h0], Amat[0][c][:, 384:512], False, False),
                           (yb[h1, ck], UT16[:, h1], Amat[1][c][:, 128:256], False, False),
                           (yb[h1, ck], Vtok[:, c, h1], Amat[1][c][:, 384:512], False, True)])
                    for h in range(2):
                        hs = slice(h * 64, h * 64 + 64)
                        K.mm(seqb[hs, 256:320], [(KBtok[:, 4 + c, hs], UT16[:, hs]),
                                                  (KBtok[:, c, hs], Vtok[:, c, hs])])
                    K.stt(S32[l][:, hp, :], S32[l][:, hp, :], WC[:, c:c + 1], seqb[:, 256:320], ALU.mult, ALU.add)
                    K.copy("act", S16[l][h0, hp, 0:64], S32[l][h0, hp, :])
                    K.copy("act" if OFF & 2 else "dve", S16[l][h1, hp, 64:128], S32[l][h1, hp, :])
                    yield
                K.copy("act", W["y32"][:], yb[:])
                K.copy("act" if OFF & 2 else "dve", H["y16"][:], yb[:])
                if it == 0 and l == 0 and hp == 0:
                    K.dump("y32", W["y32"][:], [128, TT])
                yield
                p = bank(); K.mm(p[:], [(bd16[:], H["y16"][:])])
                K.stt(W["yc"][:], p[:], -1.0 / 64, W["y32"][:], ALU.mult, ALU.add)
                yield
                K.act(H["yc2"][:], W["yc"][:], AF.Square)
                p = bank(); K.mm(p[:], [(bd16[:], H["yc2"][:])])
                K.act(W["sd"][:], p[:], AF.Ln, scale=1.0 / 64, bias=epsc[:, 2:3])
                yield
                K.act(W["rsd"][:], W["sd"][:], AF.Exp, scale=-0.5)
                K.stt(W["yn"][:], W["yc"][:], V[:, V_GNG + hp:V_GNG + hp + 1], W["rsd"][:], ALU.mult, ALU.mult)
                K.stt(W["o1"][:], W["yn"][:], V[:, V_GNB + hp:V_GNB + hp + 1], bon32[:], ALU.add, ALU.add)
                yield
                K.tt("dve", merged[hp][:], W["o1"][:], g32[:], ALU.mult)
                if it == 0 and l == 0 and hp in (0, 7):
                    K.dump("mg%d" % hp, merged[hp][:], [128, TT], BF16)
                yield

            def drain(g):
                for _ in g:
                    pass

            def interleave(g1, g2):
                a1 = a2 = True
                while a1 or a2:
                    if a1:
                        try:
                            next(g1)
                        except StopIteration:
                            a1 = False
                    if a2:
                        try:
                            next(g2)
                        except StopIteration:
                            a2 = False

            def conv(cb):
                wcb = load_piece(l, "cb%d" % cb)
                pb = proj(wcb, 128)
                K.act(sgb[:], pb[:], AF.Sigmoid)
                pa = proj(wcb, 0)
                K.copy("act", gbuf[:, 0:30], ctail[l][:, cb, :])
                K.tt("dve", gbuf[:, 30:30 + TT], pa[:], sgb[:], ALU.mult)
                K.copy("act", ctail[l][:, cb, :], gbuf[:, TT:TT + 30])
                cw = lambda k: V[:, V_CW + cb * 31 + k:V_CW + cb * 31 + k + 1]
                NT_DVE = NT_DVE_G
                if NT_DVE >= 31:
                    if OFF & 4:
                        K.act(cacc[0][:], gbuf[:, 0:TT], AF.Identity, scale=cw(0), bias=V[:, V_CB + cb:V_CB + cb + 1])
                        K.act(cacc[1][:], gbuf[:, 1:1 + TT], AF.Identity, scale=cw(1))
                    else:
                        K.ts("dve", cacc[0][:], gbuf[:, 0:TT], cw(0), V[:, V_CB + cb:V_CB + cb + 1], ALU.mult, ALU.add)
                        K.ts("dve", cacc[1][:], gbuf[:, 1:1 + TT], cw(1), None, ALU.mult)
                    for k in range(2, 31):
                        a_ = cacc[k % 2]
                        K.stt(a_[:], gbuf[:, k:k + TT], cw(k), a_[:], ALU.mult, ALU.add)
                    K.tt("dve", z16[cb][:], cacc[0][:], cacc[1][:], ALU.add)
                    wgb = None
                    return
                K.ts("dve", cacc[0][:], gbuf[:, 0:TT], cw(0), V[:, V_CB + cb:V_CB + cb + 1], ALU.mult, ALU.add)
                for k in range(1, NT_DVE):
                    K.stt(cacc[0][:], gbuf[:, k:k + TT], cw(k), cacc[0][:], ALU.mult, ALU.add)
                K.act(cacc[1][:], gbuf[:, NT_DVE:NT_DVE + TT], AF.Identity, scale=cw(NT_DVE))
                for k in range(NT_DVE + 1, 31):
                    tp_ = ctmp[k % 2]
                    K.act(tp_[:], gbuf[:, k:k + TT], AF.Identity, scale=cw(k))
                    K.tt("pool", cacc[1][:], cacc[1][:], tp_[:], ALU.add)
                K.tt("pool", z16[cb][:], cacc[0][:], cacc[1][:], ALU.add)

            if K.stage < 3:
                for hp in range(8):
                    drain(front(hp, hp % NHO))
            else:
                drain(front(0, 0))
                for hp in range(8):
                    if K.stage >= 4:
                        conv(hp)
                    if hp + 1 < 8:
                        interleave(back(hp, hp % NHO), front(hp + 1, (hp + 1) % NHO))
                    else:
                        drain(back(hp, hp % NHO))
            if K.stage < 4:
                continue
            if it == 0 and l == 0:
                K.dump("z0", z16[0][:], [128, TT], BF16)
            p = bank(); K.mm(p[:], [(ones16[:], z16[cb][:]) for cb in range(8)])
            K.act(W["nrm"][:], p[:], AF.Identity, scale=-1.0 / D)
            for cb in range(8):
                t = t32[cb % 2]
                K.tt("dve", t[:], z16[cb][:], W["nrm"][:], ALU.add)
                K.act(sq16[cb][:], t[:], AF.Square)
            p = bank(); K.mm(p[:], [(ones16[:], sq16[cb][:]) for cb in range(8)])
            K.act(W["sd"][:], p[:], AF.Ln, scale=1.0 / D, bias=epsc[:, 1:2])
            K.act(W["sd"][:], W["sd"][:], AF.Exp, scale=-0.5)
            for cb in range(8):
                t = t32[cb % 2]
                wgb = load_piece(l, "gb%d" % cb)
                pg = proj(wgb, 0)
                sg_ = sgb2[cb % 2]
                K.act(sg_[:], pg[:], AF.Sigmoid)
                K.tt("dve", t[:], z16[cb][:], W["nrm"][:], ALU.add)
                K.tt("dve", t[:], t[:], W["sd"][:], ALU.mult)
                K.act(t[:], t[:], AF.Silu, scale=V[:, V_LNG + cb:V_LNG + cb + 1], bias=V[:, V_LNB + cb:V_LNB + cb + 1])
                K.tt("pool", merged[8 + cb][:], t[:], sg_[:], ALU.mult)
            if it == 0 and l == 0:
                K.dump("mg8", merged[8][:], [128, TT], BF16)
            for g in range(4):
                wo = load_piece(l, "wo%d" % g)
                for o2 in range(2):
                    ob = g * 2 + o2
                    p = bank()
                    K.mm(p[:], [(wo[:, o2 * 16 + kc, :], merged[kc][:]) for kc in range(16)])
                    K.stt(xt[ob][:], p[:], DVl[:, DV_MOD + M_GTM + ob:DV_MOD + M_GTM + ob + 1], xt[ob][:], ALU.mult, ALU.add)
            if it == 0 and l == 0:
                K.dump("xmix0", xt[0][:], [128, TT])
            if K.stage < 5:
                continue
            rmsnorm_to_ht(l, DV_GSCF, DV_MOD + M_SHF)
            for pc in range(8):
                w1 = load_piece(l, "w1%d" % pc)
                for j in range(4):
                    hb = pc * 4 + j
                    p = proj(w1, j * 128)
                    t = t32[hb % 2]
                    K.act(t[:], p[:], AF.Relu)
                    K.tt(HID_ENG, hidv[hb], t[:], t[:], ALU.mult)
            for ob in range(8):
                w2 = load_piece(l, "w2%d" % ob)
                p = bank()
                K.mm(p[:], [(w2[:, hb, :], hidv[hb]) for hb in range(32)])
                K.stt(xt[ob][:], p[:], DVl[:, DV_MOD + M_GTF + ob:DV_MOD + M_GTF + ob + 1], xt[ob][:], ALU.mult, ALU.add)
            if it == 0 and l == 0:
                K.dump("xffn0", xt[0][:], [128, TT])
        for k in range(NB):
            K.act(sq16[k][:], xt[k][:], AF.Square)
        p = bank()
        K.mm(p[:], [(ones16[:], sq16[k][:]) for k in range(NB)])
        K.act(rs32[:], p[:], AF.Ln, scale=1.0 / D, bias=epsc[:, 0:1])
        K.act(rstd[:], rs32[:], AF.Exp, scale=-0.5)
        for k in range(NB):
            t = t32[k % 2]
            K.stt(t[:], xt[k][:], vecs[0][:, V_FG + k:V_FG + k + 1], rstd[:], ALU.mult, ALU.mult)
            K.dma("sp", oTv[k, :, t0:t0 + TT], t[:], w=["outT%d_%d" % (it, k)])
            K.out_tokens.append("outT%d_%d" % (it, k))
    K.S.wait_all("sp", K.out_tokens)
    K.S.emit(K.st)
    K.st.close()
    return K


def dblb_view(dblb, c):
    col = (c % 2) * 256
    return dblb[c // 2][:, col:col + 256].rearrange("p (a t) -> p a t", t=128)


def _fm(v):
    v = np.asarray(v, np.float32)
    return np.ascontiguousarray(v.reshape(-1, 128).T)


def prep_shared(inp):
    shared = {}
    for l in range(L):
        vec = np.zeros((128, NV), np.float32)
        vec[:, V_GMIX:V_GMIX + 8] = _fm(inp["norm_mix_gain"][l])
        vec[:, V_GFFN:V_GFFN + 8] = _fm(inp["norm_ffn_gain"][l])
        vec[:, V_MU:V_MU + 26] = _fm(inp["mu_shift"][l])
        if l >= 1:
            vec[0:32, V_MUV] = inp["mu_vres"][l - 1]
            vec[:, V_V0:V_V0 + 8] = _fm(inp["v0"][l - 1])
        vec[:, V_W0:V_W0 + 8] = _fm(inp["w0"][l])
        vec[:, V_A0:V_A0 + 8] = _fm(inp["a0"][l])
        vec[:, V_KK:V_KK + 8] = _fm(inp["k_k"][l])
        vec[:, V_KA:V_KA + 8] = _fm(inp["k_a"][l])
        vec[:, V_RK:V_RK + 8] = _fm(inp["r_k"][l].reshape(-1))
        vec[:, V_GNG:V_GNG + 8] = _fm(inp["gn_gain"][l])
        vec[:, V_GNB:V_GNB + 8] = _fm(inp["gn_bias"][l])
        vec[:, V_CB:V_CB + 8] = _fm(inp["conv_b"][l])
        vec[:, V_LNG:V_LNG + 8] = _fm(inp["conv_ln_gain"][l])
        vec[:, V_LNB:V_LNB + 8] = _fm(inp["conv_ln_bias"][l])
        vec[:, V_FG:V_FG + 8] = _fm(inp["final_gain"])
        vec[:, V_ADAB:V_ADAB + 48] = _fm(inp["ada_b"][l])
        cw = np.asarray(inp["conv_w"][l], np.float32)
        vec[:, V_CW:V_CW + 248] = cw.reshape(31, 8, 128).transpose(2, 1, 0).reshape(128, 248)
        shared["vecs%d" % l] = vec
        wsm = np.zeros((128, 4, D), np.float32)
        wsm[0:64, 0] = inp["w_decay_up"][l]
        wsm[64:128, 3] = inp["w_aaa_up"][l]
        wsm[:, 1] = inp["w_gate_up"][l]
        if l >= 1:
            wsm[0:32, 2] = inp["w_vres_up"][l - 1]
        shared["ada%d" % l] = np.ascontiguousarray(inp["ada_w"][l], dtype=np.float32)
        lay, tot = _wbig_layout(l)
        wb = np.empty((128, tot), np.float32)
        win = np.asarray(inp["w_in"][l], np.float32)
        if l >= 1:
            win = np.concatenate([win, np.asarray(inp["w_in_vres"][l - 1], np.float32)], axis=1)

        def put(key, cols_matrix):
            off, k, c = lay[key]
            wb[:, off:off + k * c] = cols_matrix.reshape(k, 128, c).transpose(1, 0, 2).reshape(128, k * c)
        lo_idx = list(range(3072, 3328)) + (list(range(N_COLS, N_COLS + 32)) if l >= 1 else [])
        put("lo", win[:, lo_idx])
        for hp in range(8):
            idx = np.concatenate([np.arange(hp * 128, hp * 128 + 128), 1024 + np.arange(hp * 128, hp * 128 + 128),
                                  2048 + np.arange(hp * 128, hp * 128 + 128), 5376 + np.arange(hp * 128, hp * 128 + 128)])
            put("hp%d" % hp, win[:, idx])
            off_, k_, c_ = lay["hp%d" % hp]
            wb[:, off_ + k_ * c_: off_ + k_ * c_ + 512] = wsm[:, :, hp * 128:(hp + 1) * 128].reshape(128, 512)
        for cb in range(8):
            idx = np.concatenate([3328 + np.arange(cb * 128, cb * 128 + 128), 4352 + np.arange(cb * 128, cb * 128 + 128)])
            put("cb%d" % cb, win[:, idx])
            put("gb%d" % cb, win[:, 6400 + np.arange(cb * 128, cb * 128 + 128)])
        wo = np.asarray(inp["w_out"][l], np.float32)
        for g in range(4):
            off, k, c = lay["wo%d" % g]
            blk = wo[:, g * 256:(g + 1) * 256].reshape(16, 128, 2, 128)
            wb[:, off:off + k * c] = blk.transpose(1, 2, 0, 3).reshape(128, 32 * 128)
        w1 = np.asarray(inp["w_ff_in"][l], np.float32)
        for pc in range(8):
            put("w1%d" % pc, w1[:, pc * 512:(pc + 1) * 512])
        w2 = np.asarray(inp["w_ff_out"][l], np.float32)
        for ob in range(8):
            put("w2%d" % ob, w2[:, ob * 128:(ob + 1) * 128])
        shared["wbig%d" % l] = wb
    return shared


def prep_core(inp, b):
    return {"xT": np.ascontiguousarray(np.asarray(inp["x"][b], np.float32).T),
            "cT": _fm(inp["c"][b])}


_CACHE = {}


def kernel(**inputs):
    inp = {k: np.asarray(v) for k, v in inputs.items()}
    if "K" not in _CACHE:
        _CACHE["K"] = build()
    K = _CACHE["K"]
    shared = prep_shared(inp)
    in_maps = []
    for b in range(NCORES):
        m = dict(shared)
        m.update(prep_core(inp, b))
        in_maps.append(m)
    res = run_bass_kernel_spmd(K.nc, in_maps, core_ids=list(range(NCORES)))
    out = np.stack([np.ascontiguousarray(res.results[b]["outT"].T) for b in range(NCORES)], axis=0)
    return out.astype(np.float32)
```
